# Optimizing a Trainium2 kernel written in Bass

```python
import math
import jax
import jax.numpy as jnp
from jax import lax
import numpy as np

D_MODEL = 1024
BATCH = 16
SEQ = 2048
DEPTH = 4

EPS = 1e-6
ROPE_THETA = 500000.0
Q_BLOCK = 128
D_FF = 2816
N_ADA = 9

D_RNN = 512
RNN_HEADS = 8
RNN_HEAD_DIM = D_RNN // RNN_HEADS
CONV_WIDTH = 4
LRU_C = 8.0

MLA_HEADS = 8
MLA_NOPE = 64
MLA_ROPE = 32
MLA_V = 64
MLA_QK = MLA_ROPE + MLA_NOPE
MLA_Q_LORA = 256
MLA_KV_LORA = 128

DSA_HEADS = 8
DSA_HEAD_DIM = 64
DSA_ROT = DSA_HEAD_DIM // 4
IDX_HEADS = 8
IDX_DIM = 32
IDX_ROT = IDX_DIM // 4
TOPK_MAX = 256

S5_GROUP = 16
S5_GROUPS = 32
D_S5 = S5_GROUP * S5_GROUPS
S5_STATE = 64

N_BRANCH = 4
BRANCH_W = 512
IN_SPLITS = (D_RNN, D_RNN, MLA_Q_LORA, MLA_KV_LORA, MLA_ROPE,
             DSA_HEADS * DSA_HEAD_DIM, DSA_HEAD_DIM, DSA_HEAD_DIM,
             IDX_HEADS * IDX_DIM, IDX_DIM, IDX_HEADS, D_S5, N_BRANCH * D_MODEL)
D_IN = 6984

kernel_name = 'hybrid_gated_rglru_mla_dsa_s5_block'


def rmsnorm(x, g):
    xf = x.astype(jnp.float32)
    y = xf * lax.rsqrt(jnp.mean(xf * xf, axis=-1, keepdims=True) + EPS)
    return (y * g.astype(jnp.float32)).astype(x.dtype)


def modulate(x, shift, scale):
    return x * (1.0 + scale) + shift


def swiglu(u, w1, w3, w2):
    return (jax.nn.silu(u @ w1) * (u @ w3)) @ w2


def rope_tables(positions, rot_dim):
    inv = ROPE_THETA ** (-jnp.arange(0, rot_dim, 2, dtype=jnp.float32) / rot_dim)
    ang = positions.astype(jnp.float32)[..., None] * inv
    return jnp.cos(ang), jnp.sin(ang)


def apply_rope(x, cs, rot_dim):
    cos, sin = cs
    if x.ndim == 4:
        cos, sin = cos[:, :, None, :], sin[:, :, None, :]
    cos, sin = cos.astype(x.dtype), sin.astype(x.dtype)
    half = rot_dim // 2
    x1, x2, rest = x[..., :half], x[..., half:rot_dim], x[..., rot_dim:]
    return jnp.concatenate([x1 * cos - x2 * sin, x2 * cos + x1 * sin, rest], axis=-1)


def to_blocks(a):
    b, t = a.shape[:2]
    return a.reshape((b, t // Q_BLOCK, Q_BLOCK) + a.shape[2:]).swapaxes(0, 1)


def from_blocks(o):
    nb, b, qb = o.shape[:3]
    return o.swapaxes(0, 1).reshape((b, nb * qb, -1))


def linear_combine(e1, e2):
    a1, b1 = e1
    a2, b2 = e2
    return a1 * a2, a2 * b1 + b2


def complex_combine(e1, e2):
    a1r, a1i, b1r, b1i = e1
    a2r, a2i, b2r, b2i = e2
    return (a2r * a1r - a2i * a1i, a2r * a1i + a2i * a1r,
            a2r * b1r - a2i * b1i + b2r, a2r * b1i + a2i * b1r + b2i)


def rglru_branch(x_rnn, gate_rnn, conv_w, conv_b, wa, ba, wx, bx, lam):
    b, t, _ = x_rnn.shape
    xc = lax.conv_general_dilated(x_rnn, conv_w[:, None, :], window_strides=(1,),
                                  padding=[(CONV_WIDTH - 1, 0)],
                                  dimension_numbers=('NWC', 'WIO', 'NWC'),
                                  feature_group_count=D_RNN) + conv_b
    xh = xc.reshape(b, t, RNN_HEADS, RNN_HEAD_DIM)
    r = jax.nn.sigmoid((jnp.einsum('bthi,hij->bthj', xh, wa).reshape(b, t, D_RNN) + ba).astype(jnp.float32))
    ig = jax.nn.sigmoid((jnp.einsum('bthi,hij->bthj', xh, wx).reshape(b, t, D_RNN) + bx).astype(jnp.float32))
    log_a = -LRU_C * r * jax.nn.softplus(-lam.astype(jnp.float32))
    a = jnp.exp(log_a)
    inp = jnp.sqrt(-jnp.expm1(2.0 * log_a)) * ig * xc.astype(jnp.float32)
    _, h = lax.associative_scan(linear_combine, (a, inp), axis=1)
    return h.astype(x_rnn.dtype) * jax.nn.gelu(gate_rnn)


def causal_dense_attention(q, k, v, scale):
    t = q.shape[1]
    kpos = jnp.arange(t)

    def block(args):
        qb, i = args
        qpos = i * Q_BLOCK + jnp.arange(Q_BLOCK)
        s = jnp.einsum('bqhd,bshd->bhqs', qb, k).astype(jnp.float32) * scale
        s = jnp.where(kpos[None, :] <= qpos[:, None], s, -jnp.inf)
        p = jax.nn.softmax(s, axis=-1).astype(v.dtype)
        return jnp.einsum('bhqs,bshd->bqhd', p, v)

    return from_blocks(lax.map(block, (to_blocks(q), jnp.arange(t // Q_BLOCK))))


def mla_branch(q_lat, kv_lat, k_pe, cs, q_norm_g, w_uq, kv_norm_g, w_ukv, qk_gain):
    b, t, _ = q_lat.shape
    q = (rmsnorm(q_lat, q_norm_g) @ w_uq).reshape(b, t, MLA_HEADS, MLA_QK)
    kv = (rmsnorm(kv_lat, kv_norm_g) @ w_ukv).reshape(b, t, MLA_HEADS, MLA_NOPE + MLA_V)
    k_nope, v = kv[..., :MLA_NOPE], kv[..., MLA_NOPE:]
    k_rope = jnp.broadcast_to(k_pe[:, :, None, :], (b, t, MLA_HEADS, MLA_ROPE))
    k = jnp.concatenate([k_rope, k_nope], axis=-1)
    q = apply_rope(rmsnorm(q, qk_gain[0]), cs, MLA_ROPE)
    k = apply_rope(rmsnorm(k, qk_gain[1]), cs, MLA_ROPE)
    return causal_dense_attention(q, k, v, MLA_QK ** -0.5)


def dsa_branch(q, k, v, q_idx, k_idx, w_idx, cs_main, cs_idx, qk_gain):
    b, t, _ = q.shape
    topk = min(TOPK_MAX, t // 4)
    q = apply_rope(rmsnorm(q.reshape(b, t, DSA_HEADS, DSA_HEAD_DIM), qk_gain[0]), cs_main, DSA_ROT)
    k = apply_rope(rmsnorm(k, qk_gain[1]), cs_main, DSA_ROT)
    q_idx = apply_rope(q_idx.reshape(b, t, IDX_HEADS, IDX_DIM), cs_idx, IDX_ROT)
    k_idx = apply_rope(k_idx, cs_idx, IDX_ROT)
    kpos = jnp.arange(t)
    gather = jax.vmap(lambda table, idx: table[idx])

    def block(args):
        qb, qib, wib, i = args
        qpos = i * Q_BLOCK + jnp.arange(Q_BLOCK)
        rel = jax.nn.relu(jnp.einsum('bqhd,bsd->bqhs', qib, k_idx).astype(jnp.float32))
        score = jnp.einsum('bqh,bqhs->bqs', wib.astype(jnp.float32), rel)
        score = jnp.where(kpos[None, :] <= qpos[:, None], score, -jnp.inf)
        _, sel = lax.top_k(score, topk)
        valid = sel <= qpos[None, :, None]
        k_sel = gather(k, sel)
        v_sel = gather(v, sel)
        s = jnp.einsum('bqhd,bqkd->bhqk', qb, k_sel).astype(jnp.float32) * DSA_HEAD_DIM ** -0.5
        s = jnp.where(valid[:, None], s, -jnp.inf)
        p = jax.nn.softmax(s, axis=-1).astype(v.dtype)
        return jnp.einsum('bhqk,bqkd->bqhd', p, v_sel)

    xs = (to_blocks(q), to_blocks(q_idx), to_blocks(w_idx), jnp.arange(t // Q_BLOCK))
    return from_blocks(lax.map(block, xs))


def s5_branch(u, lam_re, lam_im, log_dt, b_re, b_im, c_re, c_im, d, w_glu, b_glu):
    bsz, t, _ = u.shape
    f32 = jnp.float32
    lr, li = lam_re.astype(f32), lam_im.astype(f32)
    dt = jnp.exp(log_dt.astype(f32))[:, None]
    mag = jnp.exp(lr * dt)
    ab_re, ab_im = mag * jnp.cos(li * dt), mag * jnp.sin(li * dt)
    den = lr * lr + li * li
    nr, ni = ab_re - 1.0, ab_im
    f_re = (nr * lr + ni * li) / den
    f_im = (ni * lr - nr * li) / den
    br, bi = b_re.astype(f32), b_im.astype(f32)
    bb_re = f_re[..., None] * br - f_im[..., None] * bi
    bb_im = f_re[..., None] * bi + f_im[..., None] * br
    ug = u.reshape(bsz, t, S5_GROUPS, S5_GROUP).astype(f32)
    bu_re = jnp.einsum('gpj,btgj->btgp', bb_re, ug)
    bu_im = jnp.einsum('gpj,btgj->btgp', bb_im, ug)
    a_re = jnp.broadcast_to(ab_re[None, None], (1, t, S5_GROUPS, S5_STATE))
    a_im = jnp.broadcast_to(ab_im[None, None], (1, t, S5_GROUPS, S5_STATE))
    _, _, x_re, x_im = lax.associative_scan(complex_combine, (a_re, a_im, bu_re, bu_im), axis=1)
    y = (jnp.einsum('gjp,btgp->btgj', c_re.astype(f32), x_re)
         - jnp.einsum('gjp,btgp->btgj', c_im.astype(f32), x_im))
    y = y.reshape(bsz, t, D_S5) + d.astype(f32) * u.astype(f32)
    y = jax.nn.gelu(y).astype(u.dtype)
    return y * jax.nn.sigmoid(y @ w_glu + b_glu)


def setup_inputs(seed: int = 0) -> dict:
    key = jax.random.key(seed)
    ks = jax.random.split(key, 40)
    L = DEPTH

    def nrm(k, shape, scale):
        return jax.random.normal(k, shape, jnp.float32) * scale

    def gain(k, shape):
        return 1.0 + nrm(k, shape, 0.02)

    a_c = jax.random.uniform(ks[20], (L, D_RNN), jnp.float32, 0.9, 0.999)
    a0 = a_c ** (1.0 / LRU_C)
    s5_n = jnp.arange(S5_STATE, dtype=jnp.float32)
    return {
        'x': nrm(ks[0], (BATCH, SEQ, D_MODEL), 1.0),
        'c': nrm(ks[1], (BATCH, D_MODEL), 1.0),
        'positions': jnp.arange(SEQ, dtype=jnp.int32)[None, :] + jax.random.randint(ks[2], (BATCH, 1), 0, 1024, jnp.int32),
        'ada_w': nrm(ks[3], (L, D_MODEL, N_ADA * D_MODEL), 0.1 * D_MODEL ** -0.5),
        'ada_b': nrm(ks[4], (L, N_ADA * D_MODEL), 0.01),
        'norm_g': gain(ks[5], (L, 3, D_MODEL)),
        'ffn_w1': nrm(ks[6], (L, 2, D_MODEL, D_FF), D_MODEL ** -0.5),
        'ffn_w3': nrm(ks[7], (L, 2, D_MODEL, D_FF), D_MODEL ** -0.5),
        'ffn_w2': nrm(ks[8], (L, 2, D_FF, D_MODEL), D_FF ** -0.5),
        'w_in': nrm(ks[9], (L, D_MODEL, D_IN), D_MODEL ** -0.5),
        'conv_w': nrm(ks[10], (L, CONV_WIDTH, D_RNN), CONV_WIDTH ** -0.5),
        'conv_b': nrm(ks[11], (L, D_RNN), 0.01),
        'rg_wa': nrm(ks[12], (L, RNN_HEADS, RNN_HEAD_DIM, RNN_HEAD_DIM), RNN_HEAD_DIM ** -0.5),
        'rg_ba': nrm(ks[13], (L, D_RNN), 0.01),
        'rg_wx': nrm(ks[14], (L, RNN_HEADS, RNN_HEAD_DIM, RNN_HEAD_DIM), RNN_HEAD_DIM ** -0.5),
        'rg_bx': nrm(ks[15], (L, D_RNN), 0.01),
        'rg_lambda': jnp.log(a0) - jnp.log1p(-a0),
        'mla_q_norm': gain(ks[16], (L, MLA_Q_LORA)),
        'mla_w_uq': nrm(ks[17], (L, MLA_Q_LORA, MLA_HEADS * MLA_QK), MLA_Q_LORA ** -0.5),
        'mla_kv_norm': gain(ks[18], (L, MLA_KV_LORA)),
        'mla_w_ukv': nrm(ks[19], (L, MLA_KV_LORA, MLA_HEADS * (MLA_NOPE + MLA_V)), MLA_KV_LORA ** -0.5),
        'mla_qk_gain': gain(ks[21], (L, 2, MLA_QK)),
        'dsa_qk_gain': gain(ks[22], (L, 2, DSA_HEAD_DIM)),
        's5_lambda_re': -0.5 + nrm(ks[23], (L, S5_GROUPS, S5_STATE), 0.005),
        's5_lambda_im': math.pi * s5_n + nrm(ks[24], (L, S5_GROUPS, S5_STATE), 0.01),
        's5_log_dt': jax.random.uniform(ks[25], (L, S5_GROUPS), jnp.float32, math.log(0.001), math.log(0.1)),
        's5_b_re': nrm(ks[26], (L, S5_GROUPS, S5_STATE, S5_GROUP), (2.0 * S5_GROUP) ** -0.5),
        's5_b_im': nrm(ks[27], (L, S5_GROUPS, S5_STATE, S5_GROUP), (2.0 * S5_GROUP) ** -0.5),
        's5_c_re': nrm(ks[28], (L, S5_GROUPS, S5_GROUP, S5_STATE), (2.0 * S5_STATE) ** -0.5),
        's5_c_im': nrm(ks[29], (L, S5_GROUPS, S5_GROUP, S5_STATE), (2.0 * S5_STATE) ** -0.5),
        's5_d': nrm(ks[30], (L, D_S5), 1.0),
        's5_w_glu': nrm(ks[31], (L, D_S5, D_S5), D_S5 ** -0.5),
        's5_b_glu': nrm(ks[32], (L, D_S5), 0.01),
        'w_branch': nrm(ks[33], (L, N_BRANCH, BRANCH_W, D_MODEL), BRANCH_W ** -0.5),
        'w_out': nrm(ks[34], (L, D_MODEL, D_MODEL), D_MODEL ** -0.5),
    }


def reference(x, c, positions, ada_w, ada_b, norm_g, ffn_w1, ffn_w3, ffn_w2, w_in,
              conv_w, conv_b, rg_wa, rg_ba, rg_wx, rg_bx, rg_lambda,
              mla_q_norm, mla_w_uq, mla_kv_norm, mla_w_ukv, mla_qk_gain, dsa_qk_gain,
              s5_lambda_re, s5_lambda_im, s5_log_dt, s5_b_re, s5_b_im, s5_c_re, s5_c_im,
              s5_d, s5_w_glu, s5_b_glu, w_branch, w_out):
    b, t, _ = x.shape
    split_points = np.cumsum(np.array(IN_SPLITS))[:-1].tolist()
    cs_mla = rope_tables(positions, MLA_ROPE)
    cs_dsa = rope_tables(positions, DSA_ROT)
    cs_idx = rope_tables(positions, IDX_ROT)
    c_act = jax.nn.silu(c)
    for l in range(DEPTH):
        mod = (c_act @ ada_w[l] + ada_b[l])[:, None, :]
        sh1, sc1, g1, sh2, sc2, g2, sh3, sc3, g3 = jnp.split(mod, N_ADA, axis=-1)
        u = modulate(rmsnorm(x, norm_g[l, 0]), sh1, sc1)
        x = x + 0.5 * (1.0 + g1) * swiglu(u, ffn_w1[l, 0], ffn_w3[l, 0], ffn_w2[l, 0])
        u = modulate(rmsnorm(x, norm_g[l, 1]), sh2, sc2)
        z = u @ w_in[l]
        (x_rnn, gate_rnn, q_lat, kv_lat, k_pe, q_dsa, k_dsa, v_dsa,
         q_idx, k_idx, w_idx, u_s5, gate_logits) = jnp.split(z, split_points, axis=-1)
        y_a = rglru_branch(x_rnn, gate_rnn, conv_w[l], conv_b[l], rg_wa[l], rg_ba[l],
                           rg_wx[l], rg_bx[l], rg_lambda[l])
        y_b = mla_branch(q_lat, kv_lat, k_pe, cs_mla, mla_q_norm[l], mla_w_uq[l],
                         mla_kv_norm[l], mla_w_ukv[l], mla_qk_gain[l])
        y_c = dsa_branch(q_dsa, k_dsa, v_dsa, q_idx, k_idx, w_idx, cs_dsa, cs_idx, dsa_qk_gain[l])
        y_d = s5_branch(u_s5, s5_lambda_re[l], s5_lambda_im[l], s5_log_dt[l], s5_b_re[l],
                        s5_b_im[l], s5_c_re[l], s5_c_im[l], s5_d[l], s5_w_glu[l], s5_b_glu[l])
        gates = jax.nn.sigmoid(gate_logits.reshape(b, t, N_BRANCH, D_MODEL))
        ys = [y_a, y_b, y_c, y_d]
        merged = gates[:, :, 0] * (y_a @ w_branch[l, 0])
        for n in range(1, N_BRANCH):
            merged = merged + gates[:, :, n] * (ys[n] @ w_branch[l, n])
        x = x + (1.0 + g2) * (merged @ w_out[l])
        u = modulate(rmsnorm(x, norm_g[l, 2]), sh3, sc3)
        x = x + 0.5 * (1.0 + g3) * swiglu(u, ffn_w1[l, 1], ffn_w3[l, 1], ffn_w2[l, 1])
    return x
```

```python
import numpy as np
from contextlib import ExitStack
import concourse.bass as bass
import concourse.mybir as mybir
from concourse.bass_utils import run_bass_kernel_spmd

F32 = mybir.dt.float32
I32 = mybir.dt.int32
F32R = mybir.dt.float32r


def AF32(ap):
    return ap.bitcast(F32)
ALU = mybir.AluOpType
AF = mybir.ActivationFunctionType

L_ = 4
D = 1024
T = 2048
NS = 2
NT = NS * T
CH = 512
NCH = NT // CH
DFF = 2816
EPS = 1e-6
BIG = 1.0e30
PI = float(np.pi)
TWO_PI = float(2 * np.pi)
CW1 = 6.28125
CW2 = float(2 * np.pi - 6.28125)
PI_SAFE = 3.1415925

Z_TILES = ([(0 + 128 * i, 128) for i in range(4)] + [(512 + 128 * i, 128) for i in range(4)]
           + [(1024, 128), (1152, 128), (1280, 128), (1408, 32)]
           + [(1440 + 128 * i, 128) for i in range(4)] + [(1952, 64)]
           + [(2080, 128), (2208, 128), (2336, 32)] + [(2376 + 128 * i, 128) for i in range(4)])
ZI_XRNN, ZI_GATE, ZI_QLAT, ZI_KVLAT, ZI_KPE, ZI_QDSA, ZI_KDSA, ZI_QIDX, ZI_KIDX, ZI_US5 = 0, 4, 8, 10, 11, 12, 16, 17, 19, 20
NZ = len(Z_TILES)

CO = {}
_off = 0
for _n, _w in [('ident', 128), ('ones', 128), ('bd64', 128), ('r96', 128), ('r64', 128), ('r32', 128),
               ('tri_kq', 128), ('tri_qk', 128), ('neg_qk', 128), ('inv_mla', 1), ('inv_dsa', 1),
               ('inv_idx', 1), ('iota', T)]:
    CO[_n] = (_off, _w)
    _off += _w
NCONST = _off


class Reg:
    __slots__ = ('ap', 'name')

    def __init__(self, ap, name=''):
        self.ap = ap
        self.name = name


class Sched:
    ENG = ('pe', 'act', 'dve', 'pool', 'sp')
    EPOCH = 16000
    NSLOT = 12

    def __init__(self, nc, stack):
        self.nc = nc
        self.stack = stack
        self.streams = {e: [] for e in self.ENG}
        self.count = {e: 0 for e in self.ENG}
        self.sems = {e: [] for e in self.ENG}
        self.waited = {e: {} for e in self.ENG}
        self.last_w = {}
        self.readers = {}
        self.semobj = {}
        self.slots = {q: [[self._newsem(), 0] for _ in range(self.NSLOT)] for q in ('sp', 'pool')}
        self.slot_rr = {'sp': 0, 'pool': 0}
        self.all_tokens = []
        self.ninstr = 0

    def _newsem(self):
        s = self.stack.enter_context(self.nc.semaphore())
        sid = len(self.semobj)
        self.semobj[sid] = s
        return sid

    def _esem(self, e, epoch):
        while len(self.sems[e]) <= epoch:
            self.sems[e].append(self._newsem())
        return self.sems[e][epoch]

    def _deps(self, e, reads, writes):
        waits = {}

        def need(tok):
            if tok is None:
                return
            sid, val, prod = tok
            if prod == e and e == 'pe':
                return
            if self.waited[e].get(sid, 0) >= val:
                return
            if waits.get(sid, 0) < val:
                waits[sid] = val

        for k in reads:
            need(self.last_w.get(id(k)))
        for k in writes:
            need(self.last_w.get(id(k)))
            for r in self.readers.get(id(k), ()):
                need(r)
        return waits

    def _commit(self, e, tok, reads, writes, waits):
        for sid, v in waits.items():
            self.waited[e][sid] = v
        for k in reads:
            self.readers.setdefault(id(k), []).append(tok)
        for k in writes:
            self.last_w[id(k)] = tok
            self.readers[id(k)] = []

    def op(self, e, fn, reads=(), writes=()):
        waits = self._deps(e, reads, writes)
        idx = self.count[e]
        sid = self._esem(e, idx // self.EPOCH)
        tok = (sid, idx % self.EPOCH + 1, e)
        self.count[e] += 1
        self._commit(e, tok, reads, writes, waits)
        self.streams[e].append((list(waits.items()), fn, sid, 1))
        self.ninstr += 1

    def dma(self, q, out_ap, in_ap, reads=(), writes=()):
        waits = self._deps(q, reads, writes)
        si = self.slot_rr[q]
        self.slot_rr[q] = (si + 1) % self.NSLOT
        slot = self.slots[q][si]
        if slot[1] + 16 > 30000:
            slot[0] = self._newsem()
            slot[1] = 0
        if slot[1] > 0 and self.waited[q].get(slot[0], 0) < slot[1]:
            waits[slot[0]] = max(waits.get(slot[0], 0), slot[1])
        slot[1] += 16
        tok = (slot[0], slot[1], 'dma_' + q)
        self._commit(q, tok, reads, writes, waits)

        def fn(eng, o=out_ap, i=in_ap):
            return eng.dma_start(out=o, in_=i)
        self.streams[q].append((list(waits.items()), fn, slot[0], 16))
        self.all_tokens.append(tok)
        self.ninstr += 1

    def barrier(self):
        toks = []
        for e in self.ENG:
            idx = self.count[e]
            if idx > 0:
                toks.append((self._esem(e, (idx - 1) // self.EPOCH), (idx - 1) % self.EPOCH + 1, e))
        for q in ('sp', 'pool'):
            for slot in self.slots[q]:
                if slot[1] > 0:
                    toks.append((slot[0], slot[1], 'dma_' + q))
        for e in self.ENG:
            waits = {}
            for sid, val, prod in toks:
                if prod == e:
                    continue
                if self.waited[e].get(sid, 0) >= val:
                    continue
                waits[sid] = max(waits.get(sid, 0), val)
            for sid, v in waits.items():
                self.waited[e][sid] = v
            if waits:
                self.streams[e].append((list(waits.items()), None, None, 0))
        self.last_w.clear()
        self.readers.clear()

    def emit(self):
        nc = self.nc
        self.barrier()
        semobj = self.semobj
        streams = self.streams

        def replay(e, eng):
            for waits, fn, sid, inc in streams[e]:
                for s, v in waits:
                    eng.wait_ge(semobj[s], v)
                if fn is not None:
                    ins = fn(eng)
                    ins.then_inc(semobj[sid], inc)

        with nc.Block() as block:
            @block.tensor
            def _(eng):
                replay('pe', eng)

            @block.scalar
            def _(eng):
                replay('act', eng)

            @block.vector
            def _(eng):
                replay('dve', eng)

            @block.gpsimd
            def _(eng):
                replay('pool', eng)

            @block.sync
            def _(eng):
                replay('sp', eng)

    def mm(self, out, lhsT, rhs, start, stop, reads, writes):
        self.op('pe', lambda g, o=out, l=lhsT, r=rhs, a=start, b=stop: g.matmul(o, l, r, start=a, stop=b),
                reads, writes)

    def transpose(self, out, in_, ident, reads, writes):
        self.op('pe', lambda g, o=out, i=in_, d=ident: g.transpose(o, i, d), reads, writes)

    def act(self, out, in_, func, reads, writes, bias=None, scale=None):
        kw = {}
        if bias is not None:
            kw['bias'] = bias
        if scale is not None:
            kw['scale'] = scale
        self.op('act', lambda g, o=out, i=in_, f=func, k=kw: g.activation(o, i, f, **k), reads, writes)

    def tt(self, e, out, in0, in1, op, reads, writes):
        self.op(e, lambda g, o=out, a=in0, b=in1, p=op: g.tensor_tensor(o, a, b, p), reads, writes)

    def ts(self, e, out, in0, s1, s2, op0, op1, reads, writes):
        if s2 is None:
            self.op(e, lambda g, o=out, a=in0, x=s1, p=op0: g.tensor_scalar(o, a, x, None, p), reads, writes)
        else:
            self.op(e, lambda g, o=out, a=in0, x=s1, y=s2, p=op0, q=op1: g.tensor_scalar(o, a, x, y, p, q),
                    reads, writes)

    def stt(self, out, in0, scalar, in1, op0, op1, reads, writes):
        self.op('dve', lambda g, o=out, a=in0, s=scalar, b=in1, p=op0, q=op1:
                g.scalar_tensor_tensor(o, a, s, b, p, q), reads, writes)

    def copy(self, e, out, in_, reads, writes):
        if e == 'act':
            self.op('act', lambda g, o=out, i=in_: g.copy(o, i), reads, writes)
        else:
            self.op(e, lambda g, o=out, i=in_: g.tensor_copy(o, i), reads, writes)

    def recip(self, out, in_, reads, writes):
        self.op('dve', lambda g, o=out, i=in_: g.reciprocal(o, i), reads, writes)

    def memset(self, e, ap, val, writes):
        self.op(e, lambda g, a=ap, v=val: g.memset(a, v), (), writes)


class Arena:
    def __init__(self, ap, nwords):
        self.ap = ap
        self.n = nwords
        self.off = 0

    def reset(self):
        self.off = 0

    def alloc(self, words, name=''):
        assert self.off + words <= self.n, (name, self.off, words, self.n)
        a = self.ap[:, self.off:self.off + words]
        self.off += words
        return a


def build_program(n_layers=L_, debug=None):
    nc = bass.Bass("TRN2", target_bir_lowering=False)
    stack = ExitStack()

    def din(name, shape, dt=F32):
        return nc.dram_tensor(name, list(shape), dt, kind="ExternalInput").ap()

    dbg = debug or {}
    FB_LIM = dbg.get('fblim', 11)
    M_LIM = dbg.get('mlim', 8)

    def dscr(name, shape, dt=F32):
        kind = "ExternalOutput" if dbg.get(name) else "Internal"
        return nc.dram_tensor(name, list(shape), dt, kind=kind).ap()

    i_xT = din("xT", [D, NT])
    i_cT = din("cT", [128, 8, NS])
    i_pos = din("posb", [128, NS, T], I32)
    i_const = din("consts", [128, NCONST])
    i_adaw = din("ada_w", [L_, 18, 128, 8, 512])
    i_adab = din("ada_b", [L_, 128, 72, NS])
    i_normg = din("norm_g", [L_, 3, 128, 8, NS])
    i_w1 = din("w1", [L_, 2, 22, 128, 8, 128])
    i_w3 = din("w3", [L_, 2, 22, 128, 8, 128])
    i_w2 = din("w2", [L_, 2, 8, 128, 22, 128])
    i_winz = din("win_z", [L_, NZ, 128, 8, 128])
    i_wtok = din("win_tok", [L_, 128, 8, 72])
    i_wgate = din("win_gate", [L_, 4, 8, 128, 8, 128])
    i_rgp = din("rg_par", [L_, 128, 4, 8])
    i_rgw = din("rg_w", [L_, 2, 4, 128, 128])
    i_mlan = din("mla_norm", [L_, 128, 3])
    i_wuq = din("w_uq", [L_, 128, 2, 8, 96])
    i_wukvk = din("w_ukv_k", [L_, 128, 8, 64])
    i_wukvv = din("w_ukv_v", [L_, 128, 512])
    i_gains = din("qk_gains", [L_, 128, 4])
    i_s5bc = din("s5_bc", [L_, 3, 128, 2048])
    i_s5pp = din("s5_pp", [L_, 3, 128, 16])
    i_s5b = din("s5_b", [L_, 2, 128, 16, 128])
    i_s5c = din("s5_c", [L_, 2, 128, 16, 128])
    i_s5v = din("s5_vec", [L_, 128, 4, 2])
    i_wglu = din("w_glu", [L_, 128, 4, 512])
    i_wbr = din("w_branch", [L_, 4, 8, 128, 4, 128])
    i_wout = din("w_out", [L_, 8, 128, 8, 128])
    o_yT = nc.dram_tensor("yT_out", [D, NT], F32, kind="ExternalOutput").ap()

    d_xT = dscr("s_xT", [D, NT])
    d_u2T = dscr("s_u2T", [D, NT])
    d_zT = dscr("s_zT", [NZ * 128, NT])
    d_vw = dscr("s_vw", [NT, 72])
    d_yT = dscr("s_yT", [4, 512, NT])
    d_qm = dscr("s_qm", [8, 96, NT])
    d_km = dscr("s_km", [8, 96, NT])
    d_vm = dscr("s_vm", [NT, 512])
    d_qd = dscr("s_qd", [512, NT])
    d_kd = dscr("s_kd", [64, NT])
    d_yg = dscr("s_yg", [512, NT])
    d_tab = dscr("s_tab", [3, 2, 128, NT])

    def keys(n):
        return [Reg(None, n + str(i)) for i in range(NCH)]
    K_xT, K_u2T, K_zT, K_vw, K_qm, K_km, K_vm, K_qd, K_kd, K_yg = (keys(n) for n in
                                                                  ("xT", "u2T", "zT", "vw", "qm", "km", "vm", "qd", "kd", "yg"))
    K_yT = [[Reg(None, "yT%d_%d" % (b, s)) for s in range(NS)] for b in range(4)]
    K_tab = Reg(None, "tab")

    cst_t = stack.enter_context(nc.sbuf_tensor("cst", [128, NCONST], F32))
    par_t = stack.enter_context(nc.sbuf_tensor("par", [128, 1200], F32))
    iwk_t = stack.enter_context(nc.sbuf_tensor("iwk", [128, T], I32))
    banks = [stack.enter_context(nc.psum_tensor("pb%d" % i, [128, 512], F32)) for i in range(8)]
    PB = [Reg(b[:], "pb%d" % i) for i, b in enumerate(banks)]
    AR = Arena(None, 0)
    RA = Arena(None, 0)
    _uid = [0]

    def open_stage(f32_words, r_words=0):
        _uid[0] += 1
        g1 = nc.sbuf_tensor("fa%d" % _uid[0], [128, f32_words], F32)
        t1 = g1.__enter__()
        AR.ap, AR.n, AR.off = t1[:], f32_words, 0
        guards = [g1]
        if r_words:
            g2 = nc.sbuf_tensor("ra%d" % _uid[0], [128, r_words], F32R)
            t2 = g2.__enter__()
            RA.ap, RA.n, RA.off = t2[:], r_words, 0
            guards.append(g2)
        return guards

    def close_stage(guards):
        S.barrier()
        for g in reversed(guards):
            g.__exit__(None, None, None)
    CST = cst_t[:]
    PAR = par_t[:]
    IWK = Reg(iwk_t[:], "iwk")
    K_cst = Reg(None, "cst")

    S = Sched(nc, stack)

    def C(name, rows=128, c0=0, c1=None):
        o, w = CO[name]
        if c1 is None:
            c1 = w
        return CST[0:rows, o + c0:o + c1]

    S.dma('sp', CST, i_const, (), (K_cst,))

    par_off = [0]

    def palloc(w):
        a = PAR[:, par_off[0]:par_off[0] + w]
        par_off[0] += w
        return a
    P_MOD = Reg(palloc(144), "mod")
    P_ADAB = Reg(palloc(144), "adab")
    P_NG = Reg(palloc(48), "ng")
    P_A = Reg(palloc(48), "A")
    P_G = Reg(palloc(48), "G")
    P_CACT = Reg(palloc(16), "cact")
    P_RG = Reg(palloc(32), "rgp")
    P_RGD = Reg(palloc(16), "rgd")
    P_MLAN = Reg(palloc(3), "mlan")
    P_GAIN = Reg(palloc(4), "gains")
    P_S5PP = Reg(palloc(48), "s5pp")
    P_S5D = Reg(palloc(64), "s5d")
    P_S5V = Reg(palloc(8), "s5v")
    P_TMP = Reg(palloc(64), "ptmp")

    mod3 = P_MOD.ap.rearrange("p (j k s) -> p j k s", j=9, k=8)
    A3 = P_A.ap.rearrange("p (j k s) -> p j k s", j=3, k=8)
    G3 = P_G.ap.rearrange("p (j k s) -> p j k s", j=3, k=8)

    def modcol(j, k, s):
        return mod3[:, j, k, s:s + 1]

    def sincos(ang, sin_out, cos_out, tmp, rows, n, rk, wk):
        iw = IWK.ap[0:rows, 0:n]
        S.ts('dve', iw, ang, 1.0 / TWO_PI, None, ALU.mult, None, rk, [IWK])
        S.stt(tmp, iw, -CW1, ang, ALU.mult, ALU.add, rk + [IWK], wk)
        S.stt(ang, iw, -CW2, tmp, ALU.mult, ALU.add, rk + [IWK], wk)
        S.ts('dve', tmp, ang, PI, -TWO_PI, ALU.is_gt, ALU.mult, rk, wk)
        S.tt('dve', ang, ang, tmp, ALU.add, rk, wk)
        S.ts('dve', tmp, ang, -PI, TWO_PI, ALU.is_lt, ALU.mult, rk, wk)
        S.tt('dve', ang, ang, tmp, ALU.add, rk, wk)
        S.ts('dve', ang, ang, PI_SAFE, -PI_SAFE, ALU.min, ALU.max, rk, wk)
        S.act(sin_out, ang, AF.Sin, rk, wk)
        S.ts('dve', ang, ang, PI / 2, None, ALU.add, None, rk, wk)
        S.ts('dve', tmp, ang, PI, -TWO_PI, ALU.is_gt, ALU.mult, rk, wk)
        S.tt('dve', ang, ang, tmp, ALU.add, rk, wk)
        S.ts('dve', ang, ang, PI_SAFE, -PI_SAFE, ALU.min, ALU.max, rk, wk)
        S.act(cos_out, ang, AF.Sin, rk, wk)

    def rstd_from_psum(ps_ap, out_ap, n_feat, reads, writes):
        S.act(out_ap, ps_ap, AF.Sqrt, reads, writes, bias=EPS_AP[0:out_ap.shape[0], :], scale=1.0 / n_feat)
        S.recip(out_ap, out_ap, writes, writes)

    def gelu_tanh(x, out, t1, rk, wk):
        S.tt('pool', t1, x, x, ALU.mult, rk, wk)
        S.ts('pool', t1, t1, 0.044715, 1.0, ALU.mult, ALU.add, rk, wk)
        S.tt('pool', t1, t1, x, ALU.mult, rk, wk)
        S.act(t1, t1, AF.Sigmoid, rk, wk, scale=1.5957691216057308)
        S.tt('dve', out, x, t1, ALU.mult, rk, wk)

    EPS_R = Reg(palloc(1), "eps")
    EPS_AP = EPS_R.ap
    S.memset('dve', EPS_AP, EPS, [EPS_R])
    ONE_R = Reg(palloc(1), "one")
    S.memset('dve', ONE_R.ap, 1.0, [ONE_R])

    G0 = open_stage(16000)
    for c in range(dbg.get('nch', NCH)):
        S.dma('sp', d_xT[:, c * CH:(c + 1) * CH], i_xT[:, c * CH:(c + 1) * CH], (), (K_xT[c],))

    R_ang = Reg(AR.alloc(T), "ang")
    R_tmp = Reg(AR.alloc(T), "tmp")
    R_sin = Reg(AR.alloc(T), "sin")
    R_cos = Reg(AR.alloc(T), "cos")
    R_posi = Reg(AR.alloc(NS * T).bitcast(I32).rearrange("p (s t) -> p s t", s=NS), "posi")
    S.dma('sp', R_posi.ap, i_pos, (), (R_posi,))
    for f, inv in enumerate(('inv_mla', 'inv_dsa', 'inv_idx')):
        for s in range(NS):
            S.ts('dve', R_ang.ap, R_posi.ap[:, s, :], C(inv), None, ALU.mult, None, [R_posi, K_cst], [R_ang])
            sincos(R_ang.ap, R_sin.ap, R_cos.ap, R_tmp.ap, 128, T, [R_ang, R_tmp], [R_ang, R_tmp, R_sin, R_cos])
            S.dma('pool', d_tab[f, 0, :, s * T:(s + 1) * T], R_cos.ap, (R_cos,), (K_tab,))
            S.dma('pool', d_tab[f, 1, :, s * T:(s + 1) * T], R_sin.ap, (R_sin,), (K_tab,))
    S.dma('sp', P_CACT.ap.rearrange("p (k s) -> p k s", k=8), i_cT, (), (P_CACT,))
    S.act(P_CACT.ap, P_CACT.ap, AF.Silu, [P_CACT], [P_CACT])
    close_stage(G0)

    def norm_mod(Xr, Ur, UTr, SQr, RSr, j, s):
        ps = PB[0]
        for k in range(8):
            S.act(SQr[k % 2].ap, Xr[k].ap, AF.Square, [Xr[k]], [SQr[k % 2]])
            S.mm(ps.ap, C('ones'), SQr[k % 2].ap, k == 0, k == 7, [SQr[k % 2], K_cst], [ps])
        rstd_from_psum(ps.ap, RSr.ap, D, [ps, EPS_R], [RSr])
        for k in range(8):
            ut = UTr[k % 2]
            S.tt('dve', ut.ap, Xr[k].ap, RSr.ap, ALU.mult, [Xr[k], RSr], [ut])
            S.ts('pool', Ur[k].ap, ut.ap, A3[:, j, k, s:s + 1], modcol(3 * j, k, s), ALU.mult, ALU.add,
                 [ut, P_A, P_MOD], [Ur[k]])

    def chunk_regs(name, arena=None):
        base = (arena or AR).alloc(8 * CH, name)
        b3 = base.rearrange("p (k t) -> p k t", k=8)
        return base, [Reg(b3[:, k, :], name + str(k)) for k in range(8)]

    def dram_chunk(d, c):
        return d.rearrange("(k p) t -> p k t", p=128)[:, :, c * CH:(c + 1) * CH]

    def stage_mod(l):
        G = open_stage(8192)
        WB = [Reg(AR.alloc(8 * 512).rearrange("p (k n) -> p k n", k=8), "adaw%d" % i) for i in range(2)]
        cact3 = P_CACT.ap.rearrange("p (k s) -> p k s", k=8)
        S.dma('sp', P_ADAB.ap.rearrange("p (m s) -> p m s", s=NS), i_adab[l], (), (P_ADAB,))
        S.dma('sp', P_NG.ap.rearrange("p (j k s) -> p j k s", j=3, k=8), i_normg[l].rearrange("j p k s -> p j k s"),
              (), (P_NG,))
        ps = PB[1]
        for blk in range(18):
            w = WB[blk % 2]
            S.dma('sp', w.ap, i_adaw[l, blk], (), (w,))
            for mi in range(4):
                mt = blk * 4 + mi
                for k in range(8):
                    S.mm(ps.ap[:, mt * 2:mt * 2 + 2], w.ap[:, k, mi * 128:(mi + 1) * 128], cact3[:, k, :],
                         k == 0, k == 7, [w, P_CACT], [ps])
        S.tt('dve', P_MOD.ap, ps.ap[:, 0:144], P_ADAB.ap, ALU.add, [ps, P_ADAB], [P_MOD])
        ng3 = P_NG.ap.rearrange("p (j k s) -> p j k s", j=3, k=8)
        for j in range(3):
            S.ts('dve', A3[:, j], mod3[:, 3 * j + 1], 1.0, None, ALU.add, None, [P_MOD], [P_A])
            S.tt('dve', A3[:, j], A3[:, j], ng3[:, j], ALU.mult, [P_A, P_NG], [P_A])
            if j == 1:
                S.ts('dve', G3[:, j], mod3[:, 3 * j + 2], 1.0, None, ALU.add, None, [P_MOD], [P_G])
            else:
                S.ts('dve', G3[:, j], mod3[:, 3 * j + 2], 0.5, 0.5, ALU.mult, ALU.add, [P_MOD], [P_G])
        close_stage(G)

    def stage_ffn(l, jf):
        G = open_stage(17500, 25200)
        jn = 0 if jf == 0 else 2
        Xb, X = chunk_regs("X")
        Ub, U = chunk_regs("U", RA)
        UT = [Reg(AR.alloc(CH), "ut%d" % i) for i in range(2)]
        SQ = [Reg(AR.alloc(CH), "sq%d" % i) for i in range(2)]
        RS = Reg(AR.alloc(CH), "rs")
        SA = [Reg(AR.alloc(CH), "sa%d" % i) for i in range(2)]
        H = [Reg(RA.alloc(CH), "h%d" % i) for i in range(22)]
        W1s = [Reg(AR.alloc(8 * 128).rearrange("p (k n) -> p k n", k=8), "w1s%d" % i) for i in range(2)]
        W3s = [Reg(AR.alloc(8 * 128).rearrange("p (k n) -> p k n", k=8), "w3s%d" % i) for i in range(2)]
        W2s = [Reg(AR.alloc(22 * 128).rearrange("p (k n) -> p k n", k=22), "w2s%d" % i) for i in range(2)]
        W1 = [Reg(RA.alloc(8 * 128).rearrange("p (k n) -> p k n", k=8), "w1_%d" % i) for i in range(2)]
        W3 = [Reg(RA.alloc(8 * 128).rearrange("p (k n) -> p k n", k=8), "w3_%d" % i) for i in range(2)]
        W2 = [Reg(RA.alloc(22 * 128).rearrange("p (k n) -> p k n", k=22), "w2_%d" % i) for i in range(2)]
        for c in range(dbg.get('nch', NCH)):
            s = c // (NCH // NS)
            S.dma('sp', Xb.rearrange("p (k t) -> p k t", k=8), dram_chunk(d_xT, c), (K_xT[c],), X)
            norm_mod(X, U, UT, SQ, RS, jn, s)
            for ft in range(22):
                w1s, w3s, w1, w3 = W1s[ft % 2], W3s[ft % 2], W1[ft % 2], W3[ft % 2]
                S.dma('sp', w1s.ap, i_w1[l, jf, ft], (), (w1s,))
                S.dma('sp', w3s.ap, i_w3[l, jf, ft], (), (w3s,))
                S.copy('pool', w1.ap, w1s.ap, [w1s], [w1])
                S.copy('act', w3.ap, w3s.ap, [w3s], [w3])
                pa, pb = PB[2 + ft % 2], PB[4 + ft % 2]
                for k in range(8):
                    S.mm(pa.ap, w1.ap[:, k, :], U[k].ap, k == 0, k == 7, [w1, U[k]], [pa])
                for k in range(8):
                    S.mm(pb.ap, w3.ap[:, k, :], U[k].ap, k == 0, k == 7, [w3, U[k]], [pb])
                sa = SA[ft % 2]
                S.act(sa.ap, pa.ap, AF.Silu, [pa], [sa])
                S.tt('dve', H[ft].ap, sa.ap, pb.ap, ALU.mult, [sa, pb], [H[ft]])
            for m in range(8):
                w2s, w2 = W2s[m % 2], W2[m % 2]
                S.dma('sp', w2s.ap, i_w2[l, jf, m], (), (w2s,))
                S.copy('pool' if m % 2 == 0 else 'act', w2.ap, w2s.ap, [w2s], [w2])
                po = PB[6 + m % 2]
                for kt in range(22):
                    S.mm(po.ap, w2.ap[:, kt, :], H[kt].ap, kt == 0, kt == 21, [w2, H[kt]], [po])
                S.stt(X[m].ap, po.ap, G3[:, jn, m, s:s + 1], X[m].ap, ALU.mult, ALU.add, [po, P_G, X[m]], [X[m]])
            S.dma('pool', dram_chunk(d_xT, c), Xb.rearrange("p (k t) -> p k t", k=8), X, (K_xT[c],))
        close_stage(G)

    def rope_pipeline(raw, P, gcol, rname, bdname, nfeat, tabC, tabS, QG, SQ, RS, T1, psA, psB, out, rk):
        S.act(SQ.ap[0:P], raw.ap[0:P], AF.Square, [raw], [SQ])
        S.mm(psA.ap[0:P], C(bdname, P, 0, P), SQ.ap[0:P], True, True, [SQ, K_cst], [psA])
        rstd_from_psum(psA.ap[0:P], RS.ap[0:P], nfeat, [psA, EPS_R], [RS])
        S.stt(QG.ap[0:P], raw.ap[0:P], gcol, RS.ap[0:P], ALU.mult, ALU.mult, [raw, RS, P_GAIN], [QG])
        S.mm(psB.ap[0:P], C(rname, P, 0, P), QG.ap[0:P], True, True, [QG, K_cst], [psB])
        S.tt('pool', T1.ap[0:P], QG.ap[0:P], tabC.ap[0:P], ALU.mult, [QG, tabC], [T1])
        S.tt('dve', out.ap[0:P], psB.ap[0:P], tabS.ap[0:P], ALU.mult, [psB, tabS], [out])
        S.tt('dve', out.ap[0:P], out.ap[0:P], T1.ap[0:P], ALU.add, [out, T1], [out])

    def stage_win(l):
        G = open_stage(14000, 6200)
        Xb, X = chunk_regs("X")
        Ub, U = chunk_regs("U", RA)
        UT = [Reg(AR.alloc(CH), "ut%d" % i) for i in range(2)]
        SQ = [Reg(AR.alloc(CH), "sq%d" % i) for i in range(2)]
        RS = Reg(AR.alloc(CH), "rs")
        ZS = [Reg(AR.alloc(CH), "zs%d" % i) for i in range(2)]
        ZR = [Reg(AR.alloc(CH), "zr%d" % i) for i in range(2)]
        W = [Reg(AR.alloc(8 * 128).rearrange("p (k n) -> p k n", k=8), "wz%d" % i) for i in range(2)]
        WR_ = [Reg(RA.alloc(8 * 128).rearrange("p (k n) -> p k n", k=8), "wzr%d" % i) for i in range(2)]
        WT = Reg(AR.alloc(8 * 72).rearrange("p (k n) -> p k n", k=8), "wtok")
        VW = [Reg(AR.alloc(72), "vw%d" % i) for i in range(2)]
        TC = Reg(AR.alloc(CH), "tc")
        TS_ = Reg(AR.alloc(CH), "tsn")
        S.dma('sp', WT.ap, i_wtok[l], (), (WT,))
        for c in range(dbg.get('nch', NCH)):
            s = c // (NCH // NS)
            S.dma('sp', Xb.rearrange("p (k t) -> p k t", k=8), dram_chunk(d_xT, c), (K_xT[c],), X)
            S.dma('sp', TC.ap, d_tab[2, 0, :, c * CH:(c + 1) * CH], (K_tab,), (TC,))
            S.dma('sp', TS_.ap, d_tab[2, 1, :, c * CH:(c + 1) * CH], (K_tab,), (TS_,))
            norm_mod(X, U, UT, SQ, RS, 1, s)
            S.dma('pool', dram_chunk(d_u2T, c), AF32(Ub).rearrange("p (k t) -> p k t", k=8), U, (K_u2T[c],))
            for zi, (c0, wd) in enumerate(Z_TILES):
                w = W[zi % 2]
                S.dma('sp', w.ap, i_winz[l, zi], (), (w,))
                pz = PB[1 + zi % 2]
                if wd == 128 and zi not in (ZI_QIDX, ZI_QIDX + 1):
                    wr = WR_[zi % 2]
                    S.copy('pool' if zi % 2 == 0 else 'act', wr.ap, w.ap, [w], [wr])
                    for k in range(8):
                        S.mm(pz.ap, wr.ap[:, k, :], U[k].ap, k == 0, k == 7, [wr, U[k]], [pz])
                else:
                    for k in range(8):
                        S.mm(pz.ap[0:wd], w.ap[:, k, 0:wd], AF32(U[k].ap), k == 0, k == 7, [w, U[k]], [pz])
                zs = ZS[zi % 2]
                S.copy('act', zs.ap[0:wd], pz.ap[0:wd], [pz], [zs])
                if zi in (ZI_QIDX, ZI_QIDX + 1, ZI_KIDX):
                    pr = PB[3 + zi % 2]
                    zr = ZR[zi % 2]
                    S.mm(pr.ap[0:wd], C('r32', wd, 0, wd), zs.ap[0:wd], True, True, [zs, K_cst], [pr])
                    S.tt('dve', zr.ap[0:wd], pr.ap[0:wd], TS_.ap[0:wd], ALU.mult, [pr, TS_], [zr])
                    S.tt('pool', zs.ap[0:wd], zs.ap[0:wd], TC.ap[0:wd], ALU.mult, [zs, TC], [zs])
                    S.tt('dve', zs.ap[0:wd], zs.ap[0:wd], zr.ap[0:wd], ALU.add, [zs, zr], [zs])
                S.dma('pool', d_zT[zi * 128:zi * 128 + wd, c * CH:(c + 1) * CH], zs.ap[0:wd], (zs,), (K_zT[c],))
            for tt_ in range(4):
                pv = PB[5 + tt_ % 2]
                for k in range(8):
                    S.mm(pv.ap[:, 0:72], AF32(U[k].ap[:, tt_ * 128:(tt_ + 1) * 128]), WT.ap[:, k, :], k == 0, k == 7,
                         [WT, U[k]], [pv])
                vw = VW[tt_ % 2]
                S.copy('act', vw.ap, pv.ap[:, 0:72], [pv], [vw])
                t0 = c * CH + tt_ * 128
                S.dma('pool', d_vw[t0:t0 + 128, :], vw.ap, (vw,), (K_vw[c],))
        close_stage(G)

    def stage_rglru(l):
        G = open_stage(40000)
        sets = []
        for u in range(2):
            d = {}
            d['XR'] = Reg(AR.alloc(T + 4), "xr%d" % u)
            for n in ('GT', 'XC', 'RR', 'IG', 'AA', 'MM', 'T1', 'HH'):
                d[n] = Reg(AR.alloc(T), n + str(u))
            sets.append(d)
        WA = Reg(AR.alloc(4 * 128).rearrange("p (c n) -> p c n", c=4), "wa")
        WX = Reg(AR.alloc(4 * 128).rearrange("p (c n) -> p c n", c=4), "wx")
        S.dma('sp', WA.ap, i_rgw[l, 0].rearrange("c p n -> p c n"), (), (WA,))
        S.dma('sp', WX.ap, i_rgw[l, 1].rearrange("c p n -> p c n"), (), (WX,))
        S.dma('sp', P_RG.ap.rearrange("p (c j) -> p c j", c=4), i_rgp[l], (), (P_RG,))
        rg3 = P_RG.ap.rearrange("p (c j) -> p c j", c=4)
        rgd = P_RGD.ap.rearrange("p (j c) -> p j c", j=4)
        S.act(rgd[:, 0, :], rg3[:, :, 7], AF.Exp, [P_RG], [P_RGD], scale=-1.0)
        S.act(rgd[:, 0, :], rgd[:, 0, :], AF.Ln, [P_RGD], [P_RGD], bias=ONE_R.ap, scale=1.0)
        S.ts('dve', rgd[:, 1, :], rgd[:, 0, :], -8.0, None, ALU.mult, None, [P_RGD], [P_RGD])
        S.ts('dve', rgd[:, 2, :], rgd[:, 0, :], -16.0, None, ALU.mult, None, [P_RGD], [P_RGD])
        for d in sets:
            S.memset('dve', d['XR'].ap[:, 0:4], 0.0, [d['XR']])
        un = 0
        for s in range(NS):
            zk = K_zT[s * 4:(s + 1) * 4]
            for ct in range(4):
                d = sets[un % 2]
                pbo = 4 * (un % 2)
                un += 1
                XR, GT, XC, RR, IG, AA, MM, T1, HH = (d[n] for n in ('XR', 'GT', 'XC', 'RR', 'IG', 'AA', 'MM', 'T1', 'HH'))
                S.dma('sp', XR.ap[:, 4:4 + T], d_zT[(ZI_XRNN + ct) * 128:(ZI_XRNN + ct + 1) * 128, s * T:(s + 1) * T],
                      zk, (XR,))
                S.dma('sp', GT.ap, d_zT[(ZI_GATE + ct) * 128:(ZI_GATE + ct + 1) * 128, s * T:(s + 1) * T], zk, (GT,))
                S.act(XC.ap, XR.ap[:, 4:4 + T], AF.Identity, [XR, P_RG], [XC], bias=rg3[:, ct, 4:5], scale=rg3[:, ct, 3:4])
                for j in range(3):
                    S.stt(XC.ap, XR.ap[:, 1 + j:1 + j + T], rg3[:, ct, j:j + 1], XC.ap, ALU.mult, ALU.add,
                          [XR, P_RG, XC], [XC])
                for q in range(4):
                    pr, pi = PB[pbo + q % 2], PB[pbo + 2 + q % 2]
                    sl = slice(q * CH, (q + 1) * CH)
                    S.mm(pr.ap, WA.ap[:, ct, :], XC.ap[:, sl], True, True, [WA, XC], [pr])
                    S.mm(pi.ap, WX.ap[:, ct, :], XC.ap[:, sl], True, True, [WX, XC], [pi])
                    S.act(RR.ap[:, sl], pr.ap, AF.Sigmoid, [pr, P_RG], [RR], bias=rg3[:, ct, 5:6], scale=1.0)
                    S.act(IG.ap[:, sl], pi.ap, AF.Sigmoid, [pi, P_RG], [IG], bias=rg3[:, ct, 6:7], scale=1.0)
                S.act(AA.ap, RR.ap, AF.Exp, [RR, P_RGD], [AA], scale=rgd[:, 1, ct:ct + 1])
                S.act(MM.ap, RR.ap, AF.Exp, [RR, P_RGD], [MM], scale=rgd[:, 2, ct:ct + 1])
                S.act(MM.ap, MM.ap, AF.Sqrt, [MM, ONE_R], [MM], bias=ONE_R.ap, scale=-1.0)
                S.tt('dve', MM.ap, MM.ap, IG.ap, ALU.mult, [MM, IG], [MM])
                S.tt('pool', MM.ap, MM.ap, XC.ap, ALU.mult, [MM, XC], [MM])
                S.op('dve', lambda g, o=HH.ap, a=AA.ap, b=MM.ap: g.tensor_tensor_scan(o, a, b, 0.0, ALU.mult, ALU.add),
                     [AA, MM], [HH])
                gelu_tanh(GT.ap, IG.ap, T1.ap, [GT, T1, IG], [T1, IG])
                S.tt('dve', HH.ap, HH.ap, IG.ap, ALU.mult, [HH, IG], [HH])
                S.dma('pool', d_yT[0, ct * 128:(ct + 1) * 128, s * T:(s + 1) * T], HH.ap, (HH,), (K_yT[0][s],))
        close_stage(G)

    def stage_mla(l):
        G = open_stage(18500)
        QL = [Reg(AR.alloc(CH), "ql%d" % i) for i in range(2)]
        KVL = Reg(AR.alloc(CH), "kvl")
        QN = [Reg(AR.alloc(CH), "qn%d" % i) for i in range(2)]
        KVN = Reg(AR.alloc(CH), "kvn")
        SQ = Reg(AR.alloc(CH), "sq")
        RS = Reg(AR.alloc(CH), "rs")
        SQ2 = [Reg(AR.alloc(CH), "sq2_%d" % i) for i in range(2)]
        RS2 = [Reg(AR.alloc(CH), "rs2_%d" % i) for i in range(2)]
        QG2 = [Reg(AR.alloc(CH), "qg2_%d" % i) for i in range(2)]
        T12 = [Reg(AR.alloc(CH), "t12_%d" % i) for i in range(2)]
        RAW = [Reg(AR.alloc(CH), "raw%d" % i) for i in range(2)]
        KRAW = [Reg(AR.alloc(CH), "kraw%d" % i) for i in range(2)]
        KPE = Reg(AR.alloc(CH), "kpe")
        QG = Reg(AR.alloc(CH), "qg")
        T1 = Reg(AR.alloc(CH), "t1")
        OUT = [Reg(AR.alloc(CH), "out%d" % i) for i in range(2)]
        TC = Reg(AR.alloc(CH), "tc")
        TS_ = Reg(AR.alloc(CH), "tsn")
        VS = [Reg(AR.alloc(512), "vs%d" % i) for i in range(2)]
        WUQ = Reg(AR.alloc(2 * 8 * 96).rearrange("p (k h n) -> p k h n", k=2, h=8), "wuq")
        WK = Reg(AR.alloc(8 * 64).rearrange("p (h n) -> p h n", h=8), "wk")
        WV = Reg(AR.alloc(512), "wv")
        S.dma('sp', WUQ.ap, i_wuq[l], (), (WUQ,))
        S.dma('sp', WK.ap, i_wukvk[l], (), (WK,))
        S.dma('sp', WV.ap, i_wukvv[l], (), (WV,))
        S.dma('sp', P_MLAN.ap, i_mlan[l], (), (P_MLAN,))
        S.dma('sp', P_GAIN.ap, i_gains[l], (), (P_GAIN,))
        for c in range(dbg.get('nch', NCH)):
            tk = slice(c * CH, (c + 1) * CH)
            for k in range(2):
                S.dma('sp', QL[k].ap, d_zT[(ZI_QLAT + k) * 128:(ZI_QLAT + k + 1) * 128, tk], (K_zT[c],), (QL[k],))
            S.dma('sp', KVL.ap, d_zT[ZI_KVLAT * 128:(ZI_KVLAT + 1) * 128, tk], (K_zT[c],), (KVL,))
            S.dma('sp', KPE.ap[64:96], d_zT[ZI_KPE * 128:ZI_KPE * 128 + 32, tk], (K_zT[c],), (KPE,))
            S.dma('sp', TC.ap, d_tab[0, 0, :, tk], (K_tab,), (TC,))
            S.dma('sp', TS_.ap, d_tab[0, 1, :, tk], (K_tab,), (TS_,))
            ps = PB[0]
            for k in range(2):
                S.act(SQ.ap, QL[k].ap, AF.Square, [QL[k]], [SQ])
                S.mm(ps.ap, C('ones'), SQ.ap, k == 0, k == 1, [SQ, K_cst], [ps])
            rstd_from_psum(ps.ap, RS.ap, 256, [ps, EPS_R], [RS])
            for k in range(2):
                S.stt(QN[k].ap, QL[k].ap, P_MLAN.ap[:, k:k + 1], RS.ap, ALU.mult, ALU.mult, [QL[k], P_MLAN, RS], [QN[k]])
            S.act(SQ.ap, KVL.ap, AF.Square, [KVL], [SQ])
            S.mm(ps.ap, C('ones'), SQ.ap, True, True, [SQ, K_cst], [ps])
            rstd_from_psum(ps.ap, RS.ap, 128, [ps, EPS_R], [RS])
            S.stt(KVN.ap, KVL.ap, P_MLAN.ap[:, 2:3], RS.ap, ALU.mult, ALU.mult, [KVL, P_MLAN, RS], [KVN])
            for h in range(8):
                pq = PB[1]
                raw = RAW[h % 2]
                for k in range(2):
                    S.mm(pq.ap[0:96], WUQ.ap[:, k, h, :], QN[k].ap, k == 0, k == 1, [WUQ, QN[k]], [pq])
                S.copy('act', raw.ap[0:96], pq.ap[0:96], [pq], [raw])
                o = OUT[0]
                rope_pipeline(raw, 96, P_GAIN.ap[0:96, 0:1], 'r96', 'ones', 96, TC, TS_, QG2[0], SQ2[0], RS2[0], T12[0], PB[3], PB[4], o, None)
                S.dma('pool', d_qm[h, :, tk], o.ap[0:96], (o,), (K_qm[c],))
                pk = PB[2]
                kraw = KRAW[h % 2]
                S.mm(pk.ap[0:64], WK.ap[:, h, :], KVN.ap, True, True, [WK, KVN], [pk])
                S.copy('act', kraw.ap[0:64], pk.ap[0:64], [pk], [kraw])
                S.copy('act', kraw.ap[64:96], KPE.ap[64:96], [KPE], [kraw])
                o = OUT[1]
                rope_pipeline(kraw, 96, P_GAIN.ap[0:96, 1:2], 'r96', 'ones', 96, TC, TS_, QG2[1], SQ2[1], RS2[1], T12[1], PB[5], PB[6], o, None)
                S.dma('pool', d_km[h, :, tk], o.ap[0:96], (o,), (K_km[c],))
            for tt_ in range(4):
                pv = PB[7]
                S.mm(pv.ap, KVN.ap[:, tt_ * 128:(tt_ + 1) * 128], WV.ap, True, True, [KVN, WV], [pv])
                vs = VS[tt_ % 2]
                S.copy('act', vs.ap, pv.ap, [pv], [vs])
                t0 = c * CH + tt_ * 128
                S.dma('pool', d_vm[t0:t0 + 128, :], vs.ap, (vs,), (K_vm[c],))
        close_stage(G)
        G = open_stage(8000, 12000)
        KHs = Reg(AR.alloc(T), "khs")
        QHs = Reg(AR.alloc(T), "qhs")
        V1s = Reg(AR.alloc(16 * 65).rearrange("p (b n) -> p b n", b=16), "v1s")
        KH = [Reg(RA.alloc(T), "kh%d" % i) for i in range(2)]
        QH = [Reg(RA.alloc(T), "qh%d" % i) for i in range(2)]
        V1 = [Reg(RA.alloc(16 * 65).rearrange("p (b n) -> p b n", b=16), "v1_%d" % i) for i in range(2)]
        PT = [Reg(RA.alloc(CH), "pt%d" % i) for i in range(3)]
        OS = Reg(AR.alloc(CH), "os")
        RD = Reg(AR.alloc(CH), "rd")
        YO = [Reg(AR.alloc(CH), "yo%d" % i) for i in range(2)]
        S.memset('dve', V1s.ap[:, :, 64:65], 1.0, [V1s])
        sc = 96.0 ** -0.5
        it = 0
        for s in range(NS):
            ks = K_km[s * 4:(s + 1) * 4]
            qs = K_qm[s * 4:(s + 1) * 4]
            vsk = K_vm[s * 4:(s + 1) * 4]
            for h in range(8):
                kh, qh, v1 = KH[h % 2], QH[h % 2], V1[h % 2]
                S.dma('sp', KHs.ap[0:96], d_km[h, :, s * T:(s + 1) * T], ks, (KHs,))
                S.dma('sp', QHs.ap[0:96], d_qm[h, :, s * T:(s + 1) * T], qs, (QHs,))
                S.dma('sp', V1s.ap[:, :, 0:64],
                      d_vm[s * T:(s + 1) * T, h * 64:(h + 1) * 64].rearrange("(b p) n -> p b n", p=128), vsk, (V1s,))
                S.copy('pool', kh.ap[0:96], KHs.ap[0:96], [KHs], [kh])
                S.copy('pool', qh.ap[0:96], QHs.ap[0:96], [QHs], [qh])
                S.copy('pool', v1.ap, V1s.ap, [V1s], [v1])
                for qc in range(4):
                    po = PB[6 + qc % 2]
                    nkb = 4 * (qc + 1)
                    for kb in range(nkb):
                        j = max(0, kb - 4 * qc)
                        q0 = qc * CH + j * 128
                        n = CH - j * 128
                        pS = PB[it % 3]
                        pt = PT[it % 3]
                        it += 1
                        S.mm(pS.ap[:, 0:n], kh.ap[0:96, kb * 128:(kb + 1) * 128], qh.ap[0:96, q0:q0 + n], True, True,
                             [kh, qh], [pS])
                        S.act(pt.ap[:, 0:n], pS.ap[:, 0:n], AF.Exp, [pS], [pt], scale=sc)
                        if kb >= 4 * qc:
                            S.tt('pool', pt.ap[:, 0:128], pt.ap[:, 0:128], C('tri_kq'), ALU.mult, [pt, K_cst], [pt])
                        S.mm(po.ap[0:65, j * 128:CH], v1.ap[:, kb, :], pt.ap[:, 0:n], kb == 0, kb == nkb - 1,
                             [v1, pt], [po])
                    S.copy('act', OS.ap[0:65], po.ap[0:65], [po], [OS])
                    pd = PB[3 + qc % 2]
                    S.mm(pd.ap[0:64], C('ones', 128, 0, 64)[64:65], OS.ap[64:65], True, True, [OS, K_cst], [pd])
                    S.recip(RD.ap[0:64], pd.ap[0:64], [pd], [RD])
                    yo = YO[qc % 2]
                    S.tt('dve', yo.ap[0:64], OS.ap[0:64], RD.ap[0:64], ALU.mult, [OS, RD], [yo])
                    S.dma('pool', d_yT[1, h * 64:(h + 1) * 64, s * T + qc * CH:s * T + (qc + 1) * CH], yo.ap[0:64],
                          (yo,), (K_yT[1][s],))
        close_stage(G)

    def stage_dsa(l):
        G = open_stage(8000)
        QD = [Reg(AR.alloc(CH), "qd%d" % i) for i in range(2)]
        SQ2 = [Reg(AR.alloc(CH), "sq%d" % i) for i in range(2)]
        RS2 = [Reg(AR.alloc(CH), "rs%d" % i) for i in range(2)]
        QG2 = [Reg(AR.alloc(CH), "qg%d" % i) for i in range(2)]
        T12 = [Reg(AR.alloc(CH), "t1%d" % i) for i in range(2)]
        OUT = [Reg(AR.alloc(CH), "out%d" % i) for i in range(2)]
        TC = Reg(AR.alloc(CH), "tc")
        TS_ = Reg(AR.alloc(CH), "tsn")
        S.dma('sp', P_GAIN.ap, i_gains[l], (), (P_GAIN,))
        for c in range(dbg.get('nch', NCH)):
            tk = slice(c * CH, (c + 1) * CH)
            S.dma('sp', TC.ap, d_tab[1, 0, :, tk], (K_tab,), (TC,))
            S.dma('sp', TS_.ap, d_tab[1, 1, :, tk], (K_tab,), (TS_,))
            for k in range(5):
                qd = QD[k % 2]
                o = OUT[k % 2]
                if k < 4:
                    S.dma('sp', qd.ap, d_zT[(ZI_QDSA + k) * 128:(ZI_QDSA + k + 1) * 128, tk], (K_zT[c],), (qd,))
                    rope_pipeline(qd, 128, P_GAIN.ap[:, 2:3], 'r64', 'bd64', 64, TC, TS_, QG2[k % 2], SQ2[k % 2], RS2[k % 2], T12[k % 2], PB[2 * (k % 2)], PB[2 * (k % 2) + 1], o, None)
                    S.dma('pool', d_qd[k * 128:(k + 1) * 128, tk], o.ap, (o,), (K_qd[c],))
                else:
                    S.dma('sp', qd.ap[0:64], d_zT[ZI_KDSA * 128:ZI_KDSA * 128 + 64, tk], (K_zT[c],), (qd,))
                    rope_pipeline(qd, 64, P_GAIN.ap[0:64, 3:4], 'r64', 'bd64', 64, TC, TS_, QG2[k % 2], SQ2[k % 2], RS2[k % 2], T12[k % 2], PB[2 * (k % 2)], PB[2 * (k % 2) + 1], o, None)
                    S.dma('pool', d_kd[:, tk], o.ap[0:64], (o,), (K_kd[c],))
        close_stage(G)
        G = open_stage(33600, 2700)
        _uid[0] += 1
        gm = nc.sbuf_tensor("mtb%d" % _uid[0], [128, 2 * 16 * CH], mybir.dt.bfloat16)
        mt_t = gm.__enter__()
        G.append(gm)
        mt4 = mt_t[:].rearrange("p (i b n) -> p i b n", i=2, b=16)
        MTs = [Reg(mt4[:, i], "mt%d" % i) for i in range(2)]
        KD2 = Reg(AR.alloc(T), "kd2")
        QI = Reg(AR.alloc(2 * T).rearrange("p (k t) -> p k t", k=2), "qi")
        KIB = Reg(AR.alloc(4 * T).rearrange("p (h t) -> p h t", h=4), "kib")
        V1s = Reg(AR.alloc(16 * 65).rearrange("p (b n) -> p b n", b=16), "v1s")
        V1 = Reg(RA.alloc(16 * 65).rearrange("p (b n) -> p b n", b=16), "v1")
        WI = Reg(AR.alloc(16 * 8).rearrange("p (b n) -> p b n", b=16), "wi")
        SCs = [Reg(AR.alloc(T), "sc%d" % i) for i in range(2)]
        WKs = [Reg(AR.alloc(T), "wk%d" % i) for i in range(2)]
        RL = [Reg(AR.alloc(CH), "rl%d" % i) for i in range(2)]
        M8s = [Reg(AR.alloc(8), "m8_%d" % i) for i in range(2)]
        THRs = [Reg(AR.alloc(1), "thr%d" % i) for i in range(2)]
        QC = Reg(AR.alloc(4 * CH).rearrange("p (k t) -> p k t", k=4), "qc")
        PT = [Reg(RA.alloc(CH), "pt%d" % i) for i in range(3)]
        OSs = [Reg(AR.alloc(CH), "os%d" % i) for i in range(8)]
        RD = Reg(AR.alloc(CH), "rd")
        YO = [Reg(AR.alloc(CH), "yo%d" % i) for i in range(2)]
        S.memset('dve', V1s.ap[:, :, 64:65], 1.0, [V1s])
        S.memset('dve', KIB.ap, 0.0, [KIB])
        sc = 64.0 ** -0.5
        itc = [0]

        def idx_pair(s, qc, pair):
            qbs = [qc * 4 + pair * 2, qc * 4 + pair * 2 + 1]
            for i, qb in enumerate(qbs):
                SC = SCs[i]
                for kb in range(qb + 1):
                    for k in range(2):
                        pr = PB[k]
                        rl = RL[k]
                        S.mm(pr.ap, QI.ap[:, k, qb * 128:(qb + 1) * 128], KIB.ap[:, :, kb * 128:(kb + 1) * 128],
                             True, True, [QI, KIB], [pr])
                        S.act(rl.ap, pr.ap, AF.Relu, [pr], [rl])
                        for h4 in range(4):
                            hh = k * 4 + h4
                            dst = SC.ap[:, kb * 128:(kb + 1) * 128]
                            if hh == 0:
                                S.ts('dve', dst, rl.ap[:, 0:128], WI.ap[:, qb, 0:1], None, ALU.mult, None,
                                     [rl, WI], [SC])
                            else:
                                S.stt(dst, rl.ap[:, h4 * 128:(h4 + 1) * 128], WI.ap[:, qb, hh:hh + 1], dst,
                                      ALU.mult, ALU.add, [rl, WI, SC], [SC])
                dg = SC.ap[:, qb * 128:(qb + 1) * 128]
                S.tt('dve', dg, dg, C('tri_qk'), ALU.mult, [SC, K_cst], [SC])
                S.tt('dve', dg, dg, C('neg_qk'), ALU.add, [SC, K_cst], [SC])
            srcs = [SCs[0], SCs[1]]
            for r in range(32):
                for i, qb in enumerate(qbs):
                    if qb < 2:
                        continue
                    nk = (qb + 1) * 128
                    S.op('dve', lambda g, o=M8s[i].ap, a=srcs[i].ap[:, 0:nk]: g.max(o, a), [srcs[i]], [M8s[i]])
                    if r < 31:
                        S.op('dve', lambda g, o=WKs[i].ap[:, 0:nk], a=M8s[i].ap, v=srcs[i].ap[:, 0:nk]:
                             g.match_replace(o, a, v, -BIG), [M8s[i], srcs[i]], [WKs[i]])
                        srcs[i] = WKs[i]
            for i, qb in enumerate(qbs):
                nk = (qb + 1) * 128
                SC, WK_, M8, THR = SCs[i], WKs[i], M8s[i], THRs[i]
                if qb >= 2:
                    S.ts('dve', THR.ap, M8.ap[:, 7:8], -0.5 * BIG, None, ALU.max, None, [M8], [THR])
                else:
                    S.memset('dve', THR.ap, -0.5 * BIG, [THR])
                S.ts('dve', WK_.ap[:, 0:nk], SC.ap[:, 0:nk], THR.ap, None, ALU.is_ge, None, [SC, THR], [WK_])

        def idx_pair_B(s, qc, pair):
            MT = MTs[qc % 2]
            for i in range(2):
                qb = qc * 4 + pair * 2 + i
                ql = qb - qc * 4
                WK_ = WKs[i]
                for kb in range(qb + 1):
                    pt_ = PB[2 + kb % 2]
                    S.transpose(pt_.ap[:, 0:128], WK_.ap[:, kb * 128:(kb + 1) * 128], C('ident'), [WK_, K_cst], [pt_])
                    S.copy('act', MT.ap[:, kb, ql * 128:(ql + 1) * 128], pt_.ap[:, 0:128], [pt_], [MT])

        def attn_heads(s, qc, heads):
            MT = MTs[qc % 2]
            nkb = 4 * (qc + 1)
            for h in heads:
                p0 = (h % 2) * 64
                po = PB[6 + h % 2]
                for kb in range(nkb):
                    j = max(0, kb - 4 * qc)
                    n = CH - j * 128
                    pS = PB[4 + itc[0] % 2]
                    pt = PT[itc[0] % 3]
                    itc[0] += 1
                    S.mm(pS.ap[:, 0:n], KD2.ap[p0:p0 + 64, kb * 128:(kb + 1) * 128],
                         QC.ap[p0:p0 + 64, h // 2, j * 128:CH], True, True, [KD2, QC], [pS])
                    S.act(pt.ap[:, 0:n], pS.ap[:, 0:n], AF.Exp, [pS], [pt], scale=sc)
                    S.tt('pool', pt.ap[:, 0:n], pt.ap[:, 0:n], MT.ap[:, kb, j * 128:CH], ALU.mult, [pt, MT], [pt])
                    S.mm(po.ap[0:65, j * 128:CH], V1.ap[:, kb, :], pt.ap[:, 0:n], kb == 0, kb == nkb - 1,
                         [V1, pt], [po])
                S.copy('act', OSs[h].ap[0:65], po.ap[0:65], [po], [OSs[h]])

        def attn_norm(s, qc):
            for h in range(8):
                OS = OSs[h]
                pd = PB[2 + h % 2]
                S.mm(pd.ap[0:64], C('ones', 128, 0, 64)[64:65], OS.ap[64:65], True, True, [OS, K_cst], [pd])
                S.recip(RD.ap[0:64], pd.ap[0:64], [pd], [RD])
                yo = YO[h % 2]
                S.tt('dve', yo.ap[0:64], OS.ap[0:64], RD.ap[0:64], ALU.mult, [OS, RD], [yo])
                S.dma('pool', d_yT[2, h * 64:(h + 1) * 64, s * T + qc * CH:s * T + (qc + 1) * CH], yo.ap[0:64],
                      (yo,), (K_yT[2][s],))

        for s in range(NS):
            zk = K_zT[s * 4:(s + 1) * 4]
            ts_ = slice(s * T, (s + 1) * T)
            S.dma('sp', KD2.ap[0:64], d_kd[:, ts_], K_kd[s * 4:(s + 1) * 4], (KD2,))
            S.dma('sp', KD2.ap[64:128], d_kd[:, ts_], K_kd[s * 4:(s + 1) * 4], (KD2,))
            for k in range(2):
                S.dma('sp', QI.ap[:, k, :], d_zT[(ZI_QIDX + k) * 128:(ZI_QIDX + k + 1) * 128, ts_], zk, (QI,))
            for h4 in range(4):
                S.dma('sp', KIB.ap[h4 * 32:(h4 + 1) * 32, h4, :], d_zT[ZI_KIDX * 128:ZI_KIDX * 128 + 32, ts_], zk, (KIB,))
            vwk = K_vw[s * 4:(s + 1) * 4]
            S.dma('sp', V1s.ap[:, :, 0:64], d_vw[ts_, 0:64].rearrange("(b p) n -> p b n", p=128), vwk, (V1s,))
            S.copy('pool', V1.ap, V1s.ap, [V1s], [V1])
            S.dma('sp', WI.ap, d_vw[ts_, 64:72].rearrange("(b p) n -> p b n", p=128), vwk, (WI,))
            for pair in range(2):
                idx_pair(s, 0, pair)
                idx_pair_B(s, 0, pair)
            for qc in range(4):
                S.dma('sp', QC.ap, d_qd[:, s * T + qc * CH:s * T + (qc + 1) * CH].rearrange("(k p) t -> p k t", p=128),
                      K_qd[s * 4:(s + 1) * 4], (QC,))
                nxt = qc + 1 < 4
                if nxt:
                    idx_pair(s, qc + 1, 0)
                attn_heads(s, qc, range(0, 4))
                if nxt:
                    idx_pair_B(s, qc + 1, 0)
                    idx_pair(s, qc + 1, 1)
                attn_heads(s, qc, range(4, 8))
                if nxt:
                    idx_pair_B(s, qc + 1, 1)
                attn_norm(s, qc)
        close_stage(G)

    def stage_s5(l):
        G = open_stage(44000)
        YSUM[0] = Reg(AR.ap[:, AR.n - 2 * T:AR.n - T], "ysum0")
        YSUM[1] = Reg(AR.ap[:, AR.n - T:AR.n], "ysum1")
        LR = Reg(AR.alloc(T), "lr")
        LI = Reg(AR.alloc(T), "li")
        DT = Reg(AR.alloc(T), "dt")
        E1 = Reg(AR.alloc(T), "e1")
        E2 = Reg(AR.alloc(T), "e2")
        E3 = Reg(AR.alloc(T), "e3")
        E4 = Reg(AR.alloc(T), "e4")
        FR = Reg(AR.alloc(T), "fr")
        FI = Reg(AR.alloc(T), "fi")
        BRE = Reg(AR.alloc(T), "bre")
        BIM = Reg(AR.alloc(T), "bim")
        CRE = Reg(AR.alloc(T), "cre")
        CIM = Reg(AR.alloc(T), "cim")
        allk = [LR, LI, DT, E1, E2, E3, E4, FR, FI]
        S.dma('sp', LR.ap, i_s5bc[l, 0], (), (LR,))
        S.dma('sp', LI.ap, i_s5bc[l, 1], (), (LI,))
        S.dma('sp', DT.ap, i_s5bc[l, 2], (), (DT,))
        S.dma('sp', BRE.ap.rearrange("p (a m) -> p a m", a=16), i_s5b[l, 0], (), (BRE,))
        S.dma('sp', BIM.ap.rearrange("p (a m) -> p a m", a=16), i_s5b[l, 1], (), (BIM,))
        S.dma('sp', CRE.ap.rearrange("p (a m) -> p a m", a=16), i_s5c[l, 0], (), (CRE,))
        S.dma('sp', CIM.ap.rearrange("p (a m) -> p a m", a=16), i_s5c[l, 1], (), (CIM,))
        S.dma('sp', P_S5PP.ap.rearrange("p (j a) -> p j a", j=3), i_s5pp[l].rearrange("j p a -> p j a"), (), (P_S5PP,))
        S.dma('sp', P_S5V.ap.rearrange("p (a j) -> p a j", a=4), i_s5v[l], (), (P_S5V,))
        S.act(DT.ap, DT.ap, AF.Exp, allk, allk)
        S.tt('dve', E1.ap, LR.ap, DT.ap, ALU.mult, allk, allk)
        S.act(E1.ap, E1.ap, AF.Exp, allk, allk)
        S.tt('dve', E2.ap, LI.ap, DT.ap, ALU.mult, allk, allk)
        sincos(E2.ap, E3.ap, E4.ap, FR.ap, 128, T, allk, allk)
        S.tt('dve', E3.ap, E3.ap, E1.ap, ALU.mult, allk, allk)
        S.tt('dve', E4.ap, E4.ap, E1.ap, ALU.mult, allk, allk)
        S.ts('dve', E4.ap, E4.ap, -1.0, None, ALU.add, None, allk, allk)
        S.tt('dve', E1.ap, LR.ap, LR.ap, ALU.mult, allk, allk)
        S.tt('dve', E2.ap, LI.ap, LI.ap, ALU.mult, allk, allk)
        S.tt('dve', E1.ap, E1.ap, E2.ap, ALU.add, allk, allk)
        S.recip(E1.ap, E1.ap, allk, allk)
        S.tt('dve', FR.ap, E4.ap, LR.ap, ALU.mult, allk, allk)
        S.tt('dve', E2.ap, E3.ap, LI.ap, ALU.mult, allk, allk)
        S.tt('dve', FR.ap, FR.ap, E2.ap, ALU.add, allk, allk)
        S.tt('dve', FR.ap, FR.ap, E1.ap, ALU.mult, allk, allk)
        S.tt('dve', FI.ap, E3.ap, LR.ap, ALU.mult, allk, allk)
        S.tt('dve', E2.ap, E4.ap, LI.ap, ALU.mult, allk, allk)
        S.tt('dve', FI.ap, FI.ap, E2.ap, ALU.subtract, allk, allk)
        S.tt('dve', FI.ap, FI.ap, E1.ap, ALU.mult, allk, allk)
        bk = allk + [BRE, BIM]
        S.tt('dve', E1.ap, FR.ap, BRE.ap, ALU.mult, bk, bk)
        S.tt('dve', E2.ap, FI.ap, BIM.ap, ALU.mult, bk, bk)
        S.tt('dve', E1.ap, E1.ap, E2.ap, ALU.subtract, bk, bk)
        S.tt('dve', E2.ap, FR.ap, BIM.ap, ALU.mult, bk, bk)
        S.tt('dve', E3.ap, FI.ap, BRE.ap, ALU.mult, bk, bk)
        S.tt('dve', E2.ap, E2.ap, E3.ap, ALU.add, bk, bk)
        S.copy('dve', BRE.ap, E1.ap, bk, bk)
        S.copy('dve', BIM.ap, E2.ap, bk, bk)
        pp = P_S5PP.ap.rearrange("p (j a) -> p j a", j=3)
        sd = P_S5D.ap.rearrange("p (j a) -> p j a", j=4)
        kk = [P_S5PP, P_S5D, P_TMP]
        S.act(sd[:, 0, :], pp[:, 2, :], AF.Exp, kk, kk)
        S.tt('dve', sd[:, 1, :], pp[:, 0, :], sd[:, 0, :], ALU.mult, kk, kk)
        S.act(sd[:, 1, :], sd[:, 1, :], AF.Exp, kk, kk)
        S.tt('dve', sd[:, 2, :], pp[:, 1, :], sd[:, 0, :], ALU.mult, kk, kk)
        iw = IWK.ap[:, 0:16]
        tmp16 = P_TMP.ap[:, 0:16]
        S.ts('dve', iw, sd[:, 2, :], 1.0 / TWO_PI, None, ALU.mult, None, kk, [IWK])
        S.stt(tmp16, iw, -CW1, sd[:, 2, :], ALU.mult, ALU.add, kk + [IWK], kk)
        S.stt(sd[:, 2, :], iw, -CW2, tmp16, ALU.mult, ALU.add, kk + [IWK], kk)
        S.barrier()
        AR.off = 13 * T
        COS = Reg(AR.ap[:, 0:T], "cos")
        SIN = Reg(AR.ap[:, T:2 * T], "sin")
        ANG = Reg(AR.ap[:, 2 * T:3 * T], "ang")
        TMP = Reg(AR.ap[:, 3 * T:4 * T], "tmp")
        BUR = Reg(AR.ap[:, 4 * T:5 * T], "bur")
        BUI = Reg(AR.ap[:, 5 * T:6 * T], "bui")
        WR = Reg(AR.ap[:, 6 * T:7 * T], "wr")
        WI_ = Reg(AR.ap[:, 7 * T:8 * T], "wi")
        T2 = Reg(AR.ap[:, 8 * T:9 * T], "t2")
        T2b = Reg(AR.alloc(T), "t2b")
        U5 = [Reg(AR.alloc(T), "u5_%d" % i) for i in range(NS)]
        XRE = Reg(AR.alloc(T), "xre")
        XIM = Reg(AR.alloc(T), "xim")
        YS = Reg(AR.alloc(T), "ys")
        b3r = BRE.ap.rearrange("p (a m) -> p a m", a=16)
        b3i = BIM.ap.rearrange("p (a m) -> p a m", a=16)
        c3r = CRE.ap.rearrange("p (a m) -> p a m", a=16)
        c3i = CIM.ap.rearrange("p (a m) -> p a m", a=16)
        sv = P_S5V.ap.rearrange("p (a j) -> p a j", a=4)
        YP = [PB[4], PB[5], PB[6], PB[7]]
        for ot in range(4):
            for s in range(NS):
                S.dma('sp', U5[s].ap, d_zT[(ZI_US5 + ot) * 128:(ZI_US5 + ot + 1) * 128, s * T:(s + 1) * T],
                      K_zT[s * 4:(s + 1) * 4], (U5[s],))
            for sti in range(4):
                st = ot * 4 + sti
                S.ts('dve', ANG.ap, C('iota'), sd[:, 2, st:st + 1], None, ALU.mult, None, [K_cst, P_S5D], [ANG])
                sincos(ANG.ap, SIN.ap, COS.ap, TMP.ap, 128, T, [ANG, TMP], [ANG, TMP, SIN, COS])
                for s in range(NS):
                    for q in range(4):
                        sl = slice(q * CH, (q + 1) * CH)
                        pr, pi = PB[q % 2], PB[2 + q % 2]
                        S.mm(pr.ap, b3r[:, st, :], U5[s].ap[:, sl], True, True, [BRE, U5[s]], [pr])
                        S.mm(pi.ap, b3i[:, st, :], U5[s].ap[:, sl], True, True, [BIM, U5[s]], [pi])
                        S.copy('act', BUR.ap[:, sl], pr.ap, [pr], [BUR])
                        S.copy('act', BUI.ap[:, sl], pi.ap, [pi], [BUI])
                    S.tt('dve', WR.ap, BUR.ap, COS.ap, ALU.mult, [BUR, COS], [WR])
                    S.tt('pool', T2.ap, BUI.ap, SIN.ap, ALU.mult, [BUI, SIN], [T2])
                    S.tt('pool', WI_.ap, BUI.ap, COS.ap, ALU.mult, [BUI, COS], [WI_])
                    S.tt('pool', T2b.ap, BUR.ap, SIN.ap, ALU.mult, [BUR, SIN], [T2b])
                    S.tt('dve', WR.ap, WR.ap, T2.ap, ALU.add, [WR, T2], [WR])
                    S.tt('dve', WI_.ap, WI_.ap, T2b.ap, ALU.subtract, [WI_, T2b], [WI_])
                    rho = sd[:, 1, st:st + 1].to_broadcast([128, T])
                    S.op('dve', lambda g, o=BUR.ap, a=rho, b=WR.ap: g.tensor_tensor_scan(o, a, b, 0.0, ALU.mult, ALU.add),
                         [WR, P_S5D], [BUR])
                    S.op('dve', lambda g, o=BUI.ap, a=rho, b=WI_.ap: g.tensor_tensor_scan(o, a, b, 0.0, ALU.mult, ALU.add),
                         [WI_, P_S5D], [BUI])
                    S.tt('dve', XRE.ap, BUR.ap, COS.ap, ALU.mult, [BUR, COS], [XRE])
                    S.tt('pool', T2.ap, BUI.ap, SIN.ap, ALU.mult, [BUI, SIN], [T2])
                    S.tt('pool', XIM.ap, BUI.ap, COS.ap, ALU.mult, [BUI, COS], [XIM])
                    S.tt('pool', T2b.ap, BUR.ap, SIN.ap, ALU.mult, [BUR, SIN], [T2b])
                    S.tt('dve', XRE.ap, XRE.ap, T2.ap, ALU.subtract, [XRE, T2], [XRE])
                    S.stt(XIM.ap, T2b.ap, -1.0, XIM.ap, ALU.mult, ALU.subtract, [T2b, XIM], [XIM])
                    for q in range(4):
                        sl = slice(q * CH, (q + 1) * CH)
                        yp = YP[q]
                        S.mm(yp.ap, c3r[:, st, :], XRE.ap[:, sl], True, False, [CRE, XRE], [yp])
                        S.mm(yp.ap, c3i[:, st, :], XIM.ap[:, sl], False, True, [CIM, XIM], [yp])
                        ysum = YSUM[s]
                        if sti == 0:
                            S.stt(ysum.ap[:, sl], U5[s].ap[:, sl], sv[:, ot, 0:1], yp.ap, ALU.mult, ALU.add,
                                  [U5[s], P_S5V, yp], [ysum])
                        else:
                            S.tt('dve', ysum.ap[:, sl], ysum.ap[:, sl], yp.ap, ALU.add, [ysum, yp], [ysum])
            for s in range(NS):
                gelu_tanh(YSUM[s].ap, YS.ap, T2.ap, [YSUM[s], T2, YS], [T2, YS])
                S.dma('pool', d_yg[ot * 128:(ot + 1) * 128, s * T:(s + 1) * T], YS.ap, (YS,),
                      K_yg[s * 4:(s + 1) * 4])
        close_stage(G)
        G = open_stage(6000)
        YG = Reg(AR.alloc(4 * CH).rearrange("p (k t) -> p k t", k=4), "yg")
        WG = Reg(AR.alloc(4 * 512).rearrange("p (k n) -> p k n", k=4), "wglu")
        SG = [Reg(AR.alloc(CH), "sg%d" % i) for i in range(2)]
        S.dma('sp', WG.ap, i_wglu[l], (), (WG,))
        for c in range(dbg.get('nch', NCH)):
            s = c // (NCH // NS)
            tk = slice(c * CH, (c + 1) * CH)
            S.dma('sp', YG.ap, d_yg[:, tk].rearrange("(k p) t -> p k t", p=128), (K_yg[c],), (YG,))
            for m in range(4):
                pg = PB[m % 2]
                for k in range(4):
                    S.mm(pg.ap, WG.ap[:, k, m * 128:(m + 1) * 128], YG.ap[:, k, :], k == 0, k == 3, [WG, YG], [pg])
                sg = SG[m % 2]
                S.act(sg.ap, pg.ap, AF.Sigmoid, [pg, P_S5V], [sg], bias=sv[:, m, 1:2], scale=1.0)
                S.tt('dve', sg.ap, sg.ap, YG.ap[:, m, :], ALU.mult, [sg, YG], [sg])
                S.dma('pool', d_yT[3, m * 128:(m + 1) * 128, tk], sg.ap, (sg,), (K_yT[3][s],))
        close_stage(G)

    YSUM = [None, None]

    def stage_s5_wrap(l):
        stage_s5(l)

    def stage_merge(l):
        G = open_stage(21000, 22000)
        Xb, X = chunk_regs("X")
        U2sb, U2s = chunk_regs("U2s")
        U2b, U2 = chunk_regs("U2", RA)
        MGb, MG = chunk_regs("MG", RA)
        YBs = [Reg(AR.alloc(4 * CH).rearrange("p (k t) -> p k t", k=4), "ybs%d" % b) for b in range(2)]
        YB = [Reg(RA.alloc(4 * CH).rearrange("p (k t) -> p k t", k=4), "yb%d" % b) for b in range(4)]
        WGs = [Reg(AR.alloc(8 * 128).rearrange("p (k n) -> p k n", k=8), "wgs%d" % i) for i in range(2)]
        WBs = [Reg(AR.alloc(4 * 128).rearrange("p (k n) -> p k n", k=4), "wbs%d" % i) for i in range(2)]
        WOs = [Reg(AR.alloc(8 * 128).rearrange("p (k n) -> p k n", k=8), "wos%d" % i) for i in range(2)]
        WGt = [Reg(RA.alloc(8 * 128).rearrange("p (k n) -> p k n", k=8), "wgt%d" % i) for i in range(2)]
        WBr = [Reg(RA.alloc(4 * 128).rearrange("p (k n) -> p k n", k=4), "wbr%d" % i) for i in range(2)]
        WO = [Reg(RA.alloc(8 * 128).rearrange("p (k n) -> p k n", k=8), "wo%d" % i) for i in range(2)]
        SG = [Reg(AR.alloc(CH), "sg%d" % i) for i in range(2)]
        TM = [Reg(AR.alloc(CH), "tm%d" % i) for i in range(2)]
        it = 0
        for c in range(dbg.get('nch', NCH)):
            s = c // (NCH // NS)
            tk = slice(c * CH, (c + 1) * CH)
            S.dma('sp', Xb.rearrange("p (k t) -> p k t", k=8), dram_chunk(d_xT, c), (K_xT[c],), X)
            S.dma('sp', U2sb.rearrange("p (k t) -> p k t", k=8), dram_chunk(d_u2T, c), (K_u2T[c],), U2s)
            for k in range(8):
                S.copy('pool' if k % 2 == 0 else 'act', U2[k].ap, U2s[k].ap, [U2s[k]], [U2[k]])
            for b in range(4):
                ybs = YBs[b % 2]
                S.dma('sp', ybs.ap, d_yT[b, :, tk].rearrange("(k p) t -> p k t", p=128), (K_yT[b][s],), (ybs,))
                S.copy('pool' if b % 2 == 0 else 'act', YB[b].ap, ybs.ap, [ybs], [YB[b]])
            for m in range(8):
                for b in range(4):
                    wgs, wbs, wg, wb = WGs[it % 2], WBs[it % 2], WGt[it % 2], WBr[it % 2]
                    S.dma('sp', wgs.ap, i_wgate[l, b, m], (), (wgs,))
                    S.dma('sp', wbs.ap, i_wbr[l, b, m], (), (wbs,))
                    S.copy('pool', wg.ap, wgs.ap, [wgs], [wg])
                    S.copy('act', wb.ap, wbs.ap, [wbs], [wb])
                    pg, pb = PB[it % 2], PB[2 + it % 2]
                    for k in range(8):
                        S.mm(pg.ap, wg.ap[:, k, :], U2[k].ap, k == 0, k == 7, [wg, U2[k]], [pg])
                    for k in range(4):
                        S.mm(pb.ap, wb.ap[:, k, :], YB[b].ap[:, k, :], k == 0, k == 3, [wb, YB[b]], [pb])
                    sg = SG[it % 2]
                    S.act(sg.ap, pg.ap, AF.Sigmoid, [pg], [sg])
                    if b == 0:
                        S.tt('dve', MG[m].ap, sg.ap, pb.ap, ALU.mult, [sg, pb], [MG[m]])
                    else:
                        tm = TM[it % 2]
                        S.tt('dve', tm.ap, sg.ap, pb.ap, ALU.mult, [sg, pb], [tm])
                        S.tt('pool', MG[m].ap, MG[m].ap, tm.ap, ALU.add, [MG[m], tm], [MG[m]])
                    it += 1
            for m in range(8):
                wos, wo = WOs[m % 2], WO[m % 2]
                S.dma('sp', wos.ap, i_wout[l, m], (), (wos,))
                S.copy('pool' if m % 2 == 0 else 'act', wo.ap, wos.ap, [wos], [wo])
                po = PB[4 + m % 2]
                for k in range(8):
                    S.mm(po.ap, wo.ap[:, k, :], MG[k].ap, k == 0, k == 7, [wo, MG[k]], [po])
                S.stt(X[m].ap, po.ap, G3[:, 1, m, s:s + 1], X[m].ap, ALU.mult, ALU.add, [po, P_G, X[m]], [X[m]])
            S.dma('pool', dram_chunk(d_xT, c), Xb.rearrange("p (k t) -> p k t", k=8), X, (K_xT[c],))
        close_stage(G)

    stages = dbg.get('stages')
    for l in range(n_layers):
        def want(n):
            return stages is None or n in stages
        if want('mod'):
            stage_mod(l)
        if want('ffn0'):
            stage_ffn(l, 0)
        if want('win'):
            stage_win(l)
        if want('rglru'):
            stage_rglru(l)
        if want('mla'):
            stage_mla(l)
        if want('dsa'):
            stage_dsa(l)
        if want('s5'):
            stage_s5_wrap(l)
        if want('merge'):
            stage_merge(l)
        if want('ffn1'):
            stage_ffn(l, 1)

    GE = open_stage(4200)
    Xb, X = chunk_regs("X")
    for c in range(dbg.get('nch', NCH)):
        S.dma('sp', Xb.rearrange("p (k t) -> p k t", k=8), dram_chunk(d_xT, c), (K_xT[c],), X)
        S.dma('pool', dram_chunk(o_yT, c), Xb.rearrange("p (k t) -> p k t", k=8), X, ())
    S.emit()
    for g in reversed(GE):
        g.__exit__(None, None, None)
    stack.close()
    return nc, S


def _consts():
    c = np.zeros((128, NCONST), np.float32)

    def put(name, a):
        o, w = CO[name]
        c[:a.shape[0], o:o + a.shape[1]] = a
    put('ident', np.eye(128, dtype=np.float32))
    put('ones', np.ones((128, 128), np.float32))
    bd = np.zeros((128, 128), np.float32)
    bd[:64, :64] = 1
    bd[64:, 64:] = 1
    put('bd64', bd)
    r96 = np.zeros((128, 128), np.float32)
    for i in range(16):
        r96[80 + i, 64 + i] = -1.0
        r96[64 + i, 80 + i] = 1.0
    put('r96', r96)
    r64 = np.zeros((128, 128), np.float32)
    for hb in range(2):
        for i in range(8):
            r64[hb * 64 + 8 + i, hb * 64 + i] = -1.0
            r64[hb * 64 + i, hb * 64 + 8 + i] = 1.0
    put('r64', r64)
    r32 = np.zeros((128, 128), np.float32)
    for hb in range(4):
        for i in range(4):
            r32[hb * 32 + 4 + i, hb * 32 + i] = -1.0
            r32[hb * 32 + i, hb * 32 + 4 + i] = 1.0
    put('r32', r32)
    p = np.arange(128)[:, None]
    f = np.arange(128)[None, :]
    put('tri_kq', (p <= f).astype(np.float32))
    tq = (f <= p).astype(np.float32)
    put('tri_qk', tq)
    put('neg_qk', np.where(f <= p, np.float32(0), np.float32(-BIG)).astype(np.float32))

    def inv(rot):
        return (np.float32(500000.0) ** (-(np.arange(0, rot, 2, dtype=np.float32)) / np.float32(rot))).astype(np.float32)
    im = np.zeros((128, 1), np.float32)
    im[64:80, 0] = inv(32)
    im[80:96, 0] = inv(32)
    put('inv_mla', im)
    idd = np.zeros((128, 1), np.float32)
    for hb in range(2):
        idd[hb * 64:hb * 64 + 8, 0] = inv(16)
        idd[hb * 64 + 8:hb * 64 + 16, 0] = inv(16)
    put('inv_dsa', idd)
    ii = np.zeros((128, 1), np.float32)
    for hb in range(4):
        ii[hb * 32:hb * 32 + 4, 0] = inv(8)
        ii[hb * 32 + 4:hb * 32 + 8, 0] = inv(8)
    put('inv_idx', ii)
    put('iota', np.broadcast_to(np.arange(T, dtype=np.float32)[None, :], (128, T)))
    return c


def _layout_weights(I):
    f = np.float32
    A = lambda a: np.ascontiguousarray(np.asarray(a, dtype=f))
    W = {}
    W['consts'] = _consts()
    W['ada_w'] = A(np.asarray(I['ada_w']).reshape(L_, 8, 128, 18, 512).transpose(0, 3, 2, 1, 4))
    W['ada_b'] = A(np.repeat(np.asarray(I['ada_b']).reshape(L_, 72, 128).transpose(0, 2, 1)[..., None], NS, axis=-1))
    W['norm_g'] = A(np.repeat(np.asarray(I['norm_g']).reshape(L_, 3, 8, 128).transpose(0, 1, 3, 2)[..., None], NS, axis=-1))
    W['w1'] = A(np.asarray(I['ffn_w1']).reshape(L_, 2, 8, 128, 22, 128).transpose(0, 1, 4, 3, 2, 5))
    W['w3'] = A(np.asarray(I['ffn_w3']).reshape(L_, 2, 8, 128, 22, 128).transpose(0, 1, 4, 3, 2, 5))
    W['w2'] = A(np.asarray(I['ffn_w2']).reshape(L_, 2, 22, 128, 8, 128).transpose(0, 1, 4, 3, 2, 5))
    win = np.asarray(I['w_in'])
    wz = np.zeros((L_, NZ, 128, 8, 128), f)
    for zi, (c0, wd) in enumerate(Z_TILES):
        wz[:, zi, :, :, :wd] = win[:, :, c0:c0 + wd].reshape(L_, 8, 128, wd).transpose(0, 2, 1, 3)
    W['win_z'] = wz
    wt = np.concatenate([win[:, :, 2016:2080], win[:, :, 2368:2376]], axis=-1)
    W['win_tok'] = A(wt.reshape(L_, 8, 128, 72).transpose(0, 2, 1, 3))
    W['win_gate'] = A(win[:, :, 2888:].reshape(L_, 8, 128, 4, 8, 128).transpose(0, 3, 4, 2, 1, 5))
    rgp = np.zeros((L_, 128, 4, 8), f)
    cw = np.asarray(I['conv_w'])
    for j in range(4):
        rgp[:, :, :, j] = cw[:, j].reshape(L_, 4, 128).transpose(0, 2, 1)
    for j, n in enumerate(('conv_b', 'rg_ba', 'rg_bx', 'rg_lambda')):
        rgp[:, :, :, 4 + j] = np.asarray(I[n]).reshape(L_, 4, 128).transpose(0, 2, 1)
    W['rg_par'] = rgp
    rgw = np.zeros((L_, 2, 4, 128, 128), f)
    for wi_, n in enumerate(('rg_wa', 'rg_wx')):
        w = np.asarray(I[n])
        for h in range(8):
            ct, o = h // 2, (h % 2) * 64
            rgw[:, wi_, ct, o:o + 64, o:o + 64] = w[:, h]
    W['rg_w'] = rgw
    mn = np.zeros((L_, 128, 3), f)
    mn[:, :, 0:2] = np.asarray(I['mla_q_norm']).reshape(L_, 2, 128).transpose(0, 2, 1)
    mn[:, :, 2] = np.asarray(I['mla_kv_norm'])
    W['mla_norm'] = mn
    perm = np.concatenate([np.arange(32, 96), np.arange(0, 32)])
    wuq = np.asarray(I['mla_w_uq']).reshape(L_, 2, 128, 8, 96)[..., perm]
    W['w_uq'] = A(wuq.transpose(0, 2, 1, 3, 4))
    wukv = np.asarray(I['mla_w_ukv']).reshape(L_, 128, 8, 128)
    W['w_ukv_k'] = A(wukv[..., :64])
    W['w_ukv_v'] = A(wukv[..., 64:].reshape(L_, 128, 512))
    g = np.zeros((L_, 128, 4), f)
    mg = np.asarray(I['mla_qk_gain'])[..., perm]
    g[:, :96, 0] = mg[:, 0]
    g[:, :96, 1] = mg[:, 1]
    dg = np.asarray(I['dsa_qk_gain'])
    g[:, :, 2] = np.tile(dg[:, 0], (1, 2))
    g[:, :, 3] = np.tile(dg[:, 1], (1, 2))
    W['qk_gains'] = g
    lr = np.asarray(I['s5_lambda_re']).reshape(L_, 2048)
    li = np.asarray(I['s5_lambda_im']).reshape(L_, 2048)
    ld = np.repeat(np.asarray(I['s5_log_dt']), 64, axis=1)
    st3 = np.stack([lr, li, ld], axis=1)
    W['s5_bc'] = A(np.broadcast_to(st3[:, :, None, :], (L_, 3, 128, 2048)))
    W['s5_pp'] = A(st3.reshape(L_, 3, 16, 128).transpose(0, 1, 3, 2))
    sb = np.zeros((L_, 2, 128, 16, 128), f)
    scm = np.zeros((L_, 2, 128, 16, 128), f)
    for ri, (bn, cn) in enumerate((('s5_b_re', 's5_c_re'), ('s5_b_im', 's5_c_im'))):
        b = np.asarray(I[bn])
        cc = np.asarray(I[cn])
        for gi in range(32):
            st, half = gi // 2, gi % 2
            r0 = 16 * (gi % 8)
            sb[:, ri, r0:r0 + 16, st, half * 64:(half + 1) * 64] = b[:, gi].transpose(0, 2, 1)
            scm[:, ri, half * 64:(half + 1) * 64, st, r0:r0 + 16] = cc[:, gi].transpose(0, 2, 1)
    W['s5_b'] = sb
    W['s5_c'] = scm
    sv = np.zeros((L_, 128, 4, 2), f)
    sv[..., 0] = np.asarray(I['s5_d']).reshape(L_, 4, 128).transpose(0, 2, 1)
    sv[..., 1] = np.asarray(I['s5_b_glu']).reshape(L_, 4, 128).transpose(0, 2, 1)
    W['s5_vec'] = sv
    W['w_glu'] = A(np.asarray(I['s5_w_glu']).reshape(L_, 4, 128, 512).transpose(0, 2, 1, 3))
    W['w_branch'] = A(np.asarray(I['w_branch']).reshape(L_, 4, 4, 128, 8, 128).transpose(0, 1, 4, 3, 2, 5))
    W['w_out'] = A(np.asarray(I['w_out']).reshape(L_, 8, 128, 8, 128).transpose(0, 3, 2, 1, 4))
    return W


def _core_inputs(I, W, c):
    x = np.asarray(I['x'], dtype=np.float32)[NS * c:NS * (c + 1)]
    m = dict(W)
    m['xT'] = np.ascontiguousarray(x.reshape(NT, D).T)
    cc = np.asarray(I['c'], dtype=np.float32)[NS * c:NS * (c + 1)]
    m['cT'] = np.ascontiguousarray(cc.reshape(NS, 8, 128).transpose(2, 1, 0))
    pos = np.asarray(I['positions']).astype(np.int32)[NS * c:NS * (c + 1)]
    m['posb'] = np.ascontiguousarray(np.broadcast_to(pos[None], (128, NS, T)))
    return m


_CACHE = {}


def kernel(**inputs):
    if 'nc' not in _CACHE:
        _CACHE['nc'] = build_program()[0]
    nc = _CACHE['nc']
    W = _layout_weights(inputs)
    in_maps = [_core_inputs(inputs, W, c) for c in range(8)]
    res = run_bass_kernel_spmd(nc, in_maps, core_ids=list(range(8)))
    out = np.empty((16, T, D), np.float32)
    for c in range(8):
        yT = np.asarray(res.results[c]["yT_out"])
        out[NS * c:NS * (c + 1)] = yT.T.reshape(NS, T, D)
    return out
```

```python
import numpy as np
from contextlib import ExitStack
import concourse.bass as bass
import concourse.mybir as mybir
from concourse.bass_utils import run_bass_kernel_spmd

F32 = mybir.dt.float32
I32 = mybir.dt.int32
F32R = mybir.dt.float32r


def AF32(ap):
    return ap.bitcast(F32)
ALU = mybir.AluOpType
AF = mybir.ActivationFunctionType

L_ = 4
D = 1024
T = 2048
NS = 2
NT = NS * T
CH = 512
NCH = NT // CH
DFF = 2816
EPS = 1e-6
BIG = 1.0e30
PI = float(np.pi)
TWO_PI = float(2 * np.pi)
CW1 = 6.28125
CW2 = float(2 * np.pi - 6.28125)
PI_SAFE = 3.1415925

Z_TILES = ([(0 + 128 * i, 128) for i in range(4)] + [(512 + 128 * i, 128) for i in range(4)]
           + [(1024, 128), (1152, 128), (1280, 128), (1408, 32)]
           + [(1440 + 128 * i, 128) for i in range(4)] + [(1952, 64)]
           + [(2080, 128), (2208, 128), (2336, 32)] + [(2376 + 128 * i, 128) for i in range(4)])
ZI_XRNN, ZI_GATE, ZI_QLAT, ZI_KVLAT, ZI_KPE, ZI_QDSA, ZI_KDSA, ZI_QIDX, ZI_KIDX, ZI_US5 = 0, 4, 8, 10, 11, 12, 16, 17, 19, 20
NZ = len(Z_TILES)

CO = {}
_off = 0
for _n, _w in [('ident', 128), ('ones', 128), ('bd64', 128), ('r96', 128), ('r64', 128), ('r32', 128),
               ('tri_kq', 128), ('tri_qk', 128), ('neg_qk', 128), ('inv_mla', 1), ('inv_dsa', 1),
               ('inv_idx', 1), ('iota', T)]:
    CO[_n] = (_off, _w)
    _off += _w
NCONST = _off


class Reg:
    __slots__ = ('ap', 'name')

    def __init__(self, ap, name=''):
        self.ap = ap
        self.name = name


class Sched:
    ENG = ('pe', 'act', 'dve', 'pool', 'sp')
    EPOCH = 16000
    NSLOT = 12

    def __init__(self, nc, stack):
        self.nc = nc
        self.stack = stack
        self.streams = {e: [] for e in self.ENG}
        self.count = {e: 0 for e in self.ENG}
        self.sems = {e: [] for e in self.ENG}
        self.waited = {e: {} for e in self.ENG}
        self.last_w = {}
        self.readers = {}
        self.semobj = {}
        self.slots = {q: [[self._newsem(), 0] for _ in range(self.NSLOT)] for q in ('sp', 'pool')}
        self.slot_rr = {'sp': 0, 'pool': 0}
        self.all_tokens = []
        self.ninstr = 0

    def _newsem(self):
        s = self.stack.enter_context(self.nc.semaphore())
        sid = len(self.semobj)
        self.semobj[sid] = s
        return sid

    def _esem(self, e, epoch):
        while len(self.sems[e]) <= epoch:
            self.sems[e].append(self._newsem())
        return self.sems[e][epoch]

    def _deps(self, e, reads, writes):
        waits = {}

        def need(tok):
            if tok is None:
                return
            sid, val, prod = tok
            if prod == e and e == 'pe':
                return
            if self.waited[e].get(sid, 0) >= val:
                return
            if waits.get(sid, 0) < val:
                waits[sid] = val

        for k in reads:
            need(self.last_w.get(id(k)))
        for k in writes:
            need(self.last_w.get(id(k)))
            for r in self.readers.get(id(k), ()):
                need(r)
        return waits

    def _commit(self, e, tok, reads, writes, waits):
        for sid, v in waits.items():
            self.waited[e][sid] = v
        for k in reads:
            self.readers.setdefault(id(k), []).append(tok)
        for k in writes:
            self.last_w[id(k)] = tok
            self.readers[id(k)] = []

    def op(self, e, fn, reads=(), writes=()):
        waits = self._deps(e, reads, writes)
        idx = self.count[e]
        sid = self._esem(e, idx // self.EPOCH)
        tok = (sid, idx % self.EPOCH + 1, e)
        self.count[e] += 1
        self._commit(e, tok, reads, writes, waits)
        self.streams[e].append((list(waits.items()), fn, sid, 1))
        self.ninstr += 1

    def dma(self, q, out_ap, in_ap, reads=(), writes=()):
        waits = self._deps(q, reads, writes)
        si = self.slot_rr[q]
        self.slot_rr[q] = (si + 1) % self.NSLOT
        slot = self.slots[q][si]
        if slot[1] + 16 > 30000:
            slot[0] = self._newsem()
            slot[1] = 0
        if slot[1] > 0 and self.waited[q].get(slot[0], 0) < slot[1]:
            waits[slot[0]] = max(waits.get(slot[0], 0), slot[1])
        slot[1] += 16
        tok = (slot[0], slot[1], 'dma_' + q)
        self._commit(q, tok, reads, writes, waits)

        def fn(eng, o=out_ap, i=in_ap):
            return eng.dma_start(out=o, in_=i)
        self.streams[q].append((list(waits.items()), fn, slot[0], 16))
        self.all_tokens.append(tok)
        self.ninstr += 1

    def barrier(self):
        toks = []
        for e in self.ENG:
            idx = self.count[e]
            if idx > 0:
                toks.append((self._esem(e, (idx - 1) // self.EPOCH), (idx - 1) % self.EPOCH + 1, e))
        for q in ('sp', 'pool'):
            for slot in self.slots[q]:
                if slot[1] > 0:
                    toks.append((slot[0], slot[1], 'dma_' + q))
        for e in self.ENG:
            waits = {}
            for sid, val, prod in toks:
                if prod == e:
                    continue
                if self.waited[e].get(sid, 0) >= val:
                    continue
                waits[sid] = max(waits.get(sid, 0), val)
            for sid, v in waits.items():
                self.waited[e][sid] = v
            if waits:
                self.streams[e].append((list(waits.items()), None, None, 0))
        self.last_w.clear()
        self.readers.clear()

    def emit(self):
        nc = self.nc
        self.barrier()
        semobj = self.semobj
        streams = self.streams

        def replay(e, eng):
            for waits, fn, sid, inc in streams[e]:
                for s, v in waits:
                    eng.wait_ge(semobj[s], v)
                if fn is not None:
                    ins = fn(eng)
                    ins.then_inc(semobj[sid], inc)

        with nc.Block() as block:
            @block.tensor
            def _(eng):
                replay('pe', eng)

            @block.scalar
            def _(eng):
                replay('act', eng)

            @block.vector
            def _(eng):
                replay('dve', eng)

            @block.gpsimd
            def _(eng):
                replay('pool', eng)

            @block.sync
            def _(eng):
                replay('sp', eng)

    def mm(self, out, lhsT, rhs, start, stop, reads, writes):
        self.op('pe', lambda g, o=out, l=lhsT, r=rhs, a=start, b=stop: g.matmul(o, l, r, start=a, stop=b),
                reads, writes)

    def transpose(self, out, in_, ident, reads, writes):
        self.op('pe', lambda g, o=out, i=in_, d=ident: g.transpose(o, i, d), reads, writes)

    def act(self, out, in_, func, reads, writes, bias=None, scale=None):
        kw = {}
        if bias is not None:
            kw['bias'] = bias
        if scale is not None:
            kw['scale'] = scale
        self.op('act', lambda g, o=out, i=in_, f=func, k=kw: g.activation(o, i, f, **k), reads, writes)

    def tt(self, e, out, in0, in1, op, reads, writes):
        self.op(e, lambda g, o=out, a=in0, b=in1, p=op: g.tensor_tensor(o, a, b, p), reads, writes)

    def ts(self, e, out, in0, s1, s2, op0, op1, reads, writes):
        if s2 is None:
            self.op(e, lambda g, o=out, a=in0, x=s1, p=op0: g.tensor_scalar(o, a, x, None, p), reads, writes)
        else:
            self.op(e, lambda g, o=out, a=in0, x=s1, y=s2, p=op0, q=op1: g.tensor_scalar(o, a, x, y, p, q),
                    reads, writes)

    def stt(self, out, in0, scalar, in1, op0, op1, reads, writes):
        self.op('dve', lambda g, o=out, a=in0, s=scalar, b=in1, p=op0, q=op1:
                g.scalar_tensor_tensor(o, a, s, b, p, q), reads, writes)

    def copy(self, e, out, in_, reads, writes):
        if e == 'act':
            self.op('act', lambda g, o=out, i=in_: g.copy(o, i), reads, writes)
        else:
            self.op(e, lambda g, o=out, i=in_: g.tensor_copy(o, i), reads, writes)

    def recip(self, out, in_, reads, writes):
        self.op('dve', lambda g, o=out, i=in_: g.reciprocal(o, i), reads, writes)

    def memset(self, e, ap, val, writes):
        self.op(e, lambda g, a=ap, v=val: g.memset(a, v), (), writes)


class Arena:
    def __init__(self, ap, nwords):
        self.ap = ap
        self.n = nwords
        self.off = 0

    def reset(self):
        self.off = 0

    def alloc(self, words, name=''):
        assert self.off + words <= self.n, (name, self.off, words, self.n)
        a = self.ap[:, self.off:self.off + words]
        self.off += words
        return a


def build_program(n_layers=L_, debug=None):
    nc = bass.Bass("TRN2", target_bir_lowering=False)
    stack = ExitStack()

    def din(name, shape, dt=F32):
        return nc.dram_tensor(name, list(shape), dt, kind="ExternalInput").ap()

    dbg = debug or {}
    FB_LIM = dbg.get('fblim', 11)
    M_LIM = dbg.get('mlim', 8)

    def dscr(name, shape, dt=F32):
        kind = "ExternalOutput" if dbg.get(name) else "Internal"
        return nc.dram_tensor(name, list(shape), dt, kind=kind).ap()

    i_xT = din("xT", [D, NT])
    i_cT = din("cT", [128, 8, NS])
    i_pos = din("posb", [128, NS, T], I32)
    i_const = din("consts", [128, NCONST])
    i_adaw = din("ada_w", [L_, 18, 128, 8, 512])
    i_adab = din("ada_b", [L_, 128, 72, NS])
    i_normg = din("norm_g", [L_, 3, 128, 8, NS])
    i_w1 = din("w1", [L_, 2, 22, 128, 8, 128])
    i_w3 = din("w3", [L_, 2, 22, 128, 8, 128])
    i_w2 = din("w2", [L_, 2, 8, 128, 22, 128])
    i_winz = din("win_z", [L_, NZ, 128, 8, 128])
    i_wtok = din("win_tok", [L_, 128, 8, 72])
    i_wgate = din("win_gate", [L_, 4, 8, 128, 8, 128])
    i_rgp = din("rg_par", [L_, 128, 4, 8])
    i_rgw = din("rg_w", [L_, 2, 4, 128, 128])
    i_mlan = din("mla_norm", [L_, 128, 3])
    i_wuq = din("w_uq", [L_, 128, 2, 8, 96])
    i_wukvk = din("w_ukv_k", [L_, 128, 8, 64])
    i_wukvv = din("w_ukv_v", [L_, 128, 512])
    i_gains = din("qk_gains", [L_, 128, 4])
    i_s5bc = din("s5_bc", [L_, 3, 128, 2048])
    i_s5pp = din("s5_pp", [L_, 3, 128, 16])
    i_s5b = din("s5_b", [L_, 2, 128, 16, 128])
    i_s5c = din("s5_c", [L_, 2, 128, 16, 128])
    i_s5v = din("s5_vec", [L_, 128, 4, 2])
    i_wglu = din("w_glu", [L_, 128, 4, 512])
    i_wbr = din("w_branch", [L_, 4, 8, 128, 4, 128])
    i_wout = din("w_out", [L_, 8, 128, 8, 128])
    o_yT = nc.dram_tensor("yT_out", [D, NT], F32, kind="ExternalOutput").ap()

    d_xT = dscr("s_xT", [D, NT])
    d_u2T = dscr("s_u2T", [D, NT])
    d_zT = dscr("s_zT", [NZ * 128, NT])
    d_vw = dscr("s_vw", [NT, 72])
    d_yT = dscr("s_yT", [4, 512, NT])
    d_qm = dscr("s_qm", [8, 96, NT])
    d_km = dscr("s_km", [8, 96, NT])
    d_vm = dscr("s_vm", [NT, 512])
    d_qd = dscr("s_qd", [512, NT])
    d_kd = dscr("s_kd", [64, NT])
    d_yg = dscr("s_yg", [512, NT])
    d_tab = dscr("s_tab", [3, 2, 128, NT])

    def keys(n):
        return [Reg(None, n + str(i)) for i in range(NCH)]
    K_xT, K_u2T, K_zT, K_vw, K_qm, K_km, K_vm, K_qd, K_kd, K_yg = (keys(n) for n in
                                                                  ("xT", "u2T", "zT", "vw", "qm", "km", "vm", "qd", "kd", "yg"))
    K_yT = [[Reg(None, "yT%d_%d" % (b, s)) for s in range(NS)] for b in range(4)]
    K_tab = Reg(None, "tab")

    cst_t = stack.enter_context(nc.sbuf_tensor("cst", [128, NCONST], F32))
    par_t = stack.enter_context(nc.sbuf_tensor("par", [128, 1200], F32))
    iwk_t = stack.enter_context(nc.sbuf_tensor("iwk", [128, T], I32))
    banks = [stack.enter_context(nc.psum_tensor("pb%d" % i, [128, 512], F32)) for i in range(8)]
    PB = [Reg(b[:], "pb%d" % i) for i, b in enumerate(banks)]
    AR = Arena(None, 0)
    RA = Arena(None, 0)
    _uid = [0]

    def open_stage(f32_words, r_words=0):
        _uid[0] += 1
        g1 = nc.sbuf_tensor("fa%d" % _uid[0], [128, f32_words], F32)
        t1 = g1.__enter__()
        AR.ap, AR.n, AR.off = t1[:], f32_words, 0
        guards = [g1]
        if r_words:
            g2 = nc.sbuf_tensor("ra%d" % _uid[0], [128, r_words], F32R)
            t2 = g2.__enter__()
            RA.ap, RA.n, RA.off = t2[:], r_words, 0
            guards.append(g2)
        return guards

    def close_stage(guards):
        S.barrier()
        for g in reversed(guards):
            g.__exit__(None, None, None)
    CST = cst_t[:]
    PAR = par_t[:]
    IWK = Reg(iwk_t[:], "iwk")
    K_cst = Reg(None, "cst")

    S = Sched(nc, stack)

    def C(name, rows=128, c0=0, c1=None):
        o, w = CO[name]
        if c1 is None:
            c1 = w
        return CST[0:rows, o + c0:o + c1]

    S.dma('sp', CST, i_const, (), (K_cst,))

    par_off = [0]

    def palloc(w):
        a = PAR[:, par_off[0]:par_off[0] + w]
        par_off[0] += w
        return a
    P_MOD = Reg(palloc(144), "mod")
    P_ADAB = Reg(palloc(144), "adab")
    P_NG = Reg(palloc(48), "ng")
    P_A = Reg(palloc(48), "A")
    P_G = Reg(palloc(48), "G")
    P_CACT = Reg(palloc(16), "cact")
    P_RG = Reg(palloc(32), "rgp")
    P_RGD = Reg(palloc(16), "rgd")
    P_MLAN = Reg(palloc(3), "mlan")
    P_GAIN = Reg(palloc(4), "gains")
    P_S5PP = Reg(palloc(48), "s5pp")
    P_S5D = Reg(palloc(64), "s5d")
    P_S5V = Reg(palloc(8), "s5v")
    P_TMP = Reg(palloc(64), "ptmp")

    mod3 = P_MOD.ap.rearrange("p (j k s) -> p j k s", j=9, k=8)
    A3 = P_A.ap.rearrange("p (j k s) -> p j k s", j=3, k=8)
    G3 = P_G.ap.rearrange("p (j k s) -> p j k s", j=3, k=8)

    def modcol(j, k, s):
        return mod3[:, j, k, s:s + 1]

    def sincos(ang, sin_out, cos_out, tmp, rows, n, rk, wk):
        iw = IWK.ap[0:rows, 0:n]
        S.ts('dve', iw, ang, 1.0 / TWO_PI, None, ALU.mult, None, rk, [IWK])
        S.stt(tmp, iw, -CW1, ang, ALU.mult, ALU.add, rk + [IWK], wk)
        S.stt(ang, iw, -CW2, tmp, ALU.mult, ALU.add, rk + [IWK], wk)
        S.ts('dve', tmp, ang, PI, -TWO_PI, ALU.is_gt, ALU.mult, rk, wk)
        S.tt('dve', ang, ang, tmp, ALU.add, rk, wk)
        S.ts('dve', tmp, ang, -PI, TWO_PI, ALU.is_lt, ALU.mult, rk, wk)
        S.tt('dve', ang, ang, tmp, ALU.add, rk, wk)
        S.ts('dve', ang, ang, PI_SAFE, -PI_SAFE, ALU.min, ALU.max, rk, wk)
        S.act(sin_out, ang, AF.Sin, rk, wk)
        S.ts('dve', ang, ang, PI / 2, None, ALU.add, None, rk, wk)
        S.ts('dve', tmp, ang, PI, -TWO_PI, ALU.is_gt, ALU.mult, rk, wk)
        S.tt('dve', ang, ang, tmp, ALU.add, rk, wk)
        S.ts('dve', ang, ang, PI_SAFE, -PI_SAFE, ALU.min, ALU.max, rk, wk)
        S.act(cos_out, ang, AF.Sin, rk, wk)

    def rstd_from_psum(ps_ap, out_ap, n_feat, reads, writes):
        S.act(out_ap, ps_ap, AF.Sqrt, reads, writes, bias=EPS_AP[0:out_ap.shape[0], :], scale=1.0 / n_feat)
        S.recip(out_ap, out_ap, writes, writes)

    def gelu_tanh(x, out, t1, rk, wk):
        S.tt('pool', t1, x, x, ALU.mult, rk, wk)
        S.ts('pool', t1, t1, 0.044715, 1.0, ALU.mult, ALU.add, rk, wk)
        S.tt('pool', t1, t1, x, ALU.mult, rk, wk)
        S.act(t1, t1, AF.Sigmoid, rk, wk, scale=1.5957691216057308)
        S.tt('dve', out, x, t1, ALU.mult, rk, wk)

    EPS_R = Reg(palloc(1), "eps")
    EPS_AP = EPS_R.ap
    S.memset('dve', EPS_AP, EPS, [EPS_R])
    ONE_R = Reg(palloc(1), "one")
    S.memset('dve', ONE_R.ap, 1.0, [ONE_R])

    G0 = open_stage(16000)
    for c in range(dbg.get('nch', NCH)):
        S.dma('sp', d_xT[:, c * CH:(c + 1) * CH], i_xT[:, c * CH:(c + 1) * CH], (), (K_xT[c],))

    R_ang = Reg(AR.alloc(T), "ang")
    R_tmp = Reg(AR.alloc(T), "tmp")
    R_sin = Reg(AR.alloc(T), "sin")
    R_cos = Reg(AR.alloc(T), "cos")
    R_posi = Reg(AR.alloc(NS * T).bitcast(I32).rearrange("p (s t) -> p s t", s=NS), "posi")
    S.dma('sp', R_posi.ap, i_pos, (), (R_posi,))
    for f, inv in enumerate(('inv_mla', 'inv_dsa', 'inv_idx')):
        for s in range(NS):
            S.ts('dve', R_ang.ap, R_posi.ap[:, s, :], C(inv), None, ALU.mult, None, [R_posi, K_cst], [R_ang])
            sincos(R_ang.ap, R_sin.ap, R_cos.ap, R_tmp.ap, 128, T, [R_ang, R_tmp], [R_ang, R_tmp, R_sin, R_cos])
            S.dma('pool', d_tab[f, 0, :, s * T:(s + 1) * T], R_cos.ap, (R_cos,), (K_tab,))
            S.dma('pool', d_tab[f, 1, :, s * T:(s + 1) * T], R_sin.ap, (R_sin,), (K_tab,))
    S.dma('sp', P_CACT.ap.rearrange("p (k s) -> p k s", k=8), i_cT, (), (P_CACT,))
    S.act(P_CACT.ap, P_CACT.ap, AF.Silu, [P_CACT], [P_CACT])
    close_stage(G0)

    def norm_mod(Xr, Ur, UTr, SQr, RSr, j, s):
        ps = PB[0]
        for k in range(8):
            S.act(SQr[k % 2].ap, Xr[k].ap, AF.Square, [Xr[k]], [SQr[k % 2]])
            S.mm(ps.ap, C('ones'), SQr[k % 2].ap, k == 0, k == 7, [SQr[k % 2], K_cst], [ps])
        rstd_from_psum(ps.ap, RSr.ap, D, [ps, EPS_R], [RSr])
        for k in range(8):
            ut = UTr[k % 2]
            S.tt('dve', ut.ap, Xr[k].ap, RSr.ap, ALU.mult, [Xr[k], RSr], [ut])
            S.ts('pool', Ur[k].ap, ut.ap, A3[:, j, k, s:s + 1], modcol(3 * j, k, s), ALU.mult, ALU.add,
                 [ut, P_A, P_MOD], [Ur[k]])

    def chunk_regs(name, arena=None):
        base = (arena or AR).alloc(8 * CH, name)
        b3 = base.rearrange("p (k t) -> p k t", k=8)
        return base, [Reg(b3[:, k, :], name + str(k)) for k in range(8)]

    def dram_chunk(d, c):
        return d.rearrange("(k p) t -> p k t", p=128)[:, :, c * CH:(c + 1) * CH]

    def stage_mod(l):
        G = open_stage(8192)
        WB = [Reg(AR.alloc(8 * 512).rearrange("p (k n) -> p k n", k=8), "adaw%d" % i) for i in range(2)]
        cact3 = P_CACT.ap.rearrange("p (k s) -> p k s", k=8)
        S.dma('sp', P_ADAB.ap.rearrange("p (m s) -> p m s", s=NS), i_adab[l], (), (P_ADAB,))
        S.dma('sp', P_NG.ap.rearrange("p (j k s) -> p j k s", j=3, k=8), i_normg[l].rearrange("j p k s -> p j k s"),
              (), (P_NG,))
        ps = PB[1]
        for blk in range(18):
            w = WB[blk % 2]
            S.dma('sp', w.ap, i_adaw[l, blk], (), (w,))
            for mi in range(4):
                mt = blk * 4 + mi
                for k in range(8):
                    S.mm(ps.ap[:, mt * 2:mt * 2 + 2], w.ap[:, k, mi * 128:(mi + 1) * 128], cact3[:, k, :],
                         k == 0, k == 7, [w, P_CACT], [ps])
        S.tt('dve', P_MOD.ap, ps.ap[:, 0:144], P_ADAB.ap, ALU.add, [ps, P_ADAB], [P_MOD])
        ng3 = P_NG.ap.rearrange("p (j k s) -> p j k s", j=3, k=8)
        for j in range(3):
            S.ts('dve', A3[:, j], mod3[:, 3 * j + 1], 1.0, None, ALU.add, None, [P_MOD], [P_A])
            S.tt('dve', A3[:, j], A3[:, j], ng3[:, j], ALU.mult, [P_A, P_NG], [P_A])
            if j == 1:
                S.ts('dve', G3[:, j], mod3[:, 3 * j + 2], 1.0, None, ALU.add, None, [P_MOD], [P_G])
            else:
                S.ts('dve', G3[:, j], mod3[:, 3 * j + 2], 0.5, 0.5, ALU.mult, ALU.add, [P_MOD], [P_G])
        close_stage(G)

    def stage_ffn(l, jf):
        G = open_stage(21504, 25088)
        jn = 0 if jf == 0 else 2
        Xb, X = chunk_regs("X")
        Ub, U = chunk_regs("U", RA)
        UT = [Reg(AR.alloc(CH), "ut%d" % i) for i in range(2)]
        SQ = [Reg(AR.alloc(CH), "sq%d" % i) for i in range(2)]
        RS = Reg(AR.alloc(CH), "rs")
        SA = [Reg(AR.alloc(CH), "sa%d" % i) for i in range(2)]
        H = [Reg(RA.alloc(CH), "h%d" % i) for i in range(22)]
        W1s = [Reg(AR.alloc(8 * 128).rearrange("p (k n) -> p k n", k=8), "w1s%d" % i) for i in range(4)]
        W3s = [Reg(AR.alloc(8 * 128).rearrange("p (k n) -> p k n", k=8), "w3s%d" % i) for i in range(4)]
        W2s = [Reg(AR.alloc(22 * 128).rearrange("p (k n) -> p k n", k=22), "w2s%d" % i) for i in range(2)]
        W1 = [Reg(RA.alloc(8 * 128).rearrange("p (k n) -> p k n", k=8), "w1_%d" % i) for i in range(2)]
        W3 = [Reg(RA.alloc(8 * 128).rearrange("p (k n) -> p k n", k=8), "w3_%d" % i) for i in range(2)]
        W2 = [Reg(RA.alloc(22 * 128).rearrange("p (k n) -> p k n", k=22), "w2_%d" % i) for i in range(2)]
        for c in range(dbg.get('nch', NCH)):
            s = c // (NCH // NS)
            S.dma('sp', Xb.rearrange("p (k t) -> p k t", k=8), dram_chunk(d_xT, c), (K_xT[c],), X)
            norm_mod(X, U, UT, SQ, RS, jn, s)
            for ft in range(22):
                w1s, w3s, w1, w3 = W1s[ft % 4], W3s[ft % 4], W1[ft % 2], W3[ft % 2]
                S.dma('sp', w1s.ap, i_w1[l, jf, ft], (), (w1s,))
                S.dma('sp', w3s.ap, i_w3[l, jf, ft], (), (w3s,))
                S.copy('pool', w1.ap, w1s.ap, [w1s], [w1])
                S.copy('act', w3.ap, w3s.ap, [w3s], [w3])
                pa, pb = PB[2 + ft % 2], PB[4 + ft % 2]
                for k in range(8):
                    S.mm(pa.ap, w1.ap[:, k, :], U[k].ap, k == 0, k == 7, [w1, U[k]], [pa])
                for k in range(8):
                    S.mm(pb.ap, w3.ap[:, k, :], U[k].ap, k == 0, k == 7, [w3, U[k]], [pb])
                sa = SA[ft % 2]
                S.act(sa.ap, pa.ap, AF.Silu, [pa], [sa])
                S.tt('dve', H[ft].ap, sa.ap, pb.ap, ALU.mult, [sa, pb], [H[ft]])
            for m in range(8):
                w2s, w2 = W2s[m % 2], W2[m % 2]
                S.dma('sp', w2s.ap, i_w2[l, jf, m], (), (w2s,))
                S.copy('pool' if m % 2 == 0 else 'act', w2.ap, w2s.ap, [w2s], [w2])
                po = PB[6 + m % 2]
                for kt in range(22):
                    S.mm(po.ap, w2.ap[:, kt, :], H[kt].ap, kt == 0, kt == 21, [w2, H[kt]], [po])
                S.stt(X[m].ap, po.ap, G3[:, jn, m, s:s + 1], X[m].ap, ALU.mult, ALU.add, [po, P_G, X[m]], [X[m]])
            S.dma('pool', dram_chunk(d_xT, c), Xb.rearrange("p (k t) -> p k t", k=8), X, (K_xT[c],))
        close_stage(G)

    def rope_pipeline(raw, P, gcol, rname, bdname, nfeat, tabC, tabS, QG, SQ, RS, T1, psA, psB, out, rk):
        S.act(SQ.ap[0:P], raw.ap[0:P], AF.Square, [raw], [SQ])
        S.mm(psA.ap[0:P], C(bdname, P, 0, P), SQ.ap[0:P], True, True, [SQ, K_cst], [psA])
        rstd_from_psum(psA.ap[0:P], RS.ap[0:P], nfeat, [psA, EPS_R], [RS])
        S.stt(QG.ap[0:P], raw.ap[0:P], gcol, RS.ap[0:P], ALU.mult, ALU.mult, [raw, RS, P_GAIN], [QG])
        S.mm(psB.ap[0:P], C(rname, P, 0, P), QG.ap[0:P], True, True, [QG, K_cst], [psB])
        S.tt('pool', T1.ap[0:P], QG.ap[0:P], tabC.ap[0:P], ALU.mult, [QG, tabC], [T1])
        S.tt('dve', out.ap[0:P], psB.ap[0:P], tabS.ap[0:P], ALU.mult, [psB, tabS], [out])
        S.tt('dve', out.ap[0:P], out.ap[0:P], T1.ap[0:P], ALU.add, [out, T1], [out])

    def stage_win(l):
        G = open_stage(14000, 6200)
        Xb, X = chunk_regs("X")
        Ub, U = chunk_regs("U", RA)
        UT = [Reg(AR.alloc(CH), "ut%d" % i) for i in range(2)]
        SQ = [Reg(AR.alloc(CH), "sq%d" % i) for i in range(2)]
        RS = Reg(AR.alloc(CH), "rs")
        ZS = [Reg(AR.alloc(CH), "zs%d" % i) for i in range(2)]
        ZR = [Reg(AR.alloc(CH), "zr%d" % i) for i in range(2)]
        W = [Reg(AR.alloc(8 * 128).rearrange("p (k n) -> p k n", k=8), "wz%d" % i) for i in range(2)]
        WR_ = [Reg(RA.alloc(8 * 128).rearrange("p (k n) -> p k n", k=8), "wzr%d" % i) for i in range(2)]
        WT = Reg(AR.alloc(8 * 72).rearrange("p (k n) -> p k n", k=8), "wtok")
        VW = [Reg(AR.alloc(72), "vw%d" % i) for i in range(2)]
        TC = Reg(AR.alloc(CH), "tc")
        TS_ = Reg(AR.alloc(CH), "tsn")
        S.dma('sp', WT.ap, i_wtok[l], (), (WT,))
        for c in range(dbg.get('nch', NCH)):
            s = c // (NCH // NS)
            S.dma('sp', Xb.rearrange("p (k t) -> p k t", k=8), dram_chunk(d_xT, c), (K_xT[c],), X)
            S.dma('sp', TC.ap, d_tab[2, 0, :, c * CH:(c + 1) * CH], (K_tab,), (TC,))
            S.dma('sp', TS_.ap, d_tab[2, 1, :, c * CH:(c + 1) * CH], (K_tab,), (TS_,))
            norm_mod(X, U, UT, SQ, RS, 1, s)
            S.dma('pool', dram_chunk(d_u2T, c), AF32(Ub).rearrange("p (k t) -> p k t", k=8), U, (K_u2T[c],))
            for zi, (c0, wd) in enumerate(Z_TILES):
                w = W[zi % 2]
                S.dma('sp', w.ap, i_winz[l, zi], (), (w,))
                pz = PB[1 + zi % 2]
                if wd == 128 and zi not in (ZI_QIDX, ZI_QIDX + 1):
                    wr = WR_[zi % 2]
                    S.copy('pool' if zi % 2 == 0 else 'act', wr.ap, w.ap, [w], [wr])
                    for k in range(8):
                        S.mm(pz.ap, wr.ap[:, k, :], U[k].ap, k == 0, k == 7, [wr, U[k]], [pz])
                else:
                    for k in range(8):
                        S.mm(pz.ap[0:wd], w.ap[:, k, 0:wd], AF32(U[k].ap), k == 0, k == 7, [w, U[k]], [pz])
                zs = ZS[zi % 2]
                S.copy('act', zs.ap[0:wd], pz.ap[0:wd], [pz], [zs])
                if zi in (ZI_QIDX, ZI_QIDX + 1, ZI_KIDX):
                    pr = PB[3 + zi % 2]
                    zr = ZR[zi % 2]
                    S.mm(pr.ap[0:wd], C('r32', wd, 0, wd), zs.ap[0:wd], True, True, [zs, K_cst], [pr])
                    S.tt('dve', zr.ap[0:wd], pr.ap[0:wd], TS_.ap[0:wd], ALU.mult, [pr, TS_], [zr])
                    S.tt('pool', zs.ap[0:wd], zs.ap[0:wd], TC.ap[0:wd], ALU.mult, [zs, TC], [zs])
                    S.tt('dve', zs.ap[0:wd], zs.ap[0:wd], zr.ap[0:wd], ALU.add, [zs, zr], [zs])
                S.dma('pool', d_zT[zi * 128:zi * 128 + wd, c * CH:(c + 1) * CH], zs.ap[0:wd], (zs,), (K_zT[c],))
            for tt_ in range(4):
                pv = PB[5 + tt_ % 2]
                for k in range(8):
                    S.mm(pv.ap[:, 0:72], AF32(U[k].ap[:, tt_ * 128:(tt_ + 1) * 128]), WT.ap[:, k, :], k == 0, k == 7,
                         [WT, U[k]], [pv])
                vw = VW[tt_ % 2]
                S.copy('act', vw.ap, pv.ap[:, 0:72], [pv], [vw])
                t0 = c * CH + tt_ * 128
                S.dma('pool', d_vw[t0:t0 + 128, :], vw.ap, (vw,), (K_vw[c],))
        close_stage(G)

    def stage_rglru(l):
        G = open_stage(20000)
        XR = Reg(AR.alloc(T + 4), "xr")
        GT = Reg(AR.alloc(T), "gt")
        XC = Reg(AR.alloc(T), "xc")
        RR = Reg(AR.alloc(T), "rr")
        IG = Reg(AR.alloc(T), "ig")
        AA = Reg(AR.alloc(T), "aa")
        MM = Reg(AR.alloc(T), "mm")
        T1 = Reg(AR.alloc(T), "t1")
        HH = Reg(AR.alloc(T), "hh")
        WA = Reg(AR.alloc(4 * 128).rearrange("p (c n) -> p c n", c=4), "wa")
        WX = Reg(AR.alloc(4 * 128).rearrange("p (c n) -> p c n", c=4), "wx")
        S.dma('sp', WA.ap, i_rgw[l, 0].rearrange("c p n -> p c n"), (), (WA,))
        S.dma('sp', WX.ap, i_rgw[l, 1].rearrange("c p n -> p c n"), (), (WX,))
        S.dma('sp', P_RG.ap.rearrange("p (c j) -> p c j", c=4), i_rgp[l], (), (P_RG,))
        rg3 = P_RG.ap.rearrange("p (c j) -> p c j", c=4)
        rgd = P_RGD.ap.rearrange("p (j c) -> p j c", j=4)
        S.act(rgd[:, 0, :], rg3[:, :, 7], AF.Exp, [P_RG], [P_RGD], scale=-1.0)
        S.act(rgd[:, 0, :], rgd[:, 0, :], AF.Ln, [P_RGD], [P_RGD], bias=ONE_R.ap, scale=1.0)
        S.ts('dve', rgd[:, 1, :], rgd[:, 0, :], -8.0, None, ALU.mult, None, [P_RGD], [P_RGD])
        S.ts('dve', rgd[:, 2, :], rgd[:, 0, :], -16.0, None, ALU.mult, None, [P_RGD], [P_RGD])
        S.memset('dve', XR.ap[:, 0:4], 0.0, [XR])
        for s in range(NS):
            zk = K_zT[s * 4:(s + 1) * 4]
            for ct in range(4):
                S.dma('sp', XR.ap[:, 4:4 + T], d_zT[(ZI_XRNN + ct) * 128:(ZI_XRNN + ct + 1) * 128, s * T:(s + 1) * T],
                      zk, (XR,))
                S.dma('sp', GT.ap, d_zT[(ZI_GATE + ct) * 128:(ZI_GATE + ct + 1) * 128, s * T:(s + 1) * T], zk, (GT,))
                S.act(XC.ap, XR.ap[:, 4:4 + T], AF.Identity, [XR, P_RG], [XC], bias=rg3[:, ct, 4:5], scale=rg3[:, ct, 3:4])
                for j in range(3):
                    S.stt(XC.ap, XR.ap[:, 1 + j:1 + j + T], rg3[:, ct, j:j + 1], XC.ap, ALU.mult, ALU.add,
                          [XR, P_RG, XC], [XC])
                for q in range(4):
                    pr, pi = PB[q % 2], PB[2 + q % 2]
                    sl = slice(q * CH, (q + 1) * CH)
                    S.mm(pr.ap, WA.ap[:, ct, :], XC.ap[:, sl], True, True, [WA, XC], [pr])
                    S.mm(pi.ap, WX.ap[:, ct, :], XC.ap[:, sl], True, True, [WX, XC], [pi])
                    S.act(RR.ap[:, sl], pr.ap, AF.Sigmoid, [pr, P_RG], [RR], bias=rg3[:, ct, 5:6], scale=1.0)
                    S.act(IG.ap[:, sl], pi.ap, AF.Sigmoid, [pi, P_RG], [IG], bias=rg3[:, ct, 6:7], scale=1.0)
                S.act(AA.ap, RR.ap, AF.Exp, [RR, P_RGD], [AA], scale=rgd[:, 1, ct:ct + 1])
                S.act(MM.ap, RR.ap, AF.Exp, [RR, P_RGD], [MM], scale=rgd[:, 2, ct:ct + 1])
                S.act(MM.ap, MM.ap, AF.Sqrt, [MM, ONE_R], [MM], bias=ONE_R.ap, scale=-1.0)
                S.tt('dve', MM.ap, MM.ap, IG.ap, ALU.mult, [MM, IG], [MM])
                S.tt('dve', MM.ap, MM.ap, XC.ap, ALU.mult, [MM, XC], [MM])
                S.op('dve', lambda g, o=HH.ap, a=AA.ap, b=MM.ap: g.tensor_tensor_scan(o, a, b, 0.0, ALU.mult, ALU.add),
                     [AA, MM], [HH])
                gelu_tanh(GT.ap, IG.ap, T1.ap, [GT, T1, IG], [T1, IG])
                S.tt('dve', HH.ap, HH.ap, IG.ap, ALU.mult, [HH, IG], [HH])
                S.dma('pool', d_yT[0, ct * 128:(ct + 1) * 128, s * T:(s + 1) * T], HH.ap, (HH,), (K_yT[0][s],))
        close_stage(G)

    def stage_mla(l):
        G = open_stage(14000)
        QL = [Reg(AR.alloc(CH), "ql%d" % i) for i in range(2)]
        KVL = Reg(AR.alloc(CH), "kvl")
        QN = [Reg(AR.alloc(CH), "qn%d" % i) for i in range(2)]
        KVN = Reg(AR.alloc(CH), "kvn")
        SQ = Reg(AR.alloc(CH), "sq")
        RS = Reg(AR.alloc(CH), "rs")
        RAW = [Reg(AR.alloc(CH), "raw%d" % i) for i in range(2)]
        KRAW = [Reg(AR.alloc(CH), "kraw%d" % i) for i in range(2)]
        KPE = Reg(AR.alloc(CH), "kpe")
        QG = Reg(AR.alloc(CH), "qg")
        T1 = Reg(AR.alloc(CH), "t1")
        OUT = [Reg(AR.alloc(CH), "out%d" % i) for i in range(2)]
        TC = Reg(AR.alloc(CH), "tc")
        TS_ = Reg(AR.alloc(CH), "tsn")
        VS = [Reg(AR.alloc(512), "vs%d" % i) for i in range(2)]
        WUQ = Reg(AR.alloc(2 * 8 * 96).rearrange("p (k h n) -> p k h n", k=2, h=8), "wuq")
        WK = Reg(AR.alloc(8 * 64).rearrange("p (h n) -> p h n", h=8), "wk")
        WV = Reg(AR.alloc(512), "wv")
        S.dma('sp', WUQ.ap, i_wuq[l], (), (WUQ,))
        S.dma('sp', WK.ap, i_wukvk[l], (), (WK,))
        S.dma('sp', WV.ap, i_wukvv[l], (), (WV,))
        S.dma('sp', P_MLAN.ap, i_mlan[l], (), (P_MLAN,))
        S.dma('sp', P_GAIN.ap, i_gains[l], (), (P_GAIN,))
        for c in range(dbg.get('nch', NCH)):
            tk = slice(c * CH, (c + 1) * CH)
            for k in range(2):
                S.dma('sp', QL[k].ap, d_zT[(ZI_QLAT + k) * 128:(ZI_QLAT + k + 1) * 128, tk], (K_zT[c],), (QL[k],))
            S.dma('sp', KVL.ap, d_zT[ZI_KVLAT * 128:(ZI_KVLAT + 1) * 128, tk], (K_zT[c],), (KVL,))
            S.dma('sp', KPE.ap[64:96], d_zT[ZI_KPE * 128:ZI_KPE * 128 + 32, tk], (K_zT[c],), (KPE,))
            S.dma('sp', TC.ap, d_tab[0, 0, :, tk], (K_tab,), (TC,))
            S.dma('sp', TS_.ap, d_tab[0, 1, :, tk], (K_tab,), (TS_,))
            ps = PB[0]
            for k in range(2):
                S.act(SQ.ap, QL[k].ap, AF.Square, [QL[k]], [SQ])
                S.mm(ps.ap, C('ones'), SQ.ap, k == 0, k == 1, [SQ, K_cst], [ps])
            rstd_from_psum(ps.ap, RS.ap, 256, [ps, EPS_R], [RS])
            for k in range(2):
                S.stt(QN[k].ap, QL[k].ap, P_MLAN.ap[:, k:k + 1], RS.ap, ALU.mult, ALU.mult, [QL[k], P_MLAN, RS], [QN[k]])
            S.act(SQ.ap, KVL.ap, AF.Square, [KVL], [SQ])
            S.mm(ps.ap, C('ones'), SQ.ap, True, True, [SQ, K_cst], [ps])
            rstd_from_psum(ps.ap, RS.ap, 128, [ps, EPS_R], [RS])
            S.stt(KVN.ap, KVL.ap, P_MLAN.ap[:, 2:3], RS.ap, ALU.mult, ALU.mult, [KVL, P_MLAN, RS], [KVN])
            for h in range(8):
                pq = PB[1 + h % 2]
                raw = RAW[h % 2]
                for k in range(2):
                    S.mm(pq.ap[0:96], WUQ.ap[:, k, h, :], QN[k].ap, k == 0, k == 1, [WUQ, QN[k]], [pq])
                S.copy('act', raw.ap[0:96], pq.ap[0:96], [pq], [raw])
                o = OUT[0]
                rope_pipeline(raw, 96, P_GAIN.ap[0:96, 0:1], 'r96', 'ones', 96, TC, TS_, QG, SQ, RS, T1, PB[3], PB[4], o, None)
                S.dma('pool', d_qm[h, :, tk], o.ap[0:96], (o,), (K_qm[c],))
                pk = PB[5 + h % 2]
                kraw = KRAW[h % 2]
                S.mm(pk.ap[0:64], WK.ap[:, h, :], KVN.ap, True, True, [WK, KVN], [pk])
                S.copy('act', kraw.ap[0:64], pk.ap[0:64], [pk], [kraw])
                S.copy('act', kraw.ap[64:96], KPE.ap[64:96], [KPE], [kraw])
                o = OUT[1]
                rope_pipeline(kraw, 96, P_GAIN.ap[0:96, 1:2], 'r96', 'ones', 96, TC, TS_, QG, SQ, RS, T1, PB[3], PB[4], o, None)
                S.dma('pool', d_km[h, :, tk], o.ap[0:96], (o,), (K_km[c],))
            for tt_ in range(4):
                pv = PB[7]
                S.mm(pv.ap, KVN.ap[:, tt_ * 128:(tt_ + 1) * 128], WV.ap, True, True, [KVN, WV], [pv])
                vs = VS[tt_ % 2]
                S.copy('act', vs.ap, pv.ap, [pv], [vs])
                t0 = c * CH + tt_ * 128
                S.dma('pool', d_vm[t0:t0 + 128, :], vs.ap, (vs,), (K_vm[c],))
        close_stage(G)
        G = open_stage(8000, 12000)
        KHs = Reg(AR.alloc(T), "khs")
        QHs = Reg(AR.alloc(T), "qhs")
        V1s = Reg(AR.alloc(16 * 65).rearrange("p (b n) -> p b n", b=16), "v1s")
        KH = [Reg(RA.alloc(T), "kh%d" % i) for i in range(2)]
        QH = [Reg(RA.alloc(T), "qh%d" % i) for i in range(2)]
        V1 = [Reg(RA.alloc(16 * 65).rearrange("p (b n) -> p b n", b=16), "v1_%d" % i) for i in range(2)]
        PT = [Reg(RA.alloc(CH), "pt%d" % i) for i in range(3)]
        OS = Reg(AR.alloc(CH), "os")
        RD = Reg(AR.alloc(CH), "rd")
        YO = [Reg(AR.alloc(CH), "yo%d" % i) for i in range(2)]
        S.memset('dve', V1s.ap[:, :, 64:65], 1.0, [V1s])
        sc = 96.0 ** -0.5
        it = 0
        for s in range(NS):
            ks = K_km[s * 4:(s + 1) * 4]
            qs = K_qm[s * 4:(s + 1) * 4]
            vsk = K_vm[s * 4:(s + 1) * 4]
            for h in range(8):
                kh, qh, v1 = KH[h % 2], QH[h % 2], V1[h % 2]
                S.dma('sp', KHs.ap[0:96], d_km[h, :, s * T:(s + 1) * T], ks, (KHs,))
                S.dma('sp', QHs.ap[0:96], d_qm[h, :, s * T:(s + 1) * T], qs, (QHs,))
                S.dma('sp', V1s.ap[:, :, 0:64],
                      d_vm[s * T:(s + 1) * T, h * 64:(h + 1) * 64].rearrange("(b p) n -> p b n", p=128), vsk, (V1s,))
                S.copy('pool', kh.ap[0:96], KHs.ap[0:96], [KHs], [kh])
                S.copy('pool', qh.ap[0:96], QHs.ap[0:96], [QHs], [qh])
                S.copy('pool', v1.ap, V1s.ap, [V1s], [v1])
                for qc in range(4):
                    po = PB[6 + qc % 2]
                    nkb = 4 * (qc + 1)
                    for kb in range(nkb):
                        j = max(0, kb - 4 * qc)
                        q0 = qc * CH + j * 128
                        n = CH - j * 128
                        pS = PB[it % 3]
                        pt = PT[it % 3]
                        it += 1
                        S.mm(pS.ap[:, 0:n], kh.ap[0:96, kb * 128:(kb + 1) * 128], qh.ap[0:96, q0:q0 + n], True, True,
                             [kh, qh], [pS])
                        S.act(pt.ap[:, 0:n], pS.ap[:, 0:n], AF.Exp, [pS], [pt], scale=sc)
                        if kb >= 4 * qc:
                            S.tt('pool', pt.ap[:, 0:128], pt.ap[:, 0:128], C('tri_kq'), ALU.mult, [pt, K_cst], [pt])
                        S.mm(po.ap[0:65, j * 128:CH], v1.ap[:, kb, :], pt.ap[:, 0:n], kb == 0, kb == nkb - 1,
                             [v1, pt], [po])
                    S.copy('act', OS.ap[0:65], po.ap[0:65], [po], [OS])
                    pd = PB[3 + qc % 2]
                    S.mm(pd.ap[0:64], C('ones', 128, 0, 64)[64:65], OS.ap[64:65], True, True, [OS, K_cst], [pd])
                    S.recip(RD.ap[0:64], pd.ap[0:64], [pd], [RD])
                    yo = YO[qc % 2]
                    S.tt('dve', yo.ap[0:64], OS.ap[0:64], RD.ap[0:64], ALU.mult, [OS, RD], [yo])
                    S.dma('pool', d_yT[1, h * 64:(h + 1) * 64, s * T + qc * CH:s * T + (qc + 1) * CH], yo.ap[0:64],
                          (yo,), (K_yT[1][s],))
        close_stage(G)

    def stage_dsa(l):
        G = open_stage(6000)
        QD = [Reg(AR.alloc(CH), "qd%d" % i) for i in range(2)]
        SQ = Reg(AR.alloc(CH), "sq")
        RS = Reg(AR.alloc(CH), "rs")
        QG = Reg(AR.alloc(CH), "qg")
        T1 = Reg(AR.alloc(CH), "t1")
        OUT = [Reg(AR.alloc(CH), "out%d" % i) for i in range(2)]
        TC = Reg(AR.alloc(CH), "tc")
        TS_ = Reg(AR.alloc(CH), "tsn")
        S.dma('sp', P_GAIN.ap, i_gains[l], (), (P_GAIN,))
        for c in range(dbg.get('nch', NCH)):
            tk = slice(c * CH, (c + 1) * CH)
            S.dma('sp', TC.ap, d_tab[1, 0, :, tk], (K_tab,), (TC,))
            S.dma('sp', TS_.ap, d_tab[1, 1, :, tk], (K_tab,), (TS_,))
            for k in range(5):
                qd = QD[k % 2]
                o = OUT[k % 2]
                if k < 4:
                    S.dma('sp', qd.ap, d_zT[(ZI_QDSA + k) * 128:(ZI_QDSA + k + 1) * 128, tk], (K_zT[c],), (qd,))
                    rope_pipeline(qd, 128, P_GAIN.ap[:, 2:3], 'r64', 'bd64', 64, TC, TS_, QG, SQ, RS, T1, PB[0], PB[1], o, None)
                    S.dma('pool', d_qd[k * 128:(k + 1) * 128, tk], o.ap, (o,), (K_qd[c],))
                else:
                    S.dma('sp', qd.ap[0:64], d_zT[ZI_KDSA * 128:ZI_KDSA * 128 + 64, tk], (K_zT[c],), (qd,))
                    rope_pipeline(qd, 64, P_GAIN.ap[0:64, 3:4], 'r64', 'bd64', 64, TC, TS_, QG, SQ, RS, T1, PB[0], PB[1], o, None)
                    S.dma('pool', d_kd[:, tk], o.ap[0:64], (o,), (K_kd[c],))
        close_stage(G)
        G = open_stage(33600, 2700)
        _uid[0] += 1
        gm = nc.sbuf_tensor("mtb%d" % _uid[0], [128, 2 * 16 * CH], mybir.dt.bfloat16)
        mt_t = gm.__enter__()
        G.append(gm)
        mt4 = mt_t[:].rearrange("p (i b n) -> p i b n", i=2, b=16)
        MTs = [Reg(mt4[:, i], "mt%d" % i) for i in range(2)]
        KD2 = Reg(AR.alloc(T), "kd2")
        QI = Reg(AR.alloc(2 * T).rearrange("p (k t) -> p k t", k=2), "qi")
        KIB = Reg(AR.alloc(4 * T).rearrange("p (h t) -> p h t", h=4), "kib")
        V1s = Reg(AR.alloc(16 * 65).rearrange("p (b n) -> p b n", b=16), "v1s")
        V1 = Reg(RA.alloc(16 * 65).rearrange("p (b n) -> p b n", b=16), "v1")
        WI = Reg(AR.alloc(16 * 8).rearrange("p (b n) -> p b n", b=16), "wi")
        SCs = [Reg(AR.alloc(T), "sc%d" % i) for i in range(2)]
        WKs = [Reg(AR.alloc(T), "wk%d" % i) for i in range(2)]
        RL = [Reg(AR.alloc(CH), "rl%d" % i) for i in range(2)]
        M8s = [Reg(AR.alloc(8), "m8_%d" % i) for i in range(2)]
        THRs = [Reg(AR.alloc(1), "thr%d" % i) for i in range(2)]
        QC = Reg(AR.alloc(4 * CH).rearrange("p (k t) -> p k t", k=4), "qc")
        PT = [Reg(RA.alloc(CH), "pt%d" % i) for i in range(3)]
        OSs = [Reg(AR.alloc(CH), "os%d" % i) for i in range(8)]
        RD = Reg(AR.alloc(CH), "rd")
        YO = [Reg(AR.alloc(CH), "yo%d" % i) for i in range(2)]
        S.memset('dve', V1s.ap[:, :, 64:65], 1.0, [V1s])
        S.memset('dve', KIB.ap, 0.0, [KIB])
        sc = 64.0 ** -0.5
        itc = [0]

        def idx_pair(s, qc, pair):
            qbs = [qc * 4 + pair * 2, qc * 4 + pair * 2 + 1]
            for i, qb in enumerate(qbs):
                SC = SCs[i]
                for kb in range(qb + 1):
                    for k in range(2):
                        pr = PB[k]
                        rl = RL[k]
                        S.mm(pr.ap, QI.ap[:, k, qb * 128:(qb + 1) * 128], KIB.ap[:, :, kb * 128:(kb + 1) * 128],
                             True, True, [QI, KIB], [pr])
                        S.act(rl.ap, pr.ap, AF.Relu, [pr], [rl])
                        for h4 in range(4):
                            hh = k * 4 + h4
                            dst = SC.ap[:, kb * 128:(kb + 1) * 128]
                            if hh == 0:
                                S.ts('dve', dst, rl.ap[:, 0:128], WI.ap[:, qb, 0:1], None, ALU.mult, None,
                                     [rl, WI], [SC])
                            else:
                                S.stt(dst, rl.ap[:, h4 * 128:(h4 + 1) * 128], WI.ap[:, qb, hh:hh + 1], dst,
                                      ALU.mult, ALU.add, [rl, WI, SC], [SC])
                dg = SC.ap[:, qb * 128:(qb + 1) * 128]
                S.tt('dve', dg, dg, C('tri_qk'), ALU.mult, [SC, K_cst], [SC])
                S.tt('dve', dg, dg, C('neg_qk'), ALU.add, [SC, K_cst], [SC])
            srcs = [SCs[0], SCs[1]]
            for r in range(32):
                for i, qb in enumerate(qbs):
                    if qb < 2:
                        continue
                    nk = (qb + 1) * 128
                    S.op('dve', lambda g, o=M8s[i].ap, a=srcs[i].ap[:, 0:nk]: g.max(o, a), [srcs[i]], [M8s[i]])
                    if r < 31:
                        S.op('dve', lambda g, o=WKs[i].ap[:, 0:nk], a=M8s[i].ap, v=srcs[i].ap[:, 0:nk]:
                             g.match_replace(o, a, v, -BIG), [M8s[i], srcs[i]], [WKs[i]])
                        srcs[i] = WKs[i]
            for i, qb in enumerate(qbs):
                nk = (qb + 1) * 128
                SC, WK_, M8, THR = SCs[i], WKs[i], M8s[i], THRs[i]
                if qb >= 2:
                    S.ts('dve', THR.ap, M8.ap[:, 7:8], -0.5 * BIG, None, ALU.max, None, [M8], [THR])
                else:
                    S.memset('dve', THR.ap, -0.5 * BIG, [THR])
                S.ts('dve', WK_.ap[:, 0:nk], SC.ap[:, 0:nk], THR.ap, None, ALU.is_ge, None, [SC, THR], [WK_])

        def idx_pair_B(s, qc, pair):
            MT = MTs[qc % 2]
            for i in range(2):
                qb = qc * 4 + pair * 2 + i
                ql = qb - qc * 4
                WK_ = WKs[i]
                for kb in range(qb + 1):
                    pt_ = PB[2 + kb % 2]
                    S.transpose(pt_.ap[:, 0:128], WK_.ap[:, kb * 128:(kb + 1) * 128], C('ident'), [WK_, K_cst], [pt_])
                    S.copy('act', MT.ap[:, kb, ql * 128:(ql + 1) * 128], pt_.ap[:, 0:128], [pt_], [MT])

        def attn_heads(s, qc, heads):
            MT = MTs[qc % 2]
            nkb = 4 * (qc + 1)
            for h in heads:
                p0 = (h % 2) * 64
                po = PB[6 + h % 2]
                for kb in range(nkb):
                    j = max(0, kb - 4 * qc)
                    n = CH - j * 128
                    pS = PB[4 + itc[0] % 2]
                    pt = PT[itc[0] % 3]
                    itc[0] += 1
                    S.mm(pS.ap[:, 0:n], KD2.ap[p0:p0 + 64, kb * 128:(kb + 1) * 128],
                         QC.ap[p0:p0 + 64, h // 2, j * 128:CH], True, True, [KD2, QC], [pS])
                    S.act(pt.ap[:, 0:n], pS.ap[:, 0:n], AF.Exp, [pS], [pt], scale=sc)
                    S.tt('pool', pt.ap[:, 0:n], pt.ap[:, 0:n], MT.ap[:, kb, j * 128:CH], ALU.mult, [pt, MT], [pt])
                    S.mm(po.ap[0:65, j * 128:CH], V1.ap[:, kb, :], pt.ap[:, 0:n], kb == 0, kb == nkb - 1,
                         [V1, pt], [po])
                S.copy('act', OSs[h].ap[0:65], po.ap[0:65], [po], [OSs[h]])

        def attn_norm(s, qc):
            for h in range(8):
                OS = OSs[h]
                pd = PB[2 + h % 2]
                S.mm(pd.ap[0:64], C('ones', 128, 0, 64)[64:65], OS.ap[64:65], True, True, [OS, K_cst], [pd])
                S.recip(RD.ap[0:64], pd.ap[0:64], [pd], [RD])
                yo = YO[h % 2]
                S.tt('dve', yo.ap[0:64], OS.ap[0:64], RD.ap[0:64], ALU.mult, [OS, RD], [yo])
                S.dma('pool', d_yT[2, h * 64:(h + 1) * 64, s * T + qc * CH:s * T + (qc + 1) * CH], yo.ap[0:64],
                      (yo,), (K_yT[2][s],))

        for s in range(NS):
            zk = K_zT[s * 4:(s + 1) * 4]
            ts_ = slice(s * T, (s + 1) * T)
            S.dma('sp', KD2.ap[0:64], d_kd[:, ts_], K_kd[s * 4:(s + 1) * 4], (KD2,))
            S.dma('sp', KD2.ap[64:128], d_kd[:, ts_], K_kd[s * 4:(s + 1) * 4], (KD2,))
            for k in range(2):
                S.dma('sp', QI.ap[:, k, :], d_zT[(ZI_QIDX + k) * 128:(ZI_QIDX + k + 1) * 128, ts_], zk, (QI,))
            for h4 in range(4):
                S.dma('sp', KIB.ap[h4 * 32:(h4 + 1) * 32, h4, :], d_zT[ZI_KIDX * 128:ZI_KIDX * 128 + 32, ts_], zk, (KIB,))
            vwk = K_vw[s * 4:(s + 1) * 4]
            S.dma('sp', V1s.ap[:, :, 0:64], d_vw[ts_, 0:64].rearrange("(b p) n -> p b n", p=128), vwk, (V1s,))
            S.copy('pool', V1.ap, V1s.ap, [V1s], [V1])
            S.dma('sp', WI.ap, d_vw[ts_, 64:72].rearrange("(b p) n -> p b n", p=128), vwk, (WI,))
            for pair in range(2):
                idx_pair(s, 0, pair)
                idx_pair_B(s, 0, pair)
            for qc in range(4):
                S.dma('sp', QC.ap, d_qd[:, s * T + qc * CH:s * T + (qc + 1) * CH].rearrange("(k p) t -> p k t", p=128),
                      K_qd[s * 4:(s + 1) * 4], (QC,))
                nxt = qc + 1 < 4
                if nxt:
                    idx_pair(s, qc + 1, 0)
                attn_heads(s, qc, range(0, 4))
                if nxt:
                    idx_pair_B(s, qc + 1, 0)
                    idx_pair(s, qc + 1, 1)
                attn_heads(s, qc, range(4, 8))
                if nxt:
                    idx_pair_B(s, qc + 1, 1)
                attn_norm(s, qc)
        close_stage(G)

    def stage_s5(l):
        G = open_stage(44000)
        YSUM[0] = Reg(AR.ap[:, AR.n - 2 * T:AR.n - T], "ysum0")
        YSUM[1] = Reg(AR.ap[:, AR.n - T:AR.n], "ysum1")
        LR = Reg(AR.alloc(T), "lr")
        LI = Reg(AR.alloc(T), "li")
        DT = Reg(AR.alloc(T), "dt")
        E1 = Reg(AR.alloc(T), "e1")
        E2 = Reg(AR.alloc(T), "e2")
        E3 = Reg(AR.alloc(T), "e3")
        E4 = Reg(AR.alloc(T), "e4")
        FR = Reg(AR.alloc(T), "fr")
        FI = Reg(AR.alloc(T), "fi")
        BRE = Reg(AR.alloc(T), "bre")
        BIM = Reg(AR.alloc(T), "bim")
        CRE = Reg(AR.alloc(T), "cre")
        CIM = Reg(AR.alloc(T), "cim")
        allk = [LR, LI, DT, E1, E2, E3, E4, FR, FI]
        S.dma('sp', LR.ap, i_s5bc[l, 0], (), (LR,))
        S.dma('sp', LI.ap, i_s5bc[l, 1], (), (LI,))
        S.dma('sp', DT.ap, i_s5bc[l, 2], (), (DT,))
        S.dma('sp', BRE.ap.rearrange("p (a m) -> p a m", a=16), i_s5b[l, 0], (), (BRE,))
        S.dma('sp', BIM.ap.rearrange("p (a m) -> p a m", a=16), i_s5b[l, 1], (), (BIM,))
        S.dma('sp', CRE.ap.rearrange("p (a m) -> p a m", a=16), i_s5c[l, 0], (), (CRE,))
        S.dma('sp', CIM.ap.rearrange("p (a m) -> p a m", a=16), i_s5c[l, 1], (), (CIM,))
        S.dma('sp', P_S5PP.ap.rearrange("p (j a) -> p j a", j=3), i_s5pp[l].rearrange("j p a -> p j a"), (), (P_S5PP,))
        S.dma('sp', P_S5V.ap.rearrange("p (a j) -> p a j", a=4), i_s5v[l], (), (P_S5V,))
        S.act(DT.ap, DT.ap, AF.Exp, allk, allk)
        S.tt('dve', E1.ap, LR.ap, DT.ap, ALU.mult, allk, allk)
        S.act(E1.ap, E1.ap, AF.Exp, allk, allk)
        S.tt('dve', E2.ap, LI.ap, DT.ap, ALU.mult, allk, allk)
        sincos(E2.ap, E3.ap, E4.ap, FR.ap, 128, T, allk, allk)
        S.tt('dve', E3.ap, E3.ap, E1.ap, ALU.mult, allk, allk)
        S.tt('dve', E4.ap, E4.ap, E1.ap, ALU.mult, allk, allk)
        S.ts('dve', E4.ap, E4.ap, -1.0, None, ALU.add, None, allk, allk)
        S.tt('dve', E1.ap, LR.ap, LR.ap, ALU.mult, allk, allk)
        S.tt('dve', E2.ap, LI.ap, LI.ap, ALU.mult, allk, allk)
        S.tt('dve', E1.ap, E1.ap, E2.ap, ALU.add, allk, allk)
        S.recip(E1.ap, E1.ap, allk, allk)
        S.tt('dve', FR.ap, E4.ap, LR.ap, ALU.mult, allk, allk)
        S.tt('dve', E2.ap, E3.ap, LI.ap, ALU.mult, allk, allk)
        S.tt('dve', FR.ap, FR.ap, E2.ap, ALU.add, allk, allk)
        S.tt('dve', FR.ap, FR.ap, E1.ap, ALU.mult, allk, allk)
        S.tt('dve', FI.ap, E3.ap, LR.ap, ALU.mult, allk, allk)
        S.tt('dve', E2.ap, E4.ap, LI.ap, ALU.mult, allk, allk)
        S.tt('dve', FI.ap, FI.ap, E2.ap, ALU.subtract, allk, allk)
        S.tt('dve', FI.ap, FI.ap, E1.ap, ALU.mult, allk, allk)
        bk = allk + [BRE, BIM]
        S.tt('dve', E1.ap, FR.ap, BRE.ap, ALU.mult, bk, bk)
        S.tt('dve', E2.ap, FI.ap, BIM.ap, ALU.mult, bk, bk)
        S.tt('dve', E1.ap, E1.ap, E2.ap, ALU.subtract, bk, bk)
        S.tt('dve', E2.ap, FR.ap, BIM.ap, ALU.mult, bk, bk)
        S.tt('dve', E3.ap, FI.ap, BRE.ap, ALU.mult, bk, bk)
        S.tt('dve', E2.ap, E2.ap, E3.ap, ALU.add, bk, bk)
        S.copy('dve', BRE.ap, E1.ap, bk, bk)
        S.copy('dve', BIM.ap, E2.ap, bk, bk)
        pp = P_S5PP.ap.rearrange("p (j a) -> p j a", j=3)
        sd = P_S5D.ap.rearrange("p (j a) -> p j a", j=4)
        kk = [P_S5PP, P_S5D, P_TMP]
        S.act(sd[:, 0, :], pp[:, 2, :], AF.Exp, kk, kk)
        S.tt('dve', sd[:, 1, :], pp[:, 0, :], sd[:, 0, :], ALU.mult, kk, kk)
        S.act(sd[:, 1, :], sd[:, 1, :], AF.Exp, kk, kk)
        S.tt('dve', sd[:, 2, :], pp[:, 1, :], sd[:, 0, :], ALU.mult, kk, kk)
        iw = IWK.ap[:, 0:16]
        tmp16 = P_TMP.ap[:, 0:16]
        S.ts('dve', iw, sd[:, 2, :], 1.0 / TWO_PI, None, ALU.mult, None, kk, [IWK])
        S.stt(tmp16, iw, -CW1, sd[:, 2, :], ALU.mult, ALU.add, kk + [IWK], kk)
        S.stt(sd[:, 2, :], iw, -CW2, tmp16, ALU.mult, ALU.add, kk + [IWK], kk)
        S.barrier()
        AR.off = 13 * T
        COS = Reg(AR.ap[:, 0:T], "cos")
        SIN = Reg(AR.ap[:, T:2 * T], "sin")
        ANG = Reg(AR.ap[:, 2 * T:3 * T], "ang")
        TMP = Reg(AR.ap[:, 3 * T:4 * T], "tmp")
        BUR = Reg(AR.ap[:, 4 * T:5 * T], "bur")
        BUI = Reg(AR.ap[:, 5 * T:6 * T], "bui")
        WR = Reg(AR.ap[:, 6 * T:7 * T], "wr")
        WI_ = Reg(AR.ap[:, 7 * T:8 * T], "wi")
        T2 = Reg(AR.ap[:, 8 * T:9 * T], "t2")
        U5 = [Reg(AR.alloc(T), "u5_%d" % i) for i in range(NS)]
        XRE = Reg(AR.alloc(T), "xre")
        XIM = Reg(AR.alloc(T), "xim")
        YS = Reg(AR.alloc(T), "ys")
        b3r = BRE.ap.rearrange("p (a m) -> p a m", a=16)
        b3i = BIM.ap.rearrange("p (a m) -> p a m", a=16)
        c3r = CRE.ap.rearrange("p (a m) -> p a m", a=16)
        c3i = CIM.ap.rearrange("p (a m) -> p a m", a=16)
        sv = P_S5V.ap.rearrange("p (a j) -> p a j", a=4)
        YP = [PB[4], PB[5], PB[6], PB[7]]
        for ot in range(4):
            for s in range(NS):
                S.dma('sp', U5[s].ap, d_zT[(ZI_US5 + ot) * 128:(ZI_US5 + ot + 1) * 128, s * T:(s + 1) * T],
                      K_zT[s * 4:(s + 1) * 4], (U5[s],))
            for sti in range(4):
                st = ot * 4 + sti
                S.ts('dve', ANG.ap, C('iota'), sd[:, 2, st:st + 1], None, ALU.mult, None, [K_cst, P_S5D], [ANG])
                sincos(ANG.ap, SIN.ap, COS.ap, TMP.ap, 128, T, [ANG, TMP], [ANG, TMP, SIN, COS])
                for s in range(NS):
                    for q in range(4):
                        sl = slice(q * CH, (q + 1) * CH)
                        pr, pi = PB[q % 2], PB[2 + q % 2]
                        S.mm(pr.ap, b3r[:, st, :], U5[s].ap[:, sl], True, True, [BRE, U5[s]], [pr])
                        S.mm(pi.ap, b3i[:, st, :], U5[s].ap[:, sl], True, True, [BIM, U5[s]], [pi])
                        S.copy('act', BUR.ap[:, sl], pr.ap, [pr], [BUR])
                        S.copy('act', BUI.ap[:, sl], pi.ap, [pi], [BUI])
                    S.tt('dve', WR.ap, BUR.ap, COS.ap, ALU.mult, [BUR, COS], [WR])
                    S.tt('pool', T2.ap, BUI.ap, SIN.ap, ALU.mult, [BUI, SIN], [T2])
                    S.tt('dve', WR.ap, WR.ap, T2.ap, ALU.add, [WR, T2], [WR])
                    S.tt('pool', WI_.ap, BUI.ap, COS.ap, ALU.mult, [BUI, COS], [WI_])
                    S.tt('dve', T2.ap, BUR.ap, SIN.ap, ALU.mult, [BUR, SIN], [T2])
                    S.tt('dve', WI_.ap, WI_.ap, T2.ap, ALU.subtract, [WI_, T2], [WI_])
                    rho = sd[:, 1, st:st + 1].to_broadcast([128, T])
                    S.op('dve', lambda g, o=BUR.ap, a=rho, b=WR.ap: g.tensor_tensor_scan(o, a, b, 0.0, ALU.mult, ALU.add),
                         [WR, P_S5D], [BUR])
                    S.op('dve', lambda g, o=BUI.ap, a=rho, b=WI_.ap: g.tensor_tensor_scan(o, a, b, 0.0, ALU.mult, ALU.add),
                         [WI_, P_S5D], [BUI])
                    S.tt('dve', XRE.ap, BUR.ap, COS.ap, ALU.mult, [BUR, COS], [XRE])
                    S.tt('pool', T2.ap, BUI.ap, SIN.ap, ALU.mult, [BUI, SIN], [T2])
                    S.tt('dve', XRE.ap, XRE.ap, T2.ap, ALU.subtract, [XRE, T2], [XRE])
                    S.tt('pool', XIM.ap, BUI.ap, COS.ap, ALU.mult, [BUI, COS], [XIM])
                    S.tt('dve', T2.ap, BUR.ap, SIN.ap, ALU.mult, [BUR, SIN], [T2])
                    S.stt(XIM.ap, T2.ap, -1.0, XIM.ap, ALU.mult, ALU.subtract, [T2, XIM], [XIM])
                    for q in range(4):
                        sl = slice(q * CH, (q + 1) * CH)
                        yp = YP[q]
                        S.mm(yp.ap, c3r[:, st, :], XRE.ap[:, sl], True, False, [CRE, XRE], [yp])
                        S.mm(yp.ap, c3i[:, st, :], XIM.ap[:, sl], False, True, [CIM, XIM], [yp])
                        ysum = YSUM[s]
                        if sti == 0:
                            S.stt(ysum.ap[:, sl], U5[s].ap[:, sl], sv[:, ot, 0:1], yp.ap, ALU.mult, ALU.add,
                                  [U5[s], P_S5V, yp], [ysum])
                        else:
                            S.tt('dve', ysum.ap[:, sl], ysum.ap[:, sl], yp.ap, ALU.add, [ysum, yp], [ysum])
            for s in range(NS):
                gelu_tanh(YSUM[s].ap, YS.ap, T2.ap, [YSUM[s], T2, YS], [T2, YS])
                S.dma('pool', d_yg[ot * 128:(ot + 1) * 128, s * T:(s + 1) * T], YS.ap, (YS,),
                      K_yg[s * 4:(s + 1) * 4])
        close_stage(G)
        G = open_stage(6000)
        YG = Reg(AR.alloc(4 * CH).rearrange("p (k t) -> p k t", k=4), "yg")
        WG = Reg(AR.alloc(4 * 512).rearrange("p (k n) -> p k n", k=4), "wglu")
        SG = [Reg(AR.alloc(CH), "sg%d" % i) for i in range(2)]
        S.dma('sp', WG.ap, i_wglu[l], (), (WG,))
        for c in range(dbg.get('nch', NCH)):
            s = c // (NCH // NS)
            tk = slice(c * CH, (c + 1) * CH)
            S.dma('sp', YG.ap, d_yg[:, tk].rearrange("(k p) t -> p k t", p=128), (K_yg[c],), (YG,))
            for m in range(4):
                pg = PB[m % 2]
                for k in range(4):
                    S.mm(pg.ap, WG.ap[:, k, m * 128:(m + 1) * 128], YG.ap[:, k, :], k == 0, k == 3, [WG, YG], [pg])
                sg = SG[m % 2]
                S.act(sg.ap, pg.ap, AF.Sigmoid, [pg, P_S5V], [sg], bias=sv[:, m, 1:2], scale=1.0)
                S.tt('dve', sg.ap, sg.ap, YG.ap[:, m, :], ALU.mult, [sg, YG], [sg])
                S.dma('pool', d_yT[3, m * 128:(m + 1) * 128, tk], sg.ap, (sg,), (K_yT[3][s],))
        close_stage(G)

    YSUM = [None, None]

    def stage_s5_wrap(l):
        stage_s5(l)

    def stage_merge(l):
        G = open_stage(24200, 22000)
        Xb, X = chunk_regs("X")
        U2sb, U2s = chunk_regs("U2s")
        U2b, U2 = chunk_regs("U2", RA)
        MGb, MG = chunk_regs("MG", RA)
        YBs = [Reg(AR.alloc(4 * CH).rearrange("p (k t) -> p k t", k=4), "ybs%d" % b) for b in range(2)]
        YB = [Reg(RA.alloc(4 * CH).rearrange("p (k t) -> p k t", k=4), "yb%d" % b) for b in range(4)]
        WGs = [Reg(AR.alloc(8 * 128).rearrange("p (k n) -> p k n", k=8), "wgs%d" % i) for i in range(4)]
        WBs = [Reg(AR.alloc(4 * 128).rearrange("p (k n) -> p k n", k=4), "wbs%d" % i) for i in range(4)]
        WOs = [Reg(AR.alloc(8 * 128).rearrange("p (k n) -> p k n", k=8), "wos%d" % i) for i in range(2)]
        WGt = [Reg(RA.alloc(8 * 128).rearrange("p (k n) -> p k n", k=8), "wgt%d" % i) for i in range(2)]
        WBr = [Reg(RA.alloc(4 * 128).rearrange("p (k n) -> p k n", k=4), "wbr%d" % i) for i in range(2)]
        WO = [Reg(RA.alloc(8 * 128).rearrange("p (k n) -> p k n", k=8), "wo%d" % i) for i in range(2)]
        SG = [Reg(AR.alloc(CH), "sg%d" % i) for i in range(2)]
        TM = [Reg(AR.alloc(CH), "tm%d" % i) for i in range(2)]
        it = 0
        for c in range(dbg.get('nch', NCH)):
            s = c // (NCH // NS)
            tk = slice(c * CH, (c + 1) * CH)
            S.dma('sp', Xb.rearrange("p (k t) -> p k t", k=8), dram_chunk(d_xT, c), (K_xT[c],), X)
            S.dma('sp', U2sb.rearrange("p (k t) -> p k t", k=8), dram_chunk(d_u2T, c), (K_u2T[c],), U2s)
            for k in range(8):
                S.copy('pool' if k % 2 == 0 else 'act', U2[k].ap, U2s[k].ap, [U2s[k]], [U2[k]])
            for b in range(4):
                ybs = YBs[b % 2]
                S.dma('sp', ybs.ap, d_yT[b, :, tk].rearrange("(k p) t -> p k t", p=128), (K_yT[b][s],), (ybs,))
                S.copy('pool' if b % 2 == 0 else 'act', YB[b].ap, ybs.ap, [ybs], [YB[b]])
            for m in range(8):
                for b in range(4):
                    wgs, wbs, wg, wb = WGs[it % 4], WBs[it % 4], WGt[it % 2], WBr[it % 2]
                    S.dma('sp', wgs.ap, i_wgate[l, b, m], (), (wgs,))
                    S.dma('sp', wbs.ap, i_wbr[l, b, m], (), (wbs,))
                    S.copy('pool', wg.ap, wgs.ap, [wgs], [wg])
                    S.copy('act', wb.ap, wbs.ap, [wbs], [wb])
                    pg, pb = PB[it % 2], PB[2 + it % 2]
                    for k in range(8):
                        S.mm(pg.ap, wg.ap[:, k, :], U2[k].ap, k == 0, k == 7, [wg, U2[k]], [pg])
                    for k in range(4):
                        S.mm(pb.ap, wb.ap[:, k, :], YB[b].ap[:, k, :], k == 0, k == 3, [wb, YB[b]], [pb])
                    sg = SG[it % 2]
                    S.act(sg.ap, pg.ap, AF.Sigmoid, [pg], [sg])
                    if b == 0:
                        S.tt('dve', MG[m].ap, sg.ap, pb.ap, ALU.mult, [sg, pb], [MG[m]])
                    else:
                        tm = TM[it % 2]
                        S.tt('dve', tm.ap, sg.ap, pb.ap, ALU.mult, [sg, pb], [tm])
                        S.tt('pool', MG[m].ap, MG[m].ap, tm.ap, ALU.add, [MG[m], tm], [MG[m]])
                    it += 1
            for m in range(8):
                wos, wo = WOs[m % 2], WO[m % 2]
                S.dma('sp', wos.ap, i_wout[l, m], (), (wos,))
                S.copy('pool' if m % 2 == 0 else 'act', wo.ap, wos.ap, [wos], [wo])
                po = PB[4 + m % 2]
                for k in range(8):
                    S.mm(po.ap, wo.ap[:, k, :], MG[k].ap, k == 0, k == 7, [wo, MG[k]], [po])
                S.stt(X[m].ap, po.ap, G3[:, 1, m, s:s + 1], X[m].ap, ALU.mult, ALU.add, [po, P_G, X[m]], [X[m]])
            S.dma('pool', dram_chunk(d_xT, c), Xb.rearrange("p (k t) -> p k t", k=8), X, (K_xT[c],))
        close_stage(G)

    stages = dbg.get('stages')
    for l in range(n_layers):
        def want(n):
            return stages is None or n in stages
        if want('mod'):
            stage_mod(l)
        if want('ffn0'):
            stage_ffn(l, 0)
        if want('win'):
            stage_win(l)
        if want('rglru'):
            stage_rglru(l)
        if want('mla'):
            stage_mla(l)
        if want('dsa'):
            stage_dsa(l)
        if want('s5'):
            stage_s5_wrap(l)
        if want('merge'):
            stage_merge(l)
        if want('ffn1'):
            stage_ffn(l, 1)

    GE = open_stage(4200)
    Xb, X = chunk_regs("X")
    for c in range(dbg.get('nch', NCH)):
        S.dma('sp', Xb.rearrange("p (k t) -> p k t", k=8), dram_chunk(d_xT, c), (K_xT[c],), X)
        S.dma('pool', dram_chunk(o_yT, c), Xb.rearrange("p (k t) -> p k t", k=8), X, ())
    S.emit()
    for g in reversed(GE):
        g.__exit__(None, None, None)
    stack.close()
    return nc, S


def _consts():
    c = np.zeros((128, NCONST), np.float32)

    def put(name, a):
        o, w = CO[name]
        c[:a.shape[0], o:o + a.shape[1]] = a
    put('ident', np.eye(128, dtype=np.float32))
    put('ones', np.ones((128, 128), np.float32))
    bd = np.zeros((128, 128), np.float32)
    bd[:64, :64] = 1
    bd[64:, 64:] = 1
    put('bd64', bd)
    r96 = np.zeros((128, 128), np.float32)
    for i in range(16):
        r96[80 + i, 64 + i] = -1.0
        r96[64 + i, 80 + i] = 1.0
    put('r96', r96)
    r64 = np.zeros((128, 128), np.float32)
    for hb in range(2):
        for i in range(8):
            r64[hb * 64 + 8 + i, hb * 64 + i] = -1.0
            r64[hb * 64 + i, hb * 64 + 8 + i] = 1.0
    put('r64', r64)
    r32 = np.zeros((128, 128), np.float32)
    for hb in range(4):
        for i in range(4):
            r32[hb * 32 + 4 + i, hb * 32 + i] = -1.0
            r32[hb * 32 + i, hb * 32 + 4 + i] = 1.0
    put('r32', r32)
    p = np.arange(128)[:, None]
    f = np.arange(128)[None, :]
    put('tri_kq', (p <= f).astype(np.float32))
    tq = (f <= p).astype(np.float32)
    put('tri_qk', tq)
    put('neg_qk', np.where(f <= p, np.float32(0), np.float32(-BIG)).astype(np.float32))

    def inv(rot):
        return (np.float32(500000.0) ** (-(np.arange(0, rot, 2, dtype=np.float32)) / np.float32(rot))).astype(np.float32)
    im = np.zeros((128, 1), np.float32)
    im[64:80, 0] = inv(32)
    im[80:96, 0] = inv(32)
    put('inv_mla', im)
    idd = np.zeros((128, 1), np.float32)
    for hb in range(2):
        idd[hb * 64:hb * 64 + 8, 0] = inv(16)
        idd[hb * 64 + 8:hb * 64 + 16, 0] = inv(16)
    put('inv_dsa', idd)
    ii = np.zeros((128, 1), np.float32)
    for hb in range(4):
        ii[hb * 32:hb * 32 + 4, 0] = inv(8)
        ii[hb * 32 + 4:hb * 32 + 8, 0] = inv(8)
    put('inv_idx', ii)
    put('iota', np.broadcast_to(np.arange(T, dtype=np.float32)[None, :], (128, T)))
    return c


def _layout_weights(I):
    f = np.float32
    A = lambda a: np.ascontiguousarray(np.asarray(a, dtype=f))
    W = {}
    W['consts'] = _consts()
    W['ada_w'] = A(np.asarray(I['ada_w']).reshape(L_, 8, 128, 18, 512).transpose(0, 3, 2, 1, 4))
    W['ada_b'] = A(np.repeat(np.asarray(I['ada_b']).reshape(L_, 72, 128).transpose(0, 2, 1)[..., None], NS, axis=-1))
    W['norm_g'] = A(np.repeat(np.asarray(I['norm_g']).reshape(L_, 3, 8, 128).transpose(0, 1, 3, 2)[..., None], NS, axis=-1))
    W['w1'] = A(np.asarray(I['ffn_w1']).reshape(L_, 2, 8, 128, 22, 128).transpose(0, 1, 4, 3, 2, 5))
    W['w3'] = A(np.asarray(I['ffn_w3']).reshape(L_, 2, 8, 128, 22, 128).transpose(0, 1, 4, 3, 2, 5))
    W['w2'] = A(np.asarray(I['ffn_w2']).reshape(L_, 2, 22, 128, 8, 128).transpose(0, 1, 4, 3, 2, 5))
    win = np.asarray(I['w_in'])
    wz = np.zeros((L_, NZ, 128, 8, 128), f)
    for zi, (c0, wd) in enumerate(Z_TILES):
        wz[:, zi, :, :, :wd] = win[:, :, c0:c0 + wd].reshape(L_, 8, 128, wd).transpose(0, 2, 1, 3)
    W['win_z'] = wz
    wt = np.concatenate([win[:, :, 2016:2080], win[:, :, 2368:2376]], axis=-1)
    W['win_tok'] = A(wt.reshape(L_, 8, 128, 72).transpose(0, 2, 1, 3))
    W['win_gate'] = A(win[:, :, 2888:].reshape(L_, 8, 128, 4, 8, 128).transpose(0, 3, 4, 2, 1, 5))
    rgp = np.zeros((L_, 128, 4, 8), f)
    cw = np.asarray(I['conv_w'])
    for j in range(4):
        rgp[:, :, :, j] = cw[:, j].reshape(L_, 4, 128).transpose(0, 2, 1)
    for j, n in enumerate(('conv_b', 'rg_ba', 'rg_bx', 'rg_lambda')):
        rgp[:, :, :, 4 + j] = np.asarray(I[n]).reshape(L_, 4, 128).transpose(0, 2, 1)
    W['rg_par'] = rgp
    rgw = np.zeros((L_, 2, 4, 128, 128), f)
    for wi_, n in enumerate(('rg_wa', 'rg_wx')):
        w = np.asarray(I[n])
        for h in range(8):
            ct, o = h // 2, (h % 2) * 64
            rgw[:, wi_, ct, o:o + 64, o:o + 64] = w[:, h]
    W['rg_w'] = rgw
    mn = np.zeros((L_, 128, 3), f)
    mn[:, :, 0:2] = np.asarray(I['mla_q_norm']).reshape(L_, 2, 128).transpose(0, 2, 1)
    mn[:, :, 2] = np.asarray(I['mla_kv_norm'])
    W['mla_norm'] = mn
    perm = np.concatenate([np.arange(32, 96), np.arange(0, 32)])
    wuq = np.asarray(I['mla_w_uq']).reshape(L_, 2, 128, 8, 96)[..., perm]
    W['w_uq'] = A(wuq.transpose(0, 2, 1, 3, 4))
    wukv = np.asarray(I['mla_w_ukv']).reshape(L_, 128, 8, 128)
    W['w_ukv_k'] = A(wukv[..., :64])
    W['w_ukv_v'] = A(wukv[..., 64:].reshape(L_, 128, 512))
    g = np.zeros((L_, 128, 4), f)
    mg = np.asarray(I['mla_qk_gain'])[..., perm]
    g[:, :96, 0] = mg[:, 0]
    g[:, :96, 1] = mg[:, 1]
    dg = np.asarray(I['dsa_qk_gain'])
    g[:, :, 2] = np.tile(dg[:, 0], (1, 2))
    g[:, :, 3] = np.tile(dg[:, 1], (1, 2))
    W['qk_gains'] = g
    lr = np.asarray(I['s5_lambda_re']).reshape(L_, 2048)
    li = np.asarray(I['s5_lambda_im']).reshape(L_, 2048)
    ld = np.repeat(np.asarray(I['s5_log_dt']), 64, axis=1)
    st3 = np.stack([lr, li, ld], axis=1)
    W['s5_bc'] = A(np.broadcast_to(st3[:, :, None, :], (L_, 3, 128, 2048)))
    W['s5_pp'] = A(st3.reshape(L_, 3, 16, 128).transpose(0, 1, 3, 2))
    sb = np.zeros((L_, 2, 128, 16, 128), f)
    scm = np.zeros((L_, 2, 128, 16, 128), f)
    for ri, (bn, cn) in enumerate((('s5_b_re', 's5_c_re'), ('s5_b_im', 's5_c_im'))):
        b = np.asarray(I[bn])
        cc = np.asarray(I[cn])
        for gi in range(32):
            st, half = gi // 2, gi % 2
            r0 = 16 * (gi % 8)
            sb[:, ri, r0:r0 + 16, st, half * 64:(half + 1) * 64] = b[:, gi].transpose(0, 2, 1)
            scm[:, ri, half * 64:(half + 1) * 64, st, r0:r0 + 16] = cc[:, gi].transpose(0, 2, 1)
    W['s5_b'] = sb
    W['s5_c'] = scm
    sv = np.zeros((L_, 128, 4, 2), f)
    sv[..., 0] = np.asarray(I['s5_d']).reshape(L_, 4, 128).transpose(0, 2, 1)
    sv[..., 1] = np.asarray(I['s5_b_glu']).reshape(L_, 4, 128).transpose(0, 2, 1)
    W['s5_vec'] = sv
    W['w_glu'] = A(np.asarray(I['s5_w_glu']).reshape(L_, 4, 128, 512).transpose(0, 2, 1, 3))
    W['w_branch'] = A(np.asarray(I['w_branch']).reshape(L_, 4, 4, 128, 8, 128).transpose(0, 1, 4, 3, 2, 5))
    W['w_out'] = A(np.asarray(I['w_out']).reshape(L_, 8, 128, 8, 128).transpose(0, 3, 2, 1, 4))
    return W


def _core_inputs(I, W, c):
    x = np.asarray(I['x'], dtype=np.float32)[NS * c:NS * (c + 1)]
    m = dict(W)
    m['xT'] = np.ascontiguousarray(x.reshape(NT, D).T)
    cc = np.asarray(I['c'], dtype=np.float32)[NS * c:NS * (c + 1)]
    m['cT'] = np.ascontiguousarray(cc.reshape(NS, 8, 128).transpose(2, 1, 0))
    pos = np.asarray(I['positions']).astype(np.int32)[NS * c:NS * (c + 1)]
    m['posb'] = np.ascontiguousarray(np.broadcast_to(pos[None], (128, NS, T)))
    return m


_CACHE = {}


def kernel(**inputs):
    if 'nc' not in _CACHE:
        _CACHE['nc'] = build_program()[0]
    nc = _CACHE['nc']
    W = _layout_weights(inputs)
    in_maps = [_core_inputs(inputs, W, c) for c in range(8)]
    res = run_bass_kernel_spmd(nc, in_maps, core_ids=list(range(8)))
    out = np.empty((16, T, D), np.float32)
    for c in range(8):
        yT = np.asarray(res.results[c]["yT_out"])
        out[NS * c:NS * (c + 1)] = yT.T.reshape(NS, T, D)
    return out
```

```python
import numpy as np
from contextlib import ExitStack
import concourse.bass as bass
import concourse.mybir as mybir
from concourse.bass_utils import run_bass_kernel_spmd

F32 = mybir.dt.float32
I32 = mybir.dt.int32
F32R = mybir.dt.float32r


def AF32(ap):
    return ap.bitcast(F32)
ALU = mybir.AluOpType
AF = mybir.ActivationFunctionType

L_ = 4
D = 1024
T = 2048
NS = 2
NT = NS * T
CH = 512
NCH = NT // CH
DFF = 2816
EPS = 1e-6
BIG = 1.0e30
PI = float(np.pi)
TWO_PI = float(2 * np.pi)
CW1 = 6.28125
CW2 = float(2 * np.pi - 6.28125)
PI_SAFE = 3.1415925

Z_TILES = ([(0 + 128 * i, 128) for i in range(4)] + [(512 + 128 * i, 128) for i in range(4)]
           + [(1024, 128), (1152, 128), (1280, 128), (1408, 32)]
           + [(1440 + 128 * i, 128) for i in range(4)] + [(1952, 64)]
           + [(2080, 128), (2208, 128), (2336, 32)] + [(2376 + 128 * i, 128) for i in range(4)])
ZI_XRNN, ZI_GATE, ZI_QLAT, ZI_KVLAT, ZI_KPE, ZI_QDSA, ZI_KDSA, ZI_QIDX, ZI_KIDX, ZI_US5 = 0, 4, 8, 10, 11, 12, 16, 17, 19, 20
NZ = len(Z_TILES)

CO = {}
_off = 0
for _n, _w in [('ident', 128), ('ones', 128), ('bd64', 128), ('r96', 128), ('r64', 128), ('r32', 128),
               ('tri_kq', 128), ('tri_qk', 128), ('neg_qk', 128), ('inv_mla', 1), ('inv_dsa', 1),
               ('inv_idx', 1), ('iota', T)]:
    CO[_n] = (_off, _w)
    _off += _w
NCONST = _off


class Reg:
    __slots__ = ('ap', 'name')

    def __init__(self, ap, name=''):
        self.ap = ap
        self.name = name


class Sched:
    ENG = ('pe', 'act', 'dve', 'pool', 'sp')
    EPOCH = 16000
    NSLOT = 12

    def __init__(self, nc, stack):
        self.nc = nc
        self.stack = stack
        self.streams = {e: [] for e in self.ENG}
        self.count = {e: 0 for e in self.ENG}
        self.sems = {e: [] for e in self.ENG}
        self.waited = {e: {} for e in self.ENG}
        self.last_w = {}
        self.readers = {}
        self.semobj = {}
        self.slots = {q: [[self._newsem(), 0] for _ in range(self.NSLOT)] for q in ('sp', 'pool')}
        self.slot_rr = {'sp': 0, 'pool': 0}
        self.all_tokens = []
        self.ninstr = 0

    def _newsem(self):
        s = self.stack.enter_context(self.nc.semaphore())
        sid = len(self.semobj)
        self.semobj[sid] = s
        return sid

    def _esem(self, e, epoch):
        while len(self.sems[e]) <= epoch:
            self.sems[e].append(self._newsem())
        return self.sems[e][epoch]

    def _deps(self, e, reads, writes):
        waits = {}

        def need(tok):
            if tok is None:
                return
            sid, val, prod = tok
            if prod == e and e == 'pe':
                return
            if self.waited[e].get(sid, 0) >= val:
                return
            if waits.get(sid, 0) < val:
                waits[sid] = val

        for k in reads:
            need(self.last_w.get(id(k)))
        for k in writes:
            need(self.last_w.get(id(k)))
            for r in self.readers.get(id(k), ()):
                need(r)
        return waits

    def _commit(self, e, tok, reads, writes, waits):
        for sid, v in waits.items():
            self.waited[e][sid] = v
        for k in reads:
            self.readers.setdefault(id(k), []).append(tok)
        for k in writes:
            self.last_w[id(k)] = tok
            self.readers[id(k)] = []

    def op(self, e, fn, reads=(), writes=()):
        waits = self._deps(e, reads, writes)
        idx = self.count[e]
        sid = self._esem(e, idx // self.EPOCH)
        tok = (sid, idx % self.EPOCH + 1, e)
        self.count[e] += 1
        self._commit(e, tok, reads, writes, waits)
        self.streams[e].append((list(waits.items()), fn, sid, 1))
        self.ninstr += 1

    def dma(self, q, out_ap, in_ap, reads=(), writes=()):
        waits = self._deps(q, reads, writes)
        si = self.slot_rr[q]
        self.slot_rr[q] = (si + 1) % self.NSLOT
        slot = self.slots[q][si]
        if slot[1] + 16 > 30000:
            slot[0] = self._newsem()
            slot[1] = 0
        if slot[1] > 0 and self.waited[q].get(slot[0], 0) < slot[1]:
            waits[slot[0]] = max(waits.get(slot[0], 0), slot[1])
        slot[1] += 16
        tok = (slot[0], slot[1], 'dma_' + q)
        self._commit(q, tok, reads, writes, waits)

        def fn(eng, o=out_ap, i=in_ap):
            return eng.dma_start(out=o, in_=i)
        self.streams[q].append((list(waits.items()), fn, slot[0], 16))
        self.all_tokens.append(tok)
        self.ninstr += 1

    def barrier(self):
        toks = []
        for e in self.ENG:
            idx = self.count[e]
            if idx > 0:
                toks.append((self._esem(e, (idx - 1) // self.EPOCH), (idx - 1) % self.EPOCH + 1, e))
        for q in ('sp', 'pool'):
            for slot in self.slots[q]:
                if slot[1] > 0:
                    toks.append((slot[0], slot[1], 'dma_' + q))
        for e in self.ENG:
            waits = {}
            for sid, val, prod in toks:
                if prod == e:
                    continue
                if self.waited[e].get(sid, 0) >= val:
                    continue
                waits[sid] = max(waits.get(sid, 0), val)
            for sid, v in waits.items():
                self.waited[e][sid] = v
            if waits:
                self.streams[e].append((list(waits.items()), None, None, 0))
        self.last_w.clear()
        self.readers.clear()

    def emit(self):
        nc = self.nc
        self.barrier()
        semobj = self.semobj
        streams = self.streams

        def replay(e, eng):
            for waits, fn, sid, inc in streams[e]:
                for s, v in waits:
                    eng.wait_ge(semobj[s], v)
                if fn is not None:
                    ins = fn(eng)
                    ins.then_inc(semobj[sid], inc)

        with nc.Block() as block:
            @block.tensor
            def _(eng):
                replay('pe', eng)

            @block.scalar
            def _(eng):
                replay('act', eng)

            @block.vector
            def _(eng):
                replay('dve', eng)

            @block.gpsimd
            def _(eng):
                replay('pool', eng)

            @block.sync
            def _(eng):
                replay('sp', eng)

    def mm(self, out, lhsT, rhs, start, stop, reads, writes):
        self.op('pe', lambda g, o=out, l=lhsT, r=rhs, a=start, b=stop: g.matmul(o, l, r, start=a, stop=b),
                reads, writes)

    def transpose(self, out, in_, ident, reads, writes):
        self.op('pe', lambda g, o=out, i=in_, d=ident: g.transpose(o, i, d), reads, writes)

    def act(self, out, in_, func, reads, writes, bias=None, scale=None):
        kw = {}
        if bias is not None:
            kw['bias'] = bias
        if scale is not None:
            kw['scale'] = scale
        self.op('act', lambda g, o=out, i=in_, f=func, k=kw: g.activation(o, i, f, **k), reads, writes)

    def tt(self, e, out, in0, in1, op, reads, writes):
        self.op(e, lambda g, o=out, a=in0, b=in1, p=op: g.tensor_tensor(o, a, b, p), reads, writes)

    def ts(self, e, out, in0, s1, s2, op0, op1, reads, writes):
        if s2 is None:
            self.op(e, lambda g, o=out, a=in0, x=s1, p=op0: g.tensor_scalar(o, a, x, None, p), reads, writes)
        else:
            self.op(e, lambda g, o=out, a=in0, x=s1, y=s2, p=op0, q=op1: g.tensor_scalar(o, a, x, y, p, q),
                    reads, writes)

    def stt(self, out, in0, scalar, in1, op0, op1, reads, writes):
        self.op('dve', lambda g, o=out, a=in0, s=scalar, b=in1, p=op0, q=op1:
                g.scalar_tensor_tensor(o, a, s, b, p, q), reads, writes)

    def copy(self, e, out, in_, reads, writes):
        if e == 'act':
            self.op('act', lambda g, o=out, i=in_: g.copy(o, i), reads, writes)
        else:
            self.op(e, lambda g, o=out, i=in_: g.tensor_copy(o, i), reads, writes)

    def recip(self, out, in_, reads, writes):
        self.op('dve', lambda g, o=out, i=in_: g.reciprocal(o, i), reads, writes)

    def memset(self, e, ap, val, writes):
        self.op(e, lambda g, a=ap, v=val: g.memset(a, v), (), writes)


class Arena:
    def __init__(self, ap, nwords):
        self.ap = ap
        self.n = nwords
        self.off = 0

    def reset(self):
        self.off = 0

    def alloc(self, words, name=''):
        assert self.off + words <= self.n, (name, self.off, words, self.n)
        a = self.ap[:, self.off:self.off + words]
        self.off += words
        return a


def build_program(n_layers=L_, debug=None):
    nc = bass.Bass("TRN2", target_bir_lowering=False)
    stack = ExitStack()

    def din(name, shape, dt=F32):
        return nc.dram_tensor(name, list(shape), dt, kind="ExternalInput").ap()

    dbg = debug or {}
    FB_LIM = dbg.get('fblim', 11)
    M_LIM = dbg.get('mlim', 8)

    def dscr(name, shape, dt=F32):
        kind = "ExternalOutput" if dbg.get(name) else "Internal"
        return nc.dram_tensor(name, list(shape), dt, kind=kind).ap()

    i_xT = din("xT", [D, NT])
    i_cT = din("cT", [128, 8, NS])
    i_pos = din("posb", [128, NS, T], I32)
    i_const = din("consts", [128, NCONST])
    i_adaw = din("ada_w", [L_, 18, 128, 8, 512])
    i_adab = din("ada_b", [L_, 128, 72, NS])
    i_normg = din("norm_g", [L_, 3, 128, 8, NS])
    i_w1 = din("w1", [L_, 2, 22, 128, 8, 128])
    i_w3 = din("w3", [L_, 2, 22, 128, 8, 128])
    i_w2 = din("w2", [L_, 2, 8, 128, 22, 128])
    i_winz = din("win_z", [L_, NZ, 128, 8, 128])
    i_wtok = din("win_tok", [L_, 128, 8, 72])
    i_wgate = din("win_gate", [L_, 4, 8, 128, 8, 128])
    i_rgp = din("rg_par", [L_, 128, 4, 8])
    i_rgw = din("rg_w", [L_, 2, 4, 128, 128])
    i_mlan = din("mla_norm", [L_, 128, 3])
    i_wuq = din("w_uq", [L_, 128, 2, 8, 96])
    i_wukvk = din("w_ukv_k", [L_, 128, 8, 64])
    i_wukvv = din("w_ukv_v", [L_, 128, 512])
    i_gains = din("qk_gains", [L_, 128, 4])
    i_s5bc = din("s5_bc", [L_, 3, 128, 2048])
    i_s5pp = din("s5_pp", [L_, 3, 128, 16])
    i_s5b = din("s5_b", [L_, 2, 128, 16, 128])
    i_s5c = din("s5_c", [L_, 2, 128, 16, 128])
    i_s5v = din("s5_vec", [L_, 128, 4, 2])
    i_wglu = din("w_glu", [L_, 128, 4, 512])
    i_wbr = din("w_branch", [L_, 4, 8, 128, 4, 128])
    i_wout = din("w_out", [L_, 8, 128, 8, 128])
    o_yT = nc.dram_tensor("yT_out", [D, NT], F32, kind="ExternalOutput").ap()

    d_xT = dscr("s_xT", [D, NT])
    d_u2T = dscr("s_u2T", [D, NT])
    d_zT = dscr("s_zT", [NZ * 128, NT])
    d_vw = dscr("s_vw", [NT, 72])
    d_yT = dscr("s_yT", [4, 512, NT])
    d_qm = dscr("s_qm", [8, 96, NT])
    d_km = dscr("s_km", [8, 96, NT])
    d_vm = dscr("s_vm", [NT, 512])
    d_qd = dscr("s_qd", [512, NT])
    d_kd = dscr("s_kd", [64, NT])
    d_yg = dscr("s_yg", [512, NT])
    d_tab = dscr("s_tab", [3, 2, 128, NT])

    def keys(n):
        return [Reg(None, n + str(i)) for i in range(NCH)]
    K_xT, K_u2T, K_zT, K_vw, K_qm, K_km, K_vm, K_qd, K_kd, K_yg = (keys(n) for n in
                                                                  ("xT", "u2T", "zT", "vw", "qm", "km", "vm", "qd", "kd", "yg"))
    K_yT = [[Reg(None, "yT%d_%d" % (b, s)) for s in range(NS)] for b in range(4)]
    K_tab = Reg(None, "tab")

    cst_t = stack.enter_context(nc.sbuf_tensor("cst", [128, NCONST], F32))
    par_t = stack.enter_context(nc.sbuf_tensor("par", [128, 1200], F32))
    iwk_t = stack.enter_context(nc.sbuf_tensor("iwk", [128, T], I32))
    banks = [stack.enter_context(nc.psum_tensor("pb%d" % i, [128, 512], F32)) for i in range(8)]
    PB = [Reg(b[:], "pb%d" % i) for i, b in enumerate(banks)]
    AR = Arena(None, 0)
    RA = Arena(None, 0)
    _uid = [0]

    def open_stage(f32_words, r_words=0):
        _uid[0] += 1
        g1 = nc.sbuf_tensor("fa%d" % _uid[0], [128, f32_words], F32)
        t1 = g1.__enter__()
        AR.ap, AR.n, AR.off = t1[:], f32_words, 0
        guards = [g1]
        if r_words:
            g2 = nc.sbuf_tensor("ra%d" % _uid[0], [128, r_words], F32R)
            t2 = g2.__enter__()
            RA.ap, RA.n, RA.off = t2[:], r_words, 0
            guards.append(g2)
        return guards

    def close_stage(guards):
        S.barrier()
        for g in reversed(guards):
            g.__exit__(None, None, None)
    CST = cst_t[:]
    PAR = par_t[:]
    IWK = Reg(iwk_t[:], "iwk")
    K_cst = Reg(None, "cst")

    S = Sched(nc, stack)

    def C(name, rows=128, c0=0, c1=None):
        o, w = CO[name]
        if c1 is None:
            c1 = w
        return CST[0:rows, o + c0:o + c1]

    S.dma('sp', CST, i_const, (), (K_cst,))

    par_off = [0]

    def palloc(w):
        a = PAR[:, par_off[0]:par_off[0] + w]
        par_off[0] += w
        return a
    P_MOD = Reg(palloc(144), "mod")
    P_ADAB = Reg(palloc(144), "adab")
    P_NG = Reg(palloc(48), "ng")
    P_A = Reg(palloc(48), "A")
    P_G = Reg(palloc(48), "G")
    P_CACT = Reg(palloc(16), "cact")
    P_RG = Reg(palloc(32), "rgp")
    P_RGD = Reg(palloc(16), "rgd")
    P_MLAN = Reg(palloc(3), "mlan")
    P_GAIN = Reg(palloc(4), "gains")
    P_S5PP = Reg(palloc(48), "s5pp")
    P_S5D = Reg(palloc(64), "s5d")
    P_S5V = Reg(palloc(8), "s5v")
    P_TMP = Reg(palloc(64), "ptmp")

    mod3 = P_MOD.ap.rearrange("p (j k s) -> p j k s", j=9, k=8)
    A3 = P_A.ap.rearrange("p (j k s) -> p j k s", j=3, k=8)
    G3 = P_G.ap.rearrange("p (j k s) -> p j k s", j=3, k=8)

    def modcol(j, k, s):
        return mod3[:, j, k, s:s + 1]

    def sincos(ang, sin_out, cos_out, tmp, rows, n, rk, wk):
        iw = IWK.ap[0:rows, 0:n]
        S.ts('dve', iw, ang, 1.0 / TWO_PI, None, ALU.mult, None, rk, [IWK])
        S.stt(tmp, iw, -CW1, ang, ALU.mult, ALU.add, rk + [IWK], wk)
        S.stt(ang, iw, -CW2, tmp, ALU.mult, ALU.add, rk + [IWK], wk)
        S.ts('dve', tmp, ang, PI, -TWO_PI, ALU.is_gt, ALU.mult, rk, wk)
        S.tt('dve', ang, ang, tmp, ALU.add, rk, wk)
        S.ts('dve', tmp, ang, -PI, TWO_PI, ALU.is_lt, ALU.mult, rk, wk)
        S.tt('dve', ang, ang, tmp, ALU.add, rk, wk)
        S.ts('dve', ang, ang, PI_SAFE, -PI_SAFE, ALU.min, ALU.max, rk, wk)
        S.act(sin_out, ang, AF.Sin, rk, wk)
        S.ts('dve', ang, ang, PI / 2, None, ALU.add, None, rk, wk)
        S.ts('dve', tmp, ang, PI, -TWO_PI, ALU.is_gt, ALU.mult, rk, wk)
        S.tt('dve', ang, ang, tmp, ALU.add, rk, wk)
        S.ts('dve', ang, ang, PI_SAFE, -PI_SAFE, ALU.min, ALU.max, rk, wk)
        S.act(cos_out, ang, AF.Sin, rk, wk)

    def rstd_from_psum(ps_ap, out_ap, n_feat, reads, writes):
        S.act(out_ap, ps_ap, AF.Sqrt, reads, writes, bias=EPS_AP[0:out_ap.shape[0], :], scale=1.0 / n_feat)
        S.recip(out_ap, out_ap, writes, writes)

    def gelu_tanh(x, out, t1, rk, wk):
        S.tt('pool', t1, x, x, ALU.mult, rk, wk)
        S.ts('pool', t1, t1, 0.044715, 1.0, ALU.mult, ALU.add, rk, wk)
        S.tt('pool', t1, t1, x, ALU.mult, rk, wk)
        S.act(t1, t1, AF.Sigmoid, rk, wk, scale=1.5957691216057308)
        S.tt('dve', out, x, t1, ALU.mult, rk, wk)

    EPS_R = Reg(palloc(1), "eps")
    EPS_AP = EPS_R.ap
    S.memset('dve', EPS_AP, EPS, [EPS_R])
    ONE_R = Reg(palloc(1), "one")
    S.memset('dve', ONE_R.ap, 1.0, [ONE_R])

    G0 = open_stage(16000)
    for c in range(dbg.get('nch', NCH)):
        S.dma('sp', d_xT[:, c * CH:(c + 1) * CH], i_xT[:, c * CH:(c + 1) * CH], (), (K_xT[c],))

    R_ang = Reg(AR.alloc(T), "ang")
    R_tmp = Reg(AR.alloc(T), "tmp")
    R_sin = Reg(AR.alloc(T), "sin")
    R_cos = Reg(AR.alloc(T), "cos")
    R_posi = Reg(AR.alloc(NS * T).bitcast(I32).rearrange("p (s t) -> p s t", s=NS), "posi")
    S.dma('sp', R_posi.ap, i_pos, (), (R_posi,))
    for f, inv in enumerate(('inv_mla', 'inv_dsa', 'inv_idx')):
        for s in range(NS):
            S.ts('dve', R_ang.ap, R_posi.ap[:, s, :], C(inv), None, ALU.mult, None, [R_posi, K_cst], [R_ang])
            sincos(R_ang.ap, R_sin.ap, R_cos.ap, R_tmp.ap, 128, T, [R_ang, R_tmp], [R_ang, R_tmp, R_sin, R_cos])
            S.dma('pool', d_tab[f, 0, :, s * T:(s + 1) * T], R_cos.ap, (R_cos,), (K_tab,))
            S.dma('pool', d_tab[f, 1, :, s * T:(s + 1) * T], R_sin.ap, (R_sin,), (K_tab,))
    S.dma('sp', P_CACT.ap.rearrange("p (k s) -> p k s", k=8), i_cT, (), (P_CACT,))
    S.act(P_CACT.ap, P_CACT.ap, AF.Silu, [P_CACT], [P_CACT])
    close_stage(G0)

    def norm_mod(Xr, Ur, UTr, SQr, RSr, j, s):
        ps = PB[0]
        for k in range(8):
            S.act(SQr[k % 2].ap, Xr[k].ap, AF.Square, [Xr[k]], [SQr[k % 2]])
            S.mm(ps.ap, C('ones'), SQr[k % 2].ap, k == 0, k == 7, [SQr[k % 2], K_cst], [ps])
        rstd_from_psum(ps.ap, RSr.ap, D, [ps, EPS_R], [RSr])
        for k in range(8):
            ut = UTr[k % 2]
            S.tt('dve', ut.ap, Xr[k].ap, RSr.ap, ALU.mult, [Xr[k], RSr], [ut])
            S.ts('pool', Ur[k].ap, ut.ap, A3[:, j, k, s:s + 1], modcol(3 * j, k, s), ALU.mult, ALU.add,
                 [ut, P_A, P_MOD], [Ur[k]])

    def chunk_regs(name, arena=None):
        base = (arena or AR).alloc(8 * CH, name)
        b3 = base.rearrange("p (k t) -> p k t", k=8)
        return base, [Reg(b3[:, k, :], name + str(k)) for k in range(8)]

    def dram_chunk(d, c):
        return d.rearrange("(k p) t -> p k t", p=128)[:, :, c * CH:(c + 1) * CH]

    def stage_mod(l):
        G = open_stage(8192)
        WB = [Reg(AR.alloc(8 * 512).rearrange("p (k n) -> p k n", k=8), "adaw%d" % i) for i in range(2)]
        cact3 = P_CACT.ap.rearrange("p (k s) -> p k s", k=8)
        S.dma('sp', P_ADAB.ap.rearrange("p (m s) -> p m s", s=NS), i_adab[l], (), (P_ADAB,))
        S.dma('sp', P_NG.ap.rearrange("p (j k s) -> p j k s", j=3, k=8), i_normg[l].rearrange("j p k s -> p j k s"),
              (), (P_NG,))
        ps = PB[1]
        for blk in range(18):
            w = WB[blk % 2]
            S.dma('sp', w.ap, i_adaw[l, blk], (), (w,))
            for mi in range(4):
                mt = blk * 4 + mi
                for k in range(8):
                    S.mm(ps.ap[:, mt * 2:mt * 2 + 2], w.ap[:, k, mi * 128:(mi + 1) * 128], cact3[:, k, :],
                         k == 0, k == 7, [w, P_CACT], [ps])
        S.tt('dve', P_MOD.ap, ps.ap[:, 0:144], P_ADAB.ap, ALU.add, [ps, P_ADAB], [P_MOD])
        ng3 = P_NG.ap.rearrange("p (j k s) -> p j k s", j=3, k=8)
        for j in range(3):
            S.ts('dve', A3[:, j], mod3[:, 3 * j + 1], 1.0, None, ALU.add, None, [P_MOD], [P_A])
            S.tt('dve', A3[:, j], A3[:, j], ng3[:, j], ALU.mult, [P_A, P_NG], [P_A])
            if j == 1:
                S.ts('dve', G3[:, j], mod3[:, 3 * j + 2], 1.0, None, ALU.add, None, [P_MOD], [P_G])
            else:
                S.ts('dve', G3[:, j], mod3[:, 3 * j + 2], 0.5, 0.5, ALU.mult, ALU.add, [P_MOD], [P_G])
        close_stage(G)

    def stage_ffn(l, jf):
        G = open_stage(21504, 25088)
        jn = 0 if jf == 0 else 2
        Xb, X = chunk_regs("X")
        Ub, U = chunk_regs("U", RA)
        UT = [Reg(AR.alloc(CH), "ut%d" % i) for i in range(2)]
        SQ = [Reg(AR.alloc(CH), "sq%d" % i) for i in range(2)]
        RS = Reg(AR.alloc(CH), "rs")
        SA = [Reg(AR.alloc(CH), "sa%d" % i) for i in range(2)]
        H = [Reg(RA.alloc(CH), "h%d" % i) for i in range(22)]
        W1s = [Reg(AR.alloc(8 * 128).rearrange("p (k n) -> p k n", k=8), "w1s%d" % i) for i in range(4)]
        W3s = [Reg(AR.alloc(8 * 128).rearrange("p (k n) -> p k n", k=8), "w3s%d" % i) for i in range(4)]
        W2s = [Reg(AR.alloc(22 * 128).rearrange("p (k n) -> p k n", k=22), "w2s%d" % i) for i in range(2)]
        W1 = [Reg(RA.alloc(8 * 128).rearrange("p (k n) -> p k n", k=8), "w1_%d" % i) for i in range(2)]
        W3 = [Reg(RA.alloc(8 * 128).rearrange("p (k n) -> p k n", k=8), "w3_%d" % i) for i in range(2)]
        W2 = [Reg(RA.alloc(22 * 128).rearrange("p (k n) -> p k n", k=22), "w2_%d" % i) for i in range(2)]
        for c in range(dbg.get('nch', NCH)):
            s = c // (NCH // NS)
            S.dma('sp', Xb.rearrange("p (k t) -> p k t", k=8), dram_chunk(d_xT, c), (K_xT[c],), X)
            norm_mod(X, U, UT, SQ, RS, jn, s)
            for ft in range(22):
                w1s, w3s, w1, w3 = W1s[ft % 4], W3s[ft % 4], W1[ft % 2], W3[ft % 2]
                S.dma('sp', w1s.ap, i_w1[l, jf, ft], (), (w1s,))
                S.dma('sp', w3s.ap, i_w3[l, jf, ft], (), (w3s,))
                S.copy('dve', w1.ap, w1s.ap, [w1s], [w1])
                S.copy('act', w3.ap, w3s.ap, [w3s], [w3])
                pa, pb = PB[2 + ft % 2], PB[4 + ft % 2]
                for k in range(8):
                    S.mm(pa.ap, w1.ap[:, k, :], U[k].ap, k == 0, k == 7, [w1, U[k]], [pa])
                for k in range(8):
                    S.mm(pb.ap, w3.ap[:, k, :], U[k].ap, k == 0, k == 7, [w3, U[k]], [pb])
                sa = SA[ft % 2]
                S.act(sa.ap, pa.ap, AF.Silu, [pa], [sa])
                S.tt('dve', H[ft].ap, sa.ap, pb.ap, ALU.mult, [sa, pb], [H[ft]])
            for m in range(8):
                w2s, w2 = W2s[m % 2], W2[m % 2]
                S.dma('sp', w2s.ap, i_w2[l, jf, m], (), (w2s,))
                S.copy('dve', w2.ap, w2s.ap, [w2s], [w2])
                po = PB[6 + m % 2]
                for kt in range(22):
                    S.mm(po.ap, w2.ap[:, kt, :], H[kt].ap, kt == 0, kt == 21, [w2, H[kt]], [po])
                S.stt(X[m].ap, po.ap, G3[:, jn, m, s:s + 1], X[m].ap, ALU.mult, ALU.add, [po, P_G, X[m]], [X[m]])
            S.dma('pool', dram_chunk(d_xT, c), Xb.rearrange("p (k t) -> p k t", k=8), X, (K_xT[c],))
        close_stage(G)

    def rope_pipeline(raw, P, gcol, rname, bdname, nfeat, tabC, tabS, QG, SQ, RS, T1, psA, psB, out, rk):
        S.act(SQ.ap[0:P], raw.ap[0:P], AF.Square, [raw], [SQ])
        S.mm(psA.ap[0:P], C(bdname, P, 0, P), SQ.ap[0:P], True, True, [SQ, K_cst], [psA])
        rstd_from_psum(psA.ap[0:P], RS.ap[0:P], nfeat, [psA, EPS_R], [RS])
        S.stt(QG.ap[0:P], raw.ap[0:P], gcol, RS.ap[0:P], ALU.mult, ALU.mult, [raw, RS, P_GAIN], [QG])
        S.mm(psB.ap[0:P], C(rname, P, 0, P), QG.ap[0:P], True, True, [QG, K_cst], [psB])
        S.tt('pool', T1.ap[0:P], QG.ap[0:P], tabC.ap[0:P], ALU.mult, [QG, tabC], [T1])
        S.tt('dve', out.ap[0:P], psB.ap[0:P], tabS.ap[0:P], ALU.mult, [psB, tabS], [out])
        S.tt('dve', out.ap[0:P], out.ap[0:P], T1.ap[0:P], ALU.add, [out, T1], [out])

    def stage_win(l):
        G = open_stage(14000, 6200)
        Xb, X = chunk_regs("X")
        Ub, U = chunk_regs("U", RA)
        UT = [Reg(AR.alloc(CH), "ut%d" % i) for i in range(2)]
        SQ = [Reg(AR.alloc(CH), "sq%d" % i) for i in range(2)]
        RS = Reg(AR.alloc(CH), "rs")
        ZS = [Reg(AR.alloc(CH), "zs%d" % i) for i in range(2)]
        ZR = [Reg(AR.alloc(CH), "zr%d" % i) for i in range(2)]
        W = [Reg(AR.alloc(8 * 128).rearrange("p (k n) -> p k n", k=8), "wz%d" % i) for i in range(2)]
        WR_ = [Reg(RA.alloc(8 * 128).rearrange("p (k n) -> p k n", k=8), "wzr%d" % i) for i in range(2)]
        WT = Reg(AR.alloc(8 * 72).rearrange("p (k n) -> p k n", k=8), "wtok")
        VW = [Reg(AR.alloc(72), "vw%d" % i) for i in range(2)]
        TC = Reg(AR.alloc(CH), "tc")
        TS_ = Reg(AR.alloc(CH), "tsn")
        S.dma('sp', WT.ap, i_wtok[l], (), (WT,))
        for c in range(dbg.get('nch', NCH)):
            s = c // (NCH // NS)
            S.dma('sp', Xb.rearrange("p (k t) -> p k t", k=8), dram_chunk(d_xT, c), (K_xT[c],), X)
            S.dma('sp', TC.ap, d_tab[2, 0, :, c * CH:(c + 1) * CH], (K_tab,), (TC,))
            S.dma('sp', TS_.ap, d_tab[2, 1, :, c * CH:(c + 1) * CH], (K_tab,), (TS_,))
            norm_mod(X, U, UT, SQ, RS, 1, s)
            S.dma('pool', dram_chunk(d_u2T, c), AF32(Ub).rearrange("p (k t) -> p k t", k=8), U, (K_u2T[c],))
            for zi, (c0, wd) in enumerate(Z_TILES):
                w = W[zi % 2]
                S.dma('sp', w.ap, i_winz[l, zi], (), (w,))
                pz = PB[1 + zi % 2]
                if wd == 128 and zi not in (ZI_QIDX, ZI_QIDX + 1):
                    wr = WR_[zi % 2]
                    S.copy('dve' if zi % 2 == 0 else 'act', wr.ap, w.ap, [w], [wr])
                    for k in range(8):
                        S.mm(pz.ap, wr.ap[:, k, :], U[k].ap, k == 0, k == 7, [wr, U[k]], [pz])
                else:
                    for k in range(8):
                        S.mm(pz.ap[0:wd], w.ap[:, k, 0:wd], AF32(U[k].ap), k == 0, k == 7, [w, U[k]], [pz])
                zs = ZS[zi % 2]
                S.copy('act', zs.ap[0:wd], pz.ap[0:wd], [pz], [zs])
                if zi in (ZI_QIDX, ZI_QIDX + 1, ZI_KIDX):
                    pr = PB[3 + zi % 2]
                    zr = ZR[zi % 2]
                    S.mm(pr.ap[0:wd], C('r32', wd, 0, wd), zs.ap[0:wd], True, True, [zs, K_cst], [pr])
                    S.tt('dve', zr.ap[0:wd], pr.ap[0:wd], TS_.ap[0:wd], ALU.mult, [pr, TS_], [zr])
                    S.tt('pool', zs.ap[0:wd], zs.ap[0:wd], TC.ap[0:wd], ALU.mult, [zs, TC], [zs])
                    S.tt('dve', zs.ap[0:wd], zs.ap[0:wd], zr.ap[0:wd], ALU.add, [zs, zr], [zs])
                S.dma('pool', d_zT[zi * 128:zi * 128 + wd, c * CH:(c + 1) * CH], zs.ap[0:wd], (zs,), (K_zT[c],))
            for tt_ in range(4):
                pv = PB[5 + tt_ % 2]
                for k in range(8):
                    S.mm(pv.ap[:, 0:72], AF32(U[k].ap[:, tt_ * 128:(tt_ + 1) * 128]), WT.ap[:, k, :], k == 0, k == 7,
                         [WT, U[k]], [pv])
                vw = VW[tt_ % 2]
                S.copy('act', vw.ap, pv.ap[:, 0:72], [pv], [vw])
                t0 = c * CH + tt_ * 128
                S.dma('pool', d_vw[t0:t0 + 128, :], vw.ap, (vw,), (K_vw[c],))
        close_stage(G)

    def stage_rglru(l):
        G = open_stage(20000)
        XR = Reg(AR.alloc(T + 4), "xr")
        GT = Reg(AR.alloc(T), "gt")
        XC = Reg(AR.alloc(T), "xc")
        RR = Reg(AR.alloc(T), "rr")
        IG = Reg(AR.alloc(T), "ig")
        AA = Reg(AR.alloc(T), "aa")
        MM = Reg(AR.alloc(T), "mm")
        T1 = Reg(AR.alloc(T), "t1")
        HH = Reg(AR.alloc(T), "hh")
        WA = Reg(AR.alloc(4 * 128).rearrange("p (c n) -> p c n", c=4), "wa")
        WX = Reg(AR.alloc(4 * 128).rearrange("p (c n) -> p c n", c=4), "wx")
        S.dma('sp', WA.ap, i_rgw[l, 0].rearrange("c p n -> p c n"), (), (WA,))
        S.dma('sp', WX.ap, i_rgw[l, 1].rearrange("c p n -> p c n"), (), (WX,))
        S.dma('sp', P_RG.ap.rearrange("p (c j) -> p c j", c=4), i_rgp[l], (), (P_RG,))
        rg3 = P_RG.ap.rearrange("p (c j) -> p c j", c=4)
        rgd = P_RGD.ap.rearrange("p (j c) -> p j c", j=4)
        S.act(rgd[:, 0, :], rg3[:, :, 7], AF.Exp, [P_RG], [P_RGD], scale=-1.0)
        S.act(rgd[:, 0, :], rgd[:, 0, :], AF.Ln, [P_RGD], [P_RGD], bias=ONE_R.ap, scale=1.0)
        S.ts('dve', rgd[:, 1, :], rgd[:, 0, :], -8.0, None, ALU.mult, None, [P_RGD], [P_RGD])
        S.ts('dve', rgd[:, 2, :], rgd[:, 0, :], -16.0, None, ALU.mult, None, [P_RGD], [P_RGD])
        S.memset('dve', XR.ap[:, 0:4], 0.0, [XR])
        for s in range(NS):
            zk = K_zT[s * 4:(s + 1) * 4]
            for ct in range(4):
                S.dma('sp', XR.ap[:, 4:4 + T], d_zT[(ZI_XRNN + ct) * 128:(ZI_XRNN + ct + 1) * 128, s * T:(s + 1) * T],
                      zk, (XR,))
                S.dma('sp', GT.ap, d_zT[(ZI_GATE + ct) * 128:(ZI_GATE + ct + 1) * 128, s * T:(s + 1) * T], zk, (GT,))
                S.act(XC.ap, XR.ap[:, 4:4 + T], AF.Identity, [XR, P_RG], [XC], bias=rg3[:, ct, 4:5], scale=rg3[:, ct, 3:4])
                for j in range(3):
                    S.stt(XC.ap, XR.ap[:, 1 + j:1 + j + T], rg3[:, ct, j:j + 1], XC.ap, ALU.mult, ALU.add,
                          [XR, P_RG, XC], [XC])
                for q in range(4):
                    pr, pi = PB[q % 2], PB[2 + q % 2]
                    sl = slice(q * CH, (q + 1) * CH)
                    S.mm(pr.ap, WA.ap[:, ct, :], XC.ap[:, sl], True, True, [WA, XC], [pr])
                    S.mm(pi.ap, WX.ap[:, ct, :], XC.ap[:, sl], True, True, [WX, XC], [pi])
                    S.act(RR.ap[:, sl], pr.ap, AF.Sigmoid, [pr, P_RG], [RR], bias=rg3[:, ct, 5:6], scale=1.0)
                    S.act(IG.ap[:, sl], pi.ap, AF.Sigmoid, [pi, P_RG], [IG], bias=rg3[:, ct, 6:7], scale=1.0)
                S.act(AA.ap, RR.ap, AF.Exp, [RR, P_RGD], [AA], scale=rgd[:, 1, ct:ct + 1])
                S.act(MM.ap, RR.ap, AF.Exp, [RR, P_RGD], [MM], scale=rgd[:, 2, ct:ct + 1])
                S.act(MM.ap, MM.ap, AF.Sqrt, [MM, ONE_R], [MM], bias=ONE_R.ap, scale=-1.0)
                S.tt('dve', MM.ap, MM.ap, IG.ap, ALU.mult, [MM, IG], [MM])
                S.tt('dve', MM.ap, MM.ap, XC.ap, ALU.mult, [MM, XC], [MM])
                S.op('dve', lambda g, o=HH.ap, a=AA.ap, b=MM.ap: g.tensor_tensor_scan(o, a, b, 0.0, ALU.mult, ALU.add),
                     [AA, MM], [HH])
                gelu_tanh(GT.ap, IG.ap, T1.ap, [GT, T1, IG], [T1, IG])
                S.tt('dve', HH.ap, HH.ap, IG.ap, ALU.mult, [HH, IG], [HH])
                S.dma('pool', d_yT[0, ct * 128:(ct + 1) * 128, s * T:(s + 1) * T], HH.ap, (HH,), (K_yT[0][s],))
        close_stage(G)

    def stage_mla(l):
        G = open_stage(14000)
        QL = [Reg(AR.alloc(CH), "ql%d" % i) for i in range(2)]
        KVL = Reg(AR.alloc(CH), "kvl")
        QN = [Reg(AR.alloc(CH), "qn%d" % i) for i in range(2)]
        KVN = Reg(AR.alloc(CH), "kvn")
        SQ = Reg(AR.alloc(CH), "sq")
        RS = Reg(AR.alloc(CH), "rs")
        RAW = [Reg(AR.alloc(CH), "raw%d" % i) for i in range(2)]
        KRAW = [Reg(AR.alloc(CH), "kraw%d" % i) for i in range(2)]
        KPE = Reg(AR.alloc(CH), "kpe")
        QG = Reg(AR.alloc(CH), "qg")
        T1 = Reg(AR.alloc(CH), "t1")
        OUT = [Reg(AR.alloc(CH), "out%d" % i) for i in range(2)]
        TC = Reg(AR.alloc(CH), "tc")
        TS_ = Reg(AR.alloc(CH), "tsn")
        VS = [Reg(AR.alloc(512), "vs%d" % i) for i in range(2)]
        WUQ = Reg(AR.alloc(2 * 8 * 96).rearrange("p (k h n) -> p k h n", k=2, h=8), "wuq")
        WK = Reg(AR.alloc(8 * 64).rearrange("p (h n) -> p h n", h=8), "wk")
        WV = Reg(AR.alloc(512), "wv")
        S.dma('sp', WUQ.ap, i_wuq[l], (), (WUQ,))
        S.dma('sp', WK.ap, i_wukvk[l], (), (WK,))
        S.dma('sp', WV.ap, i_wukvv[l], (), (WV,))
        S.dma('sp', P_MLAN.ap, i_mlan[l], (), (P_MLAN,))
        S.dma('sp', P_GAIN.ap, i_gains[l], (), (P_GAIN,))
        for c in range(dbg.get('nch', NCH)):
            tk = slice(c * CH, (c + 1) * CH)
            for k in range(2):
                S.dma('sp', QL[k].ap, d_zT[(ZI_QLAT + k) * 128:(ZI_QLAT + k + 1) * 128, tk], (K_zT[c],), (QL[k],))
            S.dma('sp', KVL.ap, d_zT[ZI_KVLAT * 128:(ZI_KVLAT + 1) * 128, tk], (K_zT[c],), (KVL,))
            S.dma('sp', KPE.ap[64:96], d_zT[ZI_KPE * 128:ZI_KPE * 128 + 32, tk], (K_zT[c],), (KPE,))
            S.dma('sp', TC.ap, d_tab[0, 0, :, tk], (K_tab,), (TC,))
            S.dma('sp', TS_.ap, d_tab[0, 1, :, tk], (K_tab,), (TS_,))
            ps = PB[0]
            for k in range(2):
                S.act(SQ.ap, QL[k].ap, AF.Square, [QL[k]], [SQ])
                S.mm(ps.ap, C('ones'), SQ.ap, k == 0, k == 1, [SQ, K_cst], [ps])
            rstd_from_psum(ps.ap, RS.ap, 256, [ps, EPS_R], [RS])
            for k in range(2):
                S.stt(QN[k].ap, QL[k].ap, P_MLAN.ap[:, k:k + 1], RS.ap, ALU.mult, ALU.mult, [QL[k], P_MLAN, RS], [QN[k]])
            S.act(SQ.ap, KVL.ap, AF.Square, [KVL], [SQ])
            S.mm(ps.ap, C('ones'), SQ.ap, True, True, [SQ, K_cst], [ps])
            rstd_from_psum(ps.ap, RS.ap, 128, [ps, EPS_R], [RS])
            S.stt(KVN.ap, KVL.ap, P_MLAN.ap[:, 2:3], RS.ap, ALU.mult, ALU.mult, [KVL, P_MLAN, RS], [KVN])
            for h in range(8):
                pq = PB[1 + h % 2]
                raw = RAW[h % 2]
                for k in range(2):
                    S.mm(pq.ap[0:96], WUQ.ap[:, k, h, :], QN[k].ap, k == 0, k == 1, [WUQ, QN[k]], [pq])
                S.copy('act', raw.ap[0:96], pq.ap[0:96], [pq], [raw])
                o = OUT[0]
                rope_pipeline(raw, 96, P_GAIN.ap[0:96, 0:1], 'r96', 'ones', 96, TC, TS_, QG, SQ, RS, T1, PB[3], PB[4], o, None)
                S.dma('pool', d_qm[h, :, tk], o.ap[0:96], (o,), (K_qm[c],))
                pk = PB[5 + h % 2]
                kraw = KRAW[h % 2]
                S.mm(pk.ap[0:64], WK.ap[:, h, :], KVN.ap, True, True, [WK, KVN], [pk])
                S.copy('act', kraw.ap[0:64], pk.ap[0:64], [pk], [kraw])
                S.copy('act', kraw.ap[64:96], KPE.ap[64:96], [KPE], [kraw])
                o = OUT[1]
                rope_pipeline(kraw, 96, P_GAIN.ap[0:96, 1:2], 'r96', 'ones', 96, TC, TS_, QG, SQ, RS, T1, PB[3], PB[4], o, None)
                S.dma('pool', d_km[h, :, tk], o.ap[0:96], (o,), (K_km[c],))
            for tt_ in range(4):
                pv = PB[7]
                S.mm(pv.ap, KVN.ap[:, tt_ * 128:(tt_ + 1) * 128], WV.ap, True, True, [KVN, WV], [pv])
                vs = VS[tt_ % 2]
                S.copy('act', vs.ap, pv.ap, [pv], [vs])
                t0 = c * CH + tt_ * 128
                S.dma('pool', d_vm[t0:t0 + 128, :], vs.ap, (vs,), (K_vm[c],))
        close_stage(G)
        G = open_stage(8000, 12000)
        KHs = Reg(AR.alloc(T), "khs")
        QHs = Reg(AR.alloc(T), "qhs")
        V1s = Reg(AR.alloc(16 * 65).rearrange("p (b n) -> p b n", b=16), "v1s")
        KH = [Reg(RA.alloc(T), "kh%d" % i) for i in range(2)]
        QH = [Reg(RA.alloc(T), "qh%d" % i) for i in range(2)]
        V1 = [Reg(RA.alloc(16 * 65).rearrange("p (b n) -> p b n", b=16), "v1_%d" % i) for i in range(2)]
        PT = [Reg(RA.alloc(CH), "pt%d" % i) for i in range(3)]
        OS = Reg(AR.alloc(CH), "os")
        RD = Reg(AR.alloc(CH), "rd")
        YO = [Reg(AR.alloc(CH), "yo%d" % i) for i in range(2)]
        S.memset('dve', V1s.ap[:, :, 64:65], 1.0, [V1s])
        sc = 96.0 ** -0.5
        it = 0
        for s in range(NS):
            ks = K_km[s * 4:(s + 1) * 4]
            qs = K_qm[s * 4:(s + 1) * 4]
            vsk = K_vm[s * 4:(s + 1) * 4]
            for h in range(8):
                kh, qh, v1 = KH[h % 2], QH[h % 2], V1[h % 2]
                S.dma('sp', KHs.ap[0:96], d_km[h, :, s * T:(s + 1) * T], ks, (KHs,))
                S.dma('sp', QHs.ap[0:96], d_qm[h, :, s * T:(s + 1) * T], qs, (QHs,))
                S.dma('sp', V1s.ap[:, :, 0:64],
                      d_vm[s * T:(s + 1) * T, h * 64:(h + 1) * 64].rearrange("(b p) n -> p b n", p=128), vsk, (V1s,))
                S.copy('dve', kh.ap[0:96], KHs.ap[0:96], [KHs], [kh])
                S.copy('dve', qh.ap[0:96], QHs.ap[0:96], [QHs], [qh])
                S.copy('dve', v1.ap, V1s.ap, [V1s], [v1])
                for qc in range(4):
                    po = PB[6 + qc % 2]
                    nkb = 4 * (qc + 1)
                    for kb in range(nkb):
                        j = max(0, kb - 4 * qc)
                        q0 = qc * CH + j * 128
                        n = CH - j * 128
                        pS = PB[it % 3]
                        pt = PT[it % 3]
                        it += 1
                        S.mm(pS.ap[:, 0:n], kh.ap[0:96, kb * 128:(kb + 1) * 128], qh.ap[0:96, q0:q0 + n], True, True,
                             [kh, qh], [pS])
                        S.act(pt.ap[:, 0:n], pS.ap[:, 0:n], AF.Exp, [pS], [pt], scale=sc)
                        if kb >= 4 * qc:
                            S.tt('pool', pt.ap[:, 0:128], pt.ap[:, 0:128], C('tri_kq'), ALU.mult, [pt, K_cst], [pt])
                        S.mm(po.ap[0:65, j * 128:CH], v1.ap[:, kb, :], pt.ap[:, 0:n], kb == 0, kb == nkb - 1,
                             [v1, pt], [po])
                    S.copy('act', OS.ap[0:65], po.ap[0:65], [po], [OS])
                    pd = PB[3 + qc % 2]
                    S.mm(pd.ap[0:64], C('ones', 128, 0, 64)[64:65], OS.ap[64:65], True, True, [OS, K_cst], [pd])
                    S.recip(RD.ap[0:64], pd.ap[0:64], [pd], [RD])
                    yo = YO[qc % 2]
                    S.tt('dve', yo.ap[0:64], OS.ap[0:64], RD.ap[0:64], ALU.mult, [OS, RD], [yo])
                    S.dma('pool', d_yT[1, h * 64:(h + 1) * 64, s * T + qc * CH:s * T + (qc + 1) * CH], yo.ap[0:64],
                          (yo,), (K_yT[1][s],))
        close_stage(G)

    def stage_dsa(l):
        G = open_stage(6000)
        QD = [Reg(AR.alloc(CH), "qd%d" % i) for i in range(2)]
        SQ = Reg(AR.alloc(CH), "sq")
        RS = Reg(AR.alloc(CH), "rs")
        QG = Reg(AR.alloc(CH), "qg")
        T1 = Reg(AR.alloc(CH), "t1")
        OUT = [Reg(AR.alloc(CH), "out%d" % i) for i in range(2)]
        TC = Reg(AR.alloc(CH), "tc")
        TS_ = Reg(AR.alloc(CH), "tsn")
        S.dma('sp', P_GAIN.ap, i_gains[l], (), (P_GAIN,))
        for c in range(dbg.get('nch', NCH)):
            tk = slice(c * CH, (c + 1) * CH)
            S.dma('sp', TC.ap, d_tab[1, 0, :, tk], (K_tab,), (TC,))
            S.dma('sp', TS_.ap, d_tab[1, 1, :, tk], (K_tab,), (TS_,))
            for k in range(5):
                qd = QD[k % 2]
                o = OUT[k % 2]
                if k < 4:
                    S.dma('sp', qd.ap, d_zT[(ZI_QDSA + k) * 128:(ZI_QDSA + k + 1) * 128, tk], (K_zT[c],), (qd,))
                    rope_pipeline(qd, 128, P_GAIN.ap[:, 2:3], 'r64', 'bd64', 64, TC, TS_, QG, SQ, RS, T1, PB[0], PB[1], o, None)
                    S.dma('pool', d_qd[k * 128:(k + 1) * 128, tk], o.ap, (o,), (K_qd[c],))
                else:
                    S.dma('sp', qd.ap[0:64], d_zT[ZI_KDSA * 128:ZI_KDSA * 128 + 64, tk], (K_zT[c],), (qd,))
                    rope_pipeline(qd, 64, P_GAIN.ap[0:64, 3:4], 'r64', 'bd64', 64, TC, TS_, QG, SQ, RS, T1, PB[0], PB[1], o, None)
                    S.dma('pool', d_kd[:, tk], o.ap[0:64], (o,), (K_kd[c],))
        close_stage(G)
        G = open_stage(33600, 2700)
        _uid[0] += 1
        gm = nc.sbuf_tensor("mtb%d" % _uid[0], [128, 2 * 16 * CH], mybir.dt.bfloat16)
        mt_t = gm.__enter__()
        G.append(gm)
        mt4 = mt_t[:].rearrange("p (i b n) -> p i b n", i=2, b=16)
        MTs = [Reg(mt4[:, i], "mt%d" % i) for i in range(2)]
        KD2 = Reg(AR.alloc(T), "kd2")
        QI = Reg(AR.alloc(2 * T).rearrange("p (k t) -> p k t", k=2), "qi")
        KIB = Reg(AR.alloc(4 * T).rearrange("p (h t) -> p h t", h=4), "kib")
        V1s = Reg(AR.alloc(16 * 65).rearrange("p (b n) -> p b n", b=16), "v1s")
        V1 = Reg(RA.alloc(16 * 65).rearrange("p (b n) -> p b n", b=16), "v1")
        WI = Reg(AR.alloc(16 * 8).rearrange("p (b n) -> p b n", b=16), "wi")
        SCs = [Reg(AR.alloc(T), "sc%d" % i) for i in range(2)]
        WKs = [Reg(AR.alloc(T), "wk%d" % i) for i in range(2)]
        RL = [Reg(AR.alloc(CH), "rl%d" % i) for i in range(2)]
        M8s = [Reg(AR.alloc(8), "m8_%d" % i) for i in range(2)]
        THRs = [Reg(AR.alloc(1), "thr%d" % i) for i in range(2)]
        QC = Reg(AR.alloc(4 * CH).rearrange("p (k t) -> p k t", k=4), "qc")
        PT = [Reg(RA.alloc(CH), "pt%d" % i) for i in range(3)]
        OSs = [Reg(AR.alloc(CH), "os%d" % i) for i in range(8)]
        RD = Reg(AR.alloc(CH), "rd")
        YO = [Reg(AR.alloc(CH), "yo%d" % i) for i in range(2)]
        S.memset('dve', V1s.ap[:, :, 64:65], 1.0, [V1s])
        S.memset('dve', KIB.ap, 0.0, [KIB])
        sc = 64.0 ** -0.5
        itc = [0]

        def idx_pair(s, qc, pair):
            qbs = [qc * 4 + pair * 2, qc * 4 + pair * 2 + 1]
            for i, qb in enumerate(qbs):
                SC = SCs[i]
                for kb in range(qb + 1):
                    for k in range(2):
                        pr = PB[k]
                        rl = RL[k]
                        S.mm(pr.ap, QI.ap[:, k, qb * 128:(qb + 1) * 128], KIB.ap[:, :, kb * 128:(kb + 1) * 128],
                             True, True, [QI, KIB], [pr])
                        S.act(rl.ap, pr.ap, AF.Relu, [pr], [rl])
                        for h4 in range(4):
                            hh = k * 4 + h4
                            dst = SC.ap[:, kb * 128:(kb + 1) * 128]
                            if hh == 0:
                                S.ts('dve', dst, rl.ap[:, 0:128], WI.ap[:, qb, 0:1], None, ALU.mult, None,
                                     [rl, WI], [SC])
                            else:
                                S.stt(dst, rl.ap[:, h4 * 128:(h4 + 1) * 128], WI.ap[:, qb, hh:hh + 1], dst,
                                      ALU.mult, ALU.add, [rl, WI, SC], [SC])
                dg = SC.ap[:, qb * 128:(qb + 1) * 128]
                S.tt('dve', dg, dg, C('tri_qk'), ALU.mult, [SC, K_cst], [SC])
                S.tt('dve', dg, dg, C('neg_qk'), ALU.add, [SC, K_cst], [SC])
            srcs = [SCs[0], SCs[1]]
            for r in range(32):
                for i, qb in enumerate(qbs):
                    if qb < 2:
                        continue
                    nk = (qb + 1) * 128
                    S.op('dve', lambda g, o=M8s[i].ap, a=srcs[i].ap[:, 0:nk]: g.max(o, a), [srcs[i]], [M8s[i]])
                    if r < 31:
                        S.op('dve', lambda g, o=WKs[i].ap[:, 0:nk], a=M8s[i].ap, v=srcs[i].ap[:, 0:nk]:
                             g.match_replace(o, a, v, -BIG), [M8s[i], srcs[i]], [WKs[i]])
                        srcs[i] = WKs[i]
            for i, qb in enumerate(qbs):
                nk = (qb + 1) * 128
                SC, WK_, M8, THR = SCs[i], WKs[i], M8s[i], THRs[i]
                if qb >= 2:
                    S.ts('dve', THR.ap, M8.ap[:, 7:8], -0.5 * BIG, None, ALU.max, None, [M8], [THR])
                else:
                    S.memset('dve', THR.ap, -0.5 * BIG, [THR])
                S.ts('dve', WK_.ap[:, 0:nk], SC.ap[:, 0:nk], THR.ap, None, ALU.is_ge, None, [SC, THR], [WK_])

        def idx_pair_B(s, qc, pair):
            MT = MTs[qc % 2]
            for i in range(2):
                qb = qc * 4 + pair * 2 + i
                ql = qb - qc * 4
                WK_ = WKs[i]
                for kb in range(qb + 1):
                    pt_ = PB[2 + kb % 2]
                    S.transpose(pt_.ap[:, 0:128], WK_.ap[:, kb * 128:(kb + 1) * 128], C('ident'), [WK_, K_cst], [pt_])
                    S.copy('act', MT.ap[:, kb, ql * 128:(ql + 1) * 128], pt_.ap[:, 0:128], [pt_], [MT])

        def attn_heads(s, qc, heads):
            MT = MTs[qc % 2]
            nkb = 4 * (qc + 1)
            for h in heads:
                p0 = (h % 2) * 64
                po = PB[6 + h % 2]
                for kb in range(nkb):
                    j = max(0, kb - 4 * qc)
                    n = CH - j * 128
                    pS = PB[4 + itc[0] % 2]
                    pt = PT[itc[0] % 3]
                    itc[0] += 1
                    S.mm(pS.ap[:, 0:n], KD2.ap[p0:p0 + 64, kb * 128:(kb + 1) * 128],
                         QC.ap[p0:p0 + 64, h // 2, j * 128:CH], True, True, [KD2, QC], [pS])
                    S.act(pt.ap[:, 0:n], pS.ap[:, 0:n], AF.Exp, [pS], [pt], scale=sc)
                    S.tt('pool', pt.ap[:, 0:n], pt.ap[:, 0:n], MT.ap[:, kb, j * 128:CH], ALU.mult, [pt, MT], [pt])
                    S.mm(po.ap[0:65, j * 128:CH], V1.ap[:, kb, :], pt.ap[:, 0:n], kb == 0, kb == nkb - 1,
                         [V1, pt], [po])
                S.copy('act', OSs[h].ap[0:65], po.ap[0:65], [po], [OSs[h]])

        def attn_norm(s, qc):
            for h in range(8):
                OS = OSs[h]
                pd = PB[2 + h % 2]
                S.mm(pd.ap[0:64], C('ones', 128, 0, 64)[64:65], OS.ap[64:65], True, True, [OS, K_cst], [pd])
                S.recip(RD.ap[0:64], pd.ap[0:64], [pd], [RD])
                yo = YO[h % 2]
                S.tt('dve', yo.ap[0:64], OS.ap[0:64], RD.ap[0:64], ALU.mult, [OS, RD], [yo])
                S.dma('pool', d_yT[2, h * 64:(h + 1) * 64, s * T + qc * CH:s * T + (qc + 1) * CH], yo.ap[0:64],
                      (yo,), (K_yT[2][s],))

        for s in range(NS):
            zk = K_zT[s * 4:(s + 1) * 4]
            ts_ = slice(s * T, (s + 1) * T)
            S.dma('sp', KD2.ap[0:64], d_kd[:, ts_], K_kd[s * 4:(s + 1) * 4], (KD2,))
            S.dma('sp', KD2.ap[64:128], d_kd[:, ts_], K_kd[s * 4:(s + 1) * 4], (KD2,))
            for k in range(2):
                S.dma('sp', QI.ap[:, k, :], d_zT[(ZI_QIDX + k) * 128:(ZI_QIDX + k + 1) * 128, ts_], zk, (QI,))
            for h4 in range(4):
                S.dma('sp', KIB.ap[h4 * 32:(h4 + 1) * 32, h4, :], d_zT[ZI_KIDX * 128:ZI_KIDX * 128 + 32, ts_], zk, (KIB,))
            vwk = K_vw[s * 4:(s + 1) * 4]
            S.dma('sp', V1s.ap[:, :, 0:64], d_vw[ts_, 0:64].rearrange("(b p) n -> p b n", p=128), vwk, (V1s,))
            S.copy('pool', V1.ap, V1s.ap, [V1s], [V1])
            S.dma('sp', WI.ap, d_vw[ts_, 64:72].rearrange("(b p) n -> p b n", p=128), vwk, (WI,))
            for pair in range(2):
                idx_pair(s, 0, pair)
                idx_pair_B(s, 0, pair)
            for qc in range(4):
                S.dma('sp', QC.ap, d_qd[:, s * T + qc * CH:s * T + (qc + 1) * CH].rearrange("(k p) t -> p k t", p=128),
                      K_qd[s * 4:(s + 1) * 4], (QC,))
                nxt = qc + 1 < 4
                if nxt:
                    idx_pair(s, qc + 1, 0)
                attn_heads(s, qc, range(0, 4))
                if nxt:
                    idx_pair_B(s, qc + 1, 0)
                    idx_pair(s, qc + 1, 1)
                attn_heads(s, qc, range(4, 8))
                if nxt:
                    idx_pair_B(s, qc + 1, 1)
                attn_norm(s, qc)
        close_stage(G)

    def stage_s5(l):
        G = open_stage(44000)
        YSUM[0] = Reg(AR.ap[:, AR.n - 2 * T:AR.n - T], "ysum0")
        YSUM[1] = Reg(AR.ap[:, AR.n - T:AR.n], "ysum1")
        LR = Reg(AR.alloc(T), "lr")
        LI = Reg(AR.alloc(T), "li")
        DT = Reg(AR.alloc(T), "dt")
        E1 = Reg(AR.alloc(T), "e1")
        E2 = Reg(AR.alloc(T), "e2")
        E3 = Reg(AR.alloc(T), "e3")
        E4 = Reg(AR.alloc(T), "e4")
        FR = Reg(AR.alloc(T), "fr")
        FI = Reg(AR.alloc(T), "fi")
        BRE = Reg(AR.alloc(T), "bre")
        BIM = Reg(AR.alloc(T), "bim")
        CRE = Reg(AR.alloc(T), "cre")
        CIM = Reg(AR.alloc(T), "cim")
        allk = [LR, LI, DT, E1, E2, E3, E4, FR, FI]
        S.dma('sp', LR.ap, i_s5bc[l, 0], (), (LR,))
        S.dma('sp', LI.ap, i_s5bc[l, 1], (), (LI,))
        S.dma('sp', DT.ap, i_s5bc[l, 2], (), (DT,))
        S.dma('sp', BRE.ap.rearrange("p (a m) -> p a m", a=16), i_s5b[l, 0], (), (BRE,))
        S.dma('sp', BIM.ap.rearrange("p (a m) -> p a m", a=16), i_s5b[l, 1], (), (BIM,))
        S.dma('sp', CRE.ap.rearrange("p (a m) -> p a m", a=16), i_s5c[l, 0], (), (CRE,))
        S.dma('sp', CIM.ap.rearrange("p (a m) -> p a m", a=16), i_s5c[l, 1], (), (CIM,))
        S.dma('sp', P_S5PP.ap.rearrange("p (j a) -> p j a", j=3), i_s5pp[l].rearrange("j p a -> p j a"), (), (P_S5PP,))
        S.dma('sp', P_S5V.ap.rearrange("p (a j) -> p a j", a=4), i_s5v[l], (), (P_S5V,))
        S.act(DT.ap, DT.ap, AF.Exp, allk, allk)
        S.tt('dve', E1.ap, LR.ap, DT.ap, ALU.mult, allk, allk)
        S.act(E1.ap, E1.ap, AF.Exp, allk, allk)
        S.tt('dve', E2.ap, LI.ap, DT.ap, ALU.mult, allk, allk)
        sincos(E2.ap, E3.ap, E4.ap, FR.ap, 128, T, allk, allk)
        S.tt('dve', E3.ap, E3.ap, E1.ap, ALU.mult, allk, allk)
        S.tt('dve', E4.ap, E4.ap, E1.ap, ALU.mult, allk, allk)
        S.ts('dve', E4.ap, E4.ap, -1.0, None, ALU.add, None, allk, allk)
        S.tt('dve', E1.ap, LR.ap, LR.ap, ALU.mult, allk, allk)
        S.tt('dve', E2.ap, LI.ap, LI.ap, ALU.mult, allk, allk)
        S.tt('dve', E1.ap, E1.ap, E2.ap, ALU.add, allk, allk)
        S.recip(E1.ap, E1.ap, allk, allk)
        S.tt('dve', FR.ap, E4.ap, LR.ap, ALU.mult, allk, allk)
        S.tt('dve', E2.ap, E3.ap, LI.ap, ALU.mult, allk, allk)
        S.tt('dve', FR.ap, FR.ap, E2.ap, ALU.add, allk, allk)
        S.tt('dve', FR.ap, FR.ap, E1.ap, ALU.mult, allk, allk)
        S.tt('dve', FI.ap, E3.ap, LR.ap, ALU.mult, allk, allk)
        S.tt('dve', E2.ap, E4.ap, LI.ap, ALU.mult, allk, allk)
        S.tt('dve', FI.ap, FI.ap, E2.ap, ALU.subtract, allk, allk)
        S.tt('dve', FI.ap, FI.ap, E1.ap, ALU.mult, allk, allk)
        bk = allk + [BRE, BIM]
        S.tt('dve', E1.ap, FR.ap, BRE.ap, ALU.mult, bk, bk)
        S.tt('dve', E2.ap, FI.ap, BIM.ap, ALU.mult, bk, bk)
        S.tt('dve', E1.ap, E1.ap, E2.ap, ALU.subtract, bk, bk)
        S.tt('dve', E2.ap, FR.ap, BIM.ap, ALU.mult, bk, bk)
        S.tt('dve', E3.ap, FI.ap, BRE.ap, ALU.mult, bk, bk)
        S.tt('dve', E2.ap, E2.ap, E3.ap, ALU.add, bk, bk)
        S.copy('dve', BRE.ap, E1.ap, bk, bk)
        S.copy('dve', BIM.ap, E2.ap, bk, bk)
        pp = P_S5PP.ap.rearrange("p (j a) -> p j a", j=3)
        sd = P_S5D.ap.rearrange("p (j a) -> p j a", j=4)
        kk = [P_S5PP, P_S5D, P_TMP]
        S.act(sd[:, 0, :], pp[:, 2, :], AF.Exp, kk, kk)
        S.tt('dve', sd[:, 1, :], pp[:, 0, :], sd[:, 0, :], ALU.mult, kk, kk)
        S.act(sd[:, 1, :], sd[:, 1, :], AF.Exp, kk, kk)
        S.tt('dve', sd[:, 2, :], pp[:, 1, :], sd[:, 0, :], ALU.mult, kk, kk)
        iw = IWK.ap[:, 0:16]
        tmp16 = P_TMP.ap[:, 0:16]
        S.ts('dve', iw, sd[:, 2, :], 1.0 / TWO_PI, None, ALU.mult, None, kk, [IWK])
        S.stt(tmp16, iw, -CW1, sd[:, 2, :], ALU.mult, ALU.add, kk + [IWK], kk)
        S.stt(sd[:, 2, :], iw, -CW2, tmp16, ALU.mult, ALU.add, kk + [IWK], kk)
        S.barrier()
        AR.off = 13 * T
        COS = Reg(AR.ap[:, 0:T], "cos")
        SIN = Reg(AR.ap[:, T:2 * T], "sin")
        ANG = Reg(AR.ap[:, 2 * T:3 * T], "ang")
        TMP = Reg(AR.ap[:, 3 * T:4 * T], "tmp")
        BUR = Reg(AR.ap[:, 4 * T:5 * T], "bur")
        BUI = Reg(AR.ap[:, 5 * T:6 * T], "bui")
        WR = Reg(AR.ap[:, 6 * T:7 * T], "wr")
        WI_ = Reg(AR.ap[:, 7 * T:8 * T], "wi")
        T2 = Reg(AR.ap[:, 8 * T:9 * T], "t2")
        U5 = [Reg(AR.alloc(T), "u5_%d" % i) for i in range(NS)]
        XRE = Reg(AR.alloc(T), "xre")
        XIM = Reg(AR.alloc(T), "xim")
        YS = Reg(AR.alloc(T), "ys")
        b3r = BRE.ap.rearrange("p (a m) -> p a m", a=16)
        b3i = BIM.ap.rearrange("p (a m) -> p a m", a=16)
        c3r = CRE.ap.rearrange("p (a m) -> p a m", a=16)
        c3i = CIM.ap.rearrange("p (a m) -> p a m", a=16)
        sv = P_S5V.ap.rearrange("p (a j) -> p a j", a=4)
        YP = [PB[4], PB[5], PB[6], PB[7]]
        for ot in range(4):
            for s in range(NS):
                S.dma('sp', U5[s].ap, d_zT[(ZI_US5 + ot) * 128:(ZI_US5 + ot + 1) * 128, s * T:(s + 1) * T],
                      K_zT[s * 4:(s + 1) * 4], (U5[s],))
            for sti in range(4):
                st = ot * 4 + sti
                S.ts('dve', ANG.ap, C('iota'), sd[:, 2, st:st + 1], None, ALU.mult, None, [K_cst, P_S5D], [ANG])
                sincos(ANG.ap, SIN.ap, COS.ap, TMP.ap, 128, T, [ANG, TMP], [ANG, TMP, SIN, COS])
                for s in range(NS):
                    for q in range(4):
                        sl = slice(q * CH, (q + 1) * CH)
                        pr, pi = PB[q % 2], PB[2 + q % 2]
                        S.mm(pr.ap, b3r[:, st, :], U5[s].ap[:, sl], True, True, [BRE, U5[s]], [pr])
                        S.mm(pi.ap, b3i[:, st, :], U5[s].ap[:, sl], True, True, [BIM, U5[s]], [pi])
                        S.copy('act', BUR.ap[:, sl], pr.ap, [pr], [BUR])
                        S.copy('act', BUI.ap[:, sl], pi.ap, [pi], [BUI])
                    S.tt('dve', WR.ap, BUR.ap, COS.ap, ALU.mult, [BUR, COS], [WR])
                    S.tt('pool', T2.ap, BUI.ap, SIN.ap, ALU.mult, [BUI, SIN], [T2])
                    S.tt('dve', WR.ap, WR.ap, T2.ap, ALU.add, [WR, T2], [WR])
                    S.tt('pool', WI_.ap, BUI.ap, COS.ap, ALU.mult, [BUI, COS], [WI_])
                    S.tt('dve', T2.ap, BUR.ap, SIN.ap, ALU.mult, [BUR, SIN], [T2])
                    S.tt('dve', WI_.ap, WI_.ap, T2.ap, ALU.subtract, [WI_, T2], [WI_])
                    rho = sd[:, 1, st:st + 1].to_broadcast([128, T])
                    S.op('dve', lambda g, o=BUR.ap, a=rho, b=WR.ap: g.tensor_tensor_scan(o, a, b, 0.0, ALU.mult, ALU.add),
                         [WR, P_S5D], [BUR])
                    S.op('dve', lambda g, o=BUI.ap, a=rho, b=WI_.ap: g.tensor_tensor_scan(o, a, b, 0.0, ALU.mult, ALU.add),
                         [WI_, P_S5D], [BUI])
                    S.tt('dve', XRE.ap, BUR.ap, COS.ap, ALU.mult, [BUR, COS], [XRE])
                    S.tt('pool', T2.ap, BUI.ap, SIN.ap, ALU.mult, [BUI, SIN], [T2])
                    S.tt('dve', XRE.ap, XRE.ap, T2.ap, ALU.subtract, [XRE, T2], [XRE])
                    S.tt('pool', XIM.ap, BUI.ap, COS.ap, ALU.mult, [BUI, COS], [XIM])
                    S.tt('dve', T2.ap, BUR.ap, SIN.ap, ALU.mult, [BUR, SIN], [T2])
                    S.stt(XIM.ap, T2.ap, -1.0, XIM.ap, ALU.mult, ALU.subtract, [T2, XIM], [XIM])
                    for q in range(4):
                        sl = slice(q * CH, (q + 1) * CH)
                        yp = YP[q]
                        S.mm(yp.ap, c3r[:, st, :], XRE.ap[:, sl], True, False, [CRE, XRE], [yp])
                        S.mm(yp.ap, c3i[:, st, :], XIM.ap[:, sl], False, True, [CIM, XIM], [yp])
                        ysum = YSUM[s]
                        if sti == 0:
                            S.stt(ysum.ap[:, sl], U5[s].ap[:, sl], sv[:, ot, 0:1], yp.ap, ALU.mult, ALU.add,
                                  [U5[s], P_S5V, yp], [ysum])
                        else:
                            S.tt('dve', ysum.ap[:, sl], ysum.ap[:, sl], yp.ap, ALU.add, [ysum, yp], [ysum])
            for s in range(NS):
                gelu_tanh(YSUM[s].ap, YS.ap, T2.ap, [YSUM[s], T2, YS], [T2, YS])
                S.dma('pool', d_yg[ot * 128:(ot + 1) * 128, s * T:(s + 1) * T], YS.ap, (YS,),
                      K_yg[s * 4:(s + 1) * 4])
        close_stage(G)
        G = open_stage(6000)
        YG = Reg(AR.alloc(4 * CH).rearrange("p (k t) -> p k t", k=4), "yg")
        WG = Reg(AR.alloc(4 * 512).rearrange("p (k n) -> p k n", k=4), "wglu")
        SG = [Reg(AR.alloc(CH), "sg%d" % i) for i in range(2)]
        S.dma('sp', WG.ap, i_wglu[l], (), (WG,))
        for c in range(dbg.get('nch', NCH)):
            s = c // (NCH // NS)
            tk = slice(c * CH, (c + 1) * CH)
            S.dma('sp', YG.ap, d_yg[:, tk].rearrange("(k p) t -> p k t", p=128), (K_yg[c],), (YG,))
            for m in range(4):
                pg = PB[m % 2]
                for k in range(4):
                    S.mm(pg.ap, WG.ap[:, k, m * 128:(m + 1) * 128], YG.ap[:, k, :], k == 0, k == 3, [WG, YG], [pg])
                sg = SG[m % 2]
                S.act(sg.ap, pg.ap, AF.Sigmoid, [pg, P_S5V], [sg], bias=sv[:, m, 1:2], scale=1.0)
                S.tt('dve', sg.ap, sg.ap, YG.ap[:, m, :], ALU.mult, [sg, YG], [sg])
                S.dma('pool', d_yT[3, m * 128:(m + 1) * 128, tk], sg.ap, (sg,), (K_yT[3][s],))
        close_stage(G)

    YSUM = [None, None]

    def stage_s5_wrap(l):
        stage_s5(l)

    def stage_merge(l):
        G = open_stage(24200, 22000)
        Xb, X = chunk_regs("X")
        U2sb, U2s = chunk_regs("U2s")
        U2b, U2 = chunk_regs("U2", RA)
        MGb, MG = chunk_regs("MG", RA)
        YBs = [Reg(AR.alloc(4 * CH).rearrange("p (k t) -> p k t", k=4), "ybs%d" % b) for b in range(2)]
        YB = [Reg(RA.alloc(4 * CH).rearrange("p (k t) -> p k t", k=4), "yb%d" % b) for b in range(4)]
        WGs = [Reg(AR.alloc(8 * 128).rearrange("p (k n) -> p k n", k=8), "wgs%d" % i) for i in range(4)]
        WBs = [Reg(AR.alloc(4 * 128).rearrange("p (k n) -> p k n", k=4), "wbs%d" % i) for i in range(4)]
        WOs = [Reg(AR.alloc(8 * 128).rearrange("p (k n) -> p k n", k=8), "wos%d" % i) for i in range(2)]
        WGt = [Reg(RA.alloc(8 * 128).rearrange("p (k n) -> p k n", k=8), "wgt%d" % i) for i in range(2)]
        WBr = [Reg(RA.alloc(4 * 128).rearrange("p (k n) -> p k n", k=4), "wbr%d" % i) for i in range(2)]
        WO = [Reg(RA.alloc(8 * 128).rearrange("p (k n) -> p k n", k=8), "wo%d" % i) for i in range(2)]
        SG = [Reg(AR.alloc(CH), "sg%d" % i) for i in range(2)]
        TM = [Reg(AR.alloc(CH), "tm%d" % i) for i in range(2)]
        it = 0
        for c in range(dbg.get('nch', NCH)):
            s = c // (NCH // NS)
            tk = slice(c * CH, (c + 1) * CH)
            S.dma('sp', Xb.rearrange("p (k t) -> p k t", k=8), dram_chunk(d_xT, c), (K_xT[c],), X)
            S.dma('sp', U2sb.rearrange("p (k t) -> p k t", k=8), dram_chunk(d_u2T, c), (K_u2T[c],), U2s)
            for k in range(8):
                S.copy('dve' if k % 2 == 0 else 'act', U2[k].ap, U2s[k].ap, [U2s[k]], [U2[k]])
            for b in range(4):
                ybs = YBs[b % 2]
                S.dma('sp', ybs.ap, d_yT[b, :, tk].rearrange("(k p) t -> p k t", p=128), (K_yT[b][s],), (ybs,))
                S.copy('dve' if b % 2 == 0 else 'act', YB[b].ap, ybs.ap, [ybs], [YB[b]])
            for m in range(8):
                for b in range(4):
                    wgs, wbs, wg, wb = WGs[it % 4], WBs[it % 4], WGt[it % 2], WBr[it % 2]
                    S.dma('sp', wgs.ap, i_wgate[l, b, m], (), (wgs,))
                    S.dma('sp', wbs.ap, i_wbr[l, b, m], (), (wbs,))
                    S.copy('act', wg.ap, wgs.ap, [wgs], [wg])
                    S.copy('dve', wb.ap, wbs.ap, [wbs], [wb])
                    pg, pb = PB[it % 2], PB[2 + it % 2]
                    for k in range(8):
                        S.mm(pg.ap, wg.ap[:, k, :], U2[k].ap, k == 0, k == 7, [wg, U2[k]], [pg])
                    for k in range(4):
                        S.mm(pb.ap, wb.ap[:, k, :], YB[b].ap[:, k, :], k == 0, k == 3, [wb, YB[b]], [pb])
                    sg = SG[it % 2]
                    S.act(sg.ap, pg.ap, AF.Sigmoid, [pg], [sg])
                    if b == 0:
                        S.tt('dve', MG[m].ap, sg.ap, pb.ap, ALU.mult, [sg, pb], [MG[m]])
                    else:
                        tm = TM[it % 2]
                        S.tt('dve', tm.ap, sg.ap, pb.ap, ALU.mult, [sg, pb], [tm])
                        S.tt('dve', MG[m].ap, MG[m].ap, tm.ap, ALU.add, [MG[m], tm], [MG[m]])
                    it += 1
            for m in range(8):
                wos, wo = WOs[m % 2], WO[m % 2]
                S.dma('sp', wos.ap, i_wout[l, m], (), (wos,))
                S.copy('dve' if m % 2 == 0 else 'act', wo.ap, wos.ap, [wos], [wo])
                po = PB[4 + m % 2]
                for k in range(8):
                    S.mm(po.ap, wo.ap[:, k, :], MG[k].ap, k == 0, k == 7, [wo, MG[k]], [po])
                S.stt(X[m].ap, po.ap, G3[:, 1, m, s:s + 1], X[m].ap, ALU.mult, ALU.add, [po, P_G, X[m]], [X[m]])
            S.dma('pool', dram_chunk(d_xT, c), Xb.rearrange("p (k t) -> p k t", k=8), X, (K_xT[c],))
        close_stage(G)

    stages = dbg.get('stages')
    for l in range(n_layers):
        def want(n):
            return stages is None or n in stages
        if want('mod'):
            stage_mod(l)
        if want('ffn0'):
            stage_ffn(l, 0)
        if want('win'):
            stage_win(l)
        if want('rglru'):
            stage_rglru(l)
        if want('mla'):
            stage_mla(l)
        if want('dsa'):
            stage_dsa(l)
        if want('s5'):
            stage_s5_wrap(l)
        if want('merge'):
            stage_merge(l)
        if want('ffn1'):
            stage_ffn(l, 1)

    GE = open_stage(4200)
    Xb, X = chunk_regs("X")
    for c in range(dbg.get('nch', NCH)):
        S.dma('sp', Xb.rearrange("p (k t) -> p k t", k=8), dram_chunk(d_xT, c), (K_xT[c],), X)
        S.dma('pool', dram_chunk(o_yT, c), Xb.rearrange("p (k t) -> p k t", k=8), X, ())
    S.emit()
    for g in reversed(GE):
        g.__exit__(None, None, None)
    stack.close()
    return nc, S


def _consts():
    c = np.zeros((128, NCONST), np.float32)

    def put(name, a):
        o, w = CO[name]
        c[:a.shape[0], o:o + a.shape[1]] = a
    put('ident', np.eye(128, dtype=np.float32))
    put('ones', np.ones((128, 128), np.float32))
    bd = np.zeros((128, 128), np.float32)
    bd[:64, :64] = 1
    bd[64:, 64:] = 1
    put('bd64', bd)
    r96 = np.zeros((128, 128), np.float32)
    for i in range(16):
        r96[80 + i, 64 + i] = -1.0
        r96[64 + i, 80 + i] = 1.0
    put('r96', r96)
    r64 = np.zeros((128, 128), np.float32)
    for hb in range(2):
        for i in range(8):
            r64[hb * 64 + 8 + i, hb * 64 + i] = -1.0
            r64[hb * 64 + i, hb * 64 + 8 + i] = 1.0
    put('r64', r64)
    r32 = np.zeros((128, 128), np.float32)
    for hb in range(4):
        for i in range(4):
            r32[hb * 32 + 4 + i, hb * 32 + i] = -1.0
            r32[hb * 32 + i, hb * 32 + 4 + i] = 1.0
    put('r32', r32)
    p = np.arange(128)[:, None]
    f = np.arange(128)[None, :]
    put('tri_kq', (p <= f).astype(np.float32))
    tq = (f <= p).astype(np.float32)
    put('tri_qk', tq)
    put('neg_qk', np.where(f <= p, np.float32(0), np.float32(-BIG)).astype(np.float32))

    def inv(rot):
        return (np.float32(500000.0) ** (-(np.arange(0, rot, 2, dtype=np.float32)) / np.float32(rot))).astype(np.float32)
    im = np.zeros((128, 1), np.float32)
    im[64:80, 0] = inv(32)
    im[80:96, 0] = inv(32)
    put('inv_mla', im)
    idd = np.zeros((128, 1), np.float32)
    for hb in range(2):
        idd[hb * 64:hb * 64 + 8, 0] = inv(16)
        idd[hb * 64 + 8:hb * 64 + 16, 0] = inv(16)
    put('inv_dsa', idd)
    ii = np.zeros((128, 1), np.float32)
    for hb in range(4):
        ii[hb * 32:hb * 32 + 4, 0] = inv(8)
        ii[hb * 32 + 4:hb * 32 + 8, 0] = inv(8)
    put('inv_idx', ii)
    put('iota', np.broadcast_to(np.arange(T, dtype=np.float32)[None, :], (128, T)))
    return c


def _layout_weights(I):
    f = np.float32
    A = lambda a: np.ascontiguousarray(np.asarray(a, dtype=f))
    W = {}
    W['consts'] = _consts()
    W['ada_w'] = A(np.asarray(I['ada_w']).reshape(L_, 8, 128, 18, 512).transpose(0, 3, 2, 1, 4))
    W['ada_b'] = A(np.repeat(np.asarray(I['ada_b']).reshape(L_, 72, 128).transpose(0, 2, 1)[..., None], NS, axis=-1))
    W['norm_g'] = A(np.repeat(np.asarray(I['norm_g']).reshape(L_, 3, 8, 128).transpose(0, 1, 3, 2)[..., None], NS, axis=-1))
    W['w1'] = A(np.asarray(I['ffn_w1']).reshape(L_, 2, 8, 128, 22, 128).transpose(0, 1, 4, 3, 2, 5))
    W['w3'] = A(np.asarray(I['ffn_w3']).reshape(L_, 2, 8, 128, 22, 128).transpose(0, 1, 4, 3, 2, 5))
    W['w2'] = A(np.asarray(I['ffn_w2']).reshape(L_, 2, 22, 128, 8, 128).transpose(0, 1, 4, 3, 2, 5))
    win = np.asarray(I['w_in'])
    wz = np.zeros((L_, NZ, 128, 8, 128), f)
    for zi, (c0, wd) in enumerate(Z_TILES):
        wz[:, zi, :, :, :wd] = win[:, :, c0:c0 + wd].reshape(L_, 8, 128, wd).transpose(0, 2, 1, 3)
    W['win_z'] = wz
    wt = np.concatenate([win[:, :, 2016:2080], win[:, :, 2368:2376]], axis=-1)
    W['win_tok'] = A(wt.reshape(L_, 8, 128, 72).transpose(0, 2, 1, 3))
    W['win_gate'] = A(win[:, :, 2888:].reshape(L_, 8, 128, 4, 8, 128).transpose(0, 3, 4, 2, 1, 5))
    rgp = np.zeros((L_, 128, 4, 8), f)
    cw = np.asarray(I['conv_w'])
    for j in range(4):
        rgp[:, :, :, j] = cw[:, j].reshape(L_, 4, 128).transpose(0, 2, 1)
    for j, n in enumerate(('conv_b', 'rg_ba', 'rg_bx', 'rg_lambda')):
        rgp[:, :, :, 4 + j] = np.asarray(I[n]).reshape(L_, 4, 128).transpose(0, 2, 1)
    W['rg_par'] = rgp
    rgw = np.zeros((L_, 2, 4, 128, 128), f)
    for wi_, n in enumerate(('rg_wa', 'rg_wx')):
        w = np.asarray(I[n])
        for h in range(8):
            ct, o = h // 2, (h % 2) * 64
            rgw[:, wi_, ct, o:o + 64, o:o + 64] = w[:, h]
    W['rg_w'] = rgw
    mn = np.zeros((L_, 128, 3), f)
    mn[:, :, 0:2] = np.asarray(I['mla_q_norm']).reshape(L_, 2, 128).transpose(0, 2, 1)
    mn[:, :, 2] = np.asarray(I['mla_kv_norm'])
    W['mla_norm'] = mn
    perm = np.concatenate([np.arange(32, 96), np.arange(0, 32)])
    wuq = np.asarray(I['mla_w_uq']).reshape(L_, 2, 128, 8, 96)[..., perm]
    W['w_uq'] = A(wuq.transpose(0, 2, 1, 3, 4))
    wukv = np.asarray(I['mla_w_ukv']).reshape(L_, 128, 8, 128)
    W['w_ukv_k'] = A(wukv[..., :64])
    W['w_ukv_v'] = A(wukv[..., 64:].reshape(L_, 128, 512))
    g = np.zeros((L_, 128, 4), f)
    mg = np.asarray(I['mla_qk_gain'])[..., perm]
    g[:, :96, 0] = mg[:, 0]
    g[:, :96, 1] = mg[:, 1]
    dg = np.asarray(I['dsa_qk_gain'])
    g[:, :, 2] = np.tile(dg[:, 0], (1, 2))
    g[:, :, 3] = np.tile(dg[:, 1], (1, 2))
    W['qk_gains'] = g
    lr = np.asarray(I['s5_lambda_re']).reshape(L_, 2048)
    li = np.asarray(I['s5_lambda_im']).reshape(L_, 2048)
    ld = np.repeat(np.asarray(I['s5_log_dt']), 64, axis=1)
    st3 = np.stack([lr, li, ld], axis=1)
    W['s5_bc'] = A(np.broadcast_to(st3[:, :, None, :], (L_, 3, 128, 2048)))
    W['s5_pp'] = A(st3.reshape(L_, 3, 16, 128).transpose(0, 1, 3, 2))
    sb = np.zeros((L_, 2, 128, 16, 128), f)
    scm = np.zeros((L_, 2, 128, 16, 128), f)
    for ri, (bn, cn) in enumerate((('s5_b_re', 's5_c_re'), ('s5_b_im', 's5_c_im'))):
        b = np.asarray(I[bn])
        cc = np.asarray(I[cn])
        for gi in range(32):
            st, half = gi // 2, gi % 2
            r0 = 16 * (gi % 8)
            sb[:, ri, r0:r0 + 16, st, half * 64:(half + 1) * 64] = b[:, gi].transpose(0, 2, 1)
            scm[:, ri, half * 64:(half + 1) * 64, st, r0:r0 + 16] = cc[:, gi].transpose(0, 2, 1)
    W['s5_b'] = sb
    W['s5_c'] = scm
    sv = np.zeros((L_, 128, 4, 2), f)
    sv[..., 0] = np.asarray(I['s5_d']).reshape(L_, 4, 128).transpose(0, 2, 1)
    sv[..., 1] = np.asarray(I['s5_b_glu']).reshape(L_, 4, 128).transpose(0, 2, 1)
    W['s5_vec'] = sv
    W['w_glu'] = A(np.asarray(I['s5_w_glu']).reshape(L_, 4, 128, 512).transpose(0, 2, 1, 3))
    W['w_branch'] = A(np.asarray(I['w_branch']).reshape(L_, 4, 4, 128, 8, 128).transpose(0, 1, 4, 3, 2, 5))
    W['w_out'] = A(np.asarray(I['w_out']).reshape(L_, 8, 128, 8, 128).transpose(0, 3, 2, 1, 4))
    return W


def _core_inputs(I, W, c):
    x = np.asarray(I['x'], dtype=np.float32)[NS * c:NS * (c + 1)]
    m = dict(W)
    m['xT'] = np.ascontiguousarray(x.reshape(NT, D).T)
    cc = np.asarray(I['c'], dtype=np.float32)[NS * c:NS * (c + 1)]
    m['cT'] = np.ascontiguousarray(cc.reshape(NS, 8, 128).transpose(2, 1, 0))
    pos = np.asarray(I['positions']).astype(np.int32)[NS * c:NS * (c + 1)]
    m['posb'] = np.ascontiguousarray(np.broadcast_to(pos[None], (128, NS, T)))
    return m


_CACHE = {}


def kernel(**inputs):
    if 'nc' not in _CACHE:
        _CACHE['nc'] = build_program()[0]
    nc = _CACHE['nc']
    W = _layout_weights(inputs)
    in_maps = [_core_inputs(inputs, W, c) for c in range(8)]
    res = run_bass_kernel_spmd(nc, in_maps, core_ids=list(range(8)))
    out = np.empty((16, T, D), np.float32)
    for c in range(8):
        yT = np.asarray(res.results[c]["yT_out"])
        out[NS * c:NS * (c + 1)] = yT.T.reshape(NS, T, D)
    return out
```

```python
import numpy as np
from contextlib import ExitStack
import concourse.bass as bass
import concourse.mybir as mybir
from concourse.bass_utils import run_bass_kernel_spmd

F32 = mybir.dt.float32
I32 = mybir.dt.int32
F32R = mybir.dt.float32r


def AF32(ap):
    return ap.bitcast(F32)
ALU = mybir.AluOpType
AF = mybir.ActivationFunctionType

L_ = 4
D = 1024
T = 2048
NS = 2
NT = NS * T
CH = 512
NCH = NT // CH
DFF = 2816
EPS = 1e-6
BIG = 1.0e30
PI = float(np.pi)
TWO_PI = float(2 * np.pi)
CW1 = 6.28125
CW2 = float(2 * np.pi - 6.28125)
PI_SAFE = 3.1415925

Z_TILES = ([(0 + 128 * i, 128) for i in range(4)] + [(512 + 128 * i, 128) for i in range(4)]
           + [(1024, 128), (1152, 128), (1280, 128), (1408, 32)]
           + [(1440 + 128 * i, 128) for i in range(4)] + [(1952, 64)]
           + [(2080, 128), (2208, 128), (2336, 32)] + [(2376 + 128 * i, 128) for i in range(4)])
ZI_XRNN, ZI_GATE, ZI_QLAT, ZI_KVLAT, ZI_KPE, ZI_QDSA, ZI_KDSA, ZI_QIDX, ZI_KIDX, ZI_US5 = 0, 4, 8, 10, 11, 12, 16, 17, 19, 20
NZ = len(Z_TILES)

CO = {}
_off = 0
for _n, _w in [('ident', 128), ('ones', 128), ('bd64', 128), ('r96', 128), ('r64', 128), ('r32', 128),
               ('tri_kq', 128), ('tri_qk', 128), ('neg_qk', 128), ('inv_mla', 1), ('inv_dsa', 1),
               ('inv_idx', 1), ('iota', T)]:
    CO[_n] = (_off, _w)
    _off += _w
NCONST = _off


class Reg:
    __slots__ = ('ap', 'name')

    def __init__(self, ap, name=''):
        self.ap = ap
        self.name = name


class Sched:
    ENG = ('pe', 'act', 'dve', 'pool', 'sp')
    EPOCH = 16000
    NSLOT = 12

    def __init__(self, nc, stack):
        self.nc = nc
        self.stack = stack
        self.streams = {e: [] for e in self.ENG}
        self.count = {e: 0 for e in self.ENG}
        self.sems = {e: [] for e in self.ENG}
        self.waited = {e: {} for e in self.ENG}
        self.last_w = {}
        self.readers = {}
        self.semobj = {}
        self.slots = {q: [[self._newsem(), 0] for _ in range(self.NSLOT)] for q in ('sp', 'pool')}
        self.slot_rr = {'sp': 0, 'pool': 0}
        self.all_tokens = []
        self.ninstr = 0

    def _newsem(self):
        s = self.stack.enter_context(self.nc.semaphore())
        sid = len(self.semobj)
        self.semobj[sid] = s
        return sid

    def _esem(self, e, epoch):
        while len(self.sems[e]) <= epoch:
            self.sems[e].append(self._newsem())
        return self.sems[e][epoch]

    def _deps(self, e, reads, writes):
        waits = {}

        def need(tok):
            if tok is None:
                return
            sid, val, prod = tok
            if prod == e and e == 'pe':
                return
            if self.waited[e].get(sid, 0) >= val:
                return
            if waits.get(sid, 0) < val:
                waits[sid] = val

        for k in reads:
            need(self.last_w.get(id(k)))
        for k in writes:
            need(self.last_w.get(id(k)))
            for r in self.readers.get(id(k), ()):
                need(r)
        return waits

    def _commit(self, e, tok, reads, writes, waits):
        for sid, v in waits.items():
            self.waited[e][sid] = v
        for k in reads:
            self.readers.setdefault(id(k), []).append(tok)
        for k in writes:
            self.last_w[id(k)] = tok
            self.readers[id(k)] = []

    def op(self, e, fn, reads=(), writes=()):
        waits = self._deps(e, reads, writes)
        idx = self.count[e]
        sid = self._esem(e, idx // self.EPOCH)
        tok = (sid, idx % self.EPOCH + 1, e)
        self.count[e] += 1
        self._commit(e, tok, reads, writes, waits)
        self.streams[e].append((list(waits.items()), fn, sid, 1))
        self.ninstr += 1

    def dma(self, q, out_ap, in_ap, reads=(), writes=()):
        waits = self._deps(q, reads, writes)
        si = self.slot_rr[q]
        self.slot_rr[q] = (si + 1) % self.NSLOT
        slot = self.slots[q][si]
        if slot[1] + 16 > 30000:
            slot[0] = self._newsem()
            slot[1] = 0
        if slot[1] > 0 and self.waited[q].get(slot[0], 0) < slot[1]:
            waits[slot[0]] = max(waits.get(slot[0], 0), slot[1])
        slot[1] += 16
        tok = (slot[0], slot[1], 'dma_' + q)
        self._commit(q, tok, reads, writes, waits)

        def fn(eng, o=out_ap, i=in_ap):
            return eng.dma_start(out=o, in_=i)
        self.streams[q].append((list(waits.items()), fn, slot[0], 16))
        self.all_tokens.append(tok)
        self.ninstr += 1

    def barrier(self):
        toks = []
        for e in self.ENG:
            idx = self.count[e]
            if idx > 0:
                toks.append((self._esem(e, (idx - 1) // self.EPOCH), (idx - 1) % self.EPOCH + 1, e))
        for q in ('sp', 'pool'):
            for slot in self.slots[q]:
                if slot[1] > 0:
                    toks.append((slot[0], slot[1], 'dma_' + q))
        for e in self.ENG:
            waits = {}
            for sid, val, prod in toks:
                if prod == e:
                    continue
                if self.waited[e].get(sid, 0) >= val:
                    continue
                waits[sid] = max(waits.get(sid, 0), val)
            for sid, v in waits.items():
                self.waited[e][sid] = v
            if waits:
                self.streams[e].append((list(waits.items()), None, None, 0))
        self.last_w.clear()
        self.readers.clear()

    def emit(self):
        nc = self.nc
        self.barrier()
        semobj = self.semobj
        streams = self.streams

        def replay(e, eng):
            for waits, fn, sid, inc in streams[e]:
                for s, v in waits:
                    eng.wait_ge(semobj[s], v)
                if fn is not None:
                    ins = fn(eng)
                    ins.then_inc(semobj[sid], inc)

        with nc.Block() as block:
            @block.tensor
            def _(eng):
                replay('pe', eng)

            @block.scalar
            def _(eng):
                replay('act', eng)

            @block.vector
            def _(eng):
                replay('dve', eng)

            @block.gpsimd
            def _(eng):
                replay('pool', eng)

            @block.sync
            def _(eng):
                replay('sp', eng)

    def mm(self, out, lhsT, rhs, start, stop, reads, writes):
        self.op('pe', lambda g, o=out, l=lhsT, r=rhs, a=start, b=stop: g.matmul(o, l, r, start=a, stop=b),
                reads, writes)

    def transpose(self, out, in_, ident, reads, writes):
        self.op('pe', lambda g, o=out, i=in_, d=ident: g.transpose(o, i, d), reads, writes)

    def act(self, out, in_, func, reads, writes, bias=None, scale=None):
        kw = {}
        if bias is not None:
            kw['bias'] = bias
        if scale is not None:
            kw['scale'] = scale
        self.op('act', lambda g, o=out, i=in_, f=func, k=kw: g.activation(o, i, f, **k), reads, writes)

    def tt(self, e, out, in0, in1, op, reads, writes):
        self.op(e, lambda g, o=out, a=in0, b=in1, p=op: g.tensor_tensor(o, a, b, p), reads, writes)

    def ts(self, e, out, in0, s1, s2, op0, op1, reads, writes):
        if s2 is None:
            self.op(e, lambda g, o=out, a=in0, x=s1, p=op0: g.tensor_scalar(o, a, x, None, p), reads, writes)
        else:
            self.op(e, lambda g, o=out, a=in0, x=s1, y=s2, p=op0, q=op1: g.tensor_scalar(o, a, x, y, p, q),
                    reads, writes)

    def stt(self, out, in0, scalar, in1, op0, op1, reads, writes):
        self.op('dve', lambda g, o=out, a=in0, s=scalar, b=in1, p=op0, q=op1:
                g.scalar_tensor_tensor(o, a, s, b, p, q), reads, writes)

    def copy(self, e, out, in_, reads, writes):
        if e == 'act':
            self.op('act', lambda g, o=out, i=in_: g.copy(o, i), reads, writes)
        else:
            self.op(e, lambda g, o=out, i=in_: g.tensor_copy(o, i), reads, writes)

    def recip(self, out, in_, reads, writes):
        self.op('dve', lambda g, o=out, i=in_: g.reciprocal(o, i), reads, writes)

    def memset(self, e, ap, val, writes):
        self.op(e, lambda g, a=ap, v=val: g.memset(a, v), (), writes)


class Arena:
    def __init__(self, ap, nwords):
        self.ap = ap
        self.n = nwords
        self.off = 0

    def reset(self):
        self.off = 0

    def alloc(self, words, name=''):
        assert self.off + words <= self.n, (name, self.off, words, self.n)
        a = self.ap[:, self.off:self.off + words]
        self.off += words
        return a


def build_program(n_layers=L_, debug=None):
    nc = bass.Bass("TRN2", target_bir_lowering=False)
    stack = ExitStack()

    def din(name, shape, dt=F32):
        return nc.dram_tensor(name, list(shape), dt, kind="ExternalInput").ap()

    dbg = debug or {}
    FB_LIM = dbg.get('fblim', 11)
    M_LIM = dbg.get('mlim', 8)

    def dscr(name, shape, dt=F32):
        kind = "ExternalOutput" if dbg.get(name) else "Internal"
        return nc.dram_tensor(name, list(shape), dt, kind=kind).ap()

    i_xT = din("xT", [D, NT])
    i_cT = din("cT", [128, 8, NS])
    i_pos = din("posb", [128, NS, T], I32)
    i_const = din("consts", [128, NCONST])
    i_adaw = din("ada_w", [L_, 18, 128, 8, 512])
    i_adab = din("ada_b", [L_, 128, 72, NS])
    i_normg = din("norm_g", [L_, 3, 128, 8, NS])
    i_w1 = din("w1", [L_, 2, 22, 128, 8, 128])
    i_w3 = din("w3", [L_, 2, 22, 128, 8, 128])
    i_w2 = din("w2", [L_, 2, 8, 128, 22, 128])
    i_winz = din("win_z", [L_, NZ, 128, 8, 128])
    i_wtok = din("win_tok", [L_, 128, 8, 72])
    i_wgate = din("win_gate", [L_, 4, 8, 128, 8, 128])
    i_rgp = din("rg_par", [L_, 128, 4, 8])
    i_rgw = din("rg_w", [L_, 2, 4, 128, 128])
    i_mlan = din("mla_norm", [L_, 128, 3])
    i_wuq = din("w_uq", [L_, 128, 2, 8, 96])
    i_wukvk = din("w_ukv_k", [L_, 128, 8, 64])
    i_wukvv = din("w_ukv_v", [L_, 128, 512])
    i_gains = din("qk_gains", [L_, 128, 4])
    i_s5bc = din("s5_bc", [L_, 3, 128, 2048])
    i_s5pp = din("s5_pp", [L_, 3, 128, 16])
    i_s5b = din("s5_b", [L_, 2, 128, 16, 128])
    i_s5c = din("s5_c", [L_, 2, 128, 16, 128])
    i_s5v = din("s5_vec", [L_, 128, 4, 2])
    i_wglu = din("w_glu", [L_, 128, 4, 512])
    i_wbr = din("w_branch", [L_, 4, 8, 128, 4, 128])
    i_wout = din("w_out", [L_, 8, 128, 8, 128])
    o_yT = nc.dram_tensor("yT_out", [D, NT], F32, kind="ExternalOutput").ap()

    d_xT = dscr("s_xT", [D, NT])
    d_u2T = dscr("s_u2T", [D, NT])
    d_zT = dscr("s_zT", [NZ * 128, NT])
    d_vw = dscr("s_vw", [NT, 72])
    d_yT = dscr("s_yT", [4, 512, NT])
    d_qm = dscr("s_qm", [8, 96, NT])
    d_km = dscr("s_km", [8, 96, NT])
    d_vm = dscr("s_vm", [NT, 512])
    d_qd = dscr("s_qd", [512, NT])
    d_kd = dscr("s_kd", [64, NT])
    d_yg = dscr("s_yg", [512, NT])
    d_tab = dscr("s_tab", [3, 2, 128, NT])

    def keys(n):
        return [Reg(None, n + str(i)) for i in range(NCH)]
    K_xT, K_u2T, K_zT, K_vw, K_qm, K_km, K_vm, K_qd, K_kd, K_yg = (keys(n) for n in
                                                                  ("xT", "u2T", "zT", "vw", "qm", "km", "vm", "qd", "kd", "yg"))
    K_yT = [[Reg(None, "yT%d_%d" % (b, s)) for s in range(NS)] for b in range(4)]
    K_tab = Reg(None, "tab")

    cst_t = stack.enter_context(nc.sbuf_tensor("cst", [128, NCONST], F32))
    par_t = stack.enter_context(nc.sbuf_tensor("par", [128, 1200], F32))
    iwk_t = stack.enter_context(nc.sbuf_tensor("iwk", [128, T], I32))
    banks = [stack.enter_context(nc.psum_tensor("pb%d" % i, [128, 512], F32)) for i in range(8)]
    PB = [Reg(b[:], "pb%d" % i) for i, b in enumerate(banks)]
    AR = Arena(None, 0)
    RA = Arena(None, 0)
    _uid = [0]

    def open_stage(f32_words, r_words=0):
        _uid[0] += 1
        g1 = nc.sbuf_tensor("fa%d" % _uid[0], [128, f32_words], F32)
        t1 = g1.__enter__()
        AR.ap, AR.n, AR.off = t1[:], f32_words, 0
        guards = [g1]
        if r_words:
            g2 = nc.sbuf_tensor("ra%d" % _uid[0], [128, r_words], F32R)
            t2 = g2.__enter__()
            RA.ap, RA.n, RA.off = t2[:], r_words, 0
            guards.append(g2)
        return guards

    def close_stage(guards):
        S.barrier()
        for g in reversed(guards):
            g.__exit__(None, None, None)
    CST = cst_t[:]
    PAR = par_t[:]
    IWK = Reg(iwk_t[:], "iwk")
    K_cst = Reg(None, "cst")

    S = Sched(nc, stack)

    def C(name, rows=128, c0=0, c1=None):
        o, w = CO[name]
        if c1 is None:
            c1 = w
        return CST[0:rows, o + c0:o + c1]

    S.dma('sp', CST, i_const, (), (K_cst,))

    par_off = [0]

    def palloc(w):
        a = PAR[:, par_off[0]:par_off[0] + w]
        par_off[0] += w
        return a
    P_MOD = Reg(palloc(144), "mod")
    P_ADAB = Reg(palloc(144), "adab")
    P_NG = Reg(palloc(48), "ng")
    P_A = Reg(palloc(48), "A")
    P_G = Reg(palloc(48), "G")
    P_CACT = Reg(palloc(16), "cact")
    P_RG = Reg(palloc(32), "rgp")
    P_RGD = Reg(palloc(16), "rgd")
    P_MLAN = Reg(palloc(3), "mlan")
    P_GAIN = Reg(palloc(4), "gains")
    P_S5PP = Reg(palloc(48), "s5pp")
    P_S5D = Reg(palloc(64), "s5d")
    P_S5V = Reg(palloc(8), "s5v")
    P_TMP = Reg(palloc(64), "ptmp")

    mod3 = P_MOD.ap.rearrange("p (j k s) -> p j k s", j=9, k=8)
    A3 = P_A.ap.rearrange("p (j k s) -> p j k s", j=3, k=8)
    G3 = P_G.ap.rearrange("p (j k s) -> p j k s", j=3, k=8)

    def modcol(j, k, s):
        return mod3[:, j, k, s:s + 1]

    def sincos(ang, sin_out, cos_out, tmp, rows, n, rk, wk):
        iw = IWK.ap[0:rows, 0:n]
        S.ts('dve', iw, ang, 1.0 / TWO_PI, None, ALU.mult, None, rk, [IWK])
        S.stt(tmp, iw, -CW1, ang, ALU.mult, ALU.add, rk + [IWK], wk)
        S.stt(ang, iw, -CW2, tmp, ALU.mult, ALU.add, rk + [IWK], wk)
        S.ts('dve', tmp, ang, PI, -TWO_PI, ALU.is_gt, ALU.mult, rk, wk)
        S.tt('dve', ang, ang, tmp, ALU.add, rk, wk)
        S.ts('dve', tmp, ang, -PI, TWO_PI, ALU.is_lt, ALU.mult, rk, wk)
        S.tt('dve', ang, ang, tmp, ALU.add, rk, wk)
        S.ts('dve', ang, ang, PI_SAFE, -PI_SAFE, ALU.min, ALU.max, rk, wk)
        S.act(sin_out, ang, AF.Sin, rk, wk)
        S.ts('dve', ang, ang, PI / 2, None, ALU.add, None, rk, wk)
        S.ts('dve', tmp, ang, PI, -TWO_PI, ALU.is_gt, ALU.mult, rk, wk)
        S.tt('dve', ang, ang, tmp, ALU.add, rk, wk)
        S.ts('dve', ang, ang, PI_SAFE, -PI_SAFE, ALU.min, ALU.max, rk, wk)
        S.act(cos_out, ang, AF.Sin, rk, wk)

    def rstd_from_psum(ps_ap, out_ap, n_feat, reads, writes):
        S.act(out_ap, ps_ap, AF.Sqrt, reads, writes, bias=EPS_AP[0:out_ap.shape[0], :], scale=1.0 / n_feat)
        S.recip(out_ap, out_ap, writes, writes)

    def gelu_tanh(x, out, t1, rk, wk):
        S.tt('pool', t1, x, x, ALU.mult, rk, wk)
        S.ts('pool', t1, t1, 0.044715, 1.0, ALU.mult, ALU.add, rk, wk)
        S.tt('pool', t1, t1, x, ALU.mult, rk, wk)
        S.act(t1, t1, AF.Sigmoid, rk, wk, scale=1.5957691216057308)
        S.tt('dve', out, x, t1, ALU.mult, rk, wk)

    EPS_R = Reg(palloc(1), "eps")
    EPS_AP = EPS_R.ap
    S.memset('dve', EPS_AP, EPS, [EPS_R])
    ONE_R = Reg(palloc(1), "one")
    S.memset('dve', ONE_R.ap, 1.0, [ONE_R])

    G0 = open_stage(16000)
    for c in range(dbg.get('nch', NCH)):
        S.dma('sp', d_xT[:, c * CH:(c + 1) * CH], i_xT[:, c * CH:(c + 1) * CH], (), (K_xT[c],))

    R_ang = Reg(AR.alloc(T), "ang")
    R_tmp = Reg(AR.alloc(T), "tmp")
    R_sin = Reg(AR.alloc(T), "sin")
    R_cos = Reg(AR.alloc(T), "cos")
    R_posi = Reg(AR.alloc(NS * T).bitcast(I32).rearrange("p (s t) -> p s t", s=NS), "posi")
    S.dma('sp', R_posi.ap, i_pos, (), (R_posi,))
    for f, inv in enumerate(('inv_mla', 'inv_dsa', 'inv_idx')):
        for s in range(NS):
            S.ts('dve', R_ang.ap, R_posi.ap[:, s, :], C(inv), None, ALU.mult, None, [R_posi, K_cst], [R_ang])
            sincos(R_ang.ap, R_sin.ap, R_cos.ap, R_tmp.ap, 128, T, [R_ang, R_tmp], [R_ang, R_tmp, R_sin, R_cos])
            S.dma('pool', d_tab[f, 0, :, s * T:(s + 1) * T], R_cos.ap, (R_cos,), (K_tab,))
            S.dma('pool', d_tab[f, 1, :, s * T:(s + 1) * T], R_sin.ap, (R_sin,), (K_tab,))
    S.dma('sp', P_CACT.ap.rearrange("p (k s) -> p k s", k=8), i_cT, (), (P_CACT,))
    S.act(P_CACT.ap, P_CACT.ap, AF.Silu, [P_CACT], [P_CACT])
    close_stage(G0)

    def norm_mod(Xr, Ur, UTr, SQr, RSr, j, s):
        ps = PB[0]
        for k in range(8):
            S.act(SQr[k % 2].ap, Xr[k].ap, AF.Square, [Xr[k]], [SQr[k % 2]])
            S.mm(ps.ap, C('ones'), SQr[k % 2].ap, k == 0, k == 7, [SQr[k % 2], K_cst], [ps])
        rstd_from_psum(ps.ap, RSr.ap, D, [ps, EPS_R], [RSr])
        for k in range(8):
            ut = UTr[k % 2]
            S.tt('dve', ut.ap, Xr[k].ap, RSr.ap, ALU.mult, [Xr[k], RSr], [ut])
            S.ts('pool', Ur[k].ap, ut.ap, A3[:, j, k, s:s + 1], modcol(3 * j, k, s), ALU.mult, ALU.add,
                 [ut, P_A, P_MOD], [Ur[k]])

    def chunk_regs(name, arena=None):
        base = (arena or AR).alloc(8 * CH, name)
        b3 = base.rearrange("p (k t) -> p k t", k=8)
        return base, [Reg(b3[:, k, :], name + str(k)) for k in range(8)]

    def dram_chunk(d, c):
        return d.rearrange("(k p) t -> p k t", p=128)[:, :, c * CH:(c + 1) * CH]

    def stage_mod(l):
        G = open_stage(8192)
        WB = [Reg(AR.alloc(8 * 512).rearrange("p (k n) -> p k n", k=8), "adaw%d" % i) for i in range(2)]
        cact3 = P_CACT.ap.rearrange("p (k s) -> p k s", k=8)
        S.dma('sp', P_ADAB.ap.rearrange("p (m s) -> p m s", s=NS), i_adab[l], (), (P_ADAB,))
        S.dma('sp', P_NG.ap.rearrange("p (j k s) -> p j k s", j=3, k=8), i_normg[l].rearrange("j p k s -> p j k s"),
              (), (P_NG,))
        ps = PB[1]
        for blk in range(18):
            w = WB[blk % 2]
            S.dma('sp', w.ap, i_adaw[l, blk], (), (w,))
            for mi in range(4):
                mt = blk * 4 + mi
                for k in range(8):
                    S.mm(ps.ap[:, mt * 2:mt * 2 + 2], w.ap[:, k, mi * 128:(mi + 1) * 128], cact3[:, k, :],
                         k == 0, k == 7, [w, P_CACT], [ps])
        S.tt('dve', P_MOD.ap, ps.ap[:, 0:144], P_ADAB.ap, ALU.add, [ps, P_ADAB], [P_MOD])
        ng3 = P_NG.ap.rearrange("p (j k s) -> p j k s", j=3, k=8)
        for j in range(3):
            S.ts('dve', A3[:, j], mod3[:, 3 * j + 1], 1.0, None, ALU.add, None, [P_MOD], [P_A])
            S.tt('dve', A3[:, j], A3[:, j], ng3[:, j], ALU.mult, [P_A, P_NG], [P_A])
            if j == 1:
                S.ts('dve', G3[:, j], mod3[:, 3 * j + 2], 1.0, None, ALU.add, None, [P_MOD], [P_G])
            else:
                S.ts('dve', G3[:, j], mod3[:, 3 * j + 2], 0.5, 0.5, ALU.mult, ALU.add, [P_MOD], [P_G])
        close_stage(G)

    def stage_ffn(l, jf):
        G = open_stage(21504, 25088)
        jn = 0 if jf == 0 else 2
        Xb, X = chunk_regs("X")
        Ub, U = chunk_regs("U", RA)
        UT = [Reg(AR.alloc(CH), "ut%d" % i) for i in range(2)]
        SQ = [Reg(AR.alloc(CH), "sq%d" % i) for i in range(2)]
        RS = Reg(AR.alloc(CH), "rs")
        SA = [Reg(AR.alloc(CH), "sa%d" % i) for i in range(2)]
        H = [Reg(RA.alloc(CH), "h%d" % i) for i in range(22)]
        W1s = [Reg(AR.alloc(8 * 128).rearrange("p (k n) -> p k n", k=8), "w1s%d" % i) for i in range(4)]
        W3s = [Reg(AR.alloc(8 * 128).rearrange("p (k n) -> p k n", k=8), "w3s%d" % i) for i in range(4)]
        W2s = [Reg(AR.alloc(22 * 128).rearrange("p (k n) -> p k n", k=22), "w2s%d" % i) for i in range(2)]
        W1 = [Reg(RA.alloc(8 * 128).rearrange("p (k n) -> p k n", k=8), "w1_%d" % i) for i in range(2)]
        W3 = [Reg(RA.alloc(8 * 128).rearrange("p (k n) -> p k n", k=8), "w3_%d" % i) for i in range(2)]
        W2 = [Reg(RA.alloc(22 * 128).rearrange("p (k n) -> p k n", k=22), "w2_%d" % i) for i in range(2)]
        for c in range(dbg.get('nch', NCH)):
            s = c // (NCH // NS)
            S.dma('sp', Xb.rearrange("p (k t) -> p k t", k=8), dram_chunk(d_xT, c), (K_xT[c],), X)
            norm_mod(X, U, UT, SQ, RS, jn, s)
            for ft in range(22):
                w1s, w3s, w1, w3 = W1s[ft % 4], W3s[ft % 4], W1[ft % 2], W3[ft % 2]
                S.dma('sp', w1s.ap, i_w1[l, jf, ft], (), (w1s,))
                S.dma('sp', w3s.ap, i_w3[l, jf, ft], (), (w3s,))
                S.copy('dve', w1.ap, w1s.ap, [w1s], [w1])
                S.copy('act', w3.ap, w3s.ap, [w3s], [w3])
                pa, pb = PB[2 + ft % 2], PB[4 + ft % 2]
                for k in range(8):
                    S.mm(pa.ap, w1.ap[:, k, :], U[k].ap, k == 0, k == 7, [w1, U[k]], [pa])
                for k in range(8):
                    S.mm(pb.ap, w3.ap[:, k, :], U[k].ap, k == 0, k == 7, [w3, U[k]], [pb])
                sa = SA[ft % 2]
                S.act(sa.ap, pa.ap, AF.Silu, [pa], [sa])
                S.tt('dve', H[ft].ap, sa.ap, pb.ap, ALU.mult, [sa, pb], [H[ft]])
            for m in range(8):
                w2s, w2 = W2s[m % 2], W2[m % 2]
                S.dma('sp', w2s.ap, i_w2[l, jf, m], (), (w2s,))
                S.copy('dve', w2.ap, w2s.ap, [w2s], [w2])
                po = PB[6 + m % 2]
                for kt in range(22):
                    S.mm(po.ap, w2.ap[:, kt, :], H[kt].ap, kt == 0, kt == 21, [w2, H[kt]], [po])
                S.stt(X[m].ap, po.ap, G3[:, jn, m, s:s + 1], X[m].ap, ALU.mult, ALU.add, [po, P_G, X[m]], [X[m]])
            S.dma('pool', dram_chunk(d_xT, c), Xb.rearrange("p (k t) -> p k t", k=8), X, (K_xT[c],))
        close_stage(G)

    def rope_pipeline(raw, P, gcol, rname, bdname, nfeat, tabC, tabS, QG, SQ, RS, T1, psA, psB, out, rk):
        S.act(SQ.ap[0:P], raw.ap[0:P], AF.Square, [raw], [SQ])
        S.mm(psA.ap[0:P], C(bdname, P, 0, P), SQ.ap[0:P], True, True, [SQ, K_cst], [psA])
        rstd_from_psum(psA.ap[0:P], RS.ap[0:P], nfeat, [psA, EPS_R], [RS])
        S.stt(QG.ap[0:P], raw.ap[0:P], gcol, RS.ap[0:P], ALU.mult, ALU.mult, [raw, RS, P_GAIN], [QG])
        S.mm(psB.ap[0:P], C(rname, P, 0, P), QG.ap[0:P], True, True, [QG, K_cst], [psB])
        S.tt('pool', T1.ap[0:P], QG.ap[0:P], tabC.ap[0:P], ALU.mult, [QG, tabC], [T1])
        S.tt('dve', out.ap[0:P], psB.ap[0:P], tabS.ap[0:P], ALU.mult, [psB, tabS], [out])
        S.tt('dve', out.ap[0:P], out.ap[0:P], T1.ap[0:P], ALU.add, [out, T1], [out])

    def rope_steps(raw, P, gcol, rname, bdname, nfeat, tabC, tabS, QG, SQ, RS, T1, psA, psB, out):
        return [
            lambda: S.act(SQ.ap[0:P], raw.ap[0:P], AF.Square, [raw], [SQ]),
            lambda: S.mm(psA.ap[0:P], C(bdname, P, 0, P), SQ.ap[0:P], True, True, [SQ, K_cst], [psA]),
            lambda: S.act(RS.ap[0:P], psA.ap[0:P], AF.Ln, [psA, EPS_R], [RS], bias=EPS_AP[0:P, :], scale=1.0 / nfeat),
            lambda: S.act(RS.ap[0:P], RS.ap[0:P], AF.Exp, [RS], [RS], scale=-0.5),
            lambda: S.stt(QG.ap[0:P], raw.ap[0:P], gcol, RS.ap[0:P], ALU.mult, ALU.mult, [raw, RS, P_GAIN], [QG]),
            lambda: S.mm(psB.ap[0:P], C(rname, P, 0, P), QG.ap[0:P], True, True, [QG, K_cst], [psB]),
            lambda: S.tt('pool', T1.ap[0:P], QG.ap[0:P], tabC.ap[0:P], ALU.mult, [QG, tabC], [T1]),
            lambda: S.tt('dve', out.ap[0:P], psB.ap[0:P], tabS.ap[0:P], ALU.mult, [psB, tabS], [out]),
            lambda: S.tt('dve', out.ap[0:P], out.ap[0:P], T1.ap[0:P], ALU.add, [out, T1], [out]),
        ]

    def interleave(pipes):
        n = max(len(p) for p in pipes)
        for j in range(n):
            for p in pipes:
                if j < len(p):
                    p[j]()

    def stage_win(l):
        G = open_stage(14000, 6200)
        Xb, X = chunk_regs("X")
        Ub, U = chunk_regs("U", RA)
        UT = [Reg(AR.alloc(CH), "ut%d" % i) for i in range(2)]
        SQ = [Reg(AR.alloc(CH), "sq%d" % i) for i in range(2)]
        RS = Reg(AR.alloc(CH), "rs")
        ZS = [Reg(AR.alloc(CH), "zs%d" % i) for i in range(2)]
        ZR = [Reg(AR.alloc(CH), "zr%d" % i) for i in range(2)]
        W = [Reg(AR.alloc(8 * 128).rearrange("p (k n) -> p k n", k=8), "wz%d" % i) for i in range(2)]
        WR_ = [Reg(RA.alloc(8 * 128).rearrange("p (k n) -> p k n", k=8), "wzr%d" % i) for i in range(2)]
        WT = Reg(AR.alloc(8 * 72).rearrange("p (k n) -> p k n", k=8), "wtok")
        VW = [Reg(AR.alloc(72), "vw%d" % i) for i in range(2)]
        TC = Reg(AR.alloc(CH), "tc")
        TS_ = Reg(AR.alloc(CH), "tsn")
        S.dma('sp', WT.ap, i_wtok[l], (), (WT,))
        for c in range(dbg.get('nch', NCH)):
            s = c // (NCH // NS)
            S.dma('sp', Xb.rearrange("p (k t) -> p k t", k=8), dram_chunk(d_xT, c), (K_xT[c],), X)
            S.dma('sp', TC.ap, d_tab[2, 0, :, c * CH:(c + 1) * CH], (K_tab,), (TC,))
            S.dma('sp', TS_.ap, d_tab[2, 1, :, c * CH:(c + 1) * CH], (K_tab,), (TS_,))
            norm_mod(X, U, UT, SQ, RS, 1, s)
            S.dma('pool', dram_chunk(d_u2T, c), AF32(Ub).rearrange("p (k t) -> p k t", k=8), U, (K_u2T[c],))
            for zi, (c0, wd) in enumerate(Z_TILES):
                w = W[zi % 2]
                S.dma('sp', w.ap, i_winz[l, zi], (), (w,))
                pz = PB[1 + zi % 2]
                if wd == 128 and zi not in (ZI_QIDX, ZI_QIDX + 1):
                    wr = WR_[zi % 2]
                    S.copy('dve' if zi % 2 == 0 else 'act', wr.ap, w.ap, [w], [wr])
                    for k in range(8):
                        S.mm(pz.ap, wr.ap[:, k, :], U[k].ap, k == 0, k == 7, [wr, U[k]], [pz])
                else:
                    for k in range(8):
                        S.mm(pz.ap[0:wd], w.ap[:, k, 0:wd], AF32(U[k].ap), k == 0, k == 7, [w, U[k]], [pz])
                zs = ZS[zi % 2]
                S.copy('act', zs.ap[0:wd], pz.ap[0:wd], [pz], [zs])
                if zi in (ZI_QIDX, ZI_QIDX + 1, ZI_KIDX):
                    pr = PB[3 + zi % 2]
                    zr = ZR[zi % 2]
                    S.mm(pr.ap[0:wd], C('r32', wd, 0, wd), zs.ap[0:wd], True, True, [zs, K_cst], [pr])
                    S.tt('dve', zr.ap[0:wd], pr.ap[0:wd], TS_.ap[0:wd], ALU.mult, [pr, TS_], [zr])
                    S.tt('pool', zs.ap[0:wd], zs.ap[0:wd], TC.ap[0:wd], ALU.mult, [zs, TC], [zs])
                    S.tt('dve', zs.ap[0:wd], zs.ap[0:wd], zr.ap[0:wd], ALU.add, [zs, zr], [zs])
                S.dma('pool', d_zT[zi * 128:zi * 128 + wd, c * CH:(c + 1) * CH], zs.ap[0:wd], (zs,), (K_zT[c],))
            for tt_ in range(4):
                pv = PB[5 + tt_ % 2]
                for k in range(8):
                    S.mm(pv.ap[:, 0:72], AF32(U[k].ap[:, tt_ * 128:(tt_ + 1) * 128]), WT.ap[:, k, :], k == 0, k == 7,
                         [WT, U[k]], [pv])
                vw = VW[tt_ % 2]
                S.copy('act', vw.ap, pv.ap[:, 0:72], [pv], [vw])
                t0 = c * CH + tt_ * 128
                S.dma('pool', d_vw[t0:t0 + 128, :], vw.ap, (vw,), (K_vw[c],))
        close_stage(G)

    def stage_rglru(l):
        G = open_stage(20000)
        XR = Reg(AR.alloc(T + 4), "xr")
        GT = Reg(AR.alloc(T), "gt")
        XC = Reg(AR.alloc(T), "xc")
        RR = Reg(AR.alloc(T), "rr")
        IG = Reg(AR.alloc(T), "ig")
        AA = Reg(AR.alloc(T), "aa")
        MM = Reg(AR.alloc(T), "mm")
        T1 = Reg(AR.alloc(T), "t1")
        HH = Reg(AR.alloc(T), "hh")
        WA = Reg(AR.alloc(4 * 128).rearrange("p (c n) -> p c n", c=4), "wa")
        WX = Reg(AR.alloc(4 * 128).rearrange("p (c n) -> p c n", c=4), "wx")
        S.dma('sp', WA.ap, i_rgw[l, 0].rearrange("c p n -> p c n"), (), (WA,))
        S.dma('sp', WX.ap, i_rgw[l, 1].rearrange("c p n -> p c n"), (), (WX,))
        S.dma('sp', P_RG.ap.rearrange("p (c j) -> p c j", c=4), i_rgp[l], (), (P_RG,))
        rg3 = P_RG.ap.rearrange("p (c j) -> p c j", c=4)
        rgd = P_RGD.ap.rearrange("p (j c) -> p j c", j=4)
        S.act(rgd[:, 0, :], rg3[:, :, 7], AF.Exp, [P_RG], [P_RGD], scale=-1.0)
        S.act(rgd[:, 0, :], rgd[:, 0, :], AF.Ln, [P_RGD], [P_RGD], bias=ONE_R.ap, scale=1.0)
        S.ts('dve', rgd[:, 1, :], rgd[:, 0, :], -8.0, None, ALU.mult, None, [P_RGD], [P_RGD])
        S.ts('dve', rgd[:, 2, :], rgd[:, 0, :], -16.0, None, ALU.mult, None, [P_RGD], [P_RGD])
        S.memset('dve', XR.ap[:, 0:4], 0.0, [XR])
        for s in range(NS):
            zk = K_zT[s * 4:(s + 1) * 4]
            for ct in range(4):
                S.dma('sp', XR.ap[:, 4:4 + T], d_zT[(ZI_XRNN + ct) * 128:(ZI_XRNN + ct + 1) * 128, s * T:(s + 1) * T],
                      zk, (XR,))
                S.dma('sp', GT.ap, d_zT[(ZI_GATE + ct) * 128:(ZI_GATE + ct + 1) * 128, s * T:(s + 1) * T], zk, (GT,))
                S.act(XC.ap, XR.ap[:, 4:4 + T], AF.Identity, [XR, P_RG], [XC], bias=rg3[:, ct, 4:5], scale=rg3[:, ct, 3:4])
                for j in range(3):
                    S.stt(XC.ap, XR.ap[:, 1 + j:1 + j + T], rg3[:, ct, j:j + 1], XC.ap, ALU.mult, ALU.add,
                          [XR, P_RG, XC], [XC])
                for q in range(4):
                    pr, pi = PB[q % 2], PB[2 + q % 2]
                    sl = slice(q * CH, (q + 1) * CH)
                    S.mm(pr.ap, WA.ap[:, ct, :], XC.ap[:, sl], True, True, [WA, XC], [pr])
                    S.mm(pi.ap, WX.ap[:, ct, :], XC.ap[:, sl], True, True, [WX, XC], [pi])
                    S.act(RR.ap[:, sl], pr.ap, AF.Sigmoid, [pr, P_RG], [RR], bias=rg3[:, ct, 5:6], scale=1.0)
                    S.act(IG.ap[:, sl], pi.ap, AF.Sigmoid, [pi, P_RG], [IG], bias=rg3[:, ct, 6:7], scale=1.0)
                S.act(AA.ap, RR.ap, AF.Exp, [RR, P_RGD], [AA], scale=rgd[:, 1, ct:ct + 1])
                S.act(MM.ap, RR.ap, AF.Exp, [RR, P_RGD], [MM], scale=rgd[:, 2, ct:ct + 1])
                S.act(MM.ap, MM.ap, AF.Sqrt, [MM, ONE_R], [MM], bias=ONE_R.ap, scale=-1.0)
                S.tt('dve', MM.ap, MM.ap, IG.ap, ALU.mult, [MM, IG], [MM])
                S.tt('dve', MM.ap, MM.ap, XC.ap, ALU.mult, [MM, XC], [MM])
                S.op('dve', lambda g, o=HH.ap, a=AA.ap, b=MM.ap: g.tensor_tensor_scan(o, a, b, 0.0, ALU.mult, ALU.add),
                     [AA, MM], [HH])
                gelu_tanh(GT.ap, IG.ap, T1.ap, [GT, T1, IG], [T1, IG])
                S.tt('dve', HH.ap, HH.ap, IG.ap, ALU.mult, [HH, IG], [HH])
                S.dma('pool', d_yT[0, ct * 128:(ct + 1) * 128, s * T:(s + 1) * T], HH.ap, (HH,), (K_yT[0][s],))
        close_stage(G)

    def stage_mla(l):
        G = open_stage(18500)
        QL = [Reg(AR.alloc(CH), "ql%d" % i) for i in range(2)]
        KVL = Reg(AR.alloc(CH), "kvl")
        QN = [Reg(AR.alloc(CH), "qn%d" % i) for i in range(2)]
        KVN = Reg(AR.alloc(CH), "kvn")
        SQ = Reg(AR.alloc(CH), "sq")
        RS = Reg(AR.alloc(CH), "rs")
        SQ2 = [Reg(AR.alloc(CH), "sq2_%d" % i) for i in range(2)]
        RS2 = [Reg(AR.alloc(CH), "rs2_%d" % i) for i in range(2)]
        QG2 = [Reg(AR.alloc(CH), "qg2_%d" % i) for i in range(2)]
        T12 = [Reg(AR.alloc(CH), "t12_%d" % i) for i in range(2)]
        RAW = [Reg(AR.alloc(CH), "raw%d" % i) for i in range(2)]
        KRAW = [Reg(AR.alloc(CH), "kraw%d" % i) for i in range(2)]
        KPE = Reg(AR.alloc(CH), "kpe")
        QG = Reg(AR.alloc(CH), "qg")
        T1 = Reg(AR.alloc(CH), "t1")
        OUT = [Reg(AR.alloc(CH), "out%d" % i) for i in range(2)]
        TC = Reg(AR.alloc(CH), "tc")
        TS_ = Reg(AR.alloc(CH), "tsn")
        VS = [Reg(AR.alloc(512), "vs%d" % i) for i in range(2)]
        WUQ = Reg(AR.alloc(2 * 8 * 96).rearrange("p (k h n) -> p k h n", k=2, h=8), "wuq")
        WK = Reg(AR.alloc(8 * 64).rearrange("p (h n) -> p h n", h=8), "wk")
        WV = Reg(AR.alloc(512), "wv")
        S.dma('sp', WUQ.ap, i_wuq[l], (), (WUQ,))
        S.dma('sp', WK.ap, i_wukvk[l], (), (WK,))
        S.dma('sp', WV.ap, i_wukvv[l], (), (WV,))
        S.dma('sp', P_MLAN.ap, i_mlan[l], (), (P_MLAN,))
        S.dma('sp', P_GAIN.ap, i_gains[l], (), (P_GAIN,))
        for c in range(dbg.get('nch', NCH)):
            tk = slice(c * CH, (c + 1) * CH)
            for k in range(2):
                S.dma('sp', QL[k].ap, d_zT[(ZI_QLAT + k) * 128:(ZI_QLAT + k + 1) * 128, tk], (K_zT[c],), (QL[k],))
            S.dma('sp', KVL.ap, d_zT[ZI_KVLAT * 128:(ZI_KVLAT + 1) * 128, tk], (K_zT[c],), (KVL,))
            S.dma('sp', KPE.ap[64:96], d_zT[ZI_KPE * 128:ZI_KPE * 128 + 32, tk], (K_zT[c],), (KPE,))
            S.dma('sp', TC.ap, d_tab[0, 0, :, tk], (K_tab,), (TC,))
            S.dma('sp', TS_.ap, d_tab[0, 1, :, tk], (K_tab,), (TS_,))
            ps = PB[0]
            for k in range(2):
                S.act(SQ.ap, QL[k].ap, AF.Square, [QL[k]], [SQ])
                S.mm(ps.ap, C('ones'), SQ.ap, k == 0, k == 1, [SQ, K_cst], [ps])
            rstd_from_psum(ps.ap, RS.ap, 256, [ps, EPS_R], [RS])
            for k in range(2):
                S.stt(QN[k].ap, QL[k].ap, P_MLAN.ap[:, k:k + 1], RS.ap, ALU.mult, ALU.mult, [QL[k], P_MLAN, RS], [QN[k]])
            S.act(SQ.ap, KVL.ap, AF.Square, [KVL], [SQ])
            S.mm(ps.ap, C('ones'), SQ.ap, True, True, [SQ, K_cst], [ps])
            rstd_from_psum(ps.ap, RS.ap, 128, [ps, EPS_R], [RS])
            S.stt(KVN.ap, KVL.ap, P_MLAN.ap[:, 2:3], RS.ap, ALU.mult, ALU.mult, [KVL, P_MLAN, RS], [KVN])
            for h in range(8):
                pq = PB[1]
                raw = RAW[h % 2]
                for k in range(2):
                    S.mm(pq.ap[0:96], WUQ.ap[:, k, h, :], QN[k].ap, k == 0, k == 1, [WUQ, QN[k]], [pq])
                pk = PB[2]
                kraw = KRAW[h % 2]
                S.mm(pk.ap[0:64], WK.ap[:, h, :], KVN.ap, True, True, [WK, KVN], [pk])
                S.copy('act', raw.ap[0:96], pq.ap[0:96], [pq], [raw])
                S.copy('act', kraw.ap[0:64], pk.ap[0:64], [pk], [kraw])
                S.copy('dve', kraw.ap[64:96], KPE.ap[64:96], [KPE], [kraw])
                oq, ok_ = OUT[0], OUT[1]
                interleave([
                    rope_steps(raw, 96, P_GAIN.ap[0:96, 0:1], 'r96', 'ones', 96, TC, TS_, QG2[0], SQ2[0], RS2[0], T12[0], PB[3], PB[4], oq),
                    rope_steps(kraw, 96, P_GAIN.ap[0:96, 1:2], 'r96', 'ones', 96, TC, TS_, QG2[1], SQ2[1], RS2[1], T12[1], PB[5], PB[6], ok_),
                ])
                S.dma('pool', d_qm[h, :, tk], oq.ap[0:96], (oq,), (K_qm[c],))
                S.dma('pool', d_km[h, :, tk], ok_.ap[0:96], (ok_,), (K_km[c],))
            for tt_ in range(4):
                pv = PB[7]
                S.mm(pv.ap, KVN.ap[:, tt_ * 128:(tt_ + 1) * 128], WV.ap, True, True, [KVN, WV], [pv])
                vs = VS[tt_ % 2]
                S.copy('act', vs.ap, pv.ap, [pv], [vs])
                t0 = c * CH + tt_ * 128
                S.dma('pool', d_vm[t0:t0 + 128, :], vs.ap, (vs,), (K_vm[c],))
        close_stage(G)
        G = open_stage(8000, 12000)
        KHs = Reg(AR.alloc(T), "khs")
        QHs = Reg(AR.alloc(T), "qhs")
        V1s = Reg(AR.alloc(16 * 65).rearrange("p (b n) -> p b n", b=16), "v1s")
        KH = [Reg(RA.alloc(T), "kh%d" % i) for i in range(2)]
        QH = [Reg(RA.alloc(T), "qh%d" % i) for i in range(2)]
        V1 = [Reg(RA.alloc(16 * 65).rearrange("p (b n) -> p b n", b=16), "v1_%d" % i) for i in range(2)]
        PT = [Reg(RA.alloc(CH), "pt%d" % i) for i in range(3)]
        OS = Reg(AR.alloc(CH), "os")
        RD = Reg(AR.alloc(CH), "rd")
        YO = [Reg(AR.alloc(CH), "yo%d" % i) for i in range(2)]
        S.memset('dve', V1s.ap[:, :, 64:65], 1.0, [V1s])
        sc = 96.0 ** -0.5
        it = 0
        for s in range(NS):
            ks = K_km[s * 4:(s + 1) * 4]
            qs = K_qm[s * 4:(s + 1) * 4]
            vsk = K_vm[s * 4:(s + 1) * 4]
            for h in range(8):
                kh, qh, v1 = KH[h % 2], QH[h % 2], V1[h % 2]
                S.dma('sp', KHs.ap[0:96], d_km[h, :, s * T:(s + 1) * T], ks, (KHs,))
                S.dma('sp', QHs.ap[0:96], d_qm[h, :, s * T:(s + 1) * T], qs, (QHs,))
                S.dma('sp', V1s.ap[:, :, 0:64],
                      d_vm[s * T:(s + 1) * T, h * 64:(h + 1) * 64].rearrange("(b p) n -> p b n", p=128), vsk, (V1s,))
                S.copy('dve', kh.ap[0:96], KHs.ap[0:96], [KHs], [kh])
                S.copy('dve', qh.ap[0:96], QHs.ap[0:96], [QHs], [qh])
                S.copy('dve', v1.ap, V1s.ap, [V1s], [v1])
                for qc in range(4):
                    po = PB[6 + qc % 2]
                    nkb = 4 * (qc + 1)
                    for kb in range(nkb):
                        j = max(0, kb - 4 * qc)
                        q0 = qc * CH + j * 128
                        n = CH - j * 128
                        pS = PB[it % 3]
                        pt = PT[it % 3]
                        it += 1
                        S.mm(pS.ap[:, 0:n], kh.ap[0:96, kb * 128:(kb + 1) * 128], qh.ap[0:96, q0:q0 + n], True, True,
                             [kh, qh], [pS])
                        S.act(pt.ap[:, 0:n], pS.ap[:, 0:n], AF.Exp, [pS], [pt], scale=sc)
                        if kb >= 4 * qc:
                            S.tt('pool', pt.ap[:, 0:128], pt.ap[:, 0:128], C('tri_kq'), ALU.mult, [pt, K_cst], [pt])
                        S.mm(po.ap[0:65, j * 128:CH], v1.ap[:, kb, :], pt.ap[:, 0:n], kb == 0, kb == nkb - 1,
                             [v1, pt], [po])
                    S.copy('act', OS.ap[0:65], po.ap[0:65], [po], [OS])
                    pd = PB[3 + qc % 2]
                    S.mm(pd.ap[0:64], C('ones', 128, 0, 64)[64:65], OS.ap[64:65], True, True, [OS, K_cst], [pd])
                    S.recip(RD.ap[0:64], pd.ap[0:64], [pd], [RD])
                    yo = YO[qc % 2]
                    S.tt('dve', yo.ap[0:64], OS.ap[0:64], RD.ap[0:64], ALU.mult, [OS, RD], [yo])
                    S.dma('pool', d_yT[1, h * 64:(h + 1) * 64, s * T + qc * CH:s * T + (qc + 1) * CH], yo.ap[0:64],
                          (yo,), (K_yT[1][s],))
        close_stage(G)

    def stage_dsa(l):
        G = open_stage(8000)
        QD = [Reg(AR.alloc(CH), "qd%d" % i) for i in range(2)]
        SQ2 = [Reg(AR.alloc(CH), "sq%d" % i) for i in range(2)]
        RS2 = [Reg(AR.alloc(CH), "rs%d" % i) for i in range(2)]
        QG2 = [Reg(AR.alloc(CH), "qg%d" % i) for i in range(2)]
        T12 = [Reg(AR.alloc(CH), "t1%d" % i) for i in range(2)]
        OUT = [Reg(AR.alloc(CH), "out%d" % i) for i in range(2)]
        TC = Reg(AR.alloc(CH), "tc")
        TS_ = Reg(AR.alloc(CH), "tsn")
        S.dma('sp', P_GAIN.ap, i_gains[l], (), (P_GAIN,))
        for c in range(dbg.get('nch', NCH)):
            tk = slice(c * CH, (c + 1) * CH)
            S.dma('sp', TC.ap, d_tab[1, 0, :, tk], (K_tab,), (TC,))
            S.dma('sp', TS_.ap, d_tab[1, 1, :, tk], (K_tab,), (TS_,))
            for k0 in (0, 2, 4):
                pipes, stores = [], []
                for k in (k0, k0 + 1):
                    if k > 4:
                        continue
                    u = k % 2
                    qd, o = QD[u], OUT[u]
                    if k < 4:
                        S.dma('sp', qd.ap, d_zT[(ZI_QDSA + k) * 128:(ZI_QDSA + k + 1) * 128, tk], (K_zT[c],), (qd,))
                        pipes.append(rope_steps(qd, 128, P_GAIN.ap[:, 2:3], 'r64', 'bd64', 64, TC, TS_, QG2[u], SQ2[u], RS2[u], T12[u],
                                                PB[2 * u], PB[2 * u + 1], o))
                        stores.append((d_qd[k * 128:(k + 1) * 128, tk], o.ap, o, K_qd[c]))
                    else:
                        S.dma('sp', qd.ap[0:64], d_zT[ZI_KDSA * 128:ZI_KDSA * 128 + 64, tk], (K_zT[c],), (qd,))
                        pipes.append(rope_steps(qd, 64, P_GAIN.ap[0:64, 3:4], 'r64', 'bd64', 64, TC, TS_, QG2[u], SQ2[u], RS2[u], T12[u],
                                                PB[2 * u], PB[2 * u + 1], o))
                        stores.append((d_kd[:, tk], o.ap[0:64], o, K_kd[c]))
                interleave(pipes)
                for dst, src, reg, key in stores:
                    S.dma('pool', dst, src, (reg,), (key,))
        close_stage(G)
        G = open_stage(33600, 2700)
        _uid[0] += 1
        gm = nc.sbuf_tensor("mtb%d" % _uid[0], [128, 2 * 16 * CH], mybir.dt.bfloat16)
        mt_t = gm.__enter__()
        G.append(gm)
        mt4 = mt_t[:].rearrange("p (i b n) -> p i b n", i=2, b=16)
        MTs = [Reg(mt4[:, i], "mt%d" % i) for i in range(2)]
        KD2 = Reg(AR.alloc(T), "kd2")
        QI = Reg(AR.alloc(2 * T).rearrange("p (k t) -> p k t", k=2), "qi")
        KIB = Reg(AR.alloc(4 * T).rearrange("p (h t) -> p h t", h=4), "kib")
        V1s = Reg(AR.alloc(16 * 65).rearrange("p (b n) -> p b n", b=16), "v1s")
        V1 = Reg(RA.alloc(16 * 65).rearrange("p (b n) -> p b n", b=16), "v1")
        WI = Reg(AR.alloc(16 * 8).rearrange("p (b n) -> p b n", b=16), "wi")
        SCs = [Reg(AR.alloc(T), "sc%d" % i) for i in range(2)]
        WKs = [Reg(AR.alloc(T), "wk%d" % i) for i in range(2)]
        RL = [Reg(AR.alloc(CH), "rl%d" % i) for i in range(2)]
        M8s = [Reg(AR.alloc(8), "m8_%d" % i) for i in range(2)]
        THRs = [Reg(AR.alloc(1), "thr%d" % i) for i in range(2)]
        QC = Reg(AR.alloc(4 * CH).rearrange("p (k t) -> p k t", k=4), "qc")
        PT = [Reg(RA.alloc(CH), "pt%d" % i) for i in range(3)]
        OSs = [Reg(AR.alloc(CH), "os%d" % i) for i in range(8)]
        RD = Reg(AR.alloc(CH), "rd")
        YO = [Reg(AR.alloc(CH), "yo%d" % i) for i in range(2)]
        S.memset('dve', V1s.ap[:, :, 64:65], 1.0, [V1s])
        S.memset('dve', KIB.ap, 0.0, [KIB])
        sc = 64.0 ** -0.5
        itc = [0]

        def idx_pair(s, qc, pair):
            qbs = [qc * 4 + pair * 2, qc * 4 + pair * 2 + 1]
            for i, qb in enumerate(qbs):
                SC = SCs[i]
                for kb in range(qb + 1):
                    for k in range(2):
                        pr = PB[k]
                        rl = RL[k]
                        S.mm(pr.ap, QI.ap[:, k, qb * 128:(qb + 1) * 128], KIB.ap[:, :, kb * 128:(kb + 1) * 128],
                             True, True, [QI, KIB], [pr])
                        S.act(rl.ap, pr.ap, AF.Relu, [pr], [rl])
                        for h4 in range(4):
                            hh = k * 4 + h4
                            dst = SC.ap[:, kb * 128:(kb + 1) * 128]
                            if hh == 0:
                                S.ts('dve', dst, rl.ap[:, 0:128], WI.ap[:, qb, 0:1], None, ALU.mult, None,
                                     [rl, WI], [SC])
                            else:
                                S.stt(dst, rl.ap[:, h4 * 128:(h4 + 1) * 128], WI.ap[:, qb, hh:hh + 1], dst,
                                      ALU.mult, ALU.add, [rl, WI, SC], [SC])
                dg = SC.ap[:, qb * 128:(qb + 1) * 128]
                S.tt('dve', dg, dg, C('tri_qk'), ALU.mult, [SC, K_cst], [SC])
                S.tt('dve', dg, dg, C('neg_qk'), ALU.add, [SC, K_cst], [SC])
            srcs = [SCs[0], SCs[1]]
            for r in range(32):
                for i, qb in enumerate(qbs):
                    if qb < 2:
                        continue
                    nk = (qb + 1) * 128
                    S.op('dve', lambda g, o=M8s[i].ap, a=srcs[i].ap[:, 0:nk]: g.max(o, a), [srcs[i]], [M8s[i]])
                    if r < 31:
                        S.op('dve', lambda g, o=WKs[i].ap[:, 0:nk], a=M8s[i].ap, v=srcs[i].ap[:, 0:nk]:
                             g.match_replace(o, a, v, -BIG), [M8s[i], srcs[i]], [WKs[i]])
                        srcs[i] = WKs[i]
            for i, qb in enumerate(qbs):
                nk = (qb + 1) * 128
                SC, WK_, M8, THR = SCs[i], WKs[i], M8s[i], THRs[i]
                if qb >= 2:
                    S.ts('dve', THR.ap, M8.ap[:, 7:8], -0.5 * BIG, None, ALU.max, None, [M8], [THR])
                else:
                    S.memset('dve', THR.ap, -0.5 * BIG, [THR])
                S.ts('dve', WK_.ap[:, 0:nk], SC.ap[:, 0:nk], THR.ap, None, ALU.is_ge, None, [SC, THR], [WK_])

        def idx_pair_B(s, qc, pair):
            MT = MTs[qc % 2]
            for i in range(2):
                qb = qc * 4 + pair * 2 + i
                ql = qb - qc * 4
                WK_ = WKs[i]
                for kb in range(qb + 1):
                    pt_ = PB[2 + kb % 2]
                    S.transpose(pt_.ap[:, 0:128], WK_.ap[:, kb * 128:(kb + 1) * 128], C('ident'), [WK_, K_cst], [pt_])
                    S.copy('act', MT.ap[:, kb, ql * 128:(ql + 1) * 128], pt_.ap[:, 0:128], [pt_], [MT])

        def attn_heads(s, qc, heads):
            MT = MTs[qc % 2]
            nkb = 4 * (qc + 1)
            for h in heads:
                p0 = (h % 2) * 64
                po = PB[6 + h % 2]
                for kb in range(nkb):
                    j = max(0, kb - 4 * qc)
                    n = CH - j * 128
                    pS = PB[4 + itc[0] % 2]
                    pt = PT[itc[0] % 3]
                    itc[0] += 1
                    S.mm(pS.ap[:, 0:n], KD2.ap[p0:p0 + 64, kb * 128:(kb + 1) * 128],
                         QC.ap[p0:p0 + 64, h // 2, j * 128:CH], True, True, [KD2, QC], [pS])
                    S.act(pt.ap[:, 0:n], pS.ap[:, 0:n], AF.Exp, [pS], [pt], scale=sc)
                    S.tt('pool', pt.ap[:, 0:n], pt.ap[:, 0:n], MT.ap[:, kb, j * 128:CH], ALU.mult, [pt, MT], [pt])
                    S.mm(po.ap[0:65, j * 128:CH], V1.ap[:, kb, :], pt.ap[:, 0:n], kb == 0, kb == nkb - 1,
                         [V1, pt], [po])
                S.copy('act', OSs[h].ap[0:65], po.ap[0:65], [po], [OSs[h]])

        def attn_norm(s, qc):
            for h in range(8):
                OS = OSs[h]
                pd = PB[2 + h % 2]
                S.mm(pd.ap[0:64], C('ones', 128, 0, 64)[64:65], OS.ap[64:65], True, True, [OS, K_cst], [pd])
                S.recip(RD.ap[0:64], pd.ap[0:64], [pd], [RD])
                yo = YO[h % 2]
                S.tt('dve', yo.ap[0:64], OS.ap[0:64], RD.ap[0:64], ALU.mult, [OS, RD], [yo])
                S.dma('pool', d_yT[2, h * 64:(h + 1) * 64, s * T + qc * CH:s * T + (qc + 1) * CH], yo.ap[0:64],
                      (yo,), (K_yT[2][s],))

        for s in range(NS):
            zk = K_zT[s * 4:(s + 1) * 4]
            ts_ = slice(s * T, (s + 1) * T)
            S.dma('sp', KD2.ap[0:64], d_kd[:, ts_], K_kd[s * 4:(s + 1) * 4], (KD2,))
            S.dma('sp', KD2.ap[64:128], d_kd[:, ts_], K_kd[s * 4:(s + 1) * 4], (KD2,))
            for k in range(2):
                S.dma('sp', QI.ap[:, k, :], d_zT[(ZI_QIDX + k) * 128:(ZI_QIDX + k + 1) * 128, ts_], zk, (QI,))
            for h4 in range(4):
                S.dma('sp', KIB.ap[h4 * 32:(h4 + 1) * 32, h4, :], d_zT[ZI_KIDX * 128:ZI_KIDX * 128 + 32, ts_], zk, (KIB,))
            vwk = K_vw[s * 4:(s + 1) * 4]
            S.dma('sp', V1s.ap[:, :, 0:64], d_vw[ts_, 0:64].rearrange("(b p) n -> p b n", p=128), vwk, (V1s,))
            S.copy('pool', V1.ap, V1s.ap, [V1s], [V1])
            S.dma('sp', WI.ap, d_vw[ts_, 64:72].rearrange("(b p) n -> p b n", p=128), vwk, (WI,))
            for pair in range(2):
                idx_pair(s, 0, pair)
                idx_pair_B(s, 0, pair)
            for qc in range(4):
                S.dma('sp', QC.ap, d_qd[:, s * T + qc * CH:s * T + (qc + 1) * CH].rearrange("(k p) t -> p k t", p=128),
                      K_qd[s * 4:(s + 1) * 4], (QC,))
                nxt = qc + 1 < 4
                if nxt:
                    idx_pair(s, qc + 1, 0)
                attn_heads(s, qc, range(0, 4))
                if nxt:
                    idx_pair_B(s, qc + 1, 0)
                    idx_pair(s, qc + 1, 1)
                attn_heads(s, qc, range(4, 8))
                if nxt:
                    idx_pair_B(s, qc + 1, 1)
                attn_norm(s, qc)
        close_stage(G)

    def stage_s5(l):
        G = open_stage(44000)
        YSUM[0] = Reg(AR.ap[:, AR.n - 2 * T:AR.n - T], "ysum0")
        YSUM[1] = Reg(AR.ap[:, AR.n - T:AR.n], "ysum1")
        LR = Reg(AR.alloc(T), "lr")
        LI = Reg(AR.alloc(T), "li")
        DT = Reg(AR.alloc(T), "dt")
        E1 = Reg(AR.alloc(T), "e1")
        E2 = Reg(AR.alloc(T), "e2")
        E3 = Reg(AR.alloc(T), "e3")
        E4 = Reg(AR.alloc(T), "e4")
        FR = Reg(AR.alloc(T), "fr")
        FI = Reg(AR.alloc(T), "fi")
        BRE = Reg(AR.alloc(T), "bre")
        BIM = Reg(AR.alloc(T), "bim")
        CRE = Reg(AR.alloc(T), "cre")
        CIM = Reg(AR.alloc(T), "cim")
        allk = [LR, LI, DT, E1, E2, E3, E4, FR, FI]
        S.dma('sp', LR.ap, i_s5bc[l, 0], (), (LR,))
        S.dma('sp', LI.ap, i_s5bc[l, 1], (), (LI,))
        S.dma('sp', DT.ap, i_s5bc[l, 2], (), (DT,))
        S.dma('sp', BRE.ap.rearrange("p (a m) -> p a m", a=16), i_s5b[l, 0], (), (BRE,))
        S.dma('sp', BIM.ap.rearrange("p (a m) -> p a m", a=16), i_s5b[l, 1], (), (BIM,))
        S.dma('sp', CRE.ap.rearrange("p (a m) -> p a m", a=16), i_s5c[l, 0], (), (CRE,))
        S.dma('sp', CIM.ap.rearrange("p (a m) -> p a m", a=16), i_s5c[l, 1], (), (CIM,))
        S.dma('sp', P_S5PP.ap.rearrange("p (j a) -> p j a", j=3), i_s5pp[l].rearrange("j p a -> p j a"), (), (P_S5PP,))
        S.dma('sp', P_S5V.ap.rearrange("p (a j) -> p a j", a=4), i_s5v[l], (), (P_S5V,))
        S.act(DT.ap, DT.ap, AF.Exp, allk, allk)
        S.tt('dve', E1.ap, LR.ap, DT.ap, ALU.mult, allk, allk)
        S.act(E1.ap, E1.ap, AF.Exp, allk, allk)
        S.tt('dve', E2.ap, LI.ap, DT.ap, ALU.mult, allk, allk)
        sincos(E2.ap, E3.ap, E4.ap, FR.ap, 128, T, allk, allk)
        S.tt('dve', E3.ap, E3.ap, E1.ap, ALU.mult, allk, allk)
        S.tt('dve', E4.ap, E4.ap, E1.ap, ALU.mult, allk, allk)
        S.ts('dve', E4.ap, E4.ap, -1.0, None, ALU.add, None, allk, allk)
        S.tt('dve', E1.ap, LR.ap, LR.ap, ALU.mult, allk, allk)
        S.tt('dve', E2.ap, LI.ap, LI.ap, ALU.mult, allk, allk)
        S.tt('dve', E1.ap, E1.ap, E2.ap, ALU.add, allk, allk)
        S.recip(E1.ap, E1.ap, allk, allk)
        S.tt('dve', FR.ap, E4.ap, LR.ap, ALU.mult, allk, allk)
        S.tt('dve', E2.ap, E3.ap, LI.ap, ALU.mult, allk, allk)
        S.tt('dve', FR.ap, FR.ap, E2.ap, ALU.add, allk, allk)
        S.tt('dve', FR.ap, FR.ap, E1.ap, ALU.mult, allk, allk)
        S.tt('dve', FI.ap, E3.ap, LR.ap, ALU.mult, allk, allk)
        S.tt('dve', E2.ap, E4.ap, LI.ap, ALU.mult, allk, allk)
        S.tt('dve', FI.ap, FI.ap, E2.ap, ALU.subtract, allk, allk)
        S.tt('dve', FI.ap, FI.ap, E1.ap, ALU.mult, allk, allk)
        bk = allk + [BRE, BIM]
        S.tt('dve', E1.ap, FR.ap, BRE.ap, ALU.mult, bk, bk)
        S.tt('dve', E2.ap, FI.ap, BIM.ap, ALU.mult, bk, bk)
        S.tt('dve', E1.ap, E1.ap, E2.ap, ALU.subtract, bk, bk)
        S.tt('dve', E2.ap, FR.ap, BIM.ap, ALU.mult, bk, bk)
        S.tt('dve', E3.ap, FI.ap, BRE.ap, ALU.mult, bk, bk)
        S.tt('dve', E2.ap, E2.ap, E3.ap, ALU.add, bk, bk)
        S.copy('dve', BRE.ap, E1.ap, bk, bk)
        S.copy('dve', BIM.ap, E2.ap, bk, bk)
        pp = P_S5PP.ap.rearrange("p (j a) -> p j a", j=3)
        sd = P_S5D.ap.rearrange("p (j a) -> p j a", j=4)
        kk = [P_S5PP, P_S5D, P_TMP]
        S.act(sd[:, 0, :], pp[:, 2, :], AF.Exp, kk, kk)
        S.tt('dve', sd[:, 1, :], pp[:, 0, :], sd[:, 0, :], ALU.mult, kk, kk)
        S.act(sd[:, 1, :], sd[:, 1, :], AF.Exp, kk, kk)
        S.tt('dve', sd[:, 2, :], pp[:, 1, :], sd[:, 0, :], ALU.mult, kk, kk)
        iw = IWK.ap[:, 0:16]
        tmp16 = P_TMP.ap[:, 0:16]
        S.ts('dve', iw, sd[:, 2, :], 1.0 / TWO_PI, None, ALU.mult, None, kk, [IWK])
        S.stt(tmp16, iw, -CW1, sd[:, 2, :], ALU.mult, ALU.add, kk + [IWK], kk)
        S.stt(sd[:, 2, :], iw, -CW2, tmp16, ALU.mult, ALU.add, kk + [IWK], kk)
        S.barrier()
        AR.off = 13 * T
        COS = Reg(AR.ap[:, 0:T], "cos")
        SIN = Reg(AR.ap[:, T:2 * T], "sin")
        ANG = Reg(AR.ap[:, 2 * T:3 * T], "ang")
        TMP = Reg(AR.ap[:, 3 * T:4 * T], "tmp")
        BUR = Reg(AR.ap[:, 4 * T:5 * T], "bur")
        BUI = Reg(AR.ap[:, 5 * T:6 * T], "bui")
        WR = Reg(AR.ap[:, 6 * T:7 * T], "wr")
        WI_ = Reg(AR.ap[:, 7 * T:8 * T], "wi")
        T2 = Reg(AR.ap[:, 8 * T:9 * T], "t2")
        U5 = [Reg(AR.alloc(T), "u5_%d" % i) for i in range(NS)]
        XRE = Reg(AR.alloc(T), "xre")
        XIM = Reg(AR.alloc(T), "xim")
        YS = Reg(AR.alloc(T), "ys")
        b3r = BRE.ap.rearrange("p (a m) -> p a m", a=16)
        b3i = BIM.ap.rearrange("p (a m) -> p a m", a=16)
        c3r = CRE.ap.rearrange("p (a m) -> p a m", a=16)
        c3i = CIM.ap.rearrange("p (a m) -> p a m", a=16)
        sv = P_S5V.ap.rearrange("p (a j) -> p a j", a=4)
        YP = [PB[4], PB[5], PB[6], PB[7]]
        for ot in range(4):
            for s in range(NS):
                S.dma('sp', U5[s].ap, d_zT[(ZI_US5 + ot) * 128:(ZI_US5 + ot + 1) * 128, s * T:(s + 1) * T],
                      K_zT[s * 4:(s + 1) * 4], (U5[s],))
            for sti in range(4):
                st = ot * 4 + sti
                S.ts('dve', ANG.ap, C('iota'), sd[:, 2, st:st + 1], None, ALU.mult, None, [K_cst, P_S5D], [ANG])
                sincos(ANG.ap, SIN.ap, COS.ap, TMP.ap, 128, T, [ANG, TMP], [ANG, TMP, SIN, COS])
                for s in range(NS):
                    for q in range(4):
                        sl = slice(q * CH, (q + 1) * CH)
                        pr, pi = PB[q % 2], PB[2 + q % 2]
                        S.mm(pr.ap, b3r[:, st, :], U5[s].ap[:, sl], True, True, [BRE, U5[s]], [pr])
                        S.mm(pi.ap, b3i[:, st, :], U5[s].ap[:, sl], True, True, [BIM, U5[s]], [pi])
                        S.copy('act', BUR.ap[:, sl], pr.ap, [pr], [BUR])
                        S.copy('act', BUI.ap[:, sl], pi.ap, [pi], [BUI])
                    S.tt('dve', WR.ap, BUR.ap, COS.ap, ALU.mult, [BUR, COS], [WR])
                    S.tt('pool', T2.ap, BUI.ap, SIN.ap, ALU.mult, [BUI, SIN], [T2])
                    S.tt('dve', WR.ap, WR.ap, T2.ap, ALU.add, [WR, T2], [WR])
                    S.tt('pool', WI_.ap, BUI.ap, COS.ap, ALU.mult, [BUI, COS], [WI_])
                    S.tt('dve', T2.ap, BUR.ap, SIN.ap, ALU.mult, [BUR, SIN], [T2])
                    S.tt('dve', WI_.ap, WI_.ap, T2.ap, ALU.subtract, [WI_, T2], [WI_])
                    rho = sd[:, 1, st:st + 1].to_broadcast([128, T])
                    S.op('dve', lambda g, o=BUR.ap, a=rho, b=WR.ap: g.tensor_tensor_scan(o, a, b, 0.0, ALU.mult, ALU.add),
                         [WR, P_S5D], [BUR])
                    S.op('dve', lambda g, o=BUI.ap, a=rho, b=WI_.ap: g.tensor_tensor_scan(o, a, b, 0.0, ALU.mult, ALU.add),
                         [WI_, P_S5D], [BUI])
                    S.tt('dve', XRE.ap, BUR.ap, COS.ap, ALU.mult, [BUR, COS], [XRE])
                    S.tt('pool', T2.ap, BUI.ap, SIN.ap, ALU.mult, [BUI, SIN], [T2])
                    S.tt('dve', XRE.ap, XRE.ap, T2.ap, ALU.subtract, [XRE, T2], [XRE])
                    S.tt('pool', XIM.ap, BUI.ap, COS.ap, ALU.mult, [BUI, COS], [XIM])
                    S.tt('dve', T2.ap, BUR.ap, SIN.ap, ALU.mult, [BUR, SIN], [T2])
                    S.stt(XIM.ap, T2.ap, -1.0, XIM.ap, ALU.mult, ALU.subtract, [T2, XIM], [XIM])
                    for q in range(4):
                        sl = slice(q * CH, (q + 1) * CH)
                        yp = YP[q]
                        S.mm(yp.ap, c3r[:, st, :], XRE.ap[:, sl], True, False, [CRE, XRE], [yp])
                        S.mm(yp.ap, c3i[:, st, :], XIM.ap[:, sl], False, True, [CIM, XIM], [yp])
                        ysum = YSUM[s]
                        if sti == 0:
                            S.stt(ysum.ap[:, sl], U5[s].ap[:, sl], sv[:, ot, 0:1], yp.ap, ALU.mult, ALU.add,
                                  [U5[s], P_S5V, yp], [ysum])
                        else:
                            S.tt('dve', ysum.ap[:, sl], ysum.ap[:, sl], yp.ap, ALU.add, [ysum, yp], [ysum])
            for s in range(NS):
                gelu_tanh(YSUM[s].ap, YS.ap, T2.ap, [YSUM[s], T2, YS], [T2, YS])
                S.dma('pool', d_yg[ot * 128:(ot + 1) * 128, s * T:(s + 1) * T], YS.ap, (YS,),
                      K_yg[s * 4:(s + 1) * 4])
        close_stage(G)
        G = open_stage(6000)
        YG = Reg(AR.alloc(4 * CH).rearrange("p (k t) -> p k t", k=4), "yg")
        WG = Reg(AR.alloc(4 * 512).rearrange("p (k n) -> p k n", k=4), "wglu")
        SG = [Reg(AR.alloc(CH), "sg%d" % i) for i in range(2)]
        S.dma('sp', WG.ap, i_wglu[l], (), (WG,))
        for c in range(dbg.get('nch', NCH)):
            s = c // (NCH // NS)
            tk = slice(c * CH, (c + 1) * CH)
            S.dma('sp', YG.ap, d_yg[:, tk].rearrange("(k p) t -> p k t", p=128), (K_yg[c],), (YG,))
            for m in range(4):
                pg = PB[m % 2]
                for k in range(4):
                    S.mm(pg.ap, WG.ap[:, k, m * 128:(m + 1) * 128], YG.ap[:, k, :], k == 0, k == 3, [WG, YG], [pg])
                sg = SG[m % 2]
                S.act(sg.ap, pg.ap, AF.Sigmoid, [pg, P_S5V], [sg], bias=sv[:, m, 1:2], scale=1.0)
                S.tt('dve', sg.ap, sg.ap, YG.ap[:, m, :], ALU.mult, [sg, YG], [sg])
                S.dma('pool', d_yT[3, m * 128:(m + 1) * 128, tk], sg.ap, (sg,), (K_yT[3][s],))
        close_stage(G)

    YSUM = [None, None]

    def stage_s5_wrap(l):
        stage_s5(l)

    def stage_merge(l):
        G = open_stage(24200, 22000)
        Xb, X = chunk_regs("X")
        U2sb, U2s = chunk_regs("U2s")
        U2b, U2 = chunk_regs("U2", RA)
        MGb, MG = chunk_regs("MG", RA)
        YBs = [Reg(AR.alloc(4 * CH).rearrange("p (k t) -> p k t", k=4), "ybs%d" % b) for b in range(2)]
        YB = [Reg(RA.alloc(4 * CH).rearrange("p (k t) -> p k t", k=4), "yb%d" % b) for b in range(4)]
        WGs = [Reg(AR.alloc(8 * 128).rearrange("p (k n) -> p k n", k=8), "wgs%d" % i) for i in range(4)]
        WBs = [Reg(AR.alloc(4 * 128).rearrange("p (k n) -> p k n", k=4), "wbs%d" % i) for i in range(4)]
        WOs = [Reg(AR.alloc(8 * 128).rearrange("p (k n) -> p k n", k=8), "wos%d" % i) for i in range(2)]
        WGt = [Reg(RA.alloc(8 * 128).rearrange("p (k n) -> p k n", k=8), "wgt%d" % i) for i in range(2)]
        WBr = [Reg(RA.alloc(4 * 128).rearrange("p (k n) -> p k n", k=4), "wbr%d" % i) for i in range(2)]
        WO = [Reg(RA.alloc(8 * 128).rearrange("p (k n) -> p k n", k=8), "wo%d" % i) for i in range(2)]
        SG = [Reg(AR.alloc(CH), "sg%d" % i) for i in range(2)]
        TM = [Reg(AR.alloc(CH), "tm%d" % i) for i in range(2)]
        it = 0
        for c in range(dbg.get('nch', NCH)):
            s = c // (NCH // NS)
            tk = slice(c * CH, (c + 1) * CH)
            S.dma('sp', Xb.rearrange("p (k t) -> p k t", k=8), dram_chunk(d_xT, c), (K_xT[c],), X)
            S.dma('sp', U2sb.rearrange("p (k t) -> p k t", k=8), dram_chunk(d_u2T, c), (K_u2T[c],), U2s)
            for k in range(8):
                S.copy('dve' if k % 2 == 0 else 'act', U2[k].ap, U2s[k].ap, [U2s[k]], [U2[k]])
            for b in range(4):
                ybs = YBs[b % 2]
                S.dma('sp', ybs.ap, d_yT[b, :, tk].rearrange("(k p) t -> p k t", p=128), (K_yT[b][s],), (ybs,))
                S.copy('dve' if b % 2 == 0 else 'act', YB[b].ap, ybs.ap, [ybs], [YB[b]])
            for m in range(8):
                for b in range(4):
                    wgs, wbs, wg, wb = WGs[it % 4], WBs[it % 4], WGt[it % 2], WBr[it % 2]
                    S.dma('sp', wgs.ap, i_wgate[l, b, m], (), (wgs,))
                    S.dma('sp', wbs.ap, i_wbr[l, b, m], (), (wbs,))
                    S.copy('act', wg.ap, wgs.ap, [wgs], [wg])
                    S.copy('dve', wb.ap, wbs.ap, [wbs], [wb])
                    pg, pb = PB[it % 2], PB[2 + it % 2]
                    for k in range(8):
                        S.mm(pg.ap, wg.ap[:, k, :], U2[k].ap, k == 0, k == 7, [wg, U2[k]], [pg])
                    for k in range(4):
                        S.mm(pb.ap, wb.ap[:, k, :], YB[b].ap[:, k, :], k == 0, k == 3, [wb, YB[b]], [pb])
                    sg = SG[it % 2]
                    S.act(sg.ap, pg.ap, AF.Sigmoid, [pg], [sg])
                    if b == 0:
                        S.tt('dve', MG[m].ap, sg.ap, pb.ap, ALU.mult, [sg, pb], [MG[m]])
                    else:
                        tm = TM[it % 2]
                        S.tt('dve', tm.ap, sg.ap, pb.ap, ALU.mult, [sg, pb], [tm])
                        S.tt('dve', MG[m].ap, MG[m].ap, tm.ap, ALU.add, [MG[m], tm], [MG[m]])
                    it += 1
            for m in range(8):
                wos, wo = WOs[m % 2], WO[m % 2]
                S.dma('sp', wos.ap, i_wout[l, m], (), (wos,))
                S.copy('dve' if m % 2 == 0 else 'act', wo.ap, wos.ap, [wos], [wo])
                po = PB[4 + m % 2]
                for k in range(8):
                    S.mm(po.ap, wo.ap[:, k, :], MG[k].ap, k == 0, k == 7, [wo, MG[k]], [po])
                S.stt(X[m].ap, po.ap, G3[:, 1, m, s:s + 1], X[m].ap, ALU.mult, ALU.add, [po, P_G, X[m]], [X[m]])
            S.dma('pool', dram_chunk(d_xT, c), Xb.rearrange("p (k t) -> p k t", k=8), X, (K_xT[c],))
        close_stage(G)

    stages = dbg.get('stages')
    for l in range(n_layers):
        def want(n):
            return stages is None or n in stages
        if want('mod'):
            stage_mod(l)
        if want('ffn0'):
            stage_ffn(l, 0)
        if want('win'):
            stage_win(l)
        if want('rglru'):
            stage_rglru(l)
        if want('mla'):
            stage_mla(l)
        if want('dsa'):
            stage_dsa(l)
        if want('s5'):
            stage_s5_wrap(l)
        if want('merge'):
            stage_merge(l)
        if want('ffn1'):
            stage_ffn(l, 1)

    GE = open_stage(4200)
    Xb, X = chunk_regs("X")
    for c in range(dbg.get('nch', NCH)):
        S.dma('sp', Xb.rearrange("p (k t) -> p k t", k=8), dram_chunk(d_xT, c), (K_xT[c],), X)
        S.dma('pool', dram_chunk(o_yT, c), Xb.rearrange("p (k t) -> p k t", k=8), X, ())
    S.emit()
    for g in reversed(GE):
        g.__exit__(None, None, None)
    stack.close()
    return nc, S


def _consts():
    c = np.zeros((128, NCONST), np.float32)

    def put(name, a):
        o, w = CO[name]
        c[:a.shape[0], o:o + a.shape[1]] = a
    put('ident', np.eye(128, dtype=np.float32))
    put('ones', np.ones((128, 128), np.float32))
    bd = np.zeros((128, 128), np.float32)
    bd[:64, :64] = 1
    bd[64:, 64:] = 1
    put('bd64', bd)
    r96 = np.zeros((128, 128), np.float32)
    for i in range(16):
        r96[80 + i, 64 + i] = -1.0
        r96[64 + i, 80 + i] = 1.0
    put('r96', r96)
    r64 = np.zeros((128, 128), np.float32)
    for hb in range(2):
        for i in range(8):
            r64[hb * 64 + 8 + i, hb * 64 + i] = -1.0
            r64[hb * 64 + i, hb * 64 + 8 + i] = 1.0
    put('r64', r64)
    r32 = np.zeros((128, 128), np.float32)
    for hb in range(4):
        for i in range(4):
            r32[hb * 32 + 4 + i, hb * 32 + i] = -1.0
            r32[hb * 32 + i, hb * 32 + 4 + i] = 1.0
    put('r32', r32)
    p = np.arange(128)[:, None]
    f = np.arange(128)[None, :]
    put('tri_kq', (p <= f).astype(np.float32))
    tq = (f <= p).astype(np.float32)
    put('tri_qk', tq)
    put('neg_qk', np.where(f <= p, np.float32(0), np.float32(-BIG)).astype(np.float32))

    def inv(rot):
        return (np.float32(500000.0) ** (-(np.arange(0, rot, 2, dtype=np.float32)) / np.float32(rot))).astype(np.float32)
    im = np.zeros((128, 1), np.float32)
    im[64:80, 0] = inv(32)
    im[80:96, 0] = inv(32)
    put('inv_mla', im)
    idd = np.zeros((128, 1), np.float32)
    for hb in range(2):
        idd[hb * 64:hb * 64 + 8, 0] = inv(16)
        idd[hb * 64 + 8:hb * 64 + 16, 0] = inv(16)
    put('inv_dsa', idd)
    ii = np.zeros((128, 1), np.float32)
    for hb in range(4):
        ii[hb * 32:hb * 32 + 4, 0] = inv(8)
        ii[hb * 32 + 4:hb * 32 + 8, 0] = inv(8)
    put('inv_idx', ii)
    put('iota', np.broadcast_to(np.arange(T, dtype=np.float32)[None, :], (128, T)))
    return c


def _layout_weights(I):
    f = np.float32
    A = lambda a: np.ascontiguousarray(np.asarray(a, dtype=f))
    W = {}
    W['consts'] = _consts()
    W['ada_w'] = A(np.asarray(I['ada_w']).reshape(L_, 8, 128, 18, 512).transpose(0, 3, 2, 1, 4))
    W['ada_b'] = A(np.repeat(np.asarray(I['ada_b']).reshape(L_, 72, 128).transpose(0, 2, 1)[..., None], NS, axis=-1))
    W['norm_g'] = A(np.repeat(np.asarray(I['norm_g']).reshape(L_, 3, 8, 128).transpose(0, 1, 3, 2)[..., None], NS, axis=-1))
    W['w1'] = A(np.asarray(I['ffn_w1']).reshape(L_, 2, 8, 128, 22, 128).transpose(0, 1, 4, 3, 2, 5))
    W['w3'] = A(np.asarray(I['ffn_w3']).reshape(L_, 2, 8, 128, 22, 128).transpose(0, 1, 4, 3, 2, 5))
    W['w2'] = A(np.asarray(I['ffn_w2']).reshape(L_, 2, 22, 128, 8, 128).transpose(0, 1, 4, 3, 2, 5))
    win = np.asarray(I['w_in'])
    wz = np.zeros((L_, NZ, 128, 8, 128), f)
    for zi, (c0, wd) in enumerate(Z_TILES):
        wz[:, zi, :, :, :wd] = win[:, :, c0:c0 + wd].reshape(L_, 8, 128, wd).transpose(0, 2, 1, 3)
    W['win_z'] = wz
    wt = np.concatenate([win[:, :, 2016:2080], win[:, :, 2368:2376]], axis=-1)
    W['win_tok'] = A(wt.reshape(L_, 8, 128, 72).transpose(0, 2, 1, 3))
    W['win_gate'] = A(win[:, :, 2888:].reshape(L_, 8, 128, 4, 8, 128).transpose(0, 3, 4, 2, 1, 5))
    rgp = np.zeros((L_, 128, 4, 8), f)
    cw = np.asarray(I['conv_w'])
    for j in range(4):
        rgp[:, :, :, j] = cw[:, j].reshape(L_, 4, 128).transpose(0, 2, 1)
    for j, n in enumerate(('conv_b', 'rg_ba', 'rg_bx', 'rg_lambda')):
        rgp[:, :, :, 4 + j] = np.asarray(I[n]).reshape(L_, 4, 128).transpose(0, 2, 1)
    W['rg_par'] = rgp
    rgw = np.zeros((L_, 2, 4, 128, 128), f)
    for wi_, n in enumerate(('rg_wa', 'rg_wx')):
        w = np.asarray(I[n])
        for h in range(8):
            ct, o = h // 2, (h % 2) * 64
            rgw[:, wi_, ct, o:o + 64, o:o + 64] = w[:, h]
    W['rg_w'] = rgw
    mn = np.zeros((L_, 128, 3), f)
    mn[:, :, 0:2] = np.asarray(I['mla_q_norm']).reshape(L_, 2, 128).transpose(0, 2, 1)
    mn[:, :, 2] = np.asarray(I['mla_kv_norm'])
    W['mla_norm'] = mn
    perm = np.concatenate([np.arange(32, 96), np.arange(0, 32)])
    wuq = np.asarray(I['mla_w_uq']).reshape(L_, 2, 128, 8, 96)[..., perm]
    W['w_uq'] = A(wuq.transpose(0, 2, 1, 3, 4))
    wukv = np.asarray(I['mla_w_ukv']).reshape(L_, 128, 8, 128)
    W['w_ukv_k'] = A(wukv[..., :64])
    W['w_ukv_v'] = A(wukv[..., 64:].reshape(L_, 128, 512))
    g = np.zeros((L_, 128, 4), f)
    mg = np.asarray(I['mla_qk_gain'])[..., perm]
    g[:, :96, 0] = mg[:, 0]
    g[:, :96, 1] = mg[:, 1]
    dg = np.asarray(I['dsa_qk_gain'])
    g[:, :, 2] = np.tile(dg[:, 0], (1, 2))
    g[:, :, 3] = np.tile(dg[:, 1], (1, 2))
    W['qk_gains'] = g
    lr = np.asarray(I['s5_lambda_re']).reshape(L_, 2048)
    li = np.asarray(I['s5_lambda_im']).reshape(L_, 2048)
    ld = np.repeat(np.asarray(I['s5_log_dt']), 64, axis=1)
    st3 = np.stack([lr, li, ld], axis=1)
    W['s5_bc'] = A(np.broadcast_to(st3[:, :, None, :], (L_, 3, 128, 2048)))
    W['s5_pp'] = A(st3.reshape(L_, 3, 16, 128).transpose(0, 1, 3, 2))
    sb = np.zeros((L_, 2, 128, 16, 128), f)
    scm = np.zeros((L_, 2, 128, 16, 128), f)
    for ri, (bn, cn) in enumerate((('s5_b_re', 's5_c_re'), ('s5_b_im', 's5_c_im'))):
        b = np.asarray(I[bn])
        cc = np.asarray(I[cn])
        for gi in range(32):
            st, half = gi // 2, gi % 2
            r0 = 16 * (gi % 8)
            sb[:, ri, r0:r0 + 16, st, half * 64:(half + 1) * 64] = b[:, gi].transpose(0, 2, 1)
            scm[:, ri, half * 64:(half + 1) * 64, st, r0:r0 + 16] = cc[:, gi].transpose(0, 2, 1)
    W['s5_b'] = sb
    W['s5_c'] = scm
    sv = np.zeros((L_, 128, 4, 2), f)
    sv[..., 0] = np.asarray(I['s5_d']).reshape(L_, 4, 128).transpose(0, 2, 1)
    sv[..., 1] = np.asarray(I['s5_b_glu']).reshape(L_, 4, 128).transpose(0, 2, 1)
    W['s5_vec'] = sv
    W['w_glu'] = A(np.asarray(I['s5_w_glu']).reshape(L_, 4, 128, 512).transpose(0, 2, 1, 3))
    W['w_branch'] = A(np.asarray(I['w_branch']).reshape(L_, 4, 4, 128, 8, 128).transpose(0, 1, 4, 3, 2, 5))
    W['w_out'] = A(np.asarray(I['w_out']).reshape(L_, 8, 128, 8, 128).transpose(0, 3, 2, 1, 4))
    return W


def _core_inputs(I, W, c):
    x = np.asarray(I['x'], dtype=np.float32)[NS * c:NS * (c + 1)]
    m = dict(W)
    m['xT'] = np.ascontiguousarray(x.reshape(NT, D).T)
    cc = np.asarray(I['c'], dtype=np.float32)[NS * c:NS * (c + 1)]
    m['cT'] = np.ascontiguousarray(cc.reshape(NS, 8, 128).transpose(2, 1, 0))
    pos = np.asarray(I['positions']).astype(np.int32)[NS * c:NS * (c + 1)]
    m['posb'] = np.ascontiguousarray(np.broadcast_to(pos[None], (128, NS, T)))
    return m


_CACHE = {}


def kernel(**inputs):
    if 'nc' not in _CACHE:
        _CACHE['nc'] = build_program()[0]
    nc = _CACHE['nc']
    W = _layout_weights(inputs)
    in_maps = [_core_inputs(inputs, W, c) for c in range(8)]
    res = run_bass_kernel_spmd(nc, in_maps, core_ids=list(range(8)))
    out = np.empty((16, T, D), np.float32)
    for c in range(8):
        yT = np.asarray(res.results[c]["yT_out"])
        out[NS * c:NS * (c + 1)] = yT.T.reshape(NS, T, D)
    return out
```

```python
import numpy as np
from contextlib import ExitStack
import concourse.bass as bass
import concourse.mybir as mybir
from concourse.bass_utils import run_bass_kernel_spmd

F32 = mybir.dt.float32
I32 = mybir.dt.int32
F32R = mybir.dt.float32r


def AF32(ap):
    return ap.bitcast(F32)
ALU = mybir.AluOpType
AF = mybir.ActivationFunctionType

L_ = 4
D = 1024
T = 2048
NS = 2
NT = NS * T
CH = 512
NCH = NT // CH
DFF = 2816
EPS = 1e-6
BIG = 1.0e30
PI = float(np.pi)
TWO_PI = float(2 * np.pi)
CW1 = 6.28125
CW2 = float(2 * np.pi - 6.28125)
PI_SAFE = 3.1415925

Z_TILES = ([(0 + 128 * i, 128) for i in range(4)] + [(512 + 128 * i, 128) for i in range(4)]
           + [(1024, 128), (1152, 128), (1280, 128), (1408, 32)]
           + [(1440 + 128 * i, 128) for i in range(4)] + [(1952, 64)]
           + [(2080, 128), (2208, 128), (2336, 32)] + [(2376 + 128 * i, 128) for i in range(4)])
ZI_XRNN, ZI_GATE, ZI_QLAT, ZI_KVLAT, ZI_KPE, ZI_QDSA, ZI_KDSA, ZI_QIDX, ZI_KIDX, ZI_US5 = 0, 4, 8, 10, 11, 12, 16, 17, 19, 20
NZ = len(Z_TILES)

CO = {}
_off = 0
for _n, _w in [('ident', 128), ('ones', 128), ('bd64', 128), ('r96', 128), ('r64', 128), ('r32', 128),
               ('tri_kq', 128), ('tri_qk', 128), ('neg_qk', 128), ('inv_mla', 1), ('inv_dsa', 1),
               ('inv_idx', 1), ('iota', T)]:
    CO[_n] = (_off, _w)
    _off += _w
NCONST = _off


class Reg:
    __slots__ = ('ap', 'name')

    def __init__(self, ap, name=''):
        self.ap = ap
        self.name = name


class Sched:
    ENG = ('pe', 'act', 'dve', 'pool', 'sp')
    EPOCH = 16000
    NSLOT = 12

    def __init__(self, nc, stack):
        self.nc = nc
        self.stack = stack
        self.streams = {e: [] for e in self.ENG}
        self.count = {e: 0 for e in self.ENG}
        self.sems = {e: [] for e in self.ENG}
        self.waited = {e: {} for e in self.ENG}
        self.last_w = {}
        self.readers = {}
        self.semobj = {}
        self.slots = {q: [[self._newsem(), 0] for _ in range(self.NSLOT)] for q in ('sp', 'pool')}
        self.slot_rr = {'sp': 0, 'pool': 0}
        self.all_tokens = []
        self.ninstr = 0

    def _newsem(self):
        s = self.stack.enter_context(self.nc.semaphore())
        sid = len(self.semobj)
        self.semobj[sid] = s
        return sid

    def _esem(self, e, epoch):
        while len(self.sems[e]) <= epoch:
            self.sems[e].append(self._newsem())
        return self.sems[e][epoch]

    def _deps(self, e, reads, writes):
        waits = {}

        def need(tok):
            if tok is None:
                return
            sid, val, prod = tok
            if prod == e and e == 'pe':
                return
            if self.waited[e].get(sid, 0) >= val:
                return
            if waits.get(sid, 0) < val:
                waits[sid] = val

        for k in reads:
            need(self.last_w.get(id(k)))
        for k in writes:
            need(self.last_w.get(id(k)))
            for r in self.readers.get(id(k), ()):
                need(r)
        return waits

    def _commit(self, e, tok, reads, writes, waits):
        for sid, v in waits.items():
            self.waited[e][sid] = v
        for k in reads:
            self.readers.setdefault(id(k), []).append(tok)
        for k in writes:
            self.last_w[id(k)] = tok
            self.readers[id(k)] = []

    def op(self, e, fn, reads=(), writes=()):
        waits = self._deps(e, reads, writes)
        idx = self.count[e]
        sid = self._esem(e, idx // self.EPOCH)
        tok = (sid, idx % self.EPOCH + 1, e)
        self.count[e] += 1
        self._commit(e, tok, reads, writes, waits)
        self.streams[e].append((list(waits.items()), fn, sid, 1))
        self.ninstr += 1

    def dma(self, q, out_ap, in_ap, reads=(), writes=()):
        waits = self._deps(q, reads, writes)
        si = self.slot_rr[q]
        self.slot_rr[q] = (si + 1) % self.NSLOT
        slot = self.slots[q][si]
        if slot[1] + 16 > 30000:
            slot[0] = self._newsem()
            slot[1] = 0
        if slot[1] > 0 and self.waited[q].get(slot[0], 0) < slot[1]:
            waits[slot[0]] = max(waits.get(slot[0], 0), slot[1])
        slot[1] += 16
        tok = (slot[0], slot[1], 'dma_' + q)
        self._commit(q, tok, reads, writes, waits)

        def fn(eng, o=out_ap, i=in_ap):
            return eng.dma_start(out=o, in_=i)
        self.streams[q].append((list(waits.items()), fn, slot[0], 16))
        self.all_tokens.append(tok)
        self.ninstr += 1

    def barrier(self):
        toks = []
        for e in self.ENG:
            idx = self.count[e]
            if idx > 0:
                toks.append((self._esem(e, (idx - 1) // self.EPOCH), (idx - 1) % self.EPOCH + 1, e))
        for q in ('sp', 'pool'):
            for slot in self.slots[q]:
                if slot[1] > 0:
                    toks.append((slot[0], slot[1], 'dma_' + q))
        for e in self.ENG:
            waits = {}
            for sid, val, prod in toks:
                if prod == e:
                    continue
                if self.waited[e].get(sid, 0) >= val:
                    continue
                waits[sid] = max(waits.get(sid, 0), val)
            for sid, v in waits.items():
                self.waited[e][sid] = v
            if waits:
                self.streams[e].append((list(waits.items()), None, None, 0))
        self.last_w.clear()
        self.readers.clear()

    def emit(self):
        nc = self.nc
        self.barrier()
        semobj = self.semobj
        streams = self.streams

        def replay(e, eng):
            for waits, fn, sid, inc in streams[e]:
                for s, v in waits:
                    eng.wait_ge(semobj[s], v)
                if fn is not None:
                    ins = fn(eng)
                    ins.then_inc(semobj[sid], inc)

        with nc.Block() as block:
            @block.tensor
            def _(eng):
                replay('pe', eng)

            @block.scalar
            def _(eng):
                replay('act', eng)

            @block.vector
            def _(eng):
                replay('dve', eng)

            @block.gpsimd
            def _(eng):
                replay('pool', eng)

            @block.sync
            def _(eng):
                replay('sp', eng)

    def mm(self, out, lhsT, rhs, start, stop, reads, writes):
        self.op('pe', lambda g, o=out, l=lhsT, r=rhs, a=start, b=stop: g.matmul(o, l, r, start=a, stop=b),
                reads, writes)

    def transpose(self, out, in_, ident, reads, writes):
        self.op('pe', lambda g, o=out, i=in_, d=ident: g.transpose(o, i, d), reads, writes)

    def act(self, out, in_, func, reads, writes, bias=None, scale=None):
        kw = {}
        if bias is not None:
            kw['bias'] = bias
        if scale is not None:
            kw['scale'] = scale
        self.op('act', lambda g, o=out, i=in_, f=func, k=kw: g.activation(o, i, f, **k), reads, writes)

    def tt(self, e, out, in0, in1, op, reads, writes):
        self.op(e, lambda g, o=out, a=in0, b=in1, p=op: g.tensor_tensor(o, a, b, p), reads, writes)

    def ts(self, e, out, in0, s1, s2, op0, op1, reads, writes):
        if s2 is None:
            self.op(e, lambda g, o=out, a=in0, x=s1, p=op0: g.tensor_scalar(o, a, x, None, p), reads, writes)
        else:
            self.op(e, lambda g, o=out, a=in0, x=s1, y=s2, p=op0, q=op1: g.tensor_scalar(o, a, x, y, p, q),
                    reads, writes)

    def stt(self, out, in0, scalar, in1, op0, op1, reads, writes):
        self.op('dve', lambda g, o=out, a=in0, s=scalar, b=in1, p=op0, q=op1:
                g.scalar_tensor_tensor(o, a, s, b, p, q), reads, writes)

    def copy(self, e, out, in_, reads, writes):
        if e == 'act':
            self.op('act', lambda g, o=out, i=in_: g.copy(o, i), reads, writes)
        else:
            self.op(e, lambda g, o=out, i=in_: g.tensor_copy(o, i), reads, writes)

    def recip(self, out, in_, reads, writes):
        self.op('dve', lambda g, o=out, i=in_: g.reciprocal(o, i), reads, writes)

    def memset(self, e, ap, val, writes):
        self.op(e, lambda g, a=ap, v=val: g.memset(a, v), (), writes)


class Arena:
    def __init__(self, ap, nwords):
        self.ap = ap
        self.n = nwords
        self.off = 0

    def reset(self):
        self.off = 0

    def alloc(self, words, name=''):
        assert self.off + words <= self.n, (name, self.off, words, self.n)
        a = self.ap[:, self.off:self.off + words]
        self.off += words
        return a


def build_program(n_layers=L_, debug=None):
    nc = bass.Bass("TRN2", target_bir_lowering=False)
    stack = ExitStack()

    def din(name, shape, dt=F32):
        return nc.dram_tensor(name, list(shape), dt, kind="ExternalInput").ap()

    dbg = debug or {}
    FB_LIM = dbg.get('fblim', 11)
    M_LIM = dbg.get('mlim', 8)

    def dscr(name, shape, dt=F32):
        kind = "ExternalOutput" if dbg.get(name) else "Internal"
        return nc.dram_tensor(name, list(shape), dt, kind=kind).ap()

    i_xT = din("xT", [D, NT])
    i_cT = din("cT", [128, 8, NS])
    i_pos = din("posb", [128, NS, T], I32)
    i_const = din("consts", [128, NCONST])
    i_adaw = din("ada_w", [L_, 18, 128, 8, 512])
    i_adab = din("ada_b", [L_, 128, 72, NS])
    i_normg = din("norm_g", [L_, 3, 128, 8, NS])
    i_w1 = din("w1", [L_, 2, 22, 128, 8, 128])
    i_w3 = din("w3", [L_, 2, 22, 128, 8, 128])
    i_w2 = din("w2", [L_, 2, 8, 128, 22, 128])
    i_winz = din("win_z", [L_, NZ, 128, 8, 128])
    i_wtok = din("win_tok", [L_, 128, 8, 72])
    i_wgate = din("win_gate", [L_, 4, 8, 128, 8, 128])
    i_rgp = din("rg_par", [L_, 128, 4, 8])
    i_rgw = din("rg_w", [L_, 2, 4, 128, 128])
    i_mlan = din("mla_norm", [L_, 128, 3])
    i_wuq = din("w_uq", [L_, 128, 2, 8, 96])
    i_wukvk = din("w_ukv_k", [L_, 128, 8, 64])
    i_wukvv = din("w_ukv_v", [L_, 128, 512])
    i_gains = din("qk_gains", [L_, 128, 4])
    i_s5bc = din("s5_bc", [L_, 3, 128, 2048])
    i_s5pp = din("s5_pp", [L_, 3, 128, 16])
    i_s5b = din("s5_b", [L_, 2, 128, 16, 128])
    i_s5c = din("s5_c", [L_, 2, 128, 16, 128])
    i_s5v = din("s5_vec", [L_, 128, 4, 2])
    i_wglu = din("w_glu", [L_, 128, 4, 512])
    i_wbr = din("w_branch", [L_, 4, 8, 128, 4, 128])
    i_wout = din("w_out", [L_, 8, 128, 8, 128])
    o_yT = nc.dram_tensor("yT_out", [D, NT], F32, kind="ExternalOutput").ap()

    d_xT = dscr("s_xT", [D, NT])
    d_u2T = dscr("s_u2T", [D, NT])
    d_zT = dscr("s_zT", [NZ * 128, NT])
    d_vw = dscr("s_vw", [NT, 72])
    d_yT = dscr("s_yT", [4, 512, NT])
    d_qm = dscr("s_qm", [8, 96, NT])
    d_km = dscr("s_km", [8, 96, NT])
    d_vm = dscr("s_vm", [NT, 512])
    d_qd = dscr("s_qd", [512, NT])
    d_kd = dscr("s_kd", [64, NT])
    d_yg = dscr("s_yg", [512, NT])
    d_tab = dscr("s_tab", [3, 2, 128, NT])

    def keys(n):
        return [Reg(None, n + str(i)) for i in range(NCH)]
    K_xT, K_u2T, K_zT, K_vw, K_qm, K_km, K_vm, K_qd, K_kd, K_yg = (keys(n) for n in
                                                                  ("xT", "u2T", "zT", "vw", "qm", "km", "vm", "qd", "kd", "yg"))
    K_yT = [[Reg(None, "yT%d_%d" % (b, s)) for s in range(NS)] for b in range(4)]
    K_tab = Reg(None, "tab")

    cst_t = stack.enter_context(nc.sbuf_tensor("cst", [128, NCONST], F32))
    par_t = stack.enter_context(nc.sbuf_tensor("par", [128, 1200], F32))
    iwk_t = stack.enter_context(nc.sbuf_tensor("iwk", [128, T], I32))
    banks = [stack.enter_context(nc.psum_tensor("pb%d" % i, [128, 512], F32)) for i in range(8)]
    PB = [Reg(b[:], "pb%d" % i) for i, b in enumerate(banks)]
    AR = Arena(None, 0)
    RA = Arena(None, 0)
    _uid = [0]

    def open_stage(f32_words, r_words=0):
        _uid[0] += 1
        g1 = nc.sbuf_tensor("fa%d" % _uid[0], [128, f32_words], F32)
        t1 = g1.__enter__()
        AR.ap, AR.n, AR.off = t1[:], f32_words, 0
        guards = [g1]
        if r_words:
            g2 = nc.sbuf_tensor("ra%d" % _uid[0], [128, r_words], F32R)
            t2 = g2.__enter__()
            RA.ap, RA.n, RA.off = t2[:], r_words, 0
            guards.append(g2)
        return guards

    def close_stage(guards):
        S.barrier()
        for g in reversed(guards):
            g.__exit__(None, None, None)
    CST = cst_t[:]
    PAR = par_t[:]
    IWK = Reg(iwk_t[:], "iwk")
    K_cst = Reg(None, "cst")

    S = Sched(nc, stack)

    def C(name, rows=128, c0=0, c1=None):
        o, w = CO[name]
        if c1 is None:
            c1 = w
        return CST[0:rows, o + c0:o + c1]

    S.dma('sp', CST, i_const, (), (K_cst,))

    par_off = [0]

    def palloc(w):
        a = PAR[:, par_off[0]:par_off[0] + w]
        par_off[0] += w
        return a
    P_MOD = Reg(palloc(144), "mod")
    P_ADAB = Reg(palloc(144), "adab")
    P_NG = Reg(palloc(48), "ng")
    P_A = Reg(palloc(48), "A")
    P_G = Reg(palloc(48), "G")
    P_CACT = Reg(palloc(16), "cact")
    P_RG = Reg(palloc(32), "rgp")
    P_RGD = Reg(palloc(16), "rgd")
    P_MLAN = Reg(palloc(3), "mlan")
    P_GAIN = Reg(palloc(4), "gains")
    P_S5PP = Reg(palloc(48), "s5pp")
    P_S5D = Reg(palloc(64), "s5d")
    P_S5V = Reg(palloc(8), "s5v")
    P_TMP = Reg(palloc(64), "ptmp")

    mod3 = P_MOD.ap.rearrange("p (j k s) -> p j k s", j=9, k=8)
    A3 = P_A.ap.rearrange("p (j k s) -> p j k s", j=3, k=8)
    G3 = P_G.ap.rearrange("p (j k s) -> p j k s", j=3, k=8)

    def modcol(j, k, s):
        return mod3[:, j, k, s:s + 1]

    def sincos(ang, sin_out, cos_out, tmp, rows, n, rk, wk):
        iw = IWK.ap[0:rows, 0:n]
        S.ts('dve', iw, ang, 1.0 / TWO_PI, None, ALU.mult, None, rk, [IWK])
        S.stt(tmp, iw, -CW1, ang, ALU.mult, ALU.add, rk + [IWK], wk)
        S.stt(ang, iw, -CW2, tmp, ALU.mult, ALU.add, rk + [IWK], wk)
        S.ts('dve', tmp, ang, PI, -TWO_PI, ALU.is_gt, ALU.mult, rk, wk)
        S.tt('dve', ang, ang, tmp, ALU.add, rk, wk)
        S.ts('dve', tmp, ang, -PI, TWO_PI, ALU.is_lt, ALU.mult, rk, wk)
        S.tt('dve', ang, ang, tmp, ALU.add, rk, wk)
        S.ts('dve', ang, ang, PI_SAFE, -PI_SAFE, ALU.min, ALU.max, rk, wk)
        S.act(sin_out, ang, AF.Sin, rk, wk)
        S.ts('dve', ang, ang, PI / 2, None, ALU.add, None, rk, wk)
        S.ts('dve', tmp, ang, PI, -TWO_PI, ALU.is_gt, ALU.mult, rk, wk)
        S.tt('dve', ang, ang, tmp, ALU.add, rk, wk)
        S.ts('dve', ang, ang, PI_SAFE, -PI_SAFE, ALU.min, ALU.max, rk, wk)
        S.act(cos_out, ang, AF.Sin, rk, wk)

    def rstd_from_psum(ps_ap, out_ap, n_feat, reads, writes):
        S.act(out_ap, ps_ap, AF.Ln, reads, writes, bias=EPS_AP[0:out_ap.shape[0], :], scale=1.0 / n_feat)
        S.act(out_ap, out_ap, AF.Exp, writes, writes, scale=-0.5)

    def gelu_tanh(x, out, t1, rk, wk, eng='pool'):
        S.tt(eng, t1, x, x, ALU.mult, rk, wk)
        S.ts(eng, t1, t1, 0.044715, 1.0, ALU.mult, ALU.add, rk, wk)
        S.tt(eng, t1, t1, x, ALU.mult, rk, wk)
        S.act(t1, t1, AF.Sigmoid, rk, wk, scale=1.5957691216057308)
        S.tt('dve', out, x, t1, ALU.mult, rk, wk)

    EPS_R = Reg(palloc(1), "eps")
    EPS_AP = EPS_R.ap
    S.memset('dve', EPS_AP, EPS, [EPS_R])
    ONE_R = Reg(palloc(1), "one")
    S.memset('dve', ONE_R.ap, 1.0, [ONE_R])

    G0 = open_stage(16000)
    for c in range(dbg.get('nch', NCH)):
        S.dma('sp', d_xT[:, c * CH:(c + 1) * CH], i_xT[:, c * CH:(c + 1) * CH], (), (K_xT[c],))

    R_ang = Reg(AR.alloc(T), "ang")
    R_tmp = Reg(AR.alloc(T), "tmp")
    R_sin = Reg(AR.alloc(T), "sin")
    R_cos = Reg(AR.alloc(T), "cos")
    R_posi = Reg(AR.alloc(NS * T).bitcast(I32).rearrange("p (s t) -> p s t", s=NS), "posi")
    S.dma('sp', R_posi.ap, i_pos, (), (R_posi,))
    for f, inv in enumerate(('inv_mla', 'inv_dsa', 'inv_idx')):
        for s in range(NS):
            S.ts('dve', R_ang.ap, R_posi.ap[:, s, :], C(inv), None, ALU.mult, None, [R_posi, K_cst], [R_ang])
            sincos(R_ang.ap, R_sin.ap, R_cos.ap, R_tmp.ap, 128, T, [R_ang, R_tmp], [R_ang, R_tmp, R_sin, R_cos])
            S.dma('pool', d_tab[f, 0, :, s * T:(s + 1) * T], R_cos.ap, (R_cos,), (K_tab,))
            S.dma('pool', d_tab[f, 1, :, s * T:(s + 1) * T], R_sin.ap, (R_sin,), (K_tab,))
    S.dma('sp', P_CACT.ap.rearrange("p (k s) -> p k s", k=8), i_cT, (), (P_CACT,))
    S.act(P_CACT.ap, P_CACT.ap, AF.Silu, [P_CACT], [P_CACT])
    close_stage(G0)

    def norm_mod(Xr, Ur, UTr, SQr, RSr, j, s):
        ps = PB[0]
        for k in range(8):
            S.act(SQr[k % 2].ap, Xr[k].ap, AF.Square, [Xr[k]], [SQr[k % 2]])
            S.mm(ps.ap, C('ones'), SQr[k % 2].ap, k == 0, k == 7, [SQr[k % 2], K_cst], [ps])
        rstd_from_psum(ps.ap, RSr.ap, D, [ps, EPS_R], [RSr])
        for k in range(8):
            ut = UTr[k % 2]
            S.tt('dve', ut.ap, Xr[k].ap, RSr.ap, ALU.mult, [Xr[k], RSr], [ut])
            S.act(Ur[k].ap, ut.ap, AF.Identity, [ut, P_A, P_MOD], [Ur[k]], bias=modcol(3 * j, k, s),
                  scale=A3[:, j, k, s:s + 1])

    def chunk_regs(name, arena=None):
        base = (arena or AR).alloc(8 * CH, name)
        b3 = base.rearrange("p (k t) -> p k t", k=8)
        return base, [Reg(b3[:, k, :], name + str(k)) for k in range(8)]

    def dram_chunk(d, c):
        return d.rearrange("(k p) t -> p k t", p=128)[:, :, c * CH:(c + 1) * CH]

    def stage_mod(l):
        G = open_stage(8192)
        WB = [Reg(AR.alloc(8 * 512).rearrange("p (k n) -> p k n", k=8), "adaw%d" % i) for i in range(2)]
        cact3 = P_CACT.ap.rearrange("p (k s) -> p k s", k=8)
        S.dma('sp', P_ADAB.ap.rearrange("p (m s) -> p m s", s=NS), i_adab[l], (), (P_ADAB,))
        S.dma('sp', P_NG.ap.rearrange("p (j k s) -> p j k s", j=3, k=8), i_normg[l].rearrange("j p k s -> p j k s"),
              (), (P_NG,))
        ps = PB[1]
        for blk in range(18):
            w = WB[blk % 2]
            S.dma('sp', w.ap, i_adaw[l, blk], (), (w,))
            for mi in range(4):
                mt = blk * 4 + mi
                for k in range(8):
                    S.mm(ps.ap[:, mt * 2:mt * 2 + 2], w.ap[:, k, mi * 128:(mi + 1) * 128], cact3[:, k, :],
                         k == 0, k == 7, [w, P_CACT], [ps])
        S.tt('dve', P_MOD.ap, ps.ap[:, 0:144], P_ADAB.ap, ALU.add, [ps, P_ADAB], [P_MOD])
        ng3 = P_NG.ap.rearrange("p (j k s) -> p j k s", j=3, k=8)
        for j in range(3):
            S.ts('dve', A3[:, j], mod3[:, 3 * j + 1], 1.0, None, ALU.add, None, [P_MOD], [P_A])
            S.tt('dve', A3[:, j], A3[:, j], ng3[:, j], ALU.mult, [P_A, P_NG], [P_A])
            if j == 1:
                S.ts('dve', G3[:, j], mod3[:, 3 * j + 2], 1.0, None, ALU.add, None, [P_MOD], [P_G])
            else:
                S.ts('dve', G3[:, j], mod3[:, 3 * j + 2], 0.5, 0.5, ALU.mult, ALU.add, [P_MOD], [P_G])
        close_stage(G)

    def stage_ffn(l, jf):
        G = open_stage(21504, 25088)
        jn = 0 if jf == 0 else 2
        Xb, X = chunk_regs("X")
        Ub, U = chunk_regs("U", RA)
        UT = [Reg(AR.alloc(CH), "ut%d" % i) for i in range(2)]
        SQ = [Reg(AR.alloc(CH), "sq%d" % i) for i in range(2)]
        RS = Reg(AR.alloc(CH), "rs")
        SA = [Reg(AR.alloc(CH), "sa%d" % i) for i in range(2)]
        H = [Reg(RA.alloc(CH), "h%d" % i) for i in range(22)]
        W1s = [Reg(AR.alloc(8 * 128).rearrange("p (k n) -> p k n", k=8), "w1s%d" % i) for i in range(4)]
        W3s = [Reg(AR.alloc(8 * 128).rearrange("p (k n) -> p k n", k=8), "w3s%d" % i) for i in range(4)]
        W2s = [Reg(AR.alloc(22 * 128).rearrange("p (k n) -> p k n", k=22), "w2s%d" % i) for i in range(2)]
        W1 = [Reg(RA.alloc(8 * 128).rearrange("p (k n) -> p k n", k=8), "w1_%d" % i) for i in range(2)]
        W3 = [Reg(RA.alloc(8 * 128).rearrange("p (k n) -> p k n", k=8), "w3_%d" % i) for i in range(2)]
        W2 = [Reg(RA.alloc(22 * 128).rearrange("p (k n) -> p k n", k=22), "w2_%d" % i) for i in range(2)]
        for c in range(dbg.get('nch', NCH)):
            s = c // (NCH // NS)
            S.dma('sp', Xb.rearrange("p (k t) -> p k t", k=8), dram_chunk(d_xT, c), (K_xT[c],), X)
            norm_mod(X, U, UT, SQ, RS, jn, s)
            for ft in range(22):
                w1s, w3s, w1, w3 = W1s[ft % 4], W3s[ft % 4], W1[ft % 2], W3[ft % 2]
                S.dma('sp', w1s.ap, i_w1[l, jf, ft], (), (w1s,))
                S.dma('sp', w3s.ap, i_w3[l, jf, ft], (), (w3s,))
                S.copy('dve', w1.ap, w1s.ap, [w1s], [w1])
                S.copy('act', w3.ap, w3s.ap, [w3s], [w3])
                pa, pb = PB[2 + ft % 2], PB[4 + ft % 2]
                for k in range(8):
                    S.mm(pa.ap, w1.ap[:, k, :], U[k].ap, k == 0, k == 7, [w1, U[k]], [pa])
                for k in range(8):
                    S.mm(pb.ap, w3.ap[:, k, :], U[k].ap, k == 0, k == 7, [w3, U[k]], [pb])
                sa = SA[ft % 2]
                S.act(sa.ap, pa.ap, AF.Silu, [pa], [sa])
                S.tt('dve', H[ft].ap, sa.ap, pb.ap, ALU.mult, [sa, pb], [H[ft]])
            for m in range(8):
                w2s, w2 = W2s[m % 2], W2[m % 2]
                S.dma('sp', w2s.ap, i_w2[l, jf, m], (), (w2s,))
                S.copy('dve', w2.ap, w2s.ap, [w2s], [w2])
                po = PB[6 + m % 2]
                for kt in range(22):
                    S.mm(po.ap, w2.ap[:, kt, :], H[kt].ap, kt == 0, kt == 21, [w2, H[kt]], [po])
                S.stt(X[m].ap, po.ap, G3[:, jn, m, s:s + 1], X[m].ap, ALU.mult, ALU.add, [po, P_G, X[m]], [X[m]])
            S.dma('pool', dram_chunk(d_xT, c), Xb.rearrange("p (k t) -> p k t", k=8), X, (K_xT[c],))
        close_stage(G)

    def rope_pipeline(raw, P, gcol, rname, bdname, nfeat, tabC, tabS, QG, SQ, RS, T1, psA, psB, out, rk):
        S.act(SQ.ap[0:P], raw.ap[0:P], AF.Square, [raw], [SQ])
        S.mm(psA.ap[0:P], C(bdname, P, 0, P), SQ.ap[0:P], True, True, [SQ, K_cst], [psA])
        rstd_from_psum(psA.ap[0:P], RS.ap[0:P], nfeat, [psA, EPS_R], [RS])
        S.stt(QG.ap[0:P], raw.ap[0:P], gcol, RS.ap[0:P], ALU.mult, ALU.mult, [raw, RS, P_GAIN], [QG])
        S.mm(psB.ap[0:P], C(rname, P, 0, P), QG.ap[0:P], True, True, [QG, K_cst], [psB])
        S.tt('pool', T1.ap[0:P], QG.ap[0:P], tabC.ap[0:P], ALU.mult, [QG, tabC], [T1])
        S.tt('dve', out.ap[0:P], psB.ap[0:P], tabS.ap[0:P], ALU.mult, [psB, tabS], [out])
        S.tt('dve', out.ap[0:P], out.ap[0:P], T1.ap[0:P], ALU.add, [out, T1], [out])

    def rope_steps(raw, P, gcol, rname, bdname, nfeat, tabC, tabS, QG, SQ, RS, T1, psA, psB, out):
        return [
            lambda: S.act(SQ.ap[0:P], raw.ap[0:P], AF.Square, [raw], [SQ]),
            lambda: S.mm(psA.ap[0:P], C(bdname, P, 0, P), SQ.ap[0:P], True, True, [SQ, K_cst], [psA]),
            lambda: S.act(RS.ap[0:P], psA.ap[0:P], AF.Ln, [psA, EPS_R], [RS], bias=EPS_AP[0:P, :], scale=1.0 / nfeat),
            lambda: S.act(RS.ap[0:P], RS.ap[0:P], AF.Exp, [RS], [RS], scale=-0.5),
            lambda: S.stt(QG.ap[0:P], raw.ap[0:P], gcol, RS.ap[0:P], ALU.mult, ALU.mult, [raw, RS, P_GAIN], [QG]),
            lambda: S.mm(psB.ap[0:P], C(rname, P, 0, P), QG.ap[0:P], True, True, [QG, K_cst], [psB]),
            lambda: S.tt('pool', T1.ap[0:P], QG.ap[0:P], tabC.ap[0:P], ALU.mult, [QG, tabC], [T1]),
            lambda: S.tt('dve', out.ap[0:P], psB.ap[0:P], tabS.ap[0:P], ALU.mult, [psB, tabS], [out]),
            lambda: S.tt('dve', out.ap[0:P], out.ap[0:P], T1.ap[0:P], ALU.add, [out, T1], [out]),
        ]

    def interleave(pipes):
        n = max(len(p) for p in pipes)
        for j in range(n):
            for p in pipes:
                if j < len(p):
                    p[j]()

    def stage_win(l):
        G = open_stage(14000, 6200)
        Xb, X = chunk_regs("X")
        Ub, U = chunk_regs("U", RA)
        UT = [Reg(AR.alloc(CH), "ut%d" % i) for i in range(2)]
        SQ = [Reg(AR.alloc(CH), "sq%d" % i) for i in range(2)]
        RS = Reg(AR.alloc(CH), "rs")
        ZS = [Reg(AR.alloc(CH), "zs%d" % i) for i in range(2)]
        ZR = [Reg(AR.alloc(CH), "zr%d" % i) for i in range(2)]
        W = [Reg(AR.alloc(8 * 128).rearrange("p (k n) -> p k n", k=8), "wz%d" % i) for i in range(2)]
        WR_ = [Reg(RA.alloc(8 * 128).rearrange("p (k n) -> p k n", k=8), "wzr%d" % i) for i in range(2)]
        WT = Reg(AR.alloc(8 * 72).rearrange("p (k n) -> p k n", k=8), "wtok")
        VW = [Reg(AR.alloc(72), "vw%d" % i) for i in range(2)]
        TC = Reg(AR.alloc(CH), "tc")
        TS_ = Reg(AR.alloc(CH), "tsn")
        S.dma('sp', WT.ap, i_wtok[l], (), (WT,))
        for c in range(dbg.get('nch', NCH)):
            s = c // (NCH // NS)
            S.dma('sp', Xb.rearrange("p (k t) -> p k t", k=8), dram_chunk(d_xT, c), (K_xT[c],), X)
            S.dma('sp', TC.ap, d_tab[2, 0, :, c * CH:(c + 1) * CH], (K_tab,), (TC,))
            S.dma('sp', TS_.ap, d_tab[2, 1, :, c * CH:(c + 1) * CH], (K_tab,), (TS_,))
            norm_mod(X, U, UT, SQ, RS, 1, s)
            S.dma('pool', dram_chunk(d_u2T, c), AF32(Ub).rearrange("p (k t) -> p k t", k=8), U, (K_u2T[c],))
            for zi, (c0, wd) in enumerate(Z_TILES):
                w = W[zi % 2]
                S.dma('sp', w.ap, i_winz[l, zi], (), (w,))
                pz = PB[1 + zi % 2]
                if wd == 128 and zi not in (ZI_QIDX, ZI_QIDX + 1):
                    wr = WR_[zi % 2]
                    S.copy('dve' if zi % 2 == 0 else 'act', wr.ap, w.ap, [w], [wr])
                    for k in range(8):
                        S.mm(pz.ap, wr.ap[:, k, :], U[k].ap, k == 0, k == 7, [wr, U[k]], [pz])
                else:
                    for k in range(8):
                        S.mm(pz.ap[0:wd], w.ap[:, k, 0:wd], AF32(U[k].ap), k == 0, k == 7, [w, U[k]], [pz])
                zs = ZS[zi % 2]
                S.copy('act', zs.ap[0:wd], pz.ap[0:wd], [pz], [zs])
                if zi in (ZI_QIDX, ZI_QIDX + 1, ZI_KIDX):
                    pr = PB[3 + zi % 2]
                    zr = ZR[zi % 2]
                    S.mm(pr.ap[0:wd], C('r32', wd, 0, wd), zs.ap[0:wd], True, True, [zs, K_cst], [pr])
                    S.tt('dve', zr.ap[0:wd], pr.ap[0:wd], TS_.ap[0:wd], ALU.mult, [pr, TS_], [zr])
                    S.tt('pool', zs.ap[0:wd], zs.ap[0:wd], TC.ap[0:wd], ALU.mult, [zs, TC], [zs])
                    S.tt('dve', zs.ap[0:wd], zs.ap[0:wd], zr.ap[0:wd], ALU.add, [zs, zr], [zs])
                S.dma('pool', d_zT[zi * 128:zi * 128 + wd, c * CH:(c + 1) * CH], zs.ap[0:wd], (zs,), (K_zT[c],))
            for tt_ in range(4):
                pv = PB[5 + tt_ % 2]
                for k in range(8):
                    S.mm(pv.ap[:, 0:72], AF32(U[k].ap[:, tt_ * 128:(tt_ + 1) * 128]), WT.ap[:, k, :], k == 0, k == 7,
                         [WT, U[k]], [pv])
                vw = VW[tt_ % 2]
                S.copy('act', vw.ap, pv.ap[:, 0:72], [pv], [vw])
                t0 = c * CH + tt_ * 128
                S.dma('pool', d_vw[t0:t0 + 128, :], vw.ap, (vw,), (K_vw[c],))
        close_stage(G)

    def stage_rglru(l):
        G = open_stage(20000)
        XR = Reg(AR.alloc(T + 4), "xr")
        GT = Reg(AR.alloc(T), "gt")
        XC = Reg(AR.alloc(T), "xc")
        RR = Reg(AR.alloc(T), "rr")
        IG = Reg(AR.alloc(T), "ig")
        AA = Reg(AR.alloc(T), "aa")
        MM = Reg(AR.alloc(T), "mm")
        T1 = Reg(AR.alloc(T), "t1")
        HH = Reg(AR.alloc(T), "hh")
        WA = Reg(AR.alloc(4 * 128).rearrange("p (c n) -> p c n", c=4), "wa")
        WX = Reg(AR.alloc(4 * 128).rearrange("p (c n) -> p c n", c=4), "wx")
        S.dma('sp', WA.ap, i_rgw[l, 0].rearrange("c p n -> p c n"), (), (WA,))
        S.dma('sp', WX.ap, i_rgw[l, 1].rearrange("c p n -> p c n"), (), (WX,))
        S.dma('sp', P_RG.ap.rearrange("p (c j) -> p c j", c=4), i_rgp[l], (), (P_RG,))
        rg3 = P_RG.ap.rearrange("p (c j) -> p c j", c=4)
        rgd = P_RGD.ap.rearrange("p (j c) -> p j c", j=4)
        S.act(rgd[:, 0, :], rg3[:, :, 7], AF.Exp, [P_RG], [P_RGD], scale=-1.0)
        S.act(rgd[:, 0, :], rgd[:, 0, :], AF.Ln, [P_RGD], [P_RGD], bias=ONE_R.ap, scale=1.0)
        S.ts('dve', rgd[:, 1, :], rgd[:, 0, :], -8.0, None, ALU.mult, None, [P_RGD], [P_RGD])
        S.ts('dve', rgd[:, 2, :], rgd[:, 0, :], -16.0, None, ALU.mult, None, [P_RGD], [P_RGD])
        S.memset('dve', XR.ap[:, 0:4], 0.0, [XR])
        for s in range(NS):
            zk = K_zT[s * 4:(s + 1) * 4]
            for ct in range(4):
                S.dma('sp', XR.ap[:, 4:4 + T], d_zT[(ZI_XRNN + ct) * 128:(ZI_XRNN + ct + 1) * 128, s * T:(s + 1) * T],
                      zk, (XR,))
                S.dma('sp', GT.ap, d_zT[(ZI_GATE + ct) * 128:(ZI_GATE + ct + 1) * 128, s * T:(s + 1) * T], zk, (GT,))
                S.act(XC.ap, XR.ap[:, 4:4 + T], AF.Identity, [XR, P_RG], [XC], bias=rg3[:, ct, 4:5], scale=rg3[:, ct, 3:4])
                for j in range(3):
                    S.stt(XC.ap, XR.ap[:, 1 + j:1 + j + T], rg3[:, ct, j:j + 1], XC.ap, ALU.mult, ALU.add,
                          [XR, P_RG, XC], [XC])
                for q in range(4):
                    pr, pi = PB[q % 2], PB[2 + q % 2]
                    sl = slice(q * CH, (q + 1) * CH)
                    S.mm(pr.ap, WA.ap[:, ct, :], XC.ap[:, sl], True, True, [WA, XC], [pr])
                    S.mm(pi.ap, WX.ap[:, ct, :], XC.ap[:, sl], True, True, [WX, XC], [pi])
                    S.act(RR.ap[:, sl], pr.ap, AF.Sigmoid, [pr, P_RG], [RR], bias=rg3[:, ct, 5:6], scale=1.0)
                    S.act(IG.ap[:, sl], pi.ap, AF.Sigmoid, [pi, P_RG], [IG], bias=rg3[:, ct, 6:7], scale=1.0)
                S.act(AA.ap, RR.ap, AF.Exp, [RR, P_RGD], [AA], scale=rgd[:, 1, ct:ct + 1])
                S.act(MM.ap, RR.ap, AF.Exp, [RR, P_RGD], [MM], scale=rgd[:, 2, ct:ct + 1])
                S.act(MM.ap, MM.ap, AF.Sqrt, [MM, ONE_R], [MM], bias=ONE_R.ap, scale=-1.0)
                S.tt('dve', MM.ap, MM.ap, IG.ap, ALU.mult, [MM, IG], [MM])
                S.tt('dve', MM.ap, MM.ap, XC.ap, ALU.mult, [MM, XC], [MM])
                S.op('dve', lambda g, o=HH.ap, a=AA.ap, b=MM.ap: g.tensor_tensor_scan(o, a, b, 0.0, ALU.mult, ALU.add),
                     [AA, MM], [HH])
                gelu_tanh(GT.ap, IG.ap, T1.ap, [GT, T1, IG], [T1, IG], eng='dve')
                S.tt('dve', HH.ap, HH.ap, IG.ap, ALU.mult, [HH, IG], [HH])
                S.dma('pool', d_yT[0, ct * 128:(ct + 1) * 128, s * T:(s + 1) * T], HH.ap, (HH,), (K_yT[0][s],))
        close_stage(G)

    def stage_mla(l):
        G = open_stage(18500)
        QL = [Reg(AR.alloc(CH), "ql%d" % i) for i in range(2)]
        KVL = Reg(AR.alloc(CH), "kvl")
        QN = [Reg(AR.alloc(CH), "qn%d" % i) for i in range(2)]
        KVN = Reg(AR.alloc(CH), "kvn")
        SQ = Reg(AR.alloc(CH), "sq")
        RS = Reg(AR.alloc(CH), "rs")
        SQ2 = [Reg(AR.alloc(CH), "sq2_%d" % i) for i in range(2)]
        RS2 = [Reg(AR.alloc(CH), "rs2_%d" % i) for i in range(2)]
        QG2 = [Reg(AR.alloc(CH), "qg2_%d" % i) for i in range(2)]
        T12 = [Reg(AR.alloc(CH), "t12_%d" % i) for i in range(2)]
        RAW = [Reg(AR.alloc(CH), "raw%d" % i) for i in range(2)]
        KRAW = [Reg(AR.alloc(CH), "kraw%d" % i) for i in range(2)]
        KPE = Reg(AR.alloc(CH), "kpe")
        QG = Reg(AR.alloc(CH), "qg")
        T1 = Reg(AR.alloc(CH), "t1")
        OUT = [Reg(AR.alloc(CH), "out%d" % i) for i in range(2)]
        TC = Reg(AR.alloc(CH), "tc")
        TS_ = Reg(AR.alloc(CH), "tsn")
        VS = [Reg(AR.alloc(512), "vs%d" % i) for i in range(2)]
        WUQ = Reg(AR.alloc(2 * 8 * 96).rearrange("p (k h n) -> p k h n", k=2, h=8), "wuq")
        WK = Reg(AR.alloc(8 * 64).rearrange("p (h n) -> p h n", h=8), "wk")
        WV = Reg(AR.alloc(512), "wv")
        S.dma('sp', WUQ.ap, i_wuq[l], (), (WUQ,))
        S.dma('sp', WK.ap, i_wukvk[l], (), (WK,))
        S.dma('sp', WV.ap, i_wukvv[l], (), (WV,))
        S.dma('sp', P_MLAN.ap, i_mlan[l], (), (P_MLAN,))
        S.dma('sp', P_GAIN.ap, i_gains[l], (), (P_GAIN,))
        for c in range(dbg.get('nch', NCH)):
            tk = slice(c * CH, (c + 1) * CH)
            for k in range(2):
                S.dma('sp', QL[k].ap, d_zT[(ZI_QLAT + k) * 128:(ZI_QLAT + k + 1) * 128, tk], (K_zT[c],), (QL[k],))
            S.dma('sp', KVL.ap, d_zT[ZI_KVLAT * 128:(ZI_KVLAT + 1) * 128, tk], (K_zT[c],), (KVL,))
            S.dma('sp', KPE.ap[64:96], d_zT[ZI_KPE * 128:ZI_KPE * 128 + 32, tk], (K_zT[c],), (KPE,))
            S.dma('sp', TC.ap, d_tab[0, 0, :, tk], (K_tab,), (TC,))
            S.dma('sp', TS_.ap, d_tab[0, 1, :, tk], (K_tab,), (TS_,))
            ps = PB[0]
            for k in range(2):
                S.act(SQ.ap, QL[k].ap, AF.Square, [QL[k]], [SQ])
                S.mm(ps.ap, C('ones'), SQ.ap, k == 0, k == 1, [SQ, K_cst], [ps])
            rstd_from_psum(ps.ap, RS.ap, 256, [ps, EPS_R], [RS])
            for k in range(2):
                S.stt(QN[k].ap, QL[k].ap, P_MLAN.ap[:, k:k + 1], RS.ap, ALU.mult, ALU.mult, [QL[k], P_MLAN, RS], [QN[k]])
            S.act(SQ.ap, KVL.ap, AF.Square, [KVL], [SQ])
            S.mm(ps.ap, C('ones'), SQ.ap, True, True, [SQ, K_cst], [ps])
            rstd_from_psum(ps.ap, RS.ap, 128, [ps, EPS_R], [RS])
            S.stt(KVN.ap, KVL.ap, P_MLAN.ap[:, 2:3], RS.ap, ALU.mult, ALU.mult, [KVL, P_MLAN, RS], [KVN])
            for h in range(8):
                pq = PB[1]
                raw = RAW[h % 2]
                for k in range(2):
                    S.mm(pq.ap[0:96], WUQ.ap[:, k, h, :], QN[k].ap, k == 0, k == 1, [WUQ, QN[k]], [pq])
                pk = PB[2]
                kraw = KRAW[h % 2]
                S.mm(pk.ap[0:64], WK.ap[:, h, :], KVN.ap, True, True, [WK, KVN], [pk])
                S.copy('act', raw.ap[0:96], pq.ap[0:96], [pq], [raw])
                S.copy('act', kraw.ap[0:64], pk.ap[0:64], [pk], [kraw])
                S.copy('dve', kraw.ap[64:96], KPE.ap[64:96], [KPE], [kraw])
                oq, ok_ = OUT[0], OUT[1]
                interleave([
                    rope_steps(raw, 96, P_GAIN.ap[0:96, 0:1], 'r96', 'ones', 96, TC, TS_, QG2[0], SQ2[0], RS2[0], T12[0], PB[3], PB[4], oq),
                    rope_steps(kraw, 96, P_GAIN.ap[0:96, 1:2], 'r96', 'ones', 96, TC, TS_, QG2[1], SQ2[1], RS2[1], T12[1], PB[5], PB[6], ok_),
                ])
                S.dma('pool', d_qm[h, :, tk], oq.ap[0:96], (oq,), (K_qm[c],))
                S.dma('pool', d_km[h, :, tk], ok_.ap[0:96], (ok_,), (K_km[c],))
            for tt_ in range(4):
                pv = PB[7]
                S.mm(pv.ap, KVN.ap[:, tt_ * 128:(tt_ + 1) * 128], WV.ap, True, True, [KVN, WV], [pv])
                vs = VS[tt_ % 2]
                S.copy('act', vs.ap, pv.ap, [pv], [vs])
                t0 = c * CH + tt_ * 128
                S.dma('pool', d_vm[t0:t0 + 128, :], vs.ap, (vs,), (K_vm[c],))
        close_stage(G)
        G = open_stage(8000, 12000)
        KHs = Reg(AR.alloc(T), "khs")
        QHs = Reg(AR.alloc(T), "qhs")
        V1s = Reg(AR.alloc(16 * 65).rearrange("p (b n) -> p b n", b=16), "v1s")
        KH = [Reg(RA.alloc(T), "kh%d" % i) for i in range(2)]
        QH = [Reg(RA.alloc(T), "qh%d" % i) for i in range(2)]
        V1 = [Reg(RA.alloc(16 * 65).rearrange("p (b n) -> p b n", b=16), "v1_%d" % i) for i in range(2)]
        PT = [Reg(RA.alloc(CH), "pt%d" % i) for i in range(3)]
        OS = Reg(AR.alloc(CH), "os")
        RD = Reg(AR.alloc(CH), "rd")
        YO = [Reg(AR.alloc(CH), "yo%d" % i) for i in range(2)]
        S.memset('dve', V1s.ap[:, :, 64:65], 1.0, [V1s])
        sc = 96.0 ** -0.5
        it = 0
        for s in range(NS):
            ks = K_km[s * 4:(s + 1) * 4]
            qs = K_qm[s * 4:(s + 1) * 4]
            vsk = K_vm[s * 4:(s + 1) * 4]
            for h in range(8):
                kh, qh, v1 = KH[h % 2], QH[h % 2], V1[h % 2]
                S.dma('sp', KHs.ap[0:96], d_km[h, :, s * T:(s + 1) * T], ks, (KHs,))
                S.dma('sp', QHs.ap[0:96], d_qm[h, :, s * T:(s + 1) * T], qs, (QHs,))
                S.dma('sp', V1s.ap[:, :, 0:64],
                      d_vm[s * T:(s + 1) * T, h * 64:(h + 1) * 64].rearrange("(b p) n -> p b n", p=128), vsk, (V1s,))
                S.copy('dve', kh.ap[0:96], KHs.ap[0:96], [KHs], [kh])
                S.copy('dve', qh.ap[0:96], QHs.ap[0:96], [QHs], [qh])
                S.copy('dve', v1.ap, V1s.ap, [V1s], [v1])
                for qc in range(4):
                    po = PB[6 + qc % 2]
                    nkb = 4 * (qc + 1)
                    for kb in range(nkb):
                        j = max(0, kb - 4 * qc)
                        q0 = qc * CH + j * 128
                        n = CH - j * 128
                        pS = PB[it % 3]
                        pt = PT[it % 3]
                        it += 1
                        S.mm(pS.ap[:, 0:n], kh.ap[0:96, kb * 128:(kb + 1) * 128], qh.ap[0:96, q0:q0 + n], True, True,
                             [kh, qh], [pS])
                        S.act(pt.ap[:, 0:n], pS.ap[:, 0:n], AF.Exp, [pS], [pt], scale=sc)
                        if kb >= 4 * qc:
                            S.tt('pool', pt.ap[:, 0:128], pt.ap[:, 0:128], C('tri_kq'), ALU.mult, [pt, K_cst], [pt])
                        S.mm(po.ap[0:65, j * 128:CH], v1.ap[:, kb, :], pt.ap[:, 0:n], kb == 0, kb == nkb - 1,
                             [v1, pt], [po])
                    S.copy('act', OS.ap[0:65], po.ap[0:65], [po], [OS])
                    pd = PB[3 + qc % 2]
                    S.mm(pd.ap[0:64], C('ones', 128, 0, 64)[64:65], OS.ap[64:65], True, True, [OS, K_cst], [pd])
                    S.recip(RD.ap[0:64], pd.ap[0:64], [pd], [RD])
                    yo = YO[qc % 2]
                    S.tt('dve', yo.ap[0:64], OS.ap[0:64], RD.ap[0:64], ALU.mult, [OS, RD], [yo])
                    S.dma('pool', d_yT[1, h * 64:(h + 1) * 64, s * T + qc * CH:s * T + (qc + 1) * CH], yo.ap[0:64],
                          (yo,), (K_yT[1][s],))
        close_stage(G)

    def stage_dsa(l):
        G = open_stage(8000)
        QD = [Reg(AR.alloc(CH), "qd%d" % i) for i in range(2)]
        SQ2 = [Reg(AR.alloc(CH), "sq%d" % i) for i in range(2)]
        RS2 = [Reg(AR.alloc(CH), "rs%d" % i) for i in range(2)]
        QG2 = [Reg(AR.alloc(CH), "qg%d" % i) for i in range(2)]
        T12 = [Reg(AR.alloc(CH), "t1%d" % i) for i in range(2)]
        OUT = [Reg(AR.alloc(CH), "out%d" % i) for i in range(2)]
        TC = Reg(AR.alloc(CH), "tc")
        TS_ = Reg(AR.alloc(CH), "tsn")
        S.dma('sp', P_GAIN.ap, i_gains[l], (), (P_GAIN,))
        for c in range(dbg.get('nch', NCH)):
            tk = slice(c * CH, (c + 1) * CH)
            S.dma('sp', TC.ap, d_tab[1, 0, :, tk], (K_tab,), (TC,))
            S.dma('sp', TS_.ap, d_tab[1, 1, :, tk], (K_tab,), (TS_,))
            for k0 in (0, 2, 4):
                pipes, stores = [], []
                for k in (k0, k0 + 1):
                    if k > 4:
                        continue
                    u = k % 2
                    qd, o = QD[u], OUT[u]
                    if k < 4:
                        S.dma('sp', qd.ap, d_zT[(ZI_QDSA + k) * 128:(ZI_QDSA + k + 1) * 128, tk], (K_zT[c],), (qd,))
                        pipes.append(rope_steps(qd, 128, P_GAIN.ap[:, 2:3], 'r64', 'bd64', 64, TC, TS_, QG2[u], SQ2[u], RS2[u], T12[u],
                                                PB[2 * u], PB[2 * u + 1], o))
                        stores.append((d_qd[k * 128:(k + 1) * 128, tk], o.ap, o, K_qd[c]))
                    else:
                        S.dma('sp', qd.ap[0:64], d_zT[ZI_KDSA * 128:ZI_KDSA * 128 + 64, tk], (K_zT[c],), (qd,))
                        pipes.append(rope_steps(qd, 64, P_GAIN.ap[0:64, 3:4], 'r64', 'bd64', 64, TC, TS_, QG2[u], SQ2[u], RS2[u], T12[u],
                                                PB[2 * u], PB[2 * u + 1], o))
                        stores.append((d_kd[:, tk], o.ap[0:64], o, K_kd[c]))
                interleave(pipes)
                for dst, src, reg, key in stores:
                    S.dma('pool', dst, src, (reg,), (key,))
        close_stage(G)
        G = open_stage(33600, 2700)
        _uid[0] += 1
        gm = nc.sbuf_tensor("mtb%d" % _uid[0], [128, 2 * 16 * CH], mybir.dt.bfloat16)
        mt_t = gm.__enter__()
        G.append(gm)
        mt4 = mt_t[:].rearrange("p (i b n) -> p i b n", i=2, b=16)
        MTs = [Reg(mt4[:, i], "mt%d" % i) for i in range(2)]
        KD2 = Reg(AR.alloc(T), "kd2")
        QI = Reg(AR.alloc(2 * T).rearrange("p (k t) -> p k t", k=2), "qi")
        KIB = Reg(AR.alloc(4 * T).rearrange("p (h t) -> p h t", h=4), "kib")
        V1s = Reg(AR.alloc(16 * 65).rearrange("p (b n) -> p b n", b=16), "v1s")
        V1 = Reg(RA.alloc(16 * 65).rearrange("p (b n) -> p b n", b=16), "v1")
        WI = Reg(AR.alloc(16 * 8).rearrange("p (b n) -> p b n", b=16), "wi")
        SCs = [Reg(AR.alloc(T), "sc%d" % i) for i in range(2)]
        WKs = [Reg(AR.alloc(T), "wk%d" % i) for i in range(2)]
        RL = [Reg(AR.alloc(CH), "rl%d" % i) for i in range(2)]
        M8s = [Reg(AR.alloc(8), "m8_%d" % i) for i in range(2)]
        THRs = [Reg(AR.alloc(1), "thr%d" % i) for i in range(2)]
        QC = Reg(AR.alloc(4 * CH).rearrange("p (k t) -> p k t", k=4), "qc")
        PT = [Reg(RA.alloc(CH), "pt%d" % i) for i in range(3)]
        OSs = [Reg(AR.alloc(CH), "os%d" % i) for i in range(8)]
        RD = Reg(AR.alloc(CH), "rd")
        YO = [Reg(AR.alloc(CH), "yo%d" % i) for i in range(2)]
        S.memset('dve', V1s.ap[:, :, 64:65], 1.0, [V1s])
        S.memset('dve', KIB.ap, 0.0, [KIB])
        sc = 64.0 ** -0.5
        itc = [0]

        def idx_pair(s, qc, pair):
            qbs = [qc * 4 + pair * 2, qc * 4 + pair * 2 + 1]
            for i, qb in enumerate(qbs):
                SC = SCs[i]
                for kb in range(qb + 1):
                    for k in range(2):
                        pr = PB[k]
                        rl = RL[k]
                        S.mm(pr.ap, QI.ap[:, k, qb * 128:(qb + 1) * 128], KIB.ap[:, :, kb * 128:(kb + 1) * 128],
                             True, True, [QI, KIB], [pr])
                        S.act(rl.ap, pr.ap, AF.Relu, [pr], [rl])
                        for h4 in range(4):
                            hh = k * 4 + h4
                            dst = SC.ap[:, kb * 128:(kb + 1) * 128]
                            if hh == 0:
                                S.ts('dve', dst, rl.ap[:, 0:128], WI.ap[:, qb, 0:1], None, ALU.mult, None,
                                     [rl, WI], [SC])
                            else:
                                S.stt(dst, rl.ap[:, h4 * 128:(h4 + 1) * 128], WI.ap[:, qb, hh:hh + 1], dst,
                                      ALU.mult, ALU.add, [rl, WI, SC], [SC])
                dg = SC.ap[:, qb * 128:(qb + 1) * 128]
                S.tt('dve', dg, dg, C('tri_qk'), ALU.mult, [SC, K_cst], [SC])
                S.tt('dve', dg, dg, C('neg_qk'), ALU.add, [SC, K_cst], [SC])
            srcs = [SCs[0], SCs[1]]
            for r in range(32):
                for i, qb in enumerate(qbs):
                    if qb < 2:
                        continue
                    nk = (qb + 1) * 128
                    S.op('dve', lambda g, o=M8s[i].ap, a=srcs[i].ap[:, 0:nk]: g.max(o, a), [srcs[i]], [M8s[i]])
                    if r < 31:
                        S.op('dve', lambda g, o=WKs[i].ap[:, 0:nk], a=M8s[i].ap, v=srcs[i].ap[:, 0:nk]:
                             g.match_replace(o, a, v, -BIG), [M8s[i], srcs[i]], [WKs[i]])
                        srcs[i] = WKs[i]
            for i, qb in enumerate(qbs):
                nk = (qb + 1) * 128
                SC, WK_, M8, THR = SCs[i], WKs[i], M8s[i], THRs[i]
                if qb >= 2:
                    S.ts('dve', THR.ap, M8.ap[:, 7:8], -0.5 * BIG, None, ALU.max, None, [M8], [THR])
                else:
                    S.memset('dve', THR.ap, -0.5 * BIG, [THR])
                S.ts('dve', WK_.ap[:, 0:nk], SC.ap[:, 0:nk], THR.ap, None, ALU.is_ge, None, [SC, THR], [WK_])

        def idx_pair_B(s, qc, pair):
            MT = MTs[qc % 2]
            for i in range(2):
                qb = qc * 4 + pair * 2 + i
                ql = qb - qc * 4
                WK_ = WKs[i]
                for kb in range(qb + 1):
                    pt_ = PB[2 + kb % 2]
                    S.transpose(pt_.ap[:, 0:128], WK_.ap[:, kb * 128:(kb + 1) * 128], C('ident'), [WK_, K_cst], [pt_])
                    S.copy('act', MT.ap[:, kb, ql * 128:(ql + 1) * 128], pt_.ap[:, 0:128], [pt_], [MT])

        def attn_heads(s, qc, heads):
            MT = MTs[qc % 2]
            nkb = 4 * (qc + 1)
            for h in heads:
                p0 = (h % 2) * 64
                po = PB[6 + h % 2]
                for kb in range(nkb):
                    j = max(0, kb - 4 * qc)
                    n = CH - j * 128
                    pS = PB[4 + itc[0] % 2]
                    pt = PT[itc[0] % 3]
                    itc[0] += 1
                    S.mm(pS.ap[:, 0:n], KD2.ap[p0:p0 + 64, kb * 128:(kb + 1) * 128],
                         QC.ap[p0:p0 + 64, h // 2, j * 128:CH], True, True, [KD2, QC], [pS])
                    S.act(pt.ap[:, 0:n], pS.ap[:, 0:n], AF.Exp, [pS], [pt], scale=sc)
                    S.tt('pool', pt.ap[:, 0:n], pt.ap[:, 0:n], MT.ap[:, kb, j * 128:CH], ALU.mult, [pt, MT], [pt])
                    S.mm(po.ap[0:65, j * 128:CH], V1.ap[:, kb, :], pt.ap[:, 0:n], kb == 0, kb == nkb - 1,
                         [V1, pt], [po])
                S.copy('act', OSs[h].ap[0:65], po.ap[0:65], [po], [OSs[h]])

        def attn_norm(s, qc):
            for h in range(8):
                OS = OSs[h]
                pd = PB[2 + h % 2]
                S.mm(pd.ap[0:64], C('ones', 128, 0, 64)[64:65], OS.ap[64:65], True, True, [OS, K_cst], [pd])
                S.recip(RD.ap[0:64], pd.ap[0:64], [pd], [RD])
                yo = YO[h % 2]
                S.tt('dve', yo.ap[0:64], OS.ap[0:64], RD.ap[0:64], ALU.mult, [OS, RD], [yo])
                S.dma('pool', d_yT[2, h * 64:(h + 1) * 64, s * T + qc * CH:s * T + (qc + 1) * CH], yo.ap[0:64],
                      (yo,), (K_yT[2][s],))

        for s in range(NS):
            zk = K_zT[s * 4:(s + 1) * 4]
            ts_ = slice(s * T, (s + 1) * T)
            S.dma('sp', KD2.ap[0:64], d_kd[:, ts_], K_kd[s * 4:(s + 1) * 4], (KD2,))
            S.dma('sp', KD2.ap[64:128], d_kd[:, ts_], K_kd[s * 4:(s + 1) * 4], (KD2,))
            for k in range(2):
                S.dma('sp', QI.ap[:, k, :], d_zT[(ZI_QIDX + k) * 128:(ZI_QIDX + k + 1) * 128, ts_], zk, (QI,))
            for h4 in range(4):
                S.dma('sp', KIB.ap[h4 * 32:(h4 + 1) * 32, h4, :], d_zT[ZI_KIDX * 128:ZI_KIDX * 128 + 32, ts_], zk, (KIB,))
            vwk = K_vw[s * 4:(s + 1) * 4]
            S.dma('sp', V1s.ap[:, :, 0:64], d_vw[ts_, 0:64].rearrange("(b p) n -> p b n", p=128), vwk, (V1s,))
            S.copy('pool', V1.ap, V1s.ap, [V1s], [V1])
            S.dma('sp', WI.ap, d_vw[ts_, 64:72].rearrange("(b p) n -> p b n", p=128), vwk, (WI,))
            for pair in range(2):
                idx_pair(s, 0, pair)
                idx_pair_B(s, 0, pair)
            for qc in range(4):
                S.dma('sp', QC.ap, d_qd[:, s * T + qc * CH:s * T + (qc + 1) * CH].rearrange("(k p) t -> p k t", p=128),
                      K_qd[s * 4:(s + 1) * 4], (QC,))
                nxt = qc + 1 < 4
                if nxt:
                    idx_pair(s, qc + 1, 0)
                attn_heads(s, qc, range(0, 4))
                if nxt:
                    idx_pair_B(s, qc + 1, 0)
                    idx_pair(s, qc + 1, 1)
                attn_heads(s, qc, range(4, 8))
                if nxt:
                    idx_pair_B(s, qc + 1, 1)
                attn_norm(s, qc)
        close_stage(G)

    def stage_s5(l):
        G = open_stage(44000)
        YSUM[0] = Reg(AR.ap[:, AR.n - 2 * T:AR.n - T], "ysum0")
        YSUM[1] = Reg(AR.ap[:, AR.n - T:AR.n], "ysum1")
        LR = Reg(AR.alloc(T), "lr")
        LI = Reg(AR.alloc(T), "li")
        DT = Reg(AR.alloc(T), "dt")
        E1 = Reg(AR.alloc(T), "e1")
        E2 = Reg(AR.alloc(T), "e2")
        E3 = Reg(AR.alloc(T), "e3")
        E4 = Reg(AR.alloc(T), "e4")
        FR = Reg(AR.alloc(T), "fr")
        FI = Reg(AR.alloc(T), "fi")
        BRE = Reg(AR.alloc(T), "bre")
        BIM = Reg(AR.alloc(T), "bim")
        CRE = Reg(AR.alloc(T), "cre")
        CIM = Reg(AR.alloc(T), "cim")
        allk = [LR, LI, DT, E1, E2, E3, E4, FR, FI]
        S.dma('sp', LR.ap, i_s5bc[l, 0], (), (LR,))
        S.dma('sp', LI.ap, i_s5bc[l, 1], (), (LI,))
        S.dma('sp', DT.ap, i_s5bc[l, 2], (), (DT,))
        S.dma('sp', BRE.ap.rearrange("p (a m) -> p a m", a=16), i_s5b[l, 0], (), (BRE,))
        S.dma('sp', BIM.ap.rearrange("p (a m) -> p a m", a=16), i_s5b[l, 1], (), (BIM,))
        S.dma('sp', CRE.ap.rearrange("p (a m) -> p a m", a=16), i_s5c[l, 0], (), (CRE,))
        S.dma('sp', CIM.ap.rearrange("p (a m) -> p a m", a=16), i_s5c[l, 1], (), (CIM,))
        S.dma('sp', P_S5PP.ap.rearrange("p (j a) -> p j a", j=3), i_s5pp[l].rearrange("j p a -> p j a"), (), (P_S5PP,))
        S.dma('sp', P_S5V.ap.rearrange("p (a j) -> p a j", a=4), i_s5v[l], (), (P_S5V,))
        S.act(DT.ap, DT.ap, AF.Exp, allk, allk)
        S.tt('dve', E1.ap, LR.ap, DT.ap, ALU.mult, allk, allk)
        S.act(E1.ap, E1.ap, AF.Exp, allk, allk)
        S.tt('dve', E2.ap, LI.ap, DT.ap, ALU.mult, allk, allk)
        sincos(E2.ap, E3.ap, E4.ap, FR.ap, 128, T, allk, allk)
        S.tt('dve', E3.ap, E3.ap, E1.ap, ALU.mult, allk, allk)
        S.tt('dve', E4.ap, E4.ap, E1.ap, ALU.mult, allk, allk)
        S.ts('dve', E4.ap, E4.ap, -1.0, None, ALU.add, None, allk, allk)
        S.tt('dve', E1.ap, LR.ap, LR.ap, ALU.mult, allk, allk)
        S.tt('dve', E2.ap, LI.ap, LI.ap, ALU.mult, allk, allk)
        S.tt('dve', E1.ap, E1.ap, E2.ap, ALU.add, allk, allk)
        S.recip(E1.ap, E1.ap, allk, allk)
        S.tt('dve', FR.ap, E4.ap, LR.ap, ALU.mult, allk, allk)
        S.tt('dve', E2.ap, E3.ap, LI.ap, ALU.mult, allk, allk)
        S.tt('dve', FR.ap, FR.ap, E2.ap, ALU.add, allk, allk)
        S.tt('dve', FR.ap, FR.ap, E1.ap, ALU.mult, allk, allk)
        S.tt('dve', FI.ap, E3.ap, LR.ap, ALU.mult, allk, allk)
        S.tt('dve', E2.ap, E4.ap, LI.ap, ALU.mult, allk, allk)
        S.tt('dve', FI.ap, FI.ap, E2.ap, ALU.subtract, allk, allk)
        S.tt('dve', FI.ap, FI.ap, E1.ap, ALU.mult, allk, allk)
        bk = allk + [BRE, BIM]
        S.tt('dve', E1.ap, FR.ap, BRE.ap, ALU.mult, bk, bk)
        S.tt('dve', E2.ap, FI.ap, BIM.ap, ALU.mult, bk, bk)
        S.tt('dve', E1.ap, E1.ap, E2.ap, ALU.subtract, bk, bk)
        S.tt('dve', E2.ap, FR.ap, BIM.ap, ALU.mult, bk, bk)
        S.tt('dve', E3.ap, FI.ap, BRE.ap, ALU.mult, bk, bk)
        S.tt('dve', E2.ap, E2.ap, E3.ap, ALU.add, bk, bk)
        S.copy('dve', BRE.ap, E1.ap, bk, bk)
        S.copy('dve', BIM.ap, E2.ap, bk, bk)
        pp = P_S5PP.ap.rearrange("p (j a) -> p j a", j=3)
        sd = P_S5D.ap.rearrange("p (j a) -> p j a", j=4)
        kk = [P_S5PP, P_S5D, P_TMP]
        S.act(sd[:, 0, :], pp[:, 2, :], AF.Exp, kk, kk)
        S.tt('dve', sd[:, 1, :], pp[:, 0, :], sd[:, 0, :], ALU.mult, kk, kk)
        S.act(sd[:, 1, :], sd[:, 1, :], AF.Exp, kk, kk)
        S.tt('dve', sd[:, 2, :], pp[:, 1, :], sd[:, 0, :], ALU.mult, kk, kk)
        iw = IWK.ap[:, 0:16]
        tmp16 = P_TMP.ap[:, 0:16]
        S.ts('dve', iw, sd[:, 2, :], 1.0 / TWO_PI, None, ALU.mult, None, kk, [IWK])
        S.stt(tmp16, iw, -CW1, sd[:, 2, :], ALU.mult, ALU.add, kk + [IWK], kk)
        S.stt(sd[:, 2, :], iw, -CW2, tmp16, ALU.mult, ALU.add, kk + [IWK], kk)
        S.barrier()
        AR.off = 13 * T
        COS = Reg(AR.ap[:, 0:T], "cos")
        SIN = Reg(AR.ap[:, T:2 * T], "sin")
        ANG = Reg(AR.ap[:, 2 * T:3 * T], "ang")
        TMP = Reg(AR.ap[:, 3 * T:4 * T], "tmp")
        BUR = Reg(AR.ap[:, 4 * T:5 * T], "bur")
        BUI = Reg(AR.ap[:, 5 * T:6 * T], "bui")
        WR = Reg(AR.ap[:, 6 * T:7 * T], "wr")
        WI_ = Reg(AR.ap[:, 7 * T:8 * T], "wi")
        T2 = Reg(AR.ap[:, 8 * T:9 * T], "t2")
        U5 = [Reg(AR.alloc(T), "u5_%d" % i) for i in range(NS)]
        XRE = Reg(AR.alloc(T), "xre")
        XIM = Reg(AR.alloc(T), "xim")
        YS = Reg(AR.alloc(T), "ys")
        b3r = BRE.ap.rearrange("p (a m) -> p a m", a=16)
        b3i = BIM.ap.rearrange("p (a m) -> p a m", a=16)
        c3r = CRE.ap.rearrange("p (a m) -> p a m", a=16)
        c3i = CIM.ap.rearrange("p (a m) -> p a m", a=16)
        sv = P_S5V.ap.rearrange("p (a j) -> p a j", a=4)
        YP = [PB[4], PB[5], PB[6], PB[7]]
        for ot in range(4):
            for s in range(NS):
                S.dma('sp', U5[s].ap, d_zT[(ZI_US5 + ot) * 128:(ZI_US5 + ot + 1) * 128, s * T:(s + 1) * T],
                      K_zT[s * 4:(s + 1) * 4], (U5[s],))
            for sti in range(4):
                st = ot * 4 + sti
                S.ts('dve', ANG.ap, C('iota'), sd[:, 2, st:st + 1], None, ALU.mult, None, [K_cst, P_S5D], [ANG])
                sincos(ANG.ap, SIN.ap, COS.ap, TMP.ap, 128, T, [ANG, TMP], [ANG, TMP, SIN, COS])
                for s in range(NS):
                    for q in range(4):
                        sl = slice(q * CH, (q + 1) * CH)
                        pr, pi = PB[q % 2], PB[2 + q % 2]
                        S.mm(pr.ap, b3r[:, st, :], U5[s].ap[:, sl], True, True, [BRE, U5[s]], [pr])
                        S.mm(pi.ap, b3i[:, st, :], U5[s].ap[:, sl], True, True, [BIM, U5[s]], [pi])
                        S.copy('act', BUR.ap[:, sl], pr.ap, [pr], [BUR])
                        S.copy('act', BUI.ap[:, sl], pi.ap, [pi], [BUI])
                    S.tt('dve', WR.ap, BUR.ap, COS.ap, ALU.mult, [BUR, COS], [WR])
                    S.tt('pool', T2.ap, BUI.ap, SIN.ap, ALU.mult, [BUI, SIN], [T2])
                    S.tt('dve', WR.ap, WR.ap, T2.ap, ALU.add, [WR, T2], [WR])
                    S.tt('pool', WI_.ap, BUI.ap, COS.ap, ALU.mult, [BUI, COS], [WI_])
                    S.tt('dve', T2.ap, BUR.ap, SIN.ap, ALU.mult, [BUR, SIN], [T2])
                    S.tt('dve', WI_.ap, WI_.ap, T2.ap, ALU.subtract, [WI_, T2], [WI_])
                    rho = sd[:, 1, st:st + 1].to_broadcast([128, T])
                    S.op('dve', lambda g, o=BUR.ap, a=rho, b=WR.ap: g.tensor_tensor_scan(o, a, b, 0.0, ALU.mult, ALU.add),
                         [WR, P_S5D], [BUR])
                    S.op('dve', lambda g, o=BUI.ap, a=rho, b=WI_.ap: g.tensor_tensor_scan(o, a, b, 0.0, ALU.mult, ALU.add),
                         [WI_, P_S5D], [BUI])
                    S.tt('dve', XRE.ap, BUR.ap, COS.ap, ALU.mult, [BUR, COS], [XRE])
                    S.tt('pool', T2.ap, BUI.ap, SIN.ap, ALU.mult, [BUI, SIN], [T2])
                    S.tt('dve', XRE.ap, XRE.ap, T2.ap, ALU.subtract, [XRE, T2], [XRE])
                    S.tt('pool', XIM.ap, BUI.ap, COS.ap, ALU.mult, [BUI, COS], [XIM])
                    S.tt('dve', T2.ap, BUR.ap, SIN.ap, ALU.mult, [BUR, SIN], [T2])
                    S.stt(XIM.ap, T2.ap, -1.0, XIM.ap, ALU.mult, ALU.subtract, [T2, XIM], [XIM])
                    for q in range(4):
                        sl = slice(q * CH, (q + 1) * CH)
                        yp = YP[q]
                        S.mm(yp.ap, c3r[:, st, :], XRE.ap[:, sl], True, False, [CRE, XRE], [yp])
                        S.mm(yp.ap, c3i[:, st, :], XIM.ap[:, sl], False, True, [CIM, XIM], [yp])
                        ysum = YSUM[s]
                        if sti == 0:
                            S.stt(ysum.ap[:, sl], U5[s].ap[:, sl], sv[:, ot, 0:1], yp.ap, ALU.mult, ALU.add,
                                  [U5[s], P_S5V, yp], [ysum])
                        else:
                            S.tt('dve', ysum.ap[:, sl], ysum.ap[:, sl], yp.ap, ALU.add, [ysum, yp], [ysum])
            for s in range(NS):
                gelu_tanh(YSUM[s].ap, YS.ap, T2.ap, [YSUM[s], T2, YS], [T2, YS])
                S.dma('pool', d_yg[ot * 128:(ot + 1) * 128, s * T:(s + 1) * T], YS.ap, (YS,),
                      K_yg[s * 4:(s + 1) * 4])
        close_stage(G)
        G = open_stage(6000)
        YG = Reg(AR.alloc(4 * CH).rearrange("p (k t) -> p k t", k=4), "yg")
        WG = Reg(AR.alloc(4 * 512).rearrange("p (k n) -> p k n", k=4), "wglu")
        SG = [Reg(AR.alloc(CH), "sg%d" % i) for i in range(2)]
        S.dma('sp', WG.ap, i_wglu[l], (), (WG,))
        for c in range(dbg.get('nch', NCH)):
            s = c // (NCH // NS)
            tk = slice(c * CH, (c + 1) * CH)
            S.dma('sp', YG.ap, d_yg[:, tk].rearrange("(k p) t -> p k t", p=128), (K_yg[c],), (YG,))
            for m in range(4):
                pg = PB[m % 2]
                for k in range(4):
                    S.mm(pg.ap, WG.ap[:, k, m * 128:(m + 1) * 128], YG.ap[:, k, :], k == 0, k == 3, [WG, YG], [pg])
                sg = SG[m % 2]
                S.act(sg.ap, pg.ap, AF.Sigmoid, [pg, P_S5V], [sg], bias=sv[:, m, 1:2], scale=1.0)
                S.tt('dve', sg.ap, sg.ap, YG.ap[:, m, :], ALU.mult, [sg, YG], [sg])
                S.dma('pool', d_yT[3, m * 128:(m + 1) * 128, tk], sg.ap, (sg,), (K_yT[3][s],))
        close_stage(G)

    YSUM = [None, None]

    def stage_s5_wrap(l):
        stage_s5(l)

    def stage_merge(l):
        G = open_stage(24200, 22000)
        Xb, X = chunk_regs("X")
        U2sb, U2s = chunk_regs("U2s")
        U2b, U2 = chunk_regs("U2", RA)
        MGb, MG = chunk_regs("MG", RA)
        YBs = [Reg(AR.alloc(4 * CH).rearrange("p (k t) -> p k t", k=4), "ybs%d" % b) for b in range(2)]
        YB = [Reg(RA.alloc(4 * CH).rearrange("p (k t) -> p k t", k=4), "yb%d" % b) for b in range(4)]
        WGs = [Reg(AR.alloc(8 * 128).rearrange("p (k n) -> p k n", k=8), "wgs%d" % i) for i in range(4)]
        WBs = [Reg(AR.alloc(4 * 128).rearrange("p (k n) -> p k n", k=4), "wbs%d" % i) for i in range(4)]
        WOs = [Reg(AR.alloc(8 * 128).rearrange("p (k n) -> p k n", k=8), "wos%d" % i) for i in range(2)]
        WGt = [Reg(RA.alloc(8 * 128).rearrange("p (k n) -> p k n", k=8), "wgt%d" % i) for i in range(2)]
        WBr = [Reg(RA.alloc(4 * 128).rearrange("p (k n) -> p k n", k=4), "wbr%d" % i) for i in range(2)]
        WO = [Reg(RA.alloc(8 * 128).rearrange("p (k n) -> p k n", k=8), "wo%d" % i) for i in range(2)]
        SG = [Reg(AR.alloc(CH), "sg%d" % i) for i in range(2)]
        TM = [Reg(AR.alloc(CH), "tm%d" % i) for i in range(2)]
        it = 0
        for c in range(dbg.get('nch', NCH)):
            s = c // (NCH // NS)
            tk = slice(c * CH, (c + 1) * CH)
            S.dma('sp', Xb.rearrange("p (k t) -> p k t", k=8), dram_chunk(d_xT, c), (K_xT[c],), X)
            S.dma('sp', U2sb.rearrange("p (k t) -> p k t", k=8), dram_chunk(d_u2T, c), (K_u2T[c],), U2s)
            for k in range(8):
                S.copy('dve' if k % 2 == 0 else 'act', U2[k].ap, U2s[k].ap, [U2s[k]], [U2[k]])
            for b in range(4):
                ybs = YBs[b % 2]
                S.dma('sp', ybs.ap, d_yT[b, :, tk].rearrange("(k p) t -> p k t", p=128), (K_yT[b][s],), (ybs,))
                S.copy('dve' if b % 2 == 0 else 'act', YB[b].ap, ybs.ap, [ybs], [YB[b]])
            for m in range(8):
                for b in range(4):
                    wgs, wbs, wg, wb = WGs[it % 4], WBs[it % 4], WGt[it % 2], WBr[it % 2]
                    S.dma('sp', wgs.ap, i_wgate[l, b, m], (), (wgs,))
                    S.dma('sp', wbs.ap, i_wbr[l, b, m], (), (wbs,))
                    S.copy('act', wg.ap, wgs.ap, [wgs], [wg])
                    S.copy('dve', wb.ap, wbs.ap, [wbs], [wb])
                    pg, pb = PB[it % 2], PB[2 + it % 2]
                    for k in range(8):
                        S.mm(pg.ap, wg.ap[:, k, :], U2[k].ap, k == 0, k == 7, [wg, U2[k]], [pg])
                    for k in range(4):
                        S.mm(pb.ap, wb.ap[:, k, :], YB[b].ap[:, k, :], k == 0, k == 3, [wb, YB[b]], [pb])
                    sg = SG[it % 2]
                    S.act(sg.ap, pg.ap, AF.Sigmoid, [pg], [sg])
                    if b == 0:
                        S.tt('dve', MG[m].ap, sg.ap, pb.ap, ALU.mult, [sg, pb], [MG[m]])
                    else:
                        tm = TM[it % 2]
                        S.tt('dve', tm.ap, sg.ap, pb.ap, ALU.mult, [sg, pb], [tm])
                        S.tt('dve', MG[m].ap, MG[m].ap, tm.ap, ALU.add, [MG[m], tm], [MG[m]])
                    it += 1
            for m in range(8):
                wos, wo = WOs[m % 2], WO[m % 2]
                S.dma('sp', wos.ap, i_wout[l, m], (), (wos,))
                S.copy('dve' if m % 2 == 0 else 'act', wo.ap, wos.ap, [wos], [wo])
                po = PB[4 + m % 2]
                for k in range(8):
                    S.mm(po.ap, wo.ap[:, k, :], MG[k].ap, k == 0, k == 7, [wo, MG[k]], [po])
                S.stt(X[m].ap, po.ap, G3[:, 1, m, s:s + 1], X[m].ap, ALU.mult, ALU.add, [po, P_G, X[m]], [X[m]])
            S.dma('pool', dram_chunk(d_xT, c), Xb.rearrange("p (k t) -> p k t", k=8), X, (K_xT[c],))
        close_stage(G)

    stages = dbg.get('stages')
    for l in range(n_layers):
        def want(n):
            return stages is None or n in stages
        if want('mod'):
            stage_mod(l)
        if want('ffn0'):
            stage_ffn(l, 0)
        if want('win'):
            stage_win(l)
        if want('rglru'):
            stage_rglru(l)
        if want('mla'):
            stage_mla(l)
        if want('dsa'):
            stage_dsa(l)
        if want('s5'):
            stage_s5_wrap(l)
        if want('merge'):
            stage_merge(l)
        if want('ffn1'):
            stage_ffn(l, 1)

    GE = open_stage(4200)
    Xb, X = chunk_regs("X")
    for c in range(dbg.get('nch', NCH)):
        S.dma('sp', Xb.rearrange("p (k t) -> p k t", k=8), dram_chunk(d_xT, c), (K_xT[c],), X)
        S.dma('pool', dram_chunk(o_yT, c), Xb.rearrange("p (k t) -> p k t", k=8), X, ())
    S.emit()
    for g in reversed(GE):
        g.__exit__(None, None, None)
    stack.close()
    return nc, S


def _consts():
    c = np.zeros((128, NCONST), np.float32)

    def put(name, a):
        o, w = CO[name]
        c[:a.shape[0], o:o + a.shape[1]] = a
    put('ident', np.eye(128, dtype=np.float32))
    put('ones', np.ones((128, 128), np.float32))
    bd = np.zeros((128, 128), np.float32)
    bd[:64, :64] = 1
    bd[64:, 64:] = 1
    put('bd64', bd)
    r96 = np.zeros((128, 128), np.float32)
    for i in range(16):
        r96[80 + i, 64 + i] = -1.0
        r96[64 + i, 80 + i] = 1.0
    put('r96', r96)
    r64 = np.zeros((128, 128), np.float32)
    for hb in range(2):
        for i in range(8):
            r64[hb * 64 + 8 + i, hb * 64 + i] = -1.0
            r64[hb * 64 + i, hb * 64 + 8 + i] = 1.0
    put('r64', r64)
    r32 = np.zeros((128, 128), np.float32)
    for hb in range(4):
        for i in range(4):
            r32[hb * 32 + 4 + i, hb * 32 + i] = -1.0
            r32[hb * 32 + i, hb * 32 + 4 + i] = 1.0
    put('r32', r32)
    p = np.arange(128)[:, None]
    f = np.arange(128)[None, :]
    put('tri_kq', (p <= f).astype(np.float32))
    tq = (f <= p).astype(np.float32)
    put('tri_qk', tq)
    put('neg_qk', np.where(f <= p, np.float32(0), np.float32(-BIG)).astype(np.float32))

    def inv(rot):
        return (np.float32(500000.0) ** (-(np.arange(0, rot, 2, dtype=np.float32)) / np.float32(rot))).astype(np.float32)
    im = np.zeros((128, 1), np.float32)
    im[64:80, 0] = inv(32)
    im[80:96, 0] = inv(32)
    put('inv_mla', im)
    idd = np.zeros((128, 1), np.float32)
    for hb in range(2):
        idd[hb * 64:hb * 64 + 8, 0] = inv(16)
        idd[hb * 64 + 8:hb * 64 + 16, 0] = inv(16)
    put('inv_dsa', idd)
    ii = np.zeros((128, 1), np.float32)
    for hb in range(4):
        ii[hb * 32:hb * 32 + 4, 0] = inv(8)
        ii[hb * 32 + 4:hb * 32 + 8, 0] = inv(8)
    put('inv_idx', ii)
    put('iota', np.broadcast_to(np.arange(T, dtype=np.float32)[None, :], (128, T)))
    return c


def _layout_weights(I):
    f = np.float32
    A = lambda a: np.ascontiguousarray(np.asarray(a, dtype=f))
    W = {}
    W['consts'] = _consts()
    W['ada_w'] = A(np.asarray(I['ada_w']).reshape(L_, 8, 128, 18, 512).transpose(0, 3, 2, 1, 4))
    W['ada_b'] = A(np.repeat(np.asarray(I['ada_b']).reshape(L_, 72, 128).transpose(0, 2, 1)[..., None], NS, axis=-1))
    W['norm_g'] = A(np.repeat(np.asarray(I['norm_g']).reshape(L_, 3, 8, 128).transpose(0, 1, 3, 2)[..., None], NS, axis=-1))
    W['w1'] = A(np.asarray(I['ffn_w1']).reshape(L_, 2, 8, 128, 22, 128).transpose(0, 1, 4, 3, 2, 5))
    W['w3'] = A(np.asarray(I['ffn_w3']).reshape(L_, 2, 8, 128, 22, 128).transpose(0, 1, 4, 3, 2, 5))
    W['w2'] = A(np.asarray(I['ffn_w2']).reshape(L_, 2, 22, 128, 8, 128).transpose(0, 1, 4, 3, 2, 5))
    win = np.asarray(I['w_in'])
    wz = np.zeros((L_, NZ, 128, 8, 128), f)
    for zi, (c0, wd) in enumerate(Z_TILES):
        wz[:, zi, :, :, :wd] = win[:, :, c0:c0 + wd].reshape(L_, 8, 128, wd).transpose(0, 2, 1, 3)
    W['win_z'] = wz
    wt = np.concatenate([win[:, :, 2016:2080], win[:, :, 2368:2376]], axis=-1)
    W['win_tok'] = A(wt.reshape(L_, 8, 128, 72).transpose(0, 2, 1, 3))
    W['win_gate'] = A(win[:, :, 2888:].reshape(L_, 8, 128, 4, 8, 128).transpose(0, 3, 4, 2, 1, 5))
    rgp = np.zeros((L_, 128, 4, 8), f)
    cw = np.asarray(I['conv_w'])
    for j in range(4):
        rgp[:, :, :, j] = cw[:, j].reshape(L_, 4, 128).transpose(0, 2, 1)
    for j, n in enumerate(('conv_b', 'rg_ba', 'rg_bx', 'rg_lambda')):
        rgp[:, :, :, 4 + j] = np.asarray(I[n]).reshape(L_, 4, 128).transpose(0, 2, 1)
    W['rg_par'] = rgp
    rgw = np.zeros((L_, 2, 4, 128, 128), f)
    for wi_, n in enumerate(('rg_wa', 'rg_wx')):
        w = np.asarray(I[n])
        for h in range(8):
            ct, o = h // 2, (h % 2) * 64
            rgw[:, wi_, ct, o:o + 64, o:o + 64] = w[:, h]
    W['rg_w'] = rgw
    mn = np.zeros((L_, 128, 3), f)
    mn[:, :, 0:2] = np.asarray(I['mla_q_norm']).reshape(L_, 2, 128).transpose(0, 2, 1)
    mn[:, :, 2] = np.asarray(I['mla_kv_norm'])
    W['mla_norm'] = mn
    perm = np.concatenate([np.arange(32, 96), np.arange(0, 32)])
    wuq = np.asarray(I['mla_w_uq']).reshape(L_, 2, 128, 8, 96)[..., perm]
    W['w_uq'] = A(wuq.transpose(0, 2, 1, 3, 4))
    wukv = np.asarray(I['mla_w_ukv']).reshape(L_, 128, 8, 128)
    W['w_ukv_k'] = A(wukv[..., :64])
    W['w_ukv_v'] = A(wukv[..., 64:].reshape(L_, 128, 512))
    g = np.zeros((L_, 128, 4), f)
    mg = np.asarray(I['mla_qk_gain'])[..., perm]
    g[:, :96, 0] = mg[:, 0]
    g[:, :96, 1] = mg[:, 1]
    dg = np.asarray(I['dsa_qk_gain'])
    g[:, :, 2] = np.tile(dg[:, 0], (1, 2))
    g[:, :, 3] = np.tile(dg[:, 1], (1, 2))
    W['qk_gains'] = g
    lr = np.asarray(I['s5_lambda_re']).reshape(L_, 2048)
    li = np.asarray(I['s5_lambda_im']).reshape(L_, 2048)
    ld = np.repeat(np.asarray(I['s5_log_dt']), 64, axis=1)
    st3 = np.stack([lr, li, ld], axis=1)
    W['s5_bc'] = A(np.broadcast_to(st3[:, :, None, :], (L_, 3, 128, 2048)))
    W['s5_pp'] = A(st3.reshape(L_, 3, 16, 128).transpose(0, 1, 3, 2))
    sb = np.zeros((L_, 2, 128, 16, 128), f)
    scm = np.zeros((L_, 2, 128, 16, 128), f)
    for ri, (bn, cn) in enumerate((('s5_b_re', 's5_c_re'), ('s5_b_im', 's5_c_im'))):
        b = np.asarray(I[bn])
        cc = np.asarray(I[cn])
        for gi in range(32):
            st, half = gi // 2, gi % 2
            r0 = 16 * (gi % 8)
            sb[:, ri, r0:r0 + 16, st, half * 64:(half + 1) * 64] = b[:, gi].transpose(0, 2, 1)
            scm[:, ri, half * 64:(half + 1) * 64, st, r0:r0 + 16] = cc[:, gi].transpose(0, 2, 1)
    W['s5_b'] = sb
    W['s5_c'] = scm
    sv = np.zeros((L_, 128, 4, 2), f)
    sv[..., 0] = np.asarray(I['s5_d']).reshape(L_, 4, 128).transpose(0, 2, 1)
    sv[..., 1] = np.asarray(I['s5_b_glu']).reshape(L_, 4, 128).transpose(0, 2, 1)
    W['s5_vec'] = sv
    W['w_glu'] = A(np.asarray(I['s5_w_glu']).reshape(L_, 4, 128, 512).transpose(0, 2, 1, 3))
    W['w_branch'] = A(np.asarray(I['w_branch']).reshape(L_, 4, 4, 128, 8, 128).transpose(0, 1, 4, 3, 2, 5))
    W['w_out'] = A(np.asarray(I['w_out']).reshape(L_, 8, 128, 8, 128).transpose(0, 3, 2, 1, 4))
    return W


def _core_inputs(I, W, c):
    x = np.asarray(I['x'], dtype=np.float32)[NS * c:NS * (c + 1)]
    m = dict(W)
    m['xT'] = np.ascontiguousarray(x.reshape(NT, D).T)
    cc = np.asarray(I['c'], dtype=np.float32)[NS * c:NS * (c + 1)]
    m['cT'] = np.ascontiguousarray(cc.reshape(NS, 8, 128).transpose(2, 1, 0))
    pos = np.asarray(I['positions']).astype(np.int32)[NS * c:NS * (c + 1)]
    m['posb'] = np.ascontiguousarray(np.broadcast_to(pos[None], (128, NS, T)))
    return m


_CACHE = {}


def kernel(**inputs):
    if 'nc' not in _CACHE:
        _CACHE['nc'] = build_program()[0]
    nc = _CACHE['nc']
    W = _layout_weights(inputs)
    in_maps = [_core_inputs(inputs, W, c) for c in range(8)]
    res = run_bass_kernel_spmd(nc, in_maps, core_ids=list(range(8)))
    out = np.empty((16, T, D), np.float32)
    for c in range(8):
        yT = np.asarray(res.results[c]["yT_out"])
        out[NS * c:NS * (c + 1)] = yT.T.reshape(NS, T, D)
    return out
```
